# Optimizing a Trainium2 kernel written in Bass

```python
import math
import jax, jax.numpy as jnp
from jax import lax
import numpy as np

D_MODEL = 1024
BATCH = 1
SEQ = 16384
DEPTH = 1
DEC_BATCH = 32
DEC_SEQ = 8
PAST_LEN = 16384
PAGE_SIZE = 128

HEAD_DIM = 64
FOX_HEADS = D_MODEL // 128
NSA_HEADS = D_MODEL // 128
NSA_KV_GROUPS = 2
NSA_HPG = NSA_HEADS // NSA_KV_GROUPS
CMP_BLOCK = 32
CMP_STRIDE = 16
CMP_HIDDEN = 2 * HEAD_DIM
SEL_BLOCK = 64
N_SEL = 16
WINDOW = 512
REL_BUCKETS = 32
REL_MAX_DIST = 128
D_FF = 4 * D_MODEL
Q_BLOCK = 128
EPS = 1e-6
FORGET_BIAS_INIT = 3.0
FOX_W = FOX_HEADS * HEAD_DIM
NSA_W = NSA_HEADS * HEAD_DIM
KV_W = NSA_KV_GROUPS * HEAD_DIM
IN_SIZES = (FOX_W, FOX_W, FOX_W, FOX_HEADS, NSA_W, 6 * KV_W, 3 * NSA_HEADS, 2 * D_MODEL)
IN_W = sum(IN_SIZES)

kernel_name = 'fox_nsa_parallel_hybrid_step'


def rmsnorm(x, g):
    xf = x.astype(jnp.float32)
    y = xf * lax.rsqrt(jnp.mean(xf * xf, axis=-1, keepdims=True) + EPS)
    return y.astype(x.dtype) * g


def masked_softmax(s, mask, axis):
    s = jnp.where(mask, s, -jnp.inf)
    m = jnp.max(s, axis=axis, keepdims=True)
    m = jnp.where(jnp.isfinite(m), m, 0.0)
    p = jnp.where(mask, jnp.exp(s - m), 0.0)
    return p / jnp.maximum(jnp.sum(p, axis=axis, keepdims=True), 1e-30)


def t5_bucket(dist):
    d = jnp.maximum(dist, 0)
    exact = REL_BUCKETS // 2
    far = exact + (jnp.log(jnp.maximum(d, 1).astype(jnp.float32) / exact)
                   / math.log(REL_MAX_DIST / exact) * (REL_BUCKETS - exact)).astype(jnp.int32)
    return jnp.where(d < exact, d, jnp.minimum(far, REL_BUCKETS - 1))


def group_bias(rel_bias, dist):
    b = rel_bias[t5_bucket(dist)].reshape(dist.shape + (NSA_KV_GROUPS, NSA_HPG))
    return jnp.moveaxis(b, 1, 3).astype(jnp.float32)


def ada_mod(c, w_ada, b_ada):
    m = jax.nn.silu(c) @ w_ada + b_ada
    return jnp.moveaxis(m.reshape(c.shape[0], 6, 1, D_MODEL), 1, 0)


def compress(rows, pe, w1, w2):
    n_chunk = rows.shape[0] // CMP_STRIDE
    r = CMP_BLOCK // CMP_STRIDE
    n_blk = n_chunk - r + 1
    chunks = rows.reshape(n_chunk, CMP_STRIDE, NSA_KV_GROUPS, HEAD_DIM)
    blocks = jnp.concatenate([chunks[i:i + n_blk] for i in range(r)], axis=1)
    blocks = blocks + pe[None, :, None, :]
    flat = jnp.swapaxes(blocks, 1, 2).reshape(n_blk, NSA_KV_GROUPS, CMP_BLOCK * HEAD_DIM)
    return jax.nn.gelu(flat @ w1) @ w2


def compressed_kv(rows, pe_cmp, w_cmp1, w_cmp2, g_kc):
    kc = rmsnorm(compress(rows[:, 0], pe_cmp[0], w_cmp1[0], w_cmp2[0]), g_kc)
    vc = compress(rows[:, 1], pe_cmp[1], w_cmp1[1], w_cmp2[1])
    cmp_end = jnp.arange(kc.shape[0]) * CMP_STRIDE + CMP_BLOCK - 1
    return kc, vc, cmp_end


def cmp_to_sel_weights(n_cmp, n_sel):
    c0 = jnp.arange(n_cmp)[:, None] * CMP_STRIDE
    s0 = jnp.arange(n_sel)[None, :] * SEL_BLOCK
    shared = jnp.minimum(c0 + CMP_BLOCK, s0 + SEL_BLOCK) - jnp.maximum(c0, s0)
    return jnp.maximum(shared, 0).astype(jnp.float32) / CMP_BLOCK


def fox_attend(q, cq, pos_q, k, v, ck, pos_k):
    s = jnp.einsum('qhd,khd->hqk', q, k).astype(jnp.float32) * (1.0 / math.sqrt(HEAD_DIM))
    s = s + (cq.T[:, :, None] - ck.T[:, None, :])
    mask = (pos_k[None, :] <= pos_q[:, None])[None]
    p = masked_softmax(s, mask, -1)
    return jnp.einsum('hqk,khd->qhd', p.astype(v.dtype), v)


def nsa_attend(q, gates, pos_q, kc, vc, cmp_end, ks_blk, vs_blk, kw, vw, pos_w, rel_bias):
    n_q = q.shape[0]
    scale = 1.0 / math.sqrt(HEAD_DIM)
    qg = q.reshape(n_q, NSA_KV_GROUPS, NSA_HPG, HEAD_DIM)
    dist_c = pos_q[:, None] - cmp_end[None, :]
    s_c = jnp.einsum('qgjd,ngd->qgjn', qg, kc).astype(jnp.float32) * scale + group_bias(rel_bias, dist_c)
    p_c = masked_softmax(s_c, (dist_c >= 0)[:, None, None, :], -1)
    o_c = jnp.einsum('qgjn,ngd->qgjd', p_c.astype(vc.dtype), vc)
    n_sel = ks_blk.shape[0]
    imp = jnp.einsum('qgn,nm->qgm', p_c.sum(axis=2), cmp_to_sel_weights(kc.shape[0], n_sel))
    blk = jnp.arange(n_sel)[None, :]
    cur = (pos_q // SEL_BLOCK)[:, None]
    forced = (blk == 0) | (blk == cur) | (blk == cur - 1)
    causal_blk = blk * SEL_BLOCK <= pos_q[:, None]
    score = jnp.where(forced[:, None, :], jnp.inf, imp)
    score = jnp.where(causal_blk[:, None, :], score, -jnp.inf)
    top_score, idx = lax.top_k(score, min(N_SEL, n_sel))
    sel_ok = top_score > -jnp.inf
    idx_g = jnp.moveaxis(idx, 1, 0)
    ks = jax.vmap(lambda b, i: b[i])(jnp.moveaxis(ks_blk, 2, 0), idx_g)
    vs = jax.vmap(lambda b, i: b[i])(jnp.moveaxis(vs_blk, 2, 0), idx_g)
    pos_s = idx[..., None] * SEL_BLOCK + jnp.arange(SEL_BLOCK)
    dist_s = pos_q[:, None, None, None] - pos_s
    table = rel_bias.reshape(REL_BUCKETS, NSA_KV_GROUPS, NSA_HPG)
    b_s = table[t5_bucket(dist_s), jnp.arange(NSA_KV_GROUPS)[None, :, None, None]]
    s_s = (jnp.einsum('qgjd,gqnkd->qgjnk', qg, ks).astype(jnp.float32) * scale
           + jnp.moveaxis(b_s, -1, 2).astype(jnp.float32))
    mask_s = (sel_ok[..., None] & (dist_s >= 0))[:, :, None]
    p_s = masked_softmax(s_s, mask_s, (-2, -1))
    o_s = jnp.einsum('qgjnk,gqnkd->qgjd', p_s.astype(vs.dtype), vs)
    dist_w = pos_q[:, None] - pos_w[None, :]
    s_w = jnp.einsum('qgjd,kgd->qgjk', qg, kw).astype(jnp.float32) * scale + group_bias(rel_bias, dist_w)
    mask_w = ((dist_w >= 0) & (dist_w <= WINDOW) & (pos_w >= 0)[None, :])[:, None, None, :]
    p_w = masked_softmax(s_w, mask_w, -1)
    o_w = jnp.einsum('qgjk,kgd->qgjd', p_w.astype(vw.dtype), vw)
    g = gates.reshape(n_q, NSA_KV_GROUPS, NSA_HPG, 1, 3)
    o = g[..., 0] * o_c + g[..., 1] * o_s + g[..., 2] * o_w
    return o.reshape(n_q, NSA_HEADS, HEAD_DIM)


def mixer_inputs(h, w_in, b_forget, g_qk_fox, g_qk_nsa):
    b, t, _ = h.shape
    z = h @ w_in
    qa, ka, va, zf, qb, zkv, zg, zm = jnp.split(z, np.cumsum(IN_SIZES)[:-1].tolist(), axis=-1)
    qa = rmsnorm(qa.reshape(b, t, FOX_HEADS, HEAD_DIM), g_qk_fox[0])
    ka = rmsnorm(ka.reshape(b, t, FOX_HEADS, HEAD_DIM), g_qk_fox[1])
    va = va.reshape(b, t, FOX_HEADS, HEAD_DIM)
    lfa = jax.nn.log_sigmoid((zf + b_forget).astype(jnp.float32))
    qb = rmsnorm(qb.reshape(b, t, NSA_HEADS, HEAD_DIM), g_qk_nsa[0])
    kv = zkv.reshape(b, t, 6, NSA_KV_GROUPS, HEAD_DIM)
    kv_nsa = jnp.stack([kv[:, :, 0], kv[:, :, 1], rmsnorm(kv[:, :, 2], g_qk_nsa[2]), kv[:, :, 3]], axis=2)
    kv_win = jnp.stack([rmsnorm(kv[:, :, 4], g_qk_nsa[3]), kv[:, :, 5]], axis=2)
    gb = jax.nn.sigmoid(zg.reshape(b, t, NSA_HEADS, 3))
    gm = jax.nn.sigmoid(zm.reshape(b, t, 2, D_MODEL))
    return qa, ka, va, lfa, qb, gb, kv_nsa, kv_win, gm


def merge_branches(oa, ob, gm, w_out_fox, w_out_nsa, w_out):
    b, t = oa.shape[:2]
    ya = oa.reshape(b, t, FOX_W) @ w_out_fox
    yb = ob.reshape(b, t, NSA_W) @ w_out_nsa
    return (gm[:, :, 0] * ya + gm[:, :, 1] * yb) @ w_out


def channel_mixer(x, mod, g, w_up, w_down):
    h = rmsnorm(x, g) * (1 + mod[4]) + mod[3]
    return x + mod[5] * (jnp.square(jax.nn.relu(h @ w_up)) @ w_down)


def mix_prompt_seq(qa, ka, va, lfa, qb, gb, kv_nsa, kv_win, pe_cmp, w_cmp1, w_cmp2, g_kc, rel_bias):
    s_len = qa.shape[0]
    pos = jnp.arange(s_len)
    cum = jnp.cumsum(lfa.astype(jnp.float32), axis=0)
    kc, vc, cmp_end = compressed_kv(kv_nsa, pe_cmp, w_cmp1, w_cmp2, g_kc)
    ks_blk = kv_nsa[:, 2].reshape(s_len // SEL_BLOCK, SEL_BLOCK, NSA_KV_GROUPS, HEAD_DIM)
    vs_blk = kv_nsa[:, 3].reshape(s_len // SEL_BLOCK, SEL_BLOCK, NSA_KV_GROUPS, HEAD_DIM)
    win_pad = jnp.pad(kv_win, ((WINDOW, 0), (0, 0), (0, 0), (0, 0)))

    def block(i):
        q0 = i * Q_BLOCK
        pq = q0 + jnp.arange(Q_BLOCK)
        take = lambda a: lax.dynamic_slice_in_dim(a, q0, Q_BLOCK, axis=0)
        oa = fox_attend(take(qa), take(cum), pq, ka, va, cum, pos)
        band = lax.dynamic_slice_in_dim(win_pad, q0, WINDOW + Q_BLOCK, axis=0)
        pw = q0 - WINDOW + jnp.arange(WINDOW + Q_BLOCK)
        ob = nsa_attend(take(qb), take(gb), pq, kc, vc, cmp_end, ks_blk, vs_blk,
                        band[:, 0], band[:, 1], pw, rel_bias)
        return oa, ob

    oa, ob = lax.map(block, jnp.arange(s_len // Q_BLOCK))
    return oa.reshape(s_len, FOX_HEADS, HEAD_DIM), ob.reshape(s_len, NSA_HEADS, HEAD_DIM)


def mix_sample_seq(pages, qa, ka, va, lfa, qb, gb, kv_nsa, kv_win, win_buf,
                   pool_fox_kv, pool_fox_logf, pool_nsa_kv, layer,
                   pe_cmp, w_cmp1, w_cmp2, g_kc, rel_bias):
    n_new = qa.shape[0]
    past_fox = pool_fox_kv[layer, pages]
    past_len = past_fox.shape[0] * past_fox.shape[1]
    past_fox = past_fox.reshape(past_len, 2, FOX_HEADS, HEAD_DIM)
    k = jnp.concatenate([past_fox[:, 0], ka], axis=0)
    v = jnp.concatenate([past_fox[:, 1], va], axis=0)
    lf = jnp.concatenate([pool_fox_logf[layer, pages].reshape(past_len, FOX_HEADS).astype(jnp.float32),
                          lfa.astype(jnp.float32)], axis=0)
    cum = jnp.cumsum(lf, axis=0)
    total = past_len + n_new
    pos = jnp.arange(total)
    pq = past_len + jnp.arange(n_new)
    oa = fox_attend(qa, cum[past_len:], pq, k, v, cum, pos)
    rows = jnp.concatenate([pool_nsa_kv[layer, pages].reshape(past_len, 4, NSA_KV_GROUPS, HEAD_DIM), kv_nsa], axis=0)
    padded = -(-total // SEL_BLOCK) * SEL_BLOCK
    rows = jnp.pad(rows, ((0, padded - total), (0, 0), (0, 0), (0, 0)))
    kc, vc, cmp_end = compressed_kv(rows, pe_cmp, w_cmp1, w_cmp2, g_kc)
    ks_blk = rows[:, 2].reshape(padded // SEL_BLOCK, SEL_BLOCK, NSA_KV_GROUPS, HEAD_DIM)
    vs_blk = rows[:, 3].reshape(padded // SEL_BLOCK, SEL_BLOCK, NSA_KV_GROUPS, HEAD_DIM)
    n_buf = win_buf.shape[0]
    band = jnp.concatenate([win_buf, kv_win], axis=0)
    pw = past_len - n_buf + jnp.arange(n_buf + n_new)
    ob = nsa_attend(qb, gb, pq, kc, vc, cmp_end, ks_blk, vs_blk, band[:, 0], band[:, 1], pw, rel_bias)
    return oa, ob, band[n_new:]


def setup_inputs(seed: int = 0) -> dict:
    key = jax.random.key(seed)
    ks = jax.random.split(key, 25)
    n_pages = PAST_LEN // PAGE_SIZE
    n_pool = (5 * DEC_BATCH * n_pages) // 4
    win_buf = min(WINDOW, PAST_LEN)
    nrm = lambda k, shape, scale: jax.random.normal(k, shape, jnp.float32) * scale
    perm = jax.random.permutation(ks[6], n_pool)
    return {
        'x_prompt': nrm(ks[0], (BATCH, SEQ, D_MODEL), 1.0),
        'x_sample': nrm(ks[1], (DEC_BATCH, DEC_SEQ, D_MODEL), 1.0),
        'cache_fox_kv': nrm(ks[2], (DEPTH, n_pool, PAGE_SIZE, 2, FOX_HEADS, HEAD_DIM), 1.0),
        'cache_fox_logf': jax.nn.log_sigmoid(nrm(ks[3], (DEPTH, n_pool, PAGE_SIZE, FOX_HEADS), 1.0) + FORGET_BIAS_INIT),
        'cache_nsa_kv': nrm(ks[4], (DEPTH, n_pool, PAGE_SIZE, 4, NSA_KV_GROUPS, HEAD_DIM), 1.0),
        'state_nsa_win': nrm(ks[5], (DEPTH, DEC_BATCH, win_buf, 2, NSA_KV_GROUPS, HEAD_DIM), 1.0),
        'page_table': perm[:DEC_BATCH * n_pages].reshape(DEC_BATCH, n_pages).astype(jnp.int32),
        'c_prompt': nrm(ks[7], (BATCH, D_MODEL), 1.0),
        'c_sample': nrm(ks[8], (DEC_BATCH, D_MODEL), 1.0),
        'w_ada': nrm(ks[9], (DEPTH, D_MODEL, 6 * D_MODEL), 0.5 * D_MODEL ** -0.5),
        'b_ada': nrm(ks[10], (DEPTH, 6 * D_MODEL), 0.02),
        'g_norm': 1.0 + nrm(ks[11], (DEPTH, 2, D_MODEL), 0.02),
        'w_in': nrm(ks[12], (DEPTH, D_MODEL, IN_W), D_MODEL ** -0.5),
        'b_forget': FORGET_BIAS_INIT + nrm(ks[13], (DEPTH, FOX_HEADS), 0.5),
        'g_qk_fox': 1.0 + nrm(ks[14], (DEPTH, 2, HEAD_DIM), 0.02),
        'g_qk_nsa': 1.0 + nrm(ks[15], (DEPTH, 4, HEAD_DIM), 0.02),
        'pe_cmp': nrm(ks[16], (DEPTH, 2, CMP_BLOCK, HEAD_DIM), 0.5),
        'w_cmp1': nrm(ks[17], (DEPTH, 2, CMP_BLOCK * HEAD_DIM, CMP_HIDDEN), (CMP_BLOCK * HEAD_DIM) ** -0.5),
        'w_cmp2': nrm(ks[18], (DEPTH, 2, CMP_HIDDEN, HEAD_DIM), CMP_HIDDEN ** -0.5),
        'rel_bias': nrm(ks[19], (REL_BUCKETS, NSA_HEADS), 0.5),
        'w_out_fox': nrm(ks[20], (DEPTH, FOX_W, D_MODEL), FOX_W ** -0.5),
        'w_out_nsa': nrm(ks[21], (DEPTH, NSA_W, D_MODEL), NSA_W ** -0.5),
        'w_out': nrm(ks[22], (DEPTH, D_MODEL, D_MODEL), D_MODEL ** -0.5),
        'w_up': nrm(ks[23], (DEPTH, D_MODEL, D_FF), D_MODEL ** -0.5),
        'w_down': nrm(ks[24], (DEPTH, D_FF, D_MODEL), D_FF ** -0.5),
    }


def reference(x_prompt, x_sample, cache_fox_kv, cache_fox_logf, cache_nsa_kv, state_nsa_win, page_table,
              c_prompt, c_sample, w_ada, b_ada, g_norm, w_in, b_forget, g_qk_fox, g_qk_nsa,
              pe_cmp, w_cmp1, w_cmp2, rel_bias, w_out_fox, w_out_nsa, w_out, w_up, w_down):
    y_p, y_s = x_prompt, x_sample
    fox_kv_p, fox_kv_s, fox_lf_p, fox_lf_s = [], [], [], []
    nsa_kv_p, nsa_kv_s, win_p, win_s = [], [], [], []
    for l in range(DEPTH):
        mod_p = ada_mod(c_prompt, w_ada[l], b_ada[l])
        h = rmsnorm(y_p, g_norm[l, 0]) * (1 + mod_p[1]) + mod_p[0]
        qa, ka, va, lfa, qb, gb, kv_nsa, kv_win, gm = mixer_inputs(h, w_in[l], b_forget[l], g_qk_fox[l], g_qk_nsa[l])
        prompt_fn = lambda *a: mix_prompt_seq(*a, pe_cmp[l], w_cmp1[l], w_cmp2[l], g_qk_nsa[l, 1], rel_bias)
        oa, ob = jax.vmap(prompt_fn)(qa, ka, va, lfa, qb, gb, kv_nsa, kv_win)
        y_p = y_p + mod_p[2] * merge_branches(oa, ob, gm, w_out_fox[l], w_out_nsa[l], w_out[l])
        y_p = channel_mixer(y_p, mod_p, g_norm[l, 1], w_up[l], w_down[l])
        fox_kv_p.append(jnp.stack([ka, va], axis=2))
        fox_lf_p.append(lfa)
        nsa_kv_p.append(kv_nsa)
        win_p.append(kv_win[:, -min(WINDOW, kv_win.shape[1]):])
        mod_s = ada_mod(c_sample, w_ada[l], b_ada[l])
        h = rmsnorm(y_s, g_norm[l, 0]) * (1 + mod_s[1]) + mod_s[0]
        qa, ka, va, lfa, qb, gb, kv_nsa, kv_win, gm = mixer_inputs(h, w_in[l], b_forget[l], g_qk_fox[l], g_qk_nsa[l])
        sample_fn = lambda a: mix_sample_seq(*a, cache_fox_kv, cache_fox_logf, cache_nsa_kv, l,
                                             pe_cmp[l], w_cmp1[l], w_cmp2[l], g_qk_nsa[l, 1], rel_bias)
        oa, ob, new_buf = lax.map(sample_fn, (page_table, qa, ka, va, lfa, qb, gb, kv_nsa, kv_win, state_nsa_win[l]))
        y_s = y_s + mod_s[2] * merge_branches(oa, ob, gm, w_out_fox[l], w_out_nsa[l], w_out[l])
        y_s = channel_mixer(y_s, mod_s, g_norm[l, 1], w_up[l], w_down[l])
        fox_kv_s.append(jnp.stack([ka, va], axis=2))
        fox_lf_s.append(lfa)
        nsa_kv_s.append(kv_nsa)
        win_s.append(new_buf)
    y_prompt = y_p
    y_sample = y_s
    new_fox_kv_prompt = jnp.stack(fox_kv_p)
    new_fox_kv_sample = jnp.stack(fox_kv_s)
    new_fox_logf_prompt = jnp.stack(fox_lf_p)
    new_fox_logf_sample = jnp.stack(fox_lf_s)
    new_nsa_kv_prompt = jnp.stack(nsa_kv_p)
    new_nsa_kv_sample = jnp.stack(nsa_kv_s)
    new_win_prompt = jnp.stack(win_p)
    new_win_sample = jnp.stack(win_s)
    return (y_prompt, y_sample, new_fox_kv_prompt, new_fox_kv_sample, new_fox_logf_prompt, new_fox_logf_sample,
            new_nsa_kv_prompt, new_nsa_kv_sample, new_win_prompt, new_win_sample)
```

```python
import contextlib
import numpy as np
import concourse.bass as bass
import concourse.mybir as mybir
from concourse.bass_utils import run_bass_kernel_spmd

F32 = mybir.dt.float32
BF16 = mybir.dt.bfloat16
I32 = mybir.dt.int32
AF = mybir.ActivationFunctionType
ALU = mybir.AluOpType
AX = mybir.AxisListType

D = 1024
IN_W = 4896
O_QA, O_KA, O_VA, O_ZF, O_QB, O_ZKV, O_ZG, O_ZM = 0, 512, 1024, 1536, 1544, 2056, 2824, 2848
EPS = 1e-6
NEG = -30000.0
NCORES = 8


class Sched:
    ENG = ("pe", "act", "dve", "pool", "sp")

    def __init__(self, nc, nlanes=6):
        self.nc = nc
        self.ops = {e: [] for e in self.ENG}
        self.cnt = {}
        self.seen = {e: {} for e in self.ENG}
        self.lw = {}
        self.rd = {}
        self.lanes = {"sp": [f"L_sp{i}" for i in range(nlanes)],
                      "pool": [f"L_pool{i}" for i in range(nlanes)],
                      "act": [f"L_act{i}" for i in range(2)]}
        self.lane_rr = {"sp": 0, "pool": 0, "act": 0}
        self.semkeys = list(self.ENG)
        for v in self.lanes.values():
            self.semkeys += v
        for k in self.semkeys:
            self.cnt[k] = 0
        self.sems = {}
        self.n_inst = 0
        self.local = None
        self.sfx = ""

    def _deps(self, reads, writes):
        toks = {}

        def add(t):
            for k, v in t.items():
                if toks.get(k, 0) < v:
                    toks[k] = v
        for r in reads:
            if r in self.lw:
                add(self.lw[r])
        for w in writes:
            if w in self.lw:
                add(self.lw[w])
            if w in self.rd:
                add(self.rd[w])
        return toks

    def _commit(self, tok, reads, writes):
        k, v = tok
        for r in reads:
            d = self.rd.setdefault(r, {})
            if d.get(k, 0) < v:
                d[k] = v
        for w in writes:
            self.lw[w] = {k: v}
            self.rd[w] = {}

    def _waits(self, eng, toks):
        for k, v in toks.items():
            if k == "pe" and eng == "pe":
                continue
            if self.seen[eng].get(k, 0) >= v:
                continue
            self.seen[eng][k] = v
            self.ops[eng].append(("w", k, v))

    def _rn(self, names):
        if self.local is None:
            return names
        return [n + self.sfx if n in self.local else n for n in names]

    def op(self, eng, fn, reads=(), writes=()):
        reads, writes = self._rn(reads), self._rn(writes)
        toks = self._deps(reads, writes)
        self._waits(eng, toks)
        self.cnt[eng] += 1
        self.ops[eng].append(("i", fn, eng, 1))
        self._commit((eng, self.cnt[eng]), reads, writes)
        self.n_inst += 1

    def dma(self, eng, fn, reads=(), writes=()):
        reads, writes = self._rn(reads), self._rn(writes)
        toks = self._deps(reads, writes)
        lanes = self.lanes[eng]
        lane = lanes[self.lane_rr[eng] % len(lanes)]
        self.lane_rr[eng] += 1
        if self.cnt[lane] > 0:
            toks[lane] = max(toks.get(lane, 0), self.cnt[lane])
        self._waits(eng, toks)
        self.cnt[lane] += 16
        self.ops[eng].append(("i", fn, lane, 16))
        self._commit((lane, self.cnt[lane]), reads, writes)
        self.n_inst += 1

    def barrier(self):
        for e in self.ENG:
            toks = {k: v for k, v in self.cnt.items() if v > 0 and k != e}
            self._waits(e, toks)

    def flush(self, stack_sems):
        nc = self.nc
        self.barrier()
        for k in self.semkeys:
            if k not in self.sems:
                self.sems[k] = stack_sems.enter_context(nc.semaphore("s_" + k))
        sems = self.sems
        ops = self.ops

        def replay(name, eng):
            for o in ops[name]:
                if o[0] == "w":
                    eng.wait_ge(sems[o[1]], o[2])
                else:
                    o[1](eng).then_inc(sems[o[2]], o[3])
        with nc.Block() as block:
            @block.tensor
            def _(e):
                replay("pe", e)

            @block.scalar
            def _(e):
                replay("act", e)

            @block.vector
            def _(e):
                replay("dve", e)

            @block.gpsimd
            def _(e):
                replay("pool", e)

            @block.sync
            def _(e):
                replay("sp", e)
        self.ops = {e: [] for e in self.ENG}


class Cfg:
    def __init__(self, S=16384, PAST=16384, NSEQ=4, NPOOL=5120):
        self.S, self.PAST, self.NSEQ, self.NPOOL = S, PAST, NSEQ, NPOOL
        self.NB = S // 128
        self.NOWN = self.NB // 8
        self.NQ = self.NOWN * 128
        self.NPG = PAST // 128
        self.NSR = NSEQ * 8
        self.dbg_ob = False
        self.nsa = True
        self.dbg_gate = None
        self.NTp = -(-(S // 16 - 1) // 128)
        self.NMp = -(-(S // 64) // 128) * 128
        self.NTs = -(-(PAST // 16 - 1) // 128)
        self.NMs = -(-(PAST // 64 + 1) // 128) * 128


def build(cfg):
    nc = bass.Bass("TRN2", target_bir_lowering=False)
    S, NB, NOWN, NQ, NSEQ, NSR = cfg.S, cfg.NB, cfg.NOWN, cfg.NQ, cfg.NSEQ, cfg.NSR
    NPR = cfg.NPOOL * 128

    def din(name, shape, dt=F32):
        return nc.dram_tensor(name, list(shape), dt, kind="ExternalInput").ap()

    def dout(name, shape, dt=F32):
        return nc.dram_tensor(name, list(shape), dt, kind="ExternalOutput").ap()

    def dscr(name, shape, dt=F32):
        return nc.dram_tensor(name, list(shape), dt, kind="Internal").ap()

    xf = din("xf", [S, D])
    xs = din("xs", [NSR, D])
    cvec = din("cvec", [1 + NSEQ, D])
    w_ada = din("w_ada", [D, 6 * D])
    b_ada = din("b_ada", [1, 6 * D])
    g_norm = din("g_norm", [2, D])
    w_in = din("w_in", [D, IN_W])
    b_forget = din("b_forget", [1, 8])
    g_qk_fox = din("g_qk_fox", [2, 64])
    g_qk_nsa = din("g_qk_nsa", [4, 64])
    sel5 = din("sel5", [1 + NSEQ, 128 + NSR])
    ident_in = din("ident", [128, 128])

    o_fkv_p = dout("o_fkv_p", [NQ, 1024])
    o_lf_p = dout("o_lf_p", [NQ, 8])
    o_nkv_p = dout("o_nkv_p", [NQ, 512])
    o_win_p = dout("o_win_p", [NQ, 256])
    o_fkv_s = dout("o_fkv_s", [NSR, 1024])
    o_lf_s = dout("o_lf_s", [NSR, 8])
    o_nkv_s = dout("o_nkv_s", [NSR, 512])
    o_wnew_s = dout("o_wnew_s", [NSR, 256])
    tri_in = din("tri_in", [128, 128])
    iota_in = din("iota_in", [128, 1])
    bt_in = din("bt_in", [128, 128])
    LS_ = cfg.PAST + 128
    tvp_in = din("tvp_in", [128, S // 16]); pnp_in = din("pnp_in", [128, S // 16]); kwnp_in = din("kwnp_in", [128, S // 128])
    tvs_in = din("tvs_in", [128, LS_ // 16]); pns_in = din("pns_in", [128, LS_ // 16]); kwns_in = din("kwns_in", [128, LS_ // 128])
    ptab = din("ptab", [NSEQ, cfg.NPG], I32)
    pool_fox = din("pool_fox", [NPR, 1024])
    pool_lf = din("pool_lf", [NPR, 8])
    pool_nsa = din("pool_nsa", [NPR, 512])
    w_out_fox = din("w_out_fox", [512, D]); w_out_nsa = din("w_out_nsa", [512, D]); w_out = din("w_out", [D, D])
    w_up = din("w_up", [D, 4 * D]); w_down = din("w_down", [4 * D, D])
    if cfg.dbg_ob:
        dbg_ob_p_in = din("dbg_ob_p_in", [NQ, 512]); dbg_ob_s_in = din("dbg_ob_s_in", [NSR, 512])
    rel_bias = din("rel_bias", [32, 8]); pe_cmp = din("pe_cmp", [2, 32, 64])
    w_cmp1 = din("w_cmp1", [2, 2048, 128]); w_cmp2 = din("w_cmp2", [2, 128, 64])
    ohc_in = din("ohc_in", [33, 4096]); oh1_in = din("oh1_in", [33, 768]); ee_in = din("ee_in", [128, 8192])
    wcp_in = din("wcp_in", [cfg.NTp * 128, cfg.NMp]); wcs_in = din("wcs_in", [cfg.NTs * 128, cfg.NMs])
    cnegp_in = din("cnegp_in", [1, cfg.NTp * 128]); cnegs_in = din("cnegs_in", [1, cfg.NTs * 128])
    addp_in = din("addp_in", [NOWN, 128, cfg.NMp]); adds_in = din("adds_in", [8, cfg.NMs])
    dbg_ob_p = dout("dbg_ob_p", [NQ, 512], BF16); dbg_ob_s = dout("dbg_ob_s", [NSR, 512], BF16)
    dbg_oa_p = dout("dbg_oa_p", [NQ, 512], BF16)
    dbg_oa_s = dout("dbg_oa_s", [NSR, 512], BF16)
    WB = min(512, cfg.PAST)
    win_in = din("win_in", [NSEQ, WB, 256])
    o_win_s = dout("o_win_s", [NSEQ, WB, 256])
    o_y_p = dout("o_y_p", [NQ, D])
    o_y_s = dout("o_y_s", [NSR, D])

    sc = Sched(nc, nlanes=12)
    st = contextlib.ExitStack()
    with st:
        def sb(name, shape, dt=F32):
            return st.enter_context(nc.sbuf_tensor(name, list(shape), dt))

        def ps(name, shape, dt=F32):
            return st.enter_context(nc.psum_tensor(name, list(shape), dt))

        ident_f = sb("ident_f", [128, 128])
        ident_b = sb("ident_b", [128, 128], BF16)
        modrows = sb("modrows", [1 + NSEQ, 6 * D])
        sel_sb = sb("sel_sb", [1 + NSEQ, 128 + NSR])
        gk_b = sb("gk_b", [128, 8, 64])
        gq_b = sb("gq_b", [128, 8, 64])
        gnq_b = sb("gnq_b", [128, 8, 64])
        gsel_b = sb("gsel_b", [128, 2, 64])
        gwin_b = sb("gwin_b", [128, 2, 64])
        bf_b = sb("bf_b", [128, 8])
        neghalf = sb("neghalf", [128, 8])
        ones_f = sb("ones_f", [128, 128])
        gn = sb("gn", [128, 2, D])
        gb_res = sb("gb_res", [128, NOWN, 24])
        gbs_res = sb("gbs_res", [8, NSEQ, 24])
        tri_b = sb("tri_b", [128, 128], BF16)
        ones_b = sb("ones_b", [128, 512], BF16)
        zeros_b = sb("zeros_b", [128, 512], BF16)
        stW = contextlib.ExitStack()
        w_in_b = stW.enter_context(nc.sbuf_tensor("w_in_b", [128, 8, 2848], BF16))

        sc.dma("sp", lambda e: e.dma_start(out=ident_f[:], in_=ident_in[:, :]), [], ["ident_f"])
        sc.op("dve", lambda e: e.tensor_copy(out=ident_b[:], in_=ident_f[:]), ["ident_f"], ["ident_b"])
        sc.op("dve", lambda e: e.memset(neghalf[:], -0.5), [], ["neghalf"])
        sc.op("dve", lambda e: e.memset(ones_f[:], 1.0), [], ["ones_f"])
        sc.dma("sp", lambda e: e.dma_start(out=sel_sb[:], in_=sel5[:, :]), [], ["sel_sb"])

        with contextlib.ExitStack() as st0:
            def sb0(name, shape, dt=F32):
                return st0.enter_context(nc.sbuf_tensor(name, list(shape), dt))
            NR = 1 + NSEQ
            c_sb = sb0("c_sb", [NR, D])
            sig = sb0("sig", [NR, D])
            cT = sb0("cT", [128, 8, NR])
            wada = [sb0(f"wada{i}", [128, 8, 512]) for i in range(2)]
            bada = sb0("bada", [NR, 6 * D])
            small = sb0("small", [128, 8 + 6 * 64])
            ps0 = st0.enter_context(nc.psum_tensor("ps0", [128, 512], F32))
            ps1 = st0.enter_context(nc.psum_tensor("ps1", [128, 512], F32))
            sc.dma("sp", lambda e: e.dma_start(out=c_sb[:], in_=cvec[:, :]), [], ["c_sb"])
            sc.dma("sp", lambda e: e.dma_start(out=bada[:], in_=b_ada.partition_broadcast(NR)), [], ["bada"])
            sc.dma("sp", lambda e: e.dma_start(out=gn[:, 0, :], in_=g_norm[0:1, :].partition_broadcast(128)), [], ["gn"])
            sc.dma("sp", lambda e: e.dma_start(out=gn[:, 1, :], in_=g_norm[1:2, :].partition_broadcast(128)), [], ["gn"])
            sc.dma("sp", lambda e: e.dma_start(out=small[:, 0:8], in_=b_forget.partition_broadcast(128)), [], ["small"])
            sc.dma("sp", lambda e: e.dma_start(
                out=small[:, 8:8 + 128], in_=g_qk_fox.rearrange("a d -> (a d)").unsqueeze(0).partition_broadcast(128)), [], ["small"])
            sc.dma("sp", lambda e: e.dma_start(
                out=small[:, 136:136 + 256], in_=g_qk_nsa.rearrange("a d -> (a d)").unsqueeze(0).partition_broadcast(128)), [], ["small"])
            sc.op("act", lambda e: e.activation(out=sig[:], in_=c_sb[:], func=AF.Sigmoid), ["c_sb"], ["sig"])
            sc.op("dve", lambda e: e.tensor_tensor(out=sig[:], in0=sig[:], in1=c_sb[:], op=ALU.mult), ["sig", "c_sb"], ["sig"])
            for k in range(8):
                sc.op("pe", lambda e, k=k: e.transpose(out=ps0[:, k * NR:(k + 1) * NR], in_=sig[:, k * 128:(k + 1) * 128], identity=ident_f[0:NR, 0:NR]),
                      ["sig", "ident_f"], ["ps0"])
            sc.op("dve", lambda e: e.tensor_copy(out=cT[:].rearrange("p k r -> p (k r)"), in_=ps0[:, 0:8 * NR]), ["ps0"], ["cT"])
            for g in range(12):
                wt = wada[g % 2]
                wn = f"wada{g % 2}"
                sc.dma("sp", lambda e, g=g, wt=wt: e.dma_start(
                    out=wt[:], in_=w_ada[:, g * 512:(g + 1) * 512].rearrange("(k p) n -> p k n", p=128)), [], [wn])
                for k in range(8):
                    sc.op("pe", lambda e, k=k, wt=wt: e.matmul(ps1[0:NR, :], lhsT=cT[:, k, :], rhs=wt[:, k, :],
                                                               start=(k == 0), stop=(k == 7)), ["cT", wn], ["ps1"])
                sc.op("dve", lambda e, g=g: e.tensor_tensor(out=modrows[:, g * 512:(g + 1) * 512], in0=ps1[0:NR, :],
                                                             in1=bada[:, g * 512:(g + 1) * 512], op=ALU.add),
                      ["ps1", "bada"], ["modrows"])
            def bcast_mod(dst, dname, which, rows, c0):
                for half in range(2):
                    sc.op("pe", lambda e, half=half: e.matmul(
                        ps0[0:rows, :], lhsT=sel_sb[:, c0:c0 + rows],
                        rhs=modrows[:, which * D + half * 512: which * D + half * 512 + 512], start=True, stop=True),
                        ["sel_sb", "modrows"], ["ps0"])
                    sc.op("dve", lambda e, half=half: e.tensor_copy(out=dst[:, half * 512:(half + 1) * 512], in_=ps0[0:rows, :]),
                          ["ps0"], [dname])

            def mk_gain(dst, dname, rows, gi):
                sc.op("dve", lambda e: e.scalar_tensor_tensor(out=dst, in0=dst, scalar=1.0, in1=gn[0:rows, gi, :],
                                                              op0=ALU.add, op1=ALU.mult), [dname, "gn"], [dname])
            sc.op("dve", lambda e: e.tensor_copy(out=bf_b[:], in_=small[:, 0:8]), ["small"], ["bf_b"])
            for h in range(8):
                sc.op("dve", lambda e, h=h: e.tensor_scalar(out=gq_b[:, h, :], in0=small[:, 8:72], scalar1=0.125, scalar2=None,
                                                            op0=ALU.mult), ["small"], ["gq_b"])
                sc.op("dve", lambda e, h=h: e.tensor_copy(out=gk_b[:, h, :], in_=small[:, 72:136]), ["small"], ["gk_b"])
                sc.op("dve", lambda e, h=h: e.tensor_scalar(out=gnq_b[:, h, :], in0=small[:, 136:200], scalar1=0.125, scalar2=None,
                                                            op0=ALU.mult), ["small"], ["gnq_b"])
            for g in range(2):
                sc.op("dve", lambda e, g=g: e.tensor_copy(out=gsel_b[:, g, :], in_=small[:, 136 + 128:136 + 192]), ["small"], ["gsel_b"])
                sc.op("dve", lambda e, g=g: e.tensor_copy(out=gwin_b[:, g, :], in_=small[:, 136 + 192:136 + 256]), ["small"], ["gwin_b"])
            for k in range(8):
                for half in range(2):
                    wt = wada[(2 * k + half) % 2]
                    wn = f"wada{(2 * k + half) % 2}"
                    c0 = half * 1424
                    sc.dma("sp", lambda e, k=k, c0=c0, wt=wt: e.dma_start(
                        out=wt[:].rearrange("p k n -> p (k n)")[:, 0:1424], in_=w_in[k * 128:(k + 1) * 128, c0:c0 + 1424]), [], [wn])
                    sc.op("pool", lambda e, k=k, c0=c0, wt=wt: e.tensor_copy(
                        out=w_in_b[:, k, c0:c0 + 1424], in_=wt[:].rearrange("p k n -> p (k n)")[:, 0:1424]), [wn], ["w_in_b"])
            sc.flush(st)

        LS = cfg.PAST + 128
        NPG = cfg.NPG

        class Ctx:
            pass

        def mk_ctx(name, L, nqc):
            c = Ctx()
            c.name, c.L, c.NBk, c.nqc = name, L, L // 128, nqc
            c.KT = dscr(f"KT_{name}", [70, 8, L], BF16)
            c.V = dscr(f"V_{name}", [8, 128, c.NBk, 65], BF16)
            c.QT = dscr(f"QT_{name}", [70, 8, nqc], BF16)
            c.LF = dscr(f"LF_{name}", [8, L], F32)
            c.CUM = dscr(f"CUM_{name}", [8, L], F32)
            c.XC = dscr(f"XC_{name}", [2, 128, L], BF16)
            c.KS = dscr(f"KS_{name}", [2, 64, L], BF16)
            c.KW = dscr(f"KW_{name}", [2, 65, L], BF16)
            c.VS = dscr(f"VS_{name}", [2, 128, c.NBk, 65], BF16)
            c.VW = dscr(f"VW_{name}", [2, 128, c.NBk, 65], BF16)
            c.QN = dscr(f"QN_{name}", [65, 8, nqc], BF16)
            return c
        ctx_p = mk_ctx("p", S, NQ)
        ctx_s = [mk_ctx(f"s{b}", LS, 8) for b in range(NSEQ)]
        OAS = dscr("OAS", [NSR, 512], BF16)
        OAP = dscr("OAP", [NQ, 512], BF16)
        OBP = dscr("OBP", [NQ, 512], BF16)
        OBS = dscr("OBS", [NSR, 512], BF16)

        sc.op("dve", lambda e: e.memset(ones_b[:], 1.0), [], ["ones_b"])
        sc.op("dve", lambda e: e.memset(zeros_b[:], 0.0), [], ["zeros_b"])

        def bcast_rows(dst, dname, which, rows, c0, pst, pname):
            for half in range(2):
                sc.op("pe", lambda e, half=half: e.matmul(
                    pst[0:rows, :], lhsT=sel_sb[:, c0:c0 + rows],
                    rhs=modrows[:, which * D + half * 512: which * D + half * 512 + 512], start=True, stop=True),
                    ["sel_sb", "modrows"], [pname])
                sc.op("dve", lambda e, half=half: e.tensor_copy(out=dst[:, half * 512:(half + 1) * 512], in_=pst[0:rows, :]),
                      [pname], [dname])

        def load_mod(Mt, rows, c0, whichs, gi, pst, pname):
            bcast_rows(Mt[0:rows, 0, :], "Mp", whichs[0], rows, c0, pst, pname)
            sc.op("dve", lambda e: e.scalar_tensor_tensor(out=Mt[0:rows, 0, :], in0=Mt[0:rows, 0, :], scalar=1.0, in1=gn[0:rows, gi, :],
                                                          op0=ALU.add, op1=ALU.mult), ["Mp", "gn"], ["Mp"])
            bcast_rows(Mt[0:rows, 1, :], "Mp", whichs[1], rows, c0, pst, pname)
            if whichs[2] is not None:
                bcast_rows(Mt[0:rows, 2, :], "Mp", whichs[2], rows, c0, pst, pname)

        with contextlib.ExitStack() as stA:
            def sbA(name, shape, dt=F32):
                return stA.enter_context(nc.sbuf_tensor(name, list(shape), dt))

            def psA(name, shape, dt=F32):
                return stA.enter_context(nc.psum_tensor(name, list(shape), dt))
            MpA = sbA("MpA", [128, 3, D])
            trif = sbA("trif", [128, 128])
            wpage = sbA("wpage", [128, 256])
            ps_tr = psA("ps_tr", [128, 1024], BF16)
            ps_a = psA("ps_a", [128, 512]); ps_b = psA("ps_b", [128, 512])
            ps_c = psA("ps_c", [128, 512]); ps_d = psA("ps_d", [128, 512])
            ps_kt = psA("ps_kt", [128, 8, 128], BF16)
            ps_nt = psA("ps_nt", [128, 6, 128], BF16)
            ps_lf = psA("ps_lf", [8, 128])

            sc.dma("sp", lambda e: e.dma_start(out=trif[:], in_=tri_in[:, :]), [], ["trif"])
            sc.op("dve", lambda e: e.tensor_copy(out=tri_b[:], in_=trif[:]), ["trif"], ["tri_b"])

            LOCAL_A = {"xt", "junk", "tmpf", "hb", "hT", "ssum", "rstd", "sq", "hs", "hr", "kvout", "nkvout", "qf", "lfo",
                       "kb", "nb", "stg_k", "stg_x", "stg_n", "stg_l", "vaug", "vsaug", "vwaug"}

            def make_setA(sfx):
                sc.local, sc.sfx = LOCAL_A, sfx
                xt = sbA(sfx + "xt", [128, D])
                junk = sbA(sfx + "junk", [128, D], BF16)
                tmpf = sbA(sfx + "tmpf", [128, D])
                hb = sbA(sfx + "hb", [128, D], BF16)
                hT = sbA(sfx + "hT", [128, 8, 128], BF16)
                ssum = sbA(sfx + "ssum", [128, 1])
                rstd = sbA(sfx + "rstd", [128, 1])
                sq = sbA(sfx + "sq", [128, 512])
                hs = sbA(sfx + "hs", [128, 8]); hr = sbA(sfx + "hr", [128, 8])
                kvout = sbA(sfx + "kvout", [128, 1024])
                nkvout = sbA(sfx + "nkvout", [128, 768])
                qf = sbA(sfx + "qf", [128, 512])
                lfo = sbA(sfx + "lfo", [128, 8])
                kb = sbA(sfx + "kb", [128, 512], BF16)
                nb = sbA(sfx + "nb", [128, 768], BF16)
                stg_k = sbA(sfx + "stg_k", [64, 8, 128], BF16)
                stg_x = sbA(sfx + "stg_x", [128, 2, 128], BF16)
                stg_n = sbA(sfx + "stg_n", [64, 4, 128], BF16)
                stg_l = sbA(sfx + "stg_l", [8, 128])
                vaug = sbA(sfx + "vaug", [128, 8, 65], BF16)
                vsaug = sbA(sfx + "vsaug", [128, 2, 65], BF16)
                vwaug = sbA(sfx + "vwaug", [128, 2, 65], BF16)
                sc.op("dve", lambda e: e.memset(vaug[:], 1.0), [], ["vaug"])
                sc.op("dve", lambda e: e.memset(vsaug[:], 1.0), [], ["vsaug"])
                sc.op("dve", lambda e: e.memset(vwaug[:], 1.0), [], ["vwaug"])
                def headnorm(src, nh, gain, dst, dstname, rows, srcname):
                    sc.op("act", lambda e: e.activation(out=sq[0:rows, 0:nh * 64], in_=src, func=AF.Square), [srcname], ["sq"])
                    sc.op("dve", lambda e: e.tensor_reduce(out=hs[0:rows, 0:nh], in_=sq[0:rows, 0:nh * 64].rearrange("p (h d) -> p h d", d=64),
                                                           axis=AX.X, op=ALU.add), ["sq"], ["hs"])
                    sc.op("dve", lambda e: e.tensor_scalar(out=hs[0:rows, 0:nh], in0=hs[0:rows, 0:nh], scalar1=1.0 / 64, scalar2=EPS,
                                                           op0=ALU.mult, op1=ALU.add), ["hs"], ["hs"])
                    sc.op("pool", lambda e: e.tensor_tensor(out=hr[0:rows, 0:nh], in0=hs[0:rows, 0:nh], in1=neghalf[0:rows, 0:nh], op=ALU.pow),
                          ["hs", "neghalf"], ["hr"])
                    sc.op("dve", lambda e: e.tensor_tensor(out=dst, in0=src.rearrange("p (h d) -> p h d", d=64),
                                                           in1=hr[0:rows, 0:nh].unsqueeze(2).to_broadcast([rows, nh, 64]), op=ALU.mult),
                          [srcname, "hr"], [dstname])
                    sc.op("dve", lambda e: e.tensor_tensor(out=dst, in0=dst, in1=gain, op=ALU.mult), [dstname], [dstname])

                def tr_heads(src_bf, srcname, rows, nh, pst, pname, stg, sname, slot0=0, width=64):
                    for h in range(nh):
                        sc.op("pe", lambda e, h=h: e.transpose(out=pst[0:width, slot0 + h, 0:rows], in_=src_bf[0:rows, h * width:(h + 1) * width],
                                                               identity=ident_b[0:rows, 0:rows]), [srcname, "ident_b"], [pname])
                    if rows < 128:
                        sc.op("dve", lambda e: e.memset(stg[0:width, slot0:slot0 + nh, :], 0.0), [], [sname])
                    sc.op("act", lambda e: e.activation(out=stg[0:width, slot0:slot0 + nh, 0:rows], in_=pst[0:width, slot0:slot0 + nh, 0:rows],
                                                        func=AF.Copy), [pname], [sname])

                def kside_store(c, blk, rows, srcn, fk=None, fv=None, lf=None, kcmp=None, vcmp=None, ksel=None, vsel=None,
                                kwin=None, vwin=None):
                    cs = slice(blk * 128, (blk + 1) * 128)
                    if fk is not None:
                        sc.op("dve", lambda e: e.tensor_copy(out=kb[0:rows, :], in_=fk), [srcn], ["kb"])
                        tr_heads(kb, "kb", rows, 8, ps_kt, "ps_kt", stg_k, "stg_k")
                        sc.dma("sp", lambda e: e.dma_start(out=c.KT[0:64, :, cs], in_=stg_k[:]),
                               ["stg_k"], ["KT_" + c.name])
                    if fv is not None:
                        if rows < 128:
                            sc.op("dve", lambda e: e.memset(vaug[:, :, 0:64], 0.0), [], ["vaug"])
                        sc.op("act", lambda e: e.activation(out=vaug[0:rows, :, 0:64], in_=fv.rearrange("p (h d) -> p h d", d=64), func=AF.Copy),
                              [srcn], ["vaug"])
                        sc.dma("sp", lambda e: e.dma_start(out=c.V[:, :, blk, :].rearrange("h p d -> p h d"), in_=vaug[:]),
                               ["vaug"], ["V_" + c.name])
                    if lf is not None:
                        sc.op("pe", lambda e: e.transpose(out=ps_lf[0:8, 0:rows], in_=lf, identity=ident_f[0:rows, 0:rows]),
                              [srcn, "ident_f"], ["ps_lf"])
                        if rows < 128:
                            sc.op("dve", lambda e: e.memset(stg_l[:], 0.0), [], ["stg_l"])
                        sc.op("dve", lambda e: e.tensor_copy(out=stg_l[:, 0:rows], in_=ps_lf[0:8, 0:rows]), ["ps_lf"], ["stg_l"])
                        sc.dma("sp", lambda e: e.dma_start(out=c.LF[:, cs], in_=stg_l[:]), ["stg_l"], ["LF_" + c.name])
                    if kcmp is not None:
                        sc.op("dve", lambda e: e.tensor_copy(out=nb[0:rows, 0:128], in_=kcmp), [srcn], ["nb"])
                        sc.op("dve", lambda e: e.tensor_copy(out=nb[0:rows, 128:256], in_=vcmp), [srcn], ["nb"])
                        sc.op("dve", lambda e: e.tensor_copy(out=nb[0:rows, 256:384], in_=ksel), [srcn], ["nb"])
                        tr_heads(nb[:, 0:256], "nb", rows, 2, ps_nt, "ps_nt", stg_x, "stg_x", 0, 128)
                        sc.dma("sp", lambda e: e.dma_start(out=c.XC[:, :, cs].rearrange("k p t -> p k t"), in_=stg_x[:]),
                               ["stg_x"], ["XC_" + c.name])
                        tr_heads(nb[:, 256:384], "nb", rows, 2, ps_nt, "ps_nt", stg_n, "stg_n", 0, 64)
                        sc.dma("sp", lambda e: e.dma_start(out=c.KS[:, :, cs].rearrange("g p t -> p g t"), in_=stg_n[:, 0:2, :]),
                               ["stg_n"], ["KS_" + c.name])
                        if rows < 128:
                            sc.op("dve", lambda e: e.memset(vsaug[:, :, 0:64], 0.0), [], ["vsaug"])
                        sc.op("act", lambda e: e.activation(out=vsaug[0:rows, :, 0:64], in_=vsel.rearrange("p (h d) -> p h d", d=64), func=AF.Copy),
                              [srcn], ["vsaug"])
                        sc.dma("sp", lambda e: e.dma_start(out=c.VS[:, :, blk, :].rearrange("h p d -> p h d"), in_=vsaug[:]),
                               ["vsaug"], ["VS_" + c.name])
                    if kwin is not None:
                        sc.op("dve", lambda e: e.tensor_copy(out=nb[0:rows, 512:640], in_=kwin), [srcn], ["nb"])
                        tr_heads(nb[:, 512:640], "nb", rows, 2, ps_nt, "ps_nt", stg_n, "stg_n", 2, 64)
                        sc.dma("sp", lambda e: e.dma_start(out=c.KW[:, 0:64, cs].rearrange("g p t -> p g t"), in_=stg_n[:, 2:4, :]),
                               ["stg_n"], ["KW_" + c.name])
                        if rows < 128:
                            sc.op("dve", lambda e: e.memset(vwaug[:, :, 0:64], 0.0), [], ["vwaug"])
                        sc.op("act", lambda e: e.activation(out=vwaug[0:rows, :, 0:64], in_=vwin.rearrange("p (h d) -> p h d", d=64), func=AF.Copy),
                              [srcn], ["vwaug"])
                        sc.dma("sp", lambda e: e.dma_start(out=c.VW[:, :, blk, :].rearrange("h p d -> p h d"), in_=vwaug[:]),
                               ["vwaug"], ["VW_" + c.name])

                def proj_tile(x_ap, rows, G1, B1, g1n, b1n, own, outs, c, blk, qcol0, gb_dst, gbn):
                    sc.dma("sp", lambda e: e.dma_start(out=xt[0:rows, :], in_=x_ap), [], ["xt"])
                    sc.op("act", lambda e: e.activation(out=junk[0:rows, :], in_=xt[0:rows, :], func=AF.Square, accum_out=ssum[0:rows, :]),
                          ["xt"], ["junk", "ssum"])
                    sc.op("dve", lambda e: e.tensor_scalar(out=ssum[0:rows, :], in0=ssum[0:rows, :], scalar1=1.0 / D, scalar2=EPS,
                                                           op0=ALU.mult, op1=ALU.add), ["ssum"], ["ssum"])
                    sc.op("pool", lambda e: e.tensor_tensor(out=rstd[0:rows, :], in0=ssum[0:rows, :], in1=neghalf[0:rows, 0:1], op=ALU.pow),
                          ["ssum", "neghalf"], ["rstd"])
                    sc.op("dve", lambda e: e.scalar_tensor_tensor(out=tmpf[0:rows, :], in0=xt[0:rows, :], scalar=rstd[0:rows, :], in1=G1,
                                                                  op0=ALU.mult, op1=ALU.mult), ["xt", "rstd", g1n], ["tmpf"])
                    sc.op("dve", lambda e: e.tensor_tensor(out=hb[0:rows, :], in0=tmpf[0:rows, :], in1=B1, op=ALU.add),
                          ["tmpf", b1n], ["hb"])
                    for k in range(8):
                        sc.op("pe", lambda e, k=k: e.transpose(out=ps_tr[:, k * 128:k * 128 + rows], in_=hb[0:rows, k * 128:(k + 1) * 128],
                                                               identity=ident_b[0:rows, 0:rows]), ["hb", "ident_b"], ["ps_tr"])
                    sc.op("act", lambda e: e.activation(out=hT[:, :, 0:rows], in_=ps_tr[:].rearrange("p (k t) -> p k t", t=128)[:, :, 0:rows],
                                                        func=AF.Copy), ["ps_tr"], ["hT"])

                    def mm(pst, pname, c0, n, o0=0):
                        for k in range(8):
                            sc.op("pe", lambda e, k=k: e.matmul(pst[0:rows, o0:o0 + n], lhsT=hT[:, k, 0:rows], rhs=w_in_b[:, k, c0:c0 + n],
                                                                start=(k == 0), stop=(k == 7)), ["hT", "w_in_b"], [pname])
                    mm(ps_a, "ps_a", O_KA, 512)
                    mm(ps_b, "ps_b", O_VA, 512)
                    mm(ps_c, "ps_c", O_ZKV, 512)
                    mm(ps_d, "ps_d", O_ZKV + 512, 256)
                    mm(ps_d, "ps_d", O_ZF, 8, 256)
                    headnorm(ps_a[0:rows, :], 8, gk_b[0:rows], kvout[0:rows, 0:512].rearrange("p (h d) -> p h d", d=64), "kvout", rows, "ps_a")
                    sc.op("act", lambda e: e.activation(out=kvout[0:rows, 512:1024], in_=ps_b[0:rows, :], func=AF.Copy), ["ps_b"], ["kvout"])
                    sc.op("act", lambda e: e.activation(out=nkvout[0:rows, 0:512], in_=ps_c[0:rows, :], func=AF.Copy), ["ps_c"], ["nkvout"])
                    sc.op("act", lambda e: e.activation(out=nkvout[0:rows, 512:768], in_=ps_d[0:rows, 0:256], func=AF.Copy), ["ps_d"], ["nkvout"])
                    headnorm(ps_c[0:rows, 256:384], 2, gsel_b[0:rows], nkvout[0:rows, 256:384].rearrange("p (h d) -> p h d", d=64), "nkvout", rows, "ps_c")
                    headnorm(ps_d[0:rows, 0:128], 2, gwin_b[0:rows], nkvout[0:rows, 512:640].rearrange("p (h d) -> p h d", d=64), "nkvout", rows, "ps_d")
                    sc.op("dve", lambda e: e.tensor_tensor(out=lfo[0:rows, :], in0=ps_d[0:rows, 256:264], in1=bf_b[0:rows, :], op=ALU.add),
                          ["ps_d", "bf_b"], ["lfo"])
                    sc.op("act", lambda e: e.activation(out=lfo[0:rows, :], in_=lfo[0:rows, :], func=AF.Exp, scale=-1.0), ["lfo"], ["lfo"])
                    sc.op("act", lambda e: e.activation(out=lfo[0:rows, :], in_=lfo[0:rows, :], func=AF.Ln, bias=1.0), ["lfo"], ["lfo"])
                    sc.op("dve", lambda e: e.tensor_scalar(out=lfo[0:rows, :], in0=lfo[0:rows, :], scalar1=-1.0, scalar2=None, op0=ALU.mult),
                          ["lfo"], ["lfo"])
                    if outs is not None:
                        o_kv, o_lf, o_nkv, o_win = outs
                        sc.dma("pool", lambda e: e.dma_start(out=o_kv, in_=kvout[0:rows, :]), ["kvout"], [])
                        sc.dma("pool", lambda e: e.dma_start(out=o_lf, in_=lfo[0:rows, :]), ["lfo"], [])
                        sc.dma("pool", lambda e: e.dma_start(out=o_nkv, in_=nkvout[0:rows, 0:512]), ["nkvout"], [])
                        for oo in o_win:
                            sc.dma("pool", lambda e, oo=oo: e.dma_start(out=oo, in_=nkvout[0:rows, 512:768]), ["nkvout"], [])
                    kside_store(c, blk, rows, "kvout", fk=kvout[0:rows, 0:512], fv=kvout[0:rows, 512:1024], lf=lfo[0:rows, :])
                    kside_store(c, blk, rows, "nkvout", kcmp=nkvout[0:rows, 0:128], vcmp=nkvout[0:rows, 128:256], ksel=nkvout[0:rows, 256:384],
                                vsel=nkvout[0:rows, 384:512], kwin=nkvout[0:rows, 512:640], vwin=nkvout[0:rows, 640:768])
                    if own:
                        mm(ps_a, "ps_a", O_QA, 512)
                        mm(ps_b, "ps_b", O_QB, 512)
                        mm(ps_c, "ps_c", O_ZG, 24)
                        headnorm(ps_a[0:rows, :], 8, gq_b[0:rows], qf[0:rows, :].rearrange("p (h d) -> p h d", d=64), "qf", rows, "ps_a")
                        sc.op("dve", lambda e: e.tensor_copy(out=kb[0:rows, :], in_=qf[0:rows, :]), ["qf"], ["kb"])
                        tr_heads(kb, "kb", rows, 8, ps_kt, "ps_kt", stg_k, "stg_k")
                        sc.dma("sp", lambda e: e.dma_start(out=c.QT[0:64, :, qcol0:qcol0 + rows], in_=stg_k[:, :, 0:rows]),
                               ["stg_k"], ["QT_" + c.name])
                        headnorm(ps_b[0:rows, :], 8, gnq_b[0:rows], qf[0:rows, :].rearrange("p (h d) -> p h d", d=64), "qf", rows, "ps_b")
                        sc.op("dve", lambda e: e.tensor_copy(out=kb[0:rows, :], in_=qf[0:rows, :]), ["qf"], ["kb"])
                        tr_heads(kb, "kb", rows, 8, ps_kt, "ps_kt", stg_k, "stg_k")
                        sc.dma("sp", lambda e: e.dma_start(out=c.QN[0:64, :, qcol0:qcol0 + rows], in_=stg_k[:, :, 0:rows]),
                               ["stg_k"], ["QN_" + c.name])
                        sc.op("act", lambda e: e.activation(out=gb_dst, in_=ps_c[0:rows, 0:24], func=AF.Sigmoid), ["ps_c"], [gbn])
                        if cfg.dbg_gate is not None:
                            sc.op("dve", lambda e: e.memset(gb_dst, 0.0), [gbn], [gbn])
                            sc.op("dve", lambda e: e.memset(gb_dst.rearrange("p (h i) -> p h i", i=3)[:, :, cfg.dbg_gate], 1.0), [gbn], [gbn])


                sc.local = None
                return proj_tile, kside_store
            setsA = [make_setA("_a0"), make_setA("_a1")]
            tcount = {"n": 0}

            def proj_tile(*a, **k):
                i = tcount["n"] % 2
                tcount["n"] += 1
                sc.local, sc.sfx = LOCAL_A, f"_a{i}"
                setsA[i][0](*a, **k)
                sc.local = None

            def kside_store(*a, **k):
                sc.local, sc.sfx = LOCAL_A, "_a0"
                setsA[0][1](*a, **k)
                sc.local = None
            load_mod(MpA, 128, 0, (1, 0, None), 0, ps_a, "ps_a")
            for f in range(NB):
                own = (f % 8 == 7)
                j = f // 8
                outs = None
                if own:
                    outs = (o_fkv_p[j * 128:(j + 1) * 128, :], o_lf_p[j * 128:(j + 1) * 128, :],
                            o_nkv_p[j * 128:(j + 1) * 128, :], [o_win_p[j * 128:(j + 1) * 128, :]])
                proj_tile(xf[f * 128:(f + 1) * 128, :], 128, MpA[:, 0, :], MpA[:, 1, :], "Mp", "Mp", own, outs, ctx_p, f, j * 128,
                          gb_res[:, j, :], "gb_res")
            for b in range(NSEQ):
                outs = (o_fkv_s[b * 8:(b + 1) * 8, :], o_lf_s[b * 8:(b + 1) * 8, :], o_nkv_s[b * 8:(b + 1) * 8, :],
                        [o_wnew_s[b * 8:(b + 1) * 8, :], o_win_s[b, WB - 8:WB, :]])
                load_mod(MpA, 8, 128 + 8 * b, (1, 0, None), 0, ps_a, "ps_a")
                proj_tile(xs[b * 8:(b + 1) * 8, :], 8, MpA[0:8, 0, :], MpA[0:8, 1, :], "Mp", "Mp", True, outs, ctx_s[b], NPG, 0,
                          gbs_res[:, b, :], "gbs_res")
                sc.dma("sp", lambda e, b=b: e.dma_start(out=o_win_s[b, 0:WB - 8, :], in_=win_in[b, 8:WB, :]), [], [])

            for b in range(NSEQ):
                c = ctx_s[b]
                for i in range(WB // 128):
                    sc.dma("sp", lambda e, b=b, i=i: e.dma_start(out=wpage[:], in_=win_in[b, i * 128:(i + 1) * 128, :]), [], ["wpage"])
                    kside_store(c, NPG - WB // 128 + i, 128, "wpage", kwin=wpage[:, 0:128], vwin=wpage[:, 128:256])
            sc.flush(st)
        stW.close()

        with contextlib.ExitStack() as stG:
            def sbG(name, shape, dt=F32):
                return stG.enter_context(nc.sbuf_tensor("G_" + name, list(shape), dt))

            def psG(name, shape, dt=F32):
                return stG.enter_context(nc.psum_tensor("G_" + name, list(shape), dt))
            ptb_i = sbG("ptb_i", [128, NSEQ * NPG], I32)
            ptb_f = sbG("ptb_f", [128, NSEQ * NPG])
            idx_i = sbG("idx_i", [128, NSEQ * NPG], I32)
            iota_p = sbG("iota_p", [128, 1])
            sc.dma("sp", lambda e: e.dma_start(out=iota_p[:], in_=iota_in[:, :]), [], ["iota_p"])
            sc.dma("sp", lambda e: e.dma_start(out=ptb_i[:], in_=ptab.rearrange("b n -> (b n)").unsqueeze(0).partition_broadcast(128)),
                   [], ["ptb_i"])
            sc.op("dve", lambda e: e.tensor_copy(out=ptb_f[:], in_=ptb_i[:]), ["ptb_i"], ["ptb_f"])
            sc.op("dve", lambda e: e.tensor_scalar(out=ptb_f[:], in0=ptb_f[:], scalar1=128.0, scalar2=iota_p[:, 0:1],
                                                   op0=ALU.mult, op1=ALU.add), ["ptb_f", "iota_p"], ["ptb_f"])
            sc.op("dve", lambda e: e.tensor_copy(out=idx_i[:], in_=ptb_f[:]), ["ptb_f"], ["idx_i"])
            NPAR = 3
            TS = []
            for p in range(NPAR):
                T = {}
                T["fpage"] = sbG(f"fpage{p}", [128, 1024]); T["npage"] = sbG(f"npage{p}", [128, 512]); T["lpage"] = sbG(f"lpage{p}", [128, 8])
                T["kb"] = sbG(f"kb{p}", [128, 512], BF16); T["nb"] = sbG(f"nb{p}", [128, 384], BF16)
                T["stg_k"] = sbG(f"stg_k{p}", [128, 4, 128], BF16); T["stg_x"] = sbG(f"stg_x{p}", [128, 3, 128], BF16)
                T["stg_l"] = sbG(f"stg_l{p}", [8, 128])
                T["vaug"] = sbG(f"vaug{p}", [128, 8, 65], BF16); T["vsaug"] = sbG(f"vsaug{p}", [128, 2, 65], BF16)
                sc.op("dve", lambda e, T=T: e.memset(T["vaug"][:], 1.0), [], [f"vaug{p}"])
                sc.op("dve", lambda e, T=T: e.memset(T["vsaug"][:], 1.0), [], [f"vsaug{p}"])
                TS.append(T)
            PSG = []
            for p in range(2):
                PSG.append({"kt": psG(f"ps_kt{p}", [128, 8, 128], BF16)[:, 0:4, :], "nt": psG(f"ps_nt{p}", [128, 8, 128], BF16)[:, 0:3, :],
                            "lf": psG(f"ps_lf{p}", [128, 512])})
            pcount = 0
            for b in range(NSEQ):
                c = ctx_s[b]
                for pg in range(NPG):
                    p = pcount % NPAR
                    q = pcount % 2
                    pcount += 1
                    T = TS[p]
                    P = PSG[q]
                    col = b * NPG + pg
                    cs = slice(pg * 128, (pg + 1) * 128)
                    nm = c.name

                    def R(x, p=p):
                        return f"{x}{p}"

                    def Q(x, q=q):
                        return f"G{x}{q}"
                    sc.dma("pool", lambda e, col=col, T=T: e.indirect_dma_start(
                        out=T["fpage"][:, :], out_offset=None, in_=pool_fox[:, :],
                        in_offset=bass.IndirectOffsetOnAxis(ap=idx_i[:, col:col + 1], axis=0)), ["idx_i"], [R("fpage")])
                    sc.dma("pool", lambda e, col=col, T=T: e.indirect_dma_start(
                        out=T["npage"][:, :], out_offset=None, in_=pool_nsa[:, :],
                        in_offset=bass.IndirectOffsetOnAxis(ap=idx_i[:, col:col + 1], axis=0)), ["idx_i"], [R("npage")])
                    sc.dma("pool", lambda e, col=col, T=T: e.indirect_dma_start(
                        out=T["lpage"][:, :], out_offset=None, in_=pool_lf[:, :],
                        in_offset=bass.IndirectOffsetOnAxis(ap=idx_i[:, col:col + 1], axis=0)), ["idx_i"], [R("lpage")])
                    sc.op("dve", lambda e, T=T: e.tensor_copy(out=T["kb"][:], in_=T["fpage"][:, 0:512]), [R("fpage")], [R("kb")])
                    for a in range(4):
                        sc.op("pe", lambda e, a=a, T=T, P=P: e.transpose(out=P["kt"][:, a, :], in_=T["kb"][:, a * 128:(a + 1) * 128], identity=ident_b[:]),
                              [R("kb"), "ident_b"], [Q("kt")])
                    sc.op("act", lambda e, T=T, P=P: e.activation(out=T["stg_k"][:], in_=P["kt"][:], func=AF.Copy), [Q("kt")], [R("stg_k")])
                    ktv = c.KT[0:64, :, cs].rearrange("p (a two) t -> p a two t", two=2)
                    sc.dma("sp", lambda e, T=T, ktv=ktv: e.dma_start(out=ktv[:, :, 0, :], in_=T["stg_k"][0:64, :, :]), [R("stg_k")], ["KT_" + nm])
                    sc.dma("sp", lambda e, T=T, ktv=ktv: e.dma_start(out=ktv[:, :, 1, :], in_=T["stg_k"][64:128, :, :]), [R("stg_k")], ["KT_" + nm])
                    sc.op("act", lambda e, T=T: e.activation(out=T["vaug"][:, :, 0:64], in_=T["fpage"][:, 512:1024].rearrange("p (h d) -> p h d", d=64),
                                                             func=AF.Copy), [R("fpage")], [R("vaug")])
                    sc.dma("sp", lambda e, T=T, c=c, pg=pg: e.dma_start(out=c.V[:, :, pg, :].rearrange("h p d -> p h d"), in_=T["vaug"][:]),
                           [R("vaug")], ["V_" + nm])
                    sc.op("pe", lambda e, T=T, P=P: e.transpose(out=P["lf"][0:8, 0:128], in_=T["lpage"][:, :], identity=ident_f[:]),
                          [R("lpage"), "ident_f"], [Q("lf")])
                    sc.op("dve", lambda e, T=T, P=P: e.tensor_copy(out=T["stg_l"][:], in_=P["lf"][0:8, 0:128]), [Q("lf")], [R("stg_l")])
                    sc.dma("sp", lambda e, T=T, c=c, cs=cs: e.dma_start(out=c.LF[:, cs], in_=T["stg_l"][:]), [R("stg_l")], ["LF_" + nm])
                    sc.op("dve", lambda e, T=T: e.tensor_copy(out=T["nb"][:], in_=T["npage"][:, 0:384]), [R("npage")], [R("nb")])
                    for a in range(3):
                        sc.op("pe", lambda e, a=a, T=T, P=P: e.transpose(out=P["nt"][:, a, :], in_=T["nb"][:, a * 128:(a + 1) * 128], identity=ident_b[:]),
                              [R("nb"), "ident_b"], [Q("nt")])
                    sc.op("act", lambda e, T=T, P=P: e.activation(out=T["stg_x"][:], in_=P["nt"][:], func=AF.Copy), [Q("nt")], [R("stg_x")])
                    sc.dma("sp", lambda e, T=T, c=c, cs=cs: e.dma_start(out=c.XC[:, :, cs].rearrange("k p t -> p k t"), in_=T["stg_x"][:, 0:2, :]),
                           [R("stg_x")], ["XC_" + nm])
                    sc.dma("sp", lambda e, T=T, c=c, cs=cs: e.dma_start(out=c.KS[0, :, cs], in_=T["stg_x"][0:64, 2, :]), [R("stg_x")], ["KS_" + nm])
                    sc.dma("sp", lambda e, T=T, c=c, cs=cs: e.dma_start(out=c.KS[1, :, cs], in_=T["stg_x"][64:128, 2, :]), [R("stg_x")], ["KS_" + nm])
                    sc.op("act", lambda e, T=T: e.activation(out=T["vsaug"][:, :, 0:64], in_=T["npage"][:, 384:512].rearrange("p (h d) -> p h d", d=64),
                                                             func=AF.Copy), [R("npage")], [R("vsaug")])
                    sc.dma("sp", lambda e, T=T, c=c, pg=pg: e.dma_start(out=c.VS[:, :, pg, :].rearrange("h p d -> p h d"), in_=T["vsaug"][:]),
                           [R("vsaug")], ["VS_" + nm])
            sc.flush(st)

        with contextlib.ExitStack() as stC:
            def sbC(name, shape, dt=F32):
                return stC.enter_context(nc.sbuf_tensor(name, list(shape), dt))
            LsM = max(S, LS) // 16
            lf_f = sbC("lf_f", [128, LsM])
            tv_f = sbC("tv_f", [128, LsM])
            pn_f = sbC("pn_f", [128, LsM])
            cum_f = sbC("cum_f", [128, LsM])
            r_f = sbC("r_f", [128, LsM])
            hi_b = sbC("hi_b", [128, LsM], BF16)
            mid_b = sbC("mid_b", [128, LsM], BF16)
            lo_b = sbC("lo_b", [128, LsM], BF16)
            tot = sbC("tot", [128, 1])
            offs = sbC("offs", [128, 1])
            btm = sbC("btm", [128, 128])
            NQM = max(NQ, 8)
            cq = sbC("cq", [8, NQM])
            cqr = sbC("cqr", [8, NQM])
            cq_b = sbC("cq_b", [8, 3, NQM], BF16)
            kwn_f = sbC("kwn_f", [128, max(S, LS) // 128])
            kwn_b = sbC("kwn_b", [128, max(S, LS) // 128], BF16)
            ps_o = stC.enter_context(nc.psum_tensor("ps_o", [128, 8], F32))
            ones_c = sbC("ones_c", [128, LsM], BF16)
            ones_q = sbC("ones_q", [8, 3, NQM], BF16)
            sc.op("dve", lambda e: e.memset(ones_c[:], 1.0), [], ["ones_c"])
            sc.op("dve", lambda e: e.memset(ones_q[:], 1.0), [], ["ones_q"])
            sc.dma("sp", lambda e: e.dma_start(out=btm[:], in_=bt_in[:, :]), [], ["btm"])

            def crows(c, tv_in, pn_in, kwn_in, qsel):
                Ls = c.L // 16
                nm = c.name
                sc.dma("sp", lambda e: e.dma_start(out=lf_f[:, 0:Ls], in_=c.LF.rearrange("h (s t) -> (h s) t", s=16)), ["LF_" + nm], ["lf_f"])
                sc.dma("sp", lambda e: e.dma_start(out=tv_f[:, 0:Ls], in_=tv_in), [], ["tv_f"])
                sc.dma("sp", lambda e: e.dma_start(out=pn_f[:, 0:Ls], in_=pn_in), [], ["pn_f"])
                sc.op("dve", lambda e: e.tensor_tensor(out=lf_f[:, 0:Ls], in0=lf_f[:, 0:Ls], in1=tv_f[:, 0:Ls], op=ALU.mult),
                      ["lf_f", "tv_f"], ["lf_f"])
                sc.op("dve", lambda e: e.memset(r_f[:, 0:Ls], 1.0), [], ["r_f"])
                sc.op("dve", lambda e: e.tensor_tensor_scan(out=cum_f[:, 0:Ls], data0=r_f[:, 0:Ls], data1=lf_f[:, 0:Ls], initial=0.0,
                                                            op0=ALU.mult, op1=ALU.add), ["r_f", "lf_f"], ["cum_f"])
                sc.op("dve", lambda e: e.tensor_copy(out=tot[:], in_=cum_f[:, Ls - 1:Ls]), ["cum_f"], ["tot"])
                sc.op("pe", lambda e: e.matmul(ps_o[:, 0:1], lhsT=btm[:], rhs=tot[:], start=True, stop=True), ["btm", "tot"], ["ps_o"])
                sc.op("dve", lambda e: e.tensor_copy(out=offs[:], in_=ps_o[:, 0:1]), ["ps_o"], ["offs"])
                sc.op("dve", lambda e: e.tensor_scalar(out=cum_f[:, 0:Ls], in0=cum_f[:, 0:Ls], scalar1=offs[:, 0:1], scalar2=None, op0=ALU.add),
                      ["cum_f", "offs"], ["cum_f"])
                sc.dma("sp", lambda e: e.dma_start(out=c.CUM.rearrange("h (s t) -> (h s) t", s=16), in_=cum_f[:, 0:Ls]), ["cum_f"], ["CUM_" + nm])
                sc.op("dve", lambda e: e.scalar_tensor_tensor(out=r_f[:, 0:Ls], in0=cum_f[:, 0:Ls], scalar=-1.0, in1=pn_f[:, 0:Ls],
                                                              op0=ALU.mult, op1=ALU.add), ["cum_f", "pn_f"], ["r_f"])
                sc.op("dve", lambda e: e.tensor_copy(out=hi_b[:, 0:Ls], in_=r_f[:, 0:Ls]), ["r_f"], ["hi_b"])
                sc.op("dve", lambda e: e.tensor_tensor(out=r_f[:, 0:Ls], in0=r_f[:, 0:Ls], in1=hi_b[:, 0:Ls], op=ALU.subtract), ["r_f", "hi_b"], ["r_f"])
                sc.op("dve", lambda e: e.tensor_copy(out=mid_b[:, 0:Ls], in_=r_f[:, 0:Ls]), ["r_f"], ["mid_b"])
                sc.op("dve", lambda e: e.tensor_tensor(out=r_f[:, 0:Ls], in0=r_f[:, 0:Ls], in1=mid_b[:, 0:Ls], op=ALU.subtract), ["r_f", "mid_b"], ["r_f"])
                sc.op("dve", lambda e: e.tensor_copy(out=lo_b[:, 0:Ls], in_=r_f[:, 0:Ls]), ["r_f"], ["lo_b"])
                for i, (t, tn) in enumerate([(hi_b, "hi_b"), (mid_b, "mid_b"), (lo_b, "lo_b")]):
                    sc.dma("sp", lambda e, i=i, t=t: e.dma_start(out=c.KT[67 + i, :, :].rearrange("h (s t) -> (h s) t", s=16), in_=t[:, 0:Ls]),
                           [tn], ["KT_" + nm])
                    sc.dma("sp", lambda e, i=i: e.dma_start(out=c.KT[64 + i, :, :].rearrange("h (s t) -> (h s) t", s=16),
                                                            in_=ones_c[:, 0:Ls]), ["ones_c"], ["KT_" + nm])
                Lb = c.L // 128
                sc.dma("sp", lambda e: e.dma_start(out=kwn_f[:, 0:Lb], in_=kwn_in), [], ["kwn_f"])
                sc.op("dve", lambda e: e.tensor_copy(out=kwn_b[:, 0:Lb], in_=kwn_f[:, 0:Lb]), ["kwn_f"], ["kwn_b"])
                for g in range(2):
                    sc.dma("sp", lambda e, g=g: e.dma_start(out=c.KW[g, 64, :].rearrange("(p t) -> p t", p=128), in_=kwn_b[:, 0:Lb]),
                           ["kwn_b"], ["KW_" + nm])
                nq = c.nqc
                sc.dma("sp", lambda e: e.dma_start(out=qsel(c.CUM)[1], in_=qsel(c.CUM)[0]), ["CUM_" + nm], ["cq"])
                sc.op("dve", lambda e: e.tensor_copy(out=cq_b[:, 0, 0:nq], in_=cq[:, 0:nq]), ["cq"], ["cq_b"])
                sc.op("dve", lambda e: e.tensor_tensor(out=cqr[:, 0:nq], in0=cq[:, 0:nq], in1=cq_b[:, 0, 0:nq], op=ALU.subtract), ["cq", "cq_b"], ["cqr"])
                sc.op("dve", lambda e: e.tensor_copy(out=cq_b[:, 1, 0:nq], in_=cqr[:, 0:nq]), ["cqr"], ["cq_b"])
                sc.op("dve", lambda e: e.tensor_tensor(out=cqr[:, 0:nq], in0=cqr[:, 0:nq], in1=cq_b[:, 1, 0:nq], op=ALU.subtract), ["cqr", "cq_b"], ["cqr"])
                sc.op("dve", lambda e: e.tensor_copy(out=cq_b[:, 2, 0:nq], in_=cqr[:, 0:nq]), ["cqr"], ["cq_b"])
                sc.dma("sp", lambda e: e.dma_start(out=c.QT[64:67, :, :].rearrange("a h n -> h a n"), in_=cq_b[:, :, 0:nq]), ["cq_b"], ["QT_" + nm])
                sc.dma("sp", lambda e: e.dma_start(out=c.QT[67:70, :, :].rearrange("a h n -> h a n"), in_=ones_q[:, :, 0:nq]), ["ones_q"], ["QT_" + nm])
                sc.dma("sp", lambda e: e.dma_start(out=c.QN[64, :, :], in_=ones_q[:, 0, 0:nq]), ["ones_q"], ["QN_" + nm])

            crows(ctx_p, tvp_in[:, :], pnp_in[:, :], kwnp_in[:, :],
                  lambda CUM: (CUM.rearrange("h (j e t) -> h j e t", e=8, t=128)[:, :, 7, :],
                               cq[:, 0:NQ].rearrange("h (j t) -> h j t", t=128)))
            for b in range(NSEQ):
                crows(ctx_s[b], tvs_in[:, :], pns_in[:, :], kwns_in[:, :], lambda CUM: (CUM[:, cfg.PAST:cfg.PAST + 8], cq[:, 0:8]))
            sc.flush(st)

        with contextlib.ExitStack() as stB:
            def sbB(name, shape, dt=F32):
                return stB.enter_context(nc.sbuf_tensor(name, list(shape), dt))
            LM = max(S, LS)
            KTt = [sbB(f"KTt{i}", [70, LM], BF16) for i in range(2)]
            Vt = [sbB(f"Vt{i}", [128, LM // 128, 65], BF16) for i in range(2)]
            QTt = [sbB(f"QTt{i}", [70, max(NQ, 8)], BF16) for i in range(2)]
            pT = [sbB(f"pT{i}", [128, 512], BF16) for i in range(2)]
            rec = sbB("rec", [128, 4])
            oa_res = sbB("oa_res", [128, NOWN, 512], BF16)
            oas8 = sbB("oas8", [8, 512], BF16)
            ps_s = [stB.enter_context(nc.psum_tensor(f"ps_s{i}", [128, 512], F32)) for i in range(2)]
            po = stB.enter_context(nc.psum_tensor("po", [128, 4, 65], F32))

            def attn(KT, KTn, Ka, V, Vn, QT, QTn, q0, w, nsub, blocks, dst, kb=1):
                last = {}
                for (f, jjmin, diag) in blocks:
                    for jj in range(jjmin, nsub):
                        last[jj] = f
                steps = []
                for i in range(0, len(blocks), kb):
                    grp = []
                    coff = 0
                    for (f, jjmin, diag) in blocks[i:i + kb]:
                        grp.append((f, jjmin, diag, coff))
                        coff += (nsub - jjmin) * w
                    steps.append((grp, coff))
                sc.op("pe", lambda e: e.matmul(po[0:w, 0:nsub, :], lhsT=zeros_b[:, 0:w], rhs=zeros_b[:, 0:nsub * 65].rearrange("p (j d) -> p j d", d=65),
                                               start=True, stop=False), ["zeros_b"], ["po"])

                def qk(t):
                    grp, n = steps[t]
                    pst = ps_s[t % 2]
                    pn = f"ps_s{t % 2}"
                    for (f, jjmin, diag, coff) in grp:
                        nb_ = (nsub - jjmin) * w
                        sc.op("pe", lambda e, f=f, jjmin=jjmin, coff=coff, nb_=nb_, diag=diag: e.matmul(
                            pst[:, coff:coff + nb_], lhsT=KT[0:Ka, f * 128:(f + 1) * 128],
                            rhs=QT[0:Ka, q0 + jjmin * w:q0 + nsub * w], start=True, stop=not diag), [KTn, QTn], [pn])
                        if diag:
                            sc.op("pe", lambda e, coff=coff: e.matmul(pst[:, coff:coff + w], lhsT=ident_b[:], rhs=tri_b[:, 0:w], start=False, stop=True),
                                  ["ident_b", "tri_b"], [pn])
                qk(0)
                for t, (grp, n) in enumerate(steps):
                    pst = ps_s[t % 2]
                    sc.op("act", lambda e, pst=pst, t=t, n=n: e.activation(out=pT[t % 2][:, 0:n], in_=pst[:, 0:n], func=AF.Exp),
                          [f"ps_s{t % 2}"], [f"pT{t % 2}"])
                    if t + 1 < len(steps):
                        qk(t + 1)
                    for (f, jjmin, diag, coff) in grp:
                        for jj in range(jjmin, nsub):
                            sc.op("pe", lambda e, t=t, jj=jj, jjmin=jjmin, f=f, coff=coff: e.matmul(
                                po[0:w, jj, :], lhsT=pT[t % 2][:, coff + (jj - jjmin) * w:coff + (jj - jjmin + 1) * w], rhs=V[:, f, :],
                                start=False, stop=(f == last[jj])), [f"pT{t % 2}", Vn], ["po"])
                sc.op("dve", lambda e: e.reciprocal(out=rec[0:w, 0:nsub], in_=po[0:w, 0:nsub, 64]), ["po"], ["rec"])
                sc.op("dve", lambda e: e.tensor_tensor(out=dst, in0=po[0:w, 0:nsub, 0:64],
                                                       in1=rec[0:w, 0:nsub].unsqueeze(2).to_broadcast([w, nsub, 64]), op=ALU.mult),
                      ["po", "rec"], ["attn_dst"])

            nsub = min(4, NOWN)
            hcount = 0

            def load_head(c, src_kt, src_v, src_qt, Ka, h, i):
                sc.dma("sp", lambda e: e.dma_start(out=KTt[i][0:Ka, 0:c.L], in_=src_kt), ["KT_" + c.name, "KS_" + c.name, "KW_" + c.name], [f"KTt{i}"])
                sc.dma("sp", lambda e: e.dma_start(out=Vt[i][:, 0:c.NBk, :], in_=src_v), ["V_" + c.name, "VS_" + c.name, "VW_" + c.name], [f"Vt{i}"])
                sc.dma("sp", lambda e: e.dma_start(out=QTt[i][0:Ka, 0:c.nqc], in_=src_qt), ["QT_" + c.name, "QN_" + c.name], [f"QTt{i}"])

            for h in range(8):
                i = hcount % 2
                hcount += 1
                load_head(ctx_p, ctx_p.KT[:, h, :], ctx_p.V[h], ctx_p.QT[:, h, :], 70, h, i)
                for J in range(NOWN // nsub):
                    Fs = [8 * (J * nsub + jj) + 7 for jj in range(nsub)]
                    blocks = []
                    for f in range(Fs[-1] + 1):
                        jjmin = min(jj for jj in range(nsub) if Fs[jj] >= f)
                        blocks.append((f, jjmin, f == Fs[jjmin]))
                    attn(KTt[i], f"KTt{i}", 70, Vt[i], f"Vt{i}", QTt[i], f"QTt{i}", J * nsub * 128, 128, nsub, blocks,
                         oa_res[:, J * nsub:(J + 1) * nsub, h * 64:(h + 1) * 64])
            for b in range(NSEQ):
                c = ctx_s[b]
                for h in range(8):
                    i = hcount % 2
                    hcount += 1
                    load_head(c, c.KT[:, h, :], c.V[h], c.QT[:, h, :], 70, h, i)
                    blocks = [(f, 0, f == NPG) for f in range(NPG + 1)]
                    attn(KTt[i], f"KTt{i}", 70, Vt[i], f"Vt{i}", QTt[i], f"QTt{i}", 0, 8, 1, blocks,
                         oas8[:, h * 64:(h + 1) * 64].unsqueeze(1), kb=16)
                sc.dma("sp", lambda e, b=b: e.dma_start(out=OAS[b * 8:(b + 1) * 8, :], in_=oas8[:]), ["attn_dst"], ["OAS"])
            for j in range(NOWN):
                sc.dma("sp", lambda e, j=j: e.dma_start(out=OAP[j * 128:(j + 1) * 128, :], in_=oa_res[:, j, :]), ["attn_dst"], ["OAP"])
                sc.dma("sp", lambda e, j=j: e.dma_start(out=dbg_oa_p[j * 128:(j + 1) * 128, :], in_=oa_res[:, j, :]), ["attn_dst"], [])
            sc.dma("sp", lambda e: e.dma_start(out=dbg_oa_s[:, :], in_=OAS[:, :]), ["OAS"], [])
            sc.flush(st)
        if cfg.nsa:
            LPC, OFFC, LP1, OFF1 = 4096, 1856, 768, 128
            BVC = dscr("BVC", [8, 128, LPC], BF16)
            BV1 = dscr("BV1", [8, 128, LP1], BF16)
            with contextlib.ExitStack() as stN:
                def sbN(name, shape, dt=F32):
                    return stN.enter_context(nc.sbuf_tensor("N_" + name, list(shape), dt))

                def psN(name, shape, dt=F32):
                    return stN.enter_context(nc.psum_tensor("N_" + name, list(shape), dt))

                def Kof(F, w):
                    nlast = (128 * F + (w - 1) - 31) // 16
                    tb = nlast // 128
                    return 128 * F - 2048 * tb - 31, tb
                Kvars = []
                for jo in range(NOWN):
                    k_, _ = Kof(8 * jo + 7, 128)
                    if k_ not in Kvars:
                        Kvars.append(k_)
                ks_, _ = Kof(NPG, 8)
                if ks_ not in Kvars:
                    Kvars.append(ks_)
                NV = len(Kvars)
                TCB = sbN("TCB", [128, NV, 8, 128], BF16)
                TSW = sbN("TSW", [128, 5, 8, 128], BF16)
                EEB = dscr("EEB", [128, 8192], BF16)
                gkc = sbN("gkc", [64, 1])
                hbias2 = sbN("hbias2", [128, 2])
                w2b2 = sbN("w2b2", [128, 2, 64], BF16)
                W1B = dscr("W1B", [2, 128, 32, 128], BF16)
                with contextlib.ExitStack() as st0n:
                    def sb0n(name, shape, dt=F32):
                        return st0n.enter_context(nc.sbuf_tensor("N0_" + name, list(shape), dt))
                    rb = sb0n("rb", [33, 8]); rb31 = sb0n("rb31", [32, 8])
                    ohc_sb = sb0n("ohc_sb", [33, LPC]); oh1_sb = sb0n("oh1_sb", [33, LP1])
                    vrow = sb0n("vrow", [8, LPC]); vrow_b = sb0n("vrow_b", [8, LPC], BF16)
                    eestg = sb0n("eestg", [128, 2048])
                    eeb = sb0n("eeb", [128, 2048], BF16)
                    ps_v0 = st0n.enter_context(nc.psum_tensor("N0_ps_v0", [8, 512], F32))
                    sc.dma("sp", lambda e: e.dma_start(out=rb[0:32, :], in_=rel_bias[:, :]), [], ["rb"])
                    sc.dma("sp", lambda e: e.dma_start(out=rb31[:], in_=rel_bias[31:32, :].partition_broadcast(32)), [], ["rb31"])
                    sc.dma("sp", lambda e: e.dma_start(out=ohc_sb[:], in_=ohc_in[:, :]), [], ["ohc_sb"])
                    sc.dma("sp", lambda e: e.dma_start(out=oh1_sb[:], in_=oh1_in[:, :]), [], ["oh1_sb"])
                    sc.dma("sp", lambda e: e.dma_start(out=gkc[:], in_=g_qk_nsa[1:2, :].rearrange("a d -> d a"), allow_slow_non_contiguous=True), [], ["gkc"])
                    sc.op("dve", lambda e: e.tensor_tensor(out=rb[0:32, :], in0=rb[0:32, :], in1=rb31[:], op=ALU.subtract), ["rb", "rb31"], ["rb"])
                    sc.op("dve", lambda e: e.memset(rb[32:33, :], NEG), ["rb"], ["rb"])
                    for q in range(4):
                        sc.dma("sp", lambda e, q=q: e.dma_start(out=eestg[:], in_=ee_in[:, q * 2048:(q + 1) * 2048]), [], ["eestg"])
                        sc.op("pool", lambda e, q=q: e.tensor_copy(out=eeb[:], in_=eestg[:]), ["eestg"], ["eeb"])
                        sc.dma("sp", lambda e, q=q: e.dma_start(out=EEB[:, q * 2048:(q + 1) * 2048], in_=eeb[:]), ["eeb"], ["EEB"])

                    def mk_v(oh_sb, ohn, Lp, BV, bvn):
                        for c0 in range(0, Lp, 512):
                            n = min(512, Lp - c0)
                            sc.op("pe", lambda e, c0=c0, n=n: e.matmul(ps_v0[0:8, 0:n], lhsT=rb[0:33, 0:8], rhs=oh_sb[0:33, c0:c0 + n],
                                                                       start=True, stop=True), ["rb", ohn], ["ps_v0"])
                            sc.op("dve", lambda e, c0=c0, n=n: e.tensor_copy(out=vrow[:, c0:c0 + n], in_=ps_v0[0:8, 0:n]), ["ps_v0"], ["vrow"])
                        sc.op("dve", lambda e: e.tensor_copy(out=vrow_b[:, 0:Lp], in_=vrow[:, 0:Lp]), ["vrow"], ["vrow_b"])
                        for r0 in range(0, 128, 16):
                            sc.dma("sp", lambda e, r0=r0: e.dma_start(out=BV[:, r0:r0 + 16, :], in_=vrow_b[:, 0:Lp].unsqueeze(1).to_broadcast([8, 16, Lp])),
                                   ["vrow_b"], [bvn])
                    w1f = sb0n("w1f", [128, 32, 128]); w1bb = sb0n("w1bb", [128, 32, 128], BF16)
                    pef = sb0n("pef", [32, 64]); pebb = sb0n("pebb", [64, 32], BF16)
                    w2f = sb0n("w2f", [128, 64])
                    ps_w = st0n.enter_context(nc.psum_tensor("N0_ps_w", [128, 512], F32))
                    for kv in range(2):
                        for half in range(2):
                            sc.dma("sp", lambda e, kv=kv, half=half: e.dma_start(
                                out=w1f[half * 64:(half + 1) * 64], in_=w_cmp1[kv].rearrange("(r d) h -> d r h", d=64)), [], ["w1f"])
                        sc.op("pool", lambda e: e.tensor_copy(out=w1bb[:], in_=w1f[:]), ["w1f"], ["w1bb"])
                        sc.dma("sp", lambda e, kv=kv: e.dma_start(out=W1B[kv], in_=w1bb[:]), ["w1bb"], ["W1B"])
                        sc.dma("sp", lambda e, kv=kv: e.dma_start(out=pef[:], in_=pe_cmp[kv]), [], ["pef"])
                        sc.op("pe", lambda e: e.transpose(out=ps_w[0:64, 0:32], in_=pef[0:32, 0:64], identity=ident_f[0:32, 0:32]),
                              ["pef", "ident_f"], ["ps_w"])
                        sc.op("dve", lambda e: e.tensor_copy(out=pebb[:], in_=ps_w[0:64, 0:32]), ["ps_w"], ["pebb"])
                        for r in range(32):
                            sc.op("pe", lambda e, r=r: e.matmul(ps_w[:, 64:65], lhsT=w1bb[0:64, r, :], rhs=pebb[0:64, r:r + 1],
                                                                start=(r == 0), stop=(r == 31)), ["w1bb", "pebb"], ["ps_w"])
                        sc.op("dve", lambda e, kv=kv: e.tensor_copy(out=hbias2[:, kv:kv + 1], in_=ps_w[:, 64:65]), ["ps_w"], ["hbias2"])
                        sc.dma("sp", lambda e, kv=kv: e.dma_start(out=w2f[:], in_=w_cmp2[kv]), [], ["w2f"])
                        sc.op("dve", lambda e, kv=kv: e.tensor_copy(out=w2b2[:, kv, :], in_=w2f[:]), ["w2f"], ["w2b2"])
                    mk_v(ohc_sb, "ohc_sb", LPC, BVC, "BVC")
                    for vi, Kv in enumerate(Kvars):
                        for h in range(8):
                            sc.dma("sp", lambda e, vi=vi, Kv=Kv, h=h: e.dma_start(
                                out=TCB[:, vi, h, :], in_=bass.AP(tensor=BVC.tensor, offset=h * 128 * LPC + Kv + OFFC, ap=[[LPC - 16, 128], [1, 128]])),
                                ["BVC"], ["TCB"])
                    mk_v(oh1_sb, "oh1_sb", LP1, BV1, "BV1")
                    for di in range(5):
                        for h in range(8):
                            sc.dma("sp", lambda e, di=di, h=h: e.dma_start(
                                out=TSW[:, di, h, :], in_=bass.AP(tensor=BV1.tensor, offset=h * 128 * LP1 + di * 128 + OFF1, ap=[[LP1 - 1, 128], [1, 128]])),
                                ["BV1"], ["TSW"])
                    sc.flush(st)

                LM = max(S, LS)
                NTM = max(cfg.NTp, cfg.NTs)
                NMM = max(cfg.NMp, cfg.NMs)
                KC = sbN("KC", [65, 2, NTM * 128], BF16)
                VCW = sbN("VCW", [128, 2, NTM, 65 + NMM], BF16)
                att = {}
                KWt = sbN("KWt", [65, 5, 128], BF16)
                VWt = sbN("VWt", [128, 5, 65], BF16)
                ob_res = sbN("ob_res", [128, NOWN, 512], BF16)
                obs8 = sbN("obs8", [8, 512], BF16)
                obg = sbN("obg", [128, 4, 64])
                imp = sbN("imp", [128, NMM]); addt = sbN("addt", [128, NMM]); wk = sbN("wk", [128, NMM])
                mk = sbN("mk", [128, NMM]); mk2 = sbN("mk2", [128, NMM])
                negb = sbN("negb", [128, NMM], BF16)
                neg4 = sbN("neg4", [128, NMM // 128, 4, 128], BF16)
                m8a = sbN("m8a", [128, 8]); m8b = sbN("m8b", [128, 8])
                rz = sbN("rz", [128, 4]); coef = sbN("coef", [128, 4])
                pTn = [sbN(f"pTn{i}", [128, 512], BF16) for i in range(2)]
                ps_s = [psN(f"ps_s{i}", [128, 512]) for i in range(2)]
                po_c = psN("po_c", [128, 65 + NMM])
                po_s = psN("po_s", [128, 4, 65])
                po_w = psN("po_w", [128, 4, 65])
                ps_t = psN("ps_t", [128, NMM // 128, 128], BF16)
                step = {"t": 0}

                def compress(c, Lc, NT, NM, wc_in, cneg_in):
                    nblk = Lc // 16 - 1
                    nm = c.name
                    with contextlib.ExitStack() as stc:
                        def sbc(name, shape, dt=F32):
                            return stc.enter_context(nc.sbuf_tensor(f"C{nm}_" + name, list(shape), dt))
                        xct = sbc("xct", [128, c.L], BF16)
                        w1b = sbc("w1b", [128, 32, 128], BF16)
                        xs_ = sbc("xs", [128, 512]); x2 = sbc("x2", [128, 512]); sg = sbc("sg", [128, 512])
                        hidT = sbc("hidT", [128, 512], BF16)
                        sqb = sbc("sqb", [64, 512], BF16)
                        rs_ = sbc("rs", [64, 512]); rr = sbc("rr", [64, 512])
                        nh64 = sbc("nh64", [64, 512])
                        wstg_ = sbc("wstg", [128, NM])
                        cn_f = sbc("cn_f", [65, NT * 128])
                        ps_h = ps_s[0]
                        ps_k = ps_s[1]
                        ps_q = stc.enter_context(nc.psum_tensor(f"C{nm}_ps_q", [128, 512], F32))
                        sc.op("dve", lambda e: e.memset(nh64[:], -0.5), [], ["nh64"])
                        sc.op("dve", lambda e: e.memset(KC[:], 0.0), ["KC"], ["KC"])
                        sc.op("dve", lambda e: e.memset(VCW[:], 0.0), ["VCW"], ["VCW"])
                        sc.op("dve", lambda e: e.memset(VCW[:, :, :, 64:65], 1.0), ["VCW"], ["VCW"])
                        sc.dma("sp", lambda e: e.dma_start(out=cn_f[64:65, :], in_=cneg_in), [], ["cn_f"])
                        for g in range(2):
                            sc.op("dve", lambda e, g=g: e.tensor_copy(out=KC[64:65, g, 0:NT * 128], in_=cn_f[64:65, :]), ["cn_f", "KC"], ["KC"])
                        for t in range(NT):
                            sc.dma("sp", lambda e, t=t: e.dma_start(out=wstg_[:], in_=wc_in[t * 128:(t + 1) * 128, :]), [], ["wstg_"])
                            for g in range(2):
                                sc.op("dve", lambda e, t=t, g=g: e.tensor_copy(out=VCW[:, g, t, 65:65 + NM], in_=wstg_[:]), ["wstg_", "VCW"], ["VCW"])
                        for kv in range(2):
                            sc.dma("sp", lambda e, kv=kv: e.dma_start(out=xct[:], in_=c.XC[kv]), ["XC_" + nm], ["xct"])
                            sc.dma("sp", lambda e, kv=kv: e.dma_start(out=w1b[:], in_=W1B[kv]), ["W1B"], ["w1b"])
                            hbias = hbias2[:, kv:kv + 1]
                            w2b = w2b2[:, kv, :]
                            xv = xct[:, 0:Lc].rearrange("p (n s) -> p n s", s=16)
                            for g in range(2):
                                for n0 in range(0, nblk, 512):
                                    nn = min(512, nblk - n0)
                                    for r in range(32):
                                        sc.op("pe", lambda e, r=r, g=g, n0=n0, nn=nn: e.matmul(
                                            ps_h[:, 0:nn], lhsT=w1b[g * 64:(g + 1) * 64, r, :],
                                            rhs=xv[g * 64:(g + 1) * 64, n0 + r // 16:n0 + r // 16 + nn, r % 16],
                                            start=(r == 0), stop=(r == 31)), ["w1b", "xct"], ["ps_h"])
                                    sc.op("act", lambda e, nn=nn, hbias=hbias: e.activation(out=xs_[:, 0:nn], in_=ps_h[:, 0:nn], func=AF.Identity, bias=hbias),
                                          ["ps_h", "hbias2"], ["xs"])
                                    sc.op("dve", lambda e, nn=nn: e.tensor_tensor(out=x2[:, 0:nn], in0=xs_[:, 0:nn], in1=xs_[:, 0:nn], op=ALU.mult), ["xs"], ["x2"])
                                    sc.op("dve", lambda e, nn=nn: e.tensor_scalar(out=x2[:, 0:nn], in0=x2[:, 0:nn], scalar1=0.044715, scalar2=1.0,
                                                                                 op0=ALU.mult, op1=ALU.add), ["x2"], ["x2"])
                                    sc.op("dve", lambda e, nn=nn: e.tensor_tensor(out=x2[:, 0:nn], in0=x2[:, 0:nn], in1=xs_[:, 0:nn], op=ALU.mult), ["x2", "xs"], ["x2"])
                                    sc.op("act", lambda e, nn=nn: e.activation(out=sg[:, 0:nn], in_=x2[:, 0:nn], func=AF.Sigmoid, scale=1.5957691216057308),
                                          ["x2"], ["sg"])
                                    sc.op("dve", lambda e, nn=nn: e.tensor_tensor(out=hidT[:, 0:nn], in0=xs_[:, 0:nn], in1=sg[:, 0:nn], op=ALU.mult),
                                          ["xs", "sg"], ["hidT"])
                                    if kv == 0:
                                        sc.op("pe", lambda e, nn=nn, w2b=w2b: e.matmul(ps_k[0:64, 0:nn], lhsT=w2b, rhs=hidT[:, 0:nn], start=True, stop=True),
                                              ["w2b2", "hidT"], ["ps_k"])
                                        sc.op("act", lambda e, nn=nn: e.activation(out=sqb[:, 0:nn], in_=ps_k[0:64, 0:nn], func=AF.Square), ["ps_k"], ["sqb"])
                                        sc.op("pe", lambda e, nn=nn: e.matmul(ps_q[0:64, 0:nn], lhsT=ones_b[0:64, 0:64], rhs=sqb[:, 0:nn], start=True, stop=True),
                                              ["ones_b", "sqb"], ["ps_q"])
                                        sc.op("dve", lambda e, nn=nn: e.tensor_scalar(out=rs_[:, 0:nn], in0=ps_q[0:64, 0:nn], scalar1=1.0 / 64, scalar2=EPS,
                                                                                     op0=ALU.mult, op1=ALU.add), ["ps_q"], ["rs"])
                                        sc.op("pool", lambda e, nn=nn: e.tensor_tensor(out=rr[:, 0:nn], in0=rs_[:, 0:nn], in1=nh64[:, 0:nn], op=ALU.pow),
                                              ["rs", "nh64"], ["rr"])
                                        sc.op("dve", lambda e, nn=nn: e.tensor_tensor(out=rr[:, 0:nn], in0=rr[:, 0:nn], in1=ps_k[0:64, 0:nn], op=ALU.mult),
                                              ["rr", "ps_k"], ["rr"])
                                        sc.op("dve", lambda e, nn=nn, g=g, n0=n0: e.tensor_scalar(out=KC[0:64, g, n0:n0 + nn], in0=rr[:, 0:nn], scalar1=gkc[:, 0:1],
                                                                                                scalar2=None, op0=ALU.mult), ["rr", "gkc", "KC"], ["KC"])
                                    else:
                                        for sub in range(0, nn, 128):
                                            ns = min(128, nn - sub)
                                            sc.op("pe", lambda e, sub=sub, ns=ns, w2b=w2b: e.matmul(ps_k[0:ns, 0:64], lhsT=hidT[:, sub:sub + ns], rhs=w2b,
                                                                                           start=True, stop=True), ["hidT", "w2b2"], ["ps_k"])
                                            sc.op("act", lambda e, sub=sub, ns=ns, g=g, n0=n0: e.activation(
                                                out=VCW[0:ns, g, (n0 + sub) // 128, 0:64], in_=ps_k[0:ns, 0:64], func=AF.Copy), ["ps_k", "VCW"], ["VCW"])
                        sc.flush(st)

                pend = []

                def branch_block(KTap, Ka, rhs_q, extra, Vap, po, w, first, last, Vn, Ktn):
                    pend.append((KTap, rhs_q, extra, Vap, po, w, last, Vn, Ktn))

                def run_pending():
                    base = step["t"]

                    def qk(i):
                        KTap, rhs_q, extra, Vap, po, w, last, Vn, Ktn = pend[i]
                        tt = base + i
                        pst, pn = ps_s[tt % 2], f"ps_s{tt % 2}"
                        sc.op("pe", lambda e: e.matmul(pst[:, 0:4 * w].rearrange("p (j q) -> p j q", j=4), lhsT=KTap, rhs=rhs_q,
                                                       start=True, stop=(len(extra) == 0)), [Ktn, "QNg"], [pn])
                        for k_, (l_, r_, rn) in enumerate(extra):
                            sc.op("pe", lambda e, l_=l_, r_=r_, k_=k_: e.matmul(pst[:, 0:4 * w].rearrange("p (j q) -> p j q", j=4), lhsT=l_, rhs=r_,
                                                                               start=False, stop=(k_ == len(extra) - 1)), rn, [pn])
                    if pend:
                        qk(0)
                    for i in range(len(pend)):
                        KTap, rhs_q, extra, Vap, po, w, last, Vn, Ktn = pend[i]
                        tt = base + i
                        pst, pn = ps_s[tt % 2], f"ps_s{tt % 2}"
                        pt, ptn = pTn[tt % 2], f"pTn{tt % 2}"
                        sc.op("act", lambda e, pt=pt, pst=pst, w=w: e.activation(out=pt[:, 0:4 * w], in_=pst[:, 0:4 * w], func=AF.Exp), [pn], [ptn])
                        if i + 1 < len(pend):
                            qk(i + 1)
                        for j in range(4):
                            sc.op("pe", lambda e, j=j, pt=pt, po=po, w=w, Vap=Vap, last=last: e.matmul(
                                po[0:w, j, :], lhsT=pt[:, j * w:(j + 1) * w], rhs=Vap, start=False, stop=last), [ptn, Vn], ["po_sw"])
                    step["t"] += len(pend)
                    pend.clear()

                def nsa_qblock(c, g, F, w, q0, gb_ap, gbn, add_ap, NM, dst):
                    nch = NM // 128
                    KSg, VSg, QNg, EE = att["KSg"], att["VSg"], att["QNg"], att["EE"]
                    K_, tb = Kof(F, w)
                    vi = Kvars.index(K_)
                    for j in range(4):
                        h = 4 * g + j
                        for t in range(tb + 1):
                            tt = step["t"]
                            step["t"] += 1
                            pst, pn = ps_s[tt % 2], f"ps_s{tt % 2}"
                            pt, ptn = pTn[tt % 2], f"pTn{tt % 2}"
                            sc.op("pe", lambda e, t=t, j=j, pst=pst: e.matmul(pst[:, 0:w], lhsT=KC[0:65, g, t * 128:(t + 1) * 128],
                                                                           rhs=QNg[0:65, j, q0:q0 + w], start=True, stop=(t != tb)), ["KC", "QNg"], [pn])
                            if t == tb:
                                sc.op("pe", lambda e, pst=pst, h=h: e.matmul(pst[:, 0:w], lhsT=ident_b[:], rhs=TCB[:, vi, h, 0:w], start=False, stop=True),
                                      ["ident_b", "TCB"], [pn])
                            sc.op("act", lambda e, pst=pst, pt=pt: e.activation(out=pt[:, 0:w], in_=pst[:, 0:w], func=AF.Exp), [pn], [ptn])
                            sc.op("pe", lambda e, t=t, pt=pt: e.matmul(po_c[0:w, 0:65 + NM], lhsT=pt[:, 0:w], rhs=VCW[:, g, t, 0:65 + NM],
                                                                     start=(t == 0), stop=(t == tb)), [ptn, "VCW"], ["po_c"])
                        sc.op("dve", lambda e, j=j: e.tensor_scalar(out=rz[0:w, j:j + 1], in0=po_c[0:w, 64:65], scalar1=1e-30, scalar2=None, op0=ALU.max),
                              ["po_c"], ["rz"])
                        sc.op("dve", lambda e, j=j: e.reciprocal(out=rz[0:w, j:j + 1], in_=rz[0:w, j:j + 1]), ["rz"], ["rz"])
                        sc.op("dve", lambda e, j=j, h=h: e.tensor_tensor(out=coef[0:w, j:j + 1], in0=rz[0:w, j:j + 1], in1=gb_ap[:, 3 * h:3 * h + 1], op=ALU.mult),
                              ["rz", gbn], ["coef"])
                        sc.op("dve", lambda e, j=j: e.tensor_scalar(out=obg[0:w, j, :], in0=po_c[0:w, 0:64], scalar1=coef[0:w, j:j + 1], scalar2=None,
                                                                    op0=ALU.mult), ["po_c", "coef"], ["obg"])
                        if j == 0:
                            sc.op("dve", lambda e, j=j: e.tensor_scalar(out=imp[0:w, 0:NM], in0=po_c[0:w, 65:65 + NM], scalar1=rz[0:w, j:j + 1], scalar2=None,
                                                                        op0=ALU.mult), ["po_c", "rz"], ["imp"])
                        else:
                            sc.op("dve", lambda e, j=j: e.scalar_tensor_tensor(out=imp[0:w, 0:NM], in0=po_c[0:w, 65:65 + NM], scalar=rz[0:w, j:j + 1],
                                                                               in1=imp[0:w, 0:NM], op0=ALU.mult, op1=ALU.add), ["po_c", "rz", "imp"], ["imp"])
                    for po in (po_s, po_w):
                        sc.op("pe", lambda e, po=po: e.matmul(po[0:w, :, :], lhsT=zeros_b[:, 0:w], rhs=zeros_b[:, 0:260].rearrange("p (j d) -> p j d", d=65),
                                                              start=True, stop=False), ["zeros_b"], ["po_sw"])
                    f0 = max(0, F - 4)
                    nwb = F - f0 + 1
                    sc.dma("sp", lambda e: e.dma_start(out=KWt[:, 0:nwb, :], in_=c.KW[g][:, f0 * 128:(F + 1) * 128].rearrange("p (b t) -> p b t", t=128)),
                           ["KW_" + c.name], ["KWt"])
                    sc.dma("sp", lambda e: e.dma_start(out=VWt[:, 0:nwb, :], in_=c.VW[g][:, f0:F + 1, :]), ["VW_" + c.name], ["VWt"])
                    rqa = QNg[0:65, :, q0:q0 + w]
                    for f in range(f0, F + 1):
                        extra = [(ident_b[:], TSW[:, F - f, 4 * g:4 * g + 4, 0:w], ["ident_b", "TSW"])]
                        branch_block(KWt[0:65, f - f0, :], 65, rqa, extra, VWt[:, f - f0, :], po_w, w, f == f0, f == F, "VWt", "KWt")
                    run_pending()
                    sc.dma("sp", lambda e: e.dma_start(out=addt[0:w, 0:NM], in_=add_ap), [], ["addt"])
                    sc.op("dve", lambda e: e.tensor_tensor(out=imp[0:w, 0:NM], in0=imp[0:w, 0:NM], in1=addt[0:w, 0:NM], op=ALU.add), ["imp", "addt"], ["imp"])
                    sc.op("dve", lambda e: e.max(out=m8a[0:w, :], in_=imp[0:w, 0:NM]), ["imp"], ["m8a"])
                    sc.op("dve", lambda e: e.match_replace(out=wk[0:w, 0:NM], in_to_replace=m8a[0:w, :], in_values=imp[0:w, 0:NM], imm_value=-1e30),
                          ["imp", "m8a"], ["wk"])
                    sc.op("dve", lambda e: e.max(out=m8b[0:w, :], in_=wk[0:w, 0:NM]), ["wk"], ["m8b"])
                    sc.op("dve", lambda e: e.tensor_scalar(out=mk[0:w, 0:NM], in0=imp[0:w, 0:NM], scalar1=m8b[0:w, 7:8], scalar2=None, op0=ALU.is_ge),
                          ["imp", "m8b"], ["mk"])
                    sc.op("dve", lambda e: e.tensor_scalar(out=mk2[0:w, 0:NM], in0=imp[0:w, 0:NM], scalar1=-1e29, scalar2=None, op0=ALU.is_gt),
                          ["imp"], ["mk2"])
                    sc.op("dve", lambda e: e.tensor_tensor(out=mk[0:w, 0:NM], in0=mk[0:w, 0:NM], in1=mk2[0:w, 0:NM], op=ALU.mult), ["mk", "mk2"], ["mk"])
                    sc.op("dve", lambda e: e.tensor_scalar(out=negb[0:w, 0:NM], in0=mk[0:w, 0:NM], scalar1=-NEG, scalar2=NEG, op0=ALU.mult, op1=ALU.add),
                          ["mk"], ["negb"])
                    for ch in range(nch):
                        sc.op("pe", lambda e, ch=ch: e.transpose(out=ps_t[:, ch, 0:w], in_=negb[0:w, ch * 128:(ch + 1) * 128], identity=ident_b[0:w, 0:w]),
                              ["negb", "ident_b"], ["ps_t"])
                    for j in range(4):
                        sc.op("act", lambda e, j=j: e.activation(out=neg4[:, 0:nch, j, 0:w], in_=ps_t[:, 0:nch, 0:w], func=AF.Copy), ["ps_t"], ["neg4"])
                    rq = QNg[0:64, :, q0:q0 + w]
                    for f in range(F + 1):
                        extra = [(EE[:, (f % 64) * 128:(f % 64 + 1) * 128], neg4[:, f // 64, :, 0:w], ["EE", "neg4"])]
                        if f == F:
                            extra.append((ident_b[:], TSW[:, 0, 4 * g:4 * g + 4, 0:w], ["ident_b", "TSW"]))
                        elif f == F - 1:
                            extra.append((ident_b[:], TSW[:, 1, 4 * g:4 * g + 4, 0:w], ["ident_b", "TSW"]))
                        branch_block(KSg[0:64, f * 128:(f + 1) * 128], 64, rq, extra, VSg[:, f, :], po_s, w, f == 0, f == F, "VSg", "KSg")
                    run_pending()
                    for (po, gi) in ((po_s, 1), (po_w, 2)):
                        sc.op("dve", lambda e, po=po: e.tensor_scalar(out=rz[0:w, 0:4], in0=po[0:w, :, 64], scalar1=1e-30, scalar2=None, op0=ALU.max),
                              ["po_sw"], ["rz"])
                        sc.op("dve", lambda e: e.reciprocal(out=rz[0:w, 0:4], in_=rz[0:w, 0:4]), ["rz"], ["rz"])
                        sc.op("dve", lambda e, gi=gi: e.tensor_tensor(out=coef[0:w, 0:4], in0=rz[0:w, 0:4],
                                                                      in1=gb_ap[:, 12 * g:12 * g + 12].rearrange("p (j i) -> p j i", i=3)[:, :, gi], op=ALU.mult),
                              ["rz", gbn], ["coef"])
                        for j in range(4):
                            sc.op("dve", lambda e, j=j, po=po: e.scalar_tensor_tensor(out=obg[0:w, j, :], in0=po[0:w, j, 0:64], scalar=coef[0:w, j:j + 1],
                                                                                     in1=obg[0:w, j, :], op0=ALU.mult, op1=ALU.add),
                                  ["po_sw", "coef", "obg"], ["obg"])
                    sc.op("dve", lambda e: e.tensor_copy(out=dst, in_=obg[0:w, :, :].rearrange("p j d -> p (j d)")), ["obg"], ["ob_dst"])

                def nsa_ctx(c, Lc, NT, NM, wc_in, cneg_in, qblocks):
                    compress(c, Lc, NT, NM, wc_in, cneg_in)
                    with contextlib.ExitStack() as st2:
                        KSg = st2.enter_context(nc.sbuf_tensor(f"A{c.name}_KSg", [64, c.L], BF16))
                        VSg = st2.enter_context(nc.sbuf_tensor(f"A{c.name}_VSg", [128, c.NBk, 65], BF16))
                        QNg = st2.enter_context(nc.sbuf_tensor(f"A{c.name}_QNg", [65, 4, c.nqc], BF16))
                        EE = st2.enter_context(nc.sbuf_tensor(f"A{c.name}_EE", [128, 8192], BF16))
                        att["KSg"], att["VSg"], att["QNg"], att["EE"] = KSg, VSg, QNg, EE
                        sc.dma("sp", lambda e: e.dma_start(out=EE[:], in_=EEB[:, :]), ["EEB"], ["EE"])
                        for g in range(2):
                            sc.dma("sp", lambda e, g=g: e.dma_start(out=KSg[:, 0:c.L], in_=c.KS[g]), ["KS_" + c.name], ["KSg"])
                            sc.dma("sp", lambda e, g=g: e.dma_start(out=VSg[:, 0:c.NBk, :], in_=c.VS[g]), ["VS_" + c.name], ["VSg"])
                            sc.dma("sp", lambda e, g=g: e.dma_start(out=QNg[:, :, 0:c.nqc], in_=c.QN[:, 4 * g:4 * g + 4, :]), ["QN_" + c.name], ["QNg"])
                            for qb in qblocks:
                                qb(c, g)
                        sc.flush(st)

                qbl = []
                for jo in range(NOWN):
                    qbl.append(lambda c, g, jo=jo: nsa_qblock(c, g, 8 * jo + 7, 128, jo * 128, gb_res[:, jo, :], "gb_res",
                                                              addp_in[jo, :, :], cfg.NMp, ob_res[:, jo, g * 256:(g + 1) * 256]))
                nsa_ctx(ctx_p, S, cfg.NTp, cfg.NMp, wcp_in, cnegp_in[:, :], qbl)
                for j in range(NOWN):
                    sc.dma("sp", lambda e, j=j: e.dma_start(out=OBP[j * 128:(j + 1) * 128, :], in_=ob_res[:, j, :]), ["ob_dst"], ["OBP"])
                    sc.dma("sp", lambda e, j=j: e.dma_start(out=dbg_ob_p[j * 128:(j + 1) * 128, :], in_=ob_res[:, j, :]), ["ob_dst"], [])
                for b in range(NSEQ):
                    qbl = [lambda c, g, b=b: nsa_qblock(c, g, NPG, 8, 0, gbs_res[:, b, :], "gbs_res", adds_in[:, :], cfg.NMs,
                                                        obs8[:, g * 256:(g + 1) * 256])]
                    nsa_ctx(ctx_s[b], cfg.PAST, cfg.NTs, cfg.NMs, wcs_in, cnegs_in[:, :], qbl)
                    sc.dma("sp", lambda e, b=b: e.dma_start(out=OBS[b * 8:(b + 1) * 8, :], in_=obs8[:]), ["ob_dst"], ["OBS"])
                sc.dma("sp", lambda e: e.dma_start(out=dbg_ob_s[:, :], in_=OBS[:, :]), ["OBS"], [])
                sc.flush(st)
        if cfg.dbg_ob:
            with contextlib.ExitStack() as stX:
                obf = stX.enter_context(nc.sbuf_tensor("obf", [128, 512], F32))
                obb = stX.enter_context(nc.sbuf_tensor("obb", [128, 512], BF16))
                for j in range(NOWN):
                    sc.dma("sp", lambda e, j=j: e.dma_start(out=obf[:], in_=dbg_ob_p_in[j * 128:(j + 1) * 128, :]), [], ["obf"])
                    sc.op("dve", lambda e, j=j: e.tensor_copy(out=obb[:], in_=obf[:]), ["obf"], ["obb"])
                    sc.dma("sp", lambda e, j=j: e.dma_start(out=OBP[j * 128:(j + 1) * 128, :], in_=obb[:]), ["obb"], ["OBP"])
                sc.dma("sp", lambda e: e.dma_start(out=obf[0:NSR, :], in_=dbg_ob_s_in[:, :]), [], ["obf"])
                sc.op("dve", lambda e: e.tensor_copy(out=obb[0:NSR, :], in_=obf[0:NSR, :]), ["obf"], ["obb"])
                sc.dma("sp", lambda e: e.dma_start(out=OBS[:, :], in_=obb[0:NSR, :]), ["obb"], ["OBS"])
                sc.flush(st)
        Y1 = dscr("Y1", [NQ + NSR, D], F32)

        def cast_weight(dst, dname, src_rows_fn, nchunk, ncol, stgs):
            cnt = 0
            for k in range(nchunk):
                for c0 in range(0, ncol, 2048):
                    n = min(2048, ncol - c0)
                    stg, sn = stgs[cnt % 2]
                    cnt += 1
                    sc.dma("sp", lambda e, k=k, c0=c0, n=n, stg=stg: e.dma_start(out=stg[:, 0:n], in_=src_rows_fn(k)[:, c0:c0 + n]), [], [sn])
                    sc.op("pool", lambda e, k=k, c0=c0, n=n, stg=stg: e.tensor_copy(out=dst[:, k, c0:c0 + n], in_=stg[:, 0:n]), [sn], [dname])

        tiles = [("p", j, 128) for j in range(NOWN)] + [("s", b, 8) for b in range(NSEQ)]

        with contextlib.ExitStack() as stD:
            def sbD(name, shape, dt=F32):
                return stD.enter_context(nc.sbuf_tensor("D_" + name, list(shape), dt))

            def psD(name, shape, dt=F32):
                return stD.enter_context(nc.psum_tensor("D_" + name, list(shape), dt))
            wstg = [(sbD(f"wstg{i}", [128, 2048]), f"wstg{i}") for i in range(2)]
            w_zm = sbD("w_zm", [128, 8, 2048], BF16)
            wof = sbD("wof", [128, 4, 1024], BF16)
            won = sbD("won", [128, 4, 1024], BF16)
            wo = sbD("wo", [128, 8, 1024], BF16)
            Mp = sbD("Mp", [128, 3, D])
            xt = sbD("xt", [128, D])
            junk = sbD("junk", [128, D], BF16)
            tmpf = sbD("tmpf", [128, D])
            hb = sbD("hb", [128, D], BF16)
            hT = sbD("hT", [128, 8, 128], BF16)
            ssum = sbD("ssum", [128, 1]); rstd = sbD("rstd", [128, 1])
            gmt = sbD("gmt", [128, 2048], BF16)
            oat = sbD("oat", [128, 512], BF16); obt = sbD("obt", [128, 512], BF16)
            oaT = sbD("oaT", [128, 4, 128], BF16); obT = sbD("obT", [128, 4, 128], BF16)
            mt = sbD("mt", [128, D]); mb = sbD("mb", [128, D], BF16)
            mT = sbD("mT", [128, 8, 128], BF16)
            y1t = sbD("y1t", [128, D])
            ps_tr = psD("ps_tr", [128, 1024], BF16)
            ps_a = psD("ps_a", [128, 512]); ps_b = psD("ps_b", [128, 512])
            ps_c = psD("ps_c", [128, 512]); ps_d = psD("ps_d", [128, 512])
            cast_weight(w_zm, "w_zm", lambda k: w_in[k * 128:(k + 1) * 128, O_ZM:O_ZM + 2048], 8, 2048, wstg)
            cast_weight(wof, "wof", lambda k: w_out_fox[k * 128:(k + 1) * 128, :], 4, 1024, wstg)
            cast_weight(won, "won", lambda k: w_out_nsa[k * 128:(k + 1) * 128, :], 4, 1024, wstg)
            cast_weight(wo, "wo", lambda k: w_out[k * 128:(k + 1) * 128, :], 8, 1024, wstg)

            def norm_T(src, srcn, rows, G, B):
                sc.op("act", lambda e: e.activation(out=junk[0:rows, :], in_=src, func=AF.Square, accum_out=ssum[0:rows, :]),
                      [srcn], ["junk", "ssum"])
                sc.op("dve", lambda e: e.tensor_scalar(out=ssum[0:rows, :], in0=ssum[0:rows, :], scalar1=1.0 / D, scalar2=EPS,
                                                       op0=ALU.mult, op1=ALU.add), ["ssum"], ["ssum"])
                sc.op("pool", lambda e: e.tensor_tensor(out=rstd[0:rows, :], in0=ssum[0:rows, :], in1=neghalf[0:rows, 0:1], op=ALU.pow),
                      ["ssum", "neghalf"], ["rstd"])
                sc.op("dve", lambda e: e.scalar_tensor_tensor(out=tmpf[0:rows, :], in0=src, scalar=rstd[0:rows, :], in1=G,
                                                              op0=ALU.mult, op1=ALU.mult), [srcn, "rstd", "Mp"], ["tmpf"])
                sc.op("dve", lambda e: e.tensor_tensor(out=hb[0:rows, :], in0=tmpf[0:rows, :], in1=B, op=ALU.add), ["tmpf", "Mp"], ["hb"])
                trans8(hb, "hb", rows, hT, "hT")

            def trans8(src, srcn, rows, dstT, dstn, nchunk=8, c_off=0):
                for k in range(nchunk):
                    sc.op("pe", lambda e, k=k: e.transpose(out=ps_tr[:, k * 128:k * 128 + rows], in_=src[0:rows, (c_off + k) * 128:(c_off + k + 1) * 128],
                                                           identity=ident_b[0:rows, 0:rows]), [srcn, "ident_b"], ["ps_tr"])
                sc.op("act", lambda e: e.activation(out=dstT[:, 0:nchunk, 0:rows],
                                                    in_=ps_tr[:].rearrange("p (k t) -> p k t", t=128)[:, 0:nchunk, 0:rows], func=AF.Copy),
                      ["ps_tr"], [dstn])

            kind_state = {"k": None}

            def do_tileD(kind, idx, rows):
                kind_loaded = kind_state["k"]
                if kind == "p":
                    x_ap = xf[(8 * idx + 7) * 128:(8 * idx + 8) * 128, :]
                    y1_ap = Y1[idx * 128:(idx + 1) * 128, :]
                    oa_ap, ob_ap = OAP[idx * 128:(idx + 1) * 128, :], OBP[idx * 128:(idx + 1) * 128, :]
                    if kind_loaded != "p":
                        load_mod(Mp, 128, 0, (1, 0, 2), 0, ps_a, "ps_a")
                        kind_state["k"] = "p"
                else:
                    x_ap = xs[idx * 8:(idx + 1) * 8, :]
                    y1_ap = Y1[NQ + idx * 8:NQ + (idx + 1) * 8, :]
                    oa_ap, ob_ap = OAS[idx * 8:(idx + 1) * 8, :], OBS[idx * 8:(idx + 1) * 8, :]
                    load_mod(Mp, 8, 128 + 8 * idx, (1, 0, 2), 0, ps_a, "ps_a")
                    kind_state["k"] = "s"
                sc.dma("sp", lambda e, x_ap=x_ap, rows=rows: e.dma_start(out=xt[0:rows, :], in_=x_ap), [], ["xt"])
                norm_T(xt[0:rows, :], "xt", rows, Mp[0:rows, 0, :], Mp[0:rows, 1, :])
                for g4 in range(4):
                    pst, pn = [(ps_a, "ps_a"), (ps_b, "ps_b")][g4 % 2]
                    for k in range(8):
                        sc.op("pe", lambda e, k=k, g4=g4, pst=pst: e.matmul(pst[0:rows, :], lhsT=hT[:, k, 0:rows], rhs=w_zm[:, k, g4 * 512:(g4 + 1) * 512],
                                                                        start=(k == 0), stop=(k == 7)), ["hT", "w_zm"], [pn])
                    sc.op("act", lambda e, g4=g4, pst=pst: e.activation(out=gmt[0:rows, g4 * 512:(g4 + 1) * 512], in_=pst[0:rows, :], func=AF.Sigmoid),
                          [pn], ["gmt"])
                sc.dma("sp", lambda e, oa_ap=oa_ap, rows=rows: e.dma_start(out=oat[0:rows, :], in_=oa_ap), ["OAS", "OAP"], ["oat"])
                sc.dma("sp", lambda e, ob_ap=ob_ap, rows=rows: e.dma_start(out=obt[0:rows, :], in_=ob_ap), ["OBS", "OBP"], ["obt"])
                oa_src, oa_n, ob_src, ob_n = oat, "oat", obt, "obt"
                trans8(oa_src, oa_n, rows, oaT, "oaT", 4)
                trans8(ob_src, ob_n, rows, obT, "obT", 4)
                for half in range(2):
                    hs_ = slice(half * 512, (half + 1) * 512)
                    for k in range(4):
                        sc.op("pe", lambda e, k=k, hs_=hs_: e.matmul(ps_c[0:rows, :], lhsT=oaT[:, k, 0:rows], rhs=wof[:, k, hs_],
                                                                   start=(k == 0), stop=(k == 3)), ["oaT", "wof"], ["ps_c"])
                    for k in range(4):
                        sc.op("pe", lambda e, k=k, hs_=hs_: e.matmul(ps_d[0:rows, :], lhsT=obT[:, k, 0:rows], rhs=won[:, k, hs_],
                                                                   start=(k == 0), stop=(k == 3)), ["obT", "won"], ["ps_d"])
                    sc.op("dve", lambda e, hs_=hs_: e.tensor_tensor(out=mt[0:rows, hs_], in0=ps_c[0:rows, :], in1=gmt[0:rows, hs_], op=ALU.mult),
                          ["ps_c", "gmt"], ["mt"])
                    sc.op("dve", lambda e, half=half: e.tensor_tensor(out=tmpf[0:rows, 0:512], in0=ps_d[0:rows, :],
                                                                      in1=gmt[0:rows, 1024 + half * 512:1024 + (half + 1) * 512], op=ALU.mult),
                          ["ps_d", "gmt"], ["tmpf"])
                    sc.op("dve", lambda e, hs_=hs_: e.tensor_tensor(out=mb[0:rows, hs_], in0=mt[0:rows, hs_], in1=tmpf[0:rows, 0:512], op=ALU.add),
                          ["mt", "tmpf"], ["mb"])
                trans8(mb, "mb", rows, mT, "mT")
                for half in range(2):
                    hs_ = slice(half * 512, (half + 1) * 512)
                    for k in range(8):
                        sc.op("pe", lambda e, k=k, hs_=hs_: e.matmul(ps_c[0:rows, :], lhsT=mT[:, k, 0:rows], rhs=wo[:, k, hs_],
                                                                   start=(k == 0), stop=(k == 7)), ["mT", "wo"], ["ps_c"])
                    sc.op("dve", lambda e, hs_=hs_: e.tensor_tensor(out=tmpf[0:rows, hs_], in0=ps_c[0:rows, :], in1=Mp[0:rows, 2, hs_], op=ALU.mult),
                          ["ps_c", "Mp"], ["tmpf"])
                    sc.op("dve", lambda e, hs_=hs_: e.tensor_tensor(out=y1t[0:rows, hs_], in0=tmpf[0:rows, hs_], in1=xt[0:rows, hs_], op=ALU.add),
                          ["tmpf", "xt"], ["y1t"])
                sc.dma("sp", lambda e, y1_ap=y1_ap, rows=rows: e.dma_start(out=y1_ap, in_=y1t[0:rows, :]), ["y1t"], ["Y1"])
            for (kind, idx, rows) in tiles:
                do_tileD(kind, idx, rows)
            sc.flush(st)

        YP = dscr("YP", [NQ + NSR, D], F32)
        with contextlib.ExitStack() as stE:
            def sbE(name, shape, dt=F32):
                return stE.enter_context(nc.sbuf_tensor("E_" + name, list(shape), dt))

            def psE(name, shape, dt=F32):
                return stE.enter_context(nc.psum_tensor("E_" + name, list(shape), dt))
            wstg = [(sbE(f"wstg{i}", [128, 2048]), f"wstg{i}") for i in range(2)]
            wup = sbE("wup", [128, 8, 2048], BF16)
            wdn = sbE("wdn", [128, 16, 1024], BF16)
            Mp = sbE("Mp", [128, 3, D])
            y1t = sbE("y1t", [128, D])
            junk = sbE("junk", [128, D], BF16)
            tmpf = sbE("tmpf", [128, D])
            hb = sbE("hb", [128, D], BF16)
            hT = sbE("hT", [128, 8, 128], BF16)
            ssum = sbE("ssum", [128, 1]); rstd = sbE("rstd", [128, 1])
            rl = sbE("rl", [128, 512])
            ub = sbE("ub", [128, 2048], BF16)
            uT = sbE("uT", [128, 16, 128], BF16)
            yt = sbE("yt", [128, D])
            ypt = sbE("ypt", [128, D])
            ps_tr = psE("ps_tr", [128, 1024], BF16)
            ps_a = psE("ps_a", [128, 512]); ps_b = psE("ps_b", [128, 512])
            kstate = {"k": None}

            def do_tileE(hf, kind, idx, rows):
                kind_loaded = kstate["k"]
                if True:
                    if kind == "p":
                        r0 = idx * 128
                        out_ap = o_y_p[idx * 128:(idx + 1) * 128, :]
                        c0 = 0
                    else:
                        r0 = NQ + idx * 8
                        out_ap = o_y_s[idx * 8:(idx + 1) * 8, :]
                        c0 = 128 + 8 * idx
                    y1_ap = Y1[r0:r0 + rows, :]
                    yp_ap = YP[r0:r0 + rows, :]
                    if kind != kind_loaded or kind == "s":
                        load_mod(Mp, rows, c0, (4, 3, 5), 1, ps_a, "ps_a")
                        kstate["k"] = kind
                    sc.dma("sp", lambda e, y1_ap=y1_ap, rows=rows: e.dma_start(out=y1t[0:rows, :], in_=y1_ap), ["Y1"], ["y1t"])
                    sc.op("act", lambda e, rows=rows: e.activation(out=junk[0:rows, :], in_=y1t[0:rows, :], func=AF.Square, accum_out=ssum[0:rows, :]),
                          ["y1t"], ["junk", "ssum"])
                    sc.op("dve", lambda e, rows=rows: e.tensor_scalar(out=ssum[0:rows, :], in0=ssum[0:rows, :], scalar1=1.0 / D, scalar2=EPS,
                                                                      op0=ALU.mult, op1=ALU.add), ["ssum"], ["ssum"])
                    sc.op("pool", lambda e, rows=rows: e.tensor_tensor(out=rstd[0:rows, :], in0=ssum[0:rows, :], in1=neghalf[0:rows, 0:1], op=ALU.pow),
                          ["ssum", "neghalf"], ["rstd"])
                    sc.op("dve", lambda e, rows=rows: e.scalar_tensor_tensor(out=tmpf[0:rows, :], in0=y1t[0:rows, :], scalar=rstd[0:rows, :],
                                                                             in1=Mp[0:rows, 0, :], op0=ALU.mult, op1=ALU.mult),
                          ["y1t", "rstd", "Mp"], ["tmpf"])
                    sc.op("dve", lambda e, rows=rows: e.tensor_tensor(out=hb[0:rows, :], in0=tmpf[0:rows, :], in1=Mp[0:rows, 1, :], op=ALU.add),
                          ["tmpf", "Mp"], ["hb"])
                    for k in range(8):
                        sc.op("pe", lambda e, k=k, rows=rows: e.transpose(out=ps_tr[:, k * 128:k * 128 + rows], in_=hb[0:rows, k * 128:(k + 1) * 128],
                                                                          identity=ident_b[0:rows, 0:rows]), ["hb", "ident_b"], ["ps_tr"])
                    sc.op("act", lambda e, rows=rows: e.activation(out=hT[:, :, 0:rows], in_=ps_tr[:].rearrange("p (k t) -> p k t", t=128)[:, :, 0:rows],
                                                                   func=AF.Copy), ["ps_tr"], ["hT"])
                    for g4 in range(4):
                        pst, pn = [(ps_a, "ps_a"), (ps_b, "ps_b")][g4 % 2]
                        for k in range(8):
                            sc.op("pe", lambda e, k=k, g4=g4, pst=pst, rows=rows: e.matmul(pst[0:rows, :], lhsT=hT[:, k, 0:rows],
                                                                                        rhs=wup[:, k, g4 * 512:(g4 + 1) * 512],
                                                                                        start=(k == 0), stop=(k == 7)), ["hT", "wup"], [pn])
                        sc.op("act", lambda e, pst=pst, rows=rows: e.activation(out=rl[0:rows, :], in_=pst[0:rows, :], func=AF.Relu), [pn], ["rl"])
                        sc.op("dve", lambda e, g4=g4, rows=rows: e.tensor_tensor(out=ub[0:rows, g4 * 512:(g4 + 1) * 512], in0=rl[0:rows, :], in1=rl[0:rows, :],
                                                                                op=ALU.mult), ["rl"], ["ub"])
                    for q2 in range(2):
                        for k in range(8):
                            sc.op("pe", lambda e, k=k, q2=q2, rows=rows: e.transpose(out=ps_tr[:, k * 128:k * 128 + rows],
                                                                                    in_=ub[0:rows, (q2 * 8 + k) * 128:(q2 * 8 + k + 1) * 128],
                                                                                    identity=ident_b[0:rows, 0:rows]), ["ub", "ident_b"], ["ps_tr"])
                        sc.op("act", lambda e, q2=q2, rows=rows: e.activation(out=uT[:, q2 * 8:(q2 + 1) * 8, 0:rows],
                                                                              in_=ps_tr[:].rearrange("p (k t) -> p k t", t=128)[:, :, 0:rows], func=AF.Copy),
                              ["ps_tr"], ["uT"])
                    if hf == 1:
                        sc.dma("sp", lambda e, yp_ap=yp_ap, rows=rows: e.dma_start(out=ypt[0:rows, :], in_=yp_ap), ["YP"], ["ypt"])
                    for half in range(2):
                        hs_ = slice(half * 512, (half + 1) * 512)
                        for k in range(16):
                            sc.op("pe", lambda e, k=k, hs_=hs_, rows=rows: e.matmul(ps_a[0:rows, :], lhsT=uT[:, k, 0:rows], rhs=wdn[:, k, hs_],
                                                                                  start=(k == 0), stop=(k == 15)), ["uT", "wdn"], ["ps_a"])
                        if hf == 0:
                            sc.op("act", lambda e, hs_=hs_, rows=rows: e.activation(out=ypt[0:rows, hs_], in_=ps_a[0:rows, :], func=AF.Copy),
                                  ["ps_a"], ["ypt"])
                        else:
                            sc.op("dve", lambda e, hs_=hs_, rows=rows: e.tensor_tensor(out=tmpf[0:rows, hs_], in0=ps_a[0:rows, :], in1=ypt[0:rows, hs_], op=ALU.add),
                                  ["ps_a", "ypt"], ["tmpf"])
                            sc.op("dve", lambda e, hs_=hs_, rows=rows: e.tensor_tensor(out=tmpf[0:rows, hs_], in0=tmpf[0:rows, hs_], in1=Mp[0:rows, 2, hs_], op=ALU.mult),
                                  ["tmpf", "Mp"], ["tmpf"])
                            sc.op("dve", lambda e, hs_=hs_, rows=rows: e.tensor_tensor(out=yt[0:rows, hs_], in0=tmpf[0:rows, hs_], in1=y1t[0:rows, hs_], op=ALU.add),
                                  ["tmpf", "y1t"], ["yt"])
                    if hf == 0:
                        sc.dma("sp", lambda e, yp_ap=yp_ap, rows=rows: e.dma_start(out=yp_ap, in_=ypt[0:rows, :]), ["ypt"], ["YP"])
                    else:
                        sc.dma("pool", lambda e, out_ap=out_ap, rows=rows: e.dma_start(out=out_ap, in_=yt[0:rows, :]), ["yt"], [])
            for hf in range(2):
                cast_weight(wup, "wup", lambda k, hf=hf: w_up[k * 128:(k + 1) * 128, hf * 2048:(hf + 1) * 2048], 8, 2048, wstg)
                cast_weight(wdn, "wdn", lambda k, hf=hf: w_down[hf * 2048 + k * 128:hf * 2048 + (k + 1) * 128, :], 16, 1024, wstg)
                kstate["k"] = None
                for (kind, idx, rows) in tiles:
                    do_tileE(hf, kind, idx, rows)
            sc.flush(st)
    return nc


def host_consts(cfg):
    NR = 1 + cfg.NSEQ
    sel = np.zeros((NR, 128 + cfg.NSR), np.float32)
    sel[0, 0:128] = 1.0
    for r in range(cfg.NSR):
        sel[1 + r // 8, 128 + r] = 1.0
    ki = np.arange(128)[:, None]; qi = np.arange(128)[None, :]
    tri = np.where(ki <= qi, 0.0, NEG).astype(np.float32)
    hs = np.arange(128)
    bt = ((hs[:, None] // 16 == hs[None, :] // 16) & (hs[:, None] % 16 < hs[None, :] % 16)).astype(np.float32)
    LS = cfg.PAST + 128
    vs = (np.arange(LS) < cfg.PAST + 8).astype(np.float32)
    out = {"sel5": sel, "ident": np.eye(128, dtype=np.float32), "tri_in": tri, "iota_in": np.arange(128, dtype=np.float32)[:, None],
           "bt_in": bt, "tvs_in": np.tile(vs.reshape(16, -1), (8, 1)), "pns_in": np.tile(((1 - vs) * NEG).reshape(16, -1), (8, 1)),
           "kwns_in": ((1 - vs) * NEG).reshape(128, -1)}
    def bucket(d):
        d = np.maximum(d, 0)
        far = 16 + (np.log(np.maximum(d, 1).astype(np.float32) / np.float32(16)) / np.float32(np.log(8.0)) * np.float32(16)).astype(np.int32)
        return np.where(d < 16, d, np.minimum(far, 31))
    dc = np.arange(4096) - 1856
    ohc = np.zeros((33, 4096), np.float32)
    ohc[bucket(dc), np.arange(4096)] = (dc >= 0)
    ohc[32] = (dc < 0)
    d1 = np.arange(768) - 128
    ok1 = (d1 >= 0) & (d1 <= 512)
    oh1 = np.zeros((33, 768), np.float32)
    oh1[bucket(d1), np.arange(768)] = ok1
    oh1[32] = ~ok1
    out["ohc_in"] = ohc; out["oh1_in"] = oh1
    out["ee_in"] = (np.arange(128)[:, None] == (np.arange(8192)[None, :] // 64)).astype(np.float32)

    def wmat(NT, NM):
        c0 = np.arange(NT * 128)[:, None] * 16
        s0 = np.arange(NM)[None, :] * 64
        sh = np.minimum(c0 + 32, s0 + 64) - np.maximum(c0, s0)
        return (np.maximum(sh, 0) / 32.0).astype(np.float32)
    out["wcp_in"] = wmat(cfg.NTp, cfg.NMp); out["wcs_in"] = wmat(cfg.NTs, cfg.NMs)
    ns = np.arange(cfg.NTs * 128)
    out["cnegs_in"] = np.where(ns >= cfg.PAST // 16 - 1, NEG, 0.0)[None, :]
    pos = cfg.PAST + np.arange(8)[:, None]; m = np.arange(cfg.NMs)[None, :]
    cur = pos // 64
    forced = (m == 0) | (m == cur) | (m == cur - 1)
    out["adds_in"] = np.where(m > cur, -1e30, np.where(forced, 100.0 + m, 0.0))
    return {k: np.ascontiguousarray(v, dtype=np.float32) for k, v in out.items()}


def make_in_maps(cfg, inp):
    S, NB = cfg.S, cfg.NB
    x = np.asarray(inp["x_prompt"], np.float32).reshape(S, D)
    consts = host_consts(cfg)
    pool_fox = np.asarray(inp["cache_fox_kv"], np.float32).reshape(-1, 1024)
    pool_lf = np.asarray(inp["cache_fox_logf"], np.float32).reshape(-1, 8)
    pool_nsa = np.asarray(inp["cache_nsa_kv"], np.float32).reshape(-1, 512)
    maps = []
    for c in range(NCORES):
        pad = (7 - c) * 128
        xfr = np.zeros((S, D), np.float32)
        xfr[pad:] = x[:S - pad]
        sl = slice(c * cfg.NSEQ, (c + 1) * cfg.NSEQ)
        m = {
            "xf": xfr,
            "xs": np.ascontiguousarray(np.asarray(inp["x_sample"], np.float32)[sl].reshape(cfg.NSR, D)),
            "cvec": np.concatenate([np.asarray(inp["c_prompt"], np.float32), np.asarray(inp["c_sample"], np.float32)[sl]], 0),
            "w_ada": np.asarray(inp["w_ada"], np.float32)[0],
            "b_ada": np.asarray(inp["b_ada"], np.float32),
            "g_norm": np.asarray(inp["g_norm"], np.float32)[0],
            "w_in": np.asarray(inp["w_in"], np.float32)[0],
            "b_forget": np.asarray(inp["b_forget"], np.float32),
            "g_qk_fox": np.asarray(inp["g_qk_fox"], np.float32)[0],
            "g_qk_nsa": np.asarray(inp["g_qk_nsa"], np.float32)[0],
            "rel_bias": np.asarray(inp["rel_bias"], np.float32), "pe_cmp": np.asarray(inp["pe_cmp"], np.float32)[0],
            "w_cmp1": np.asarray(inp["w_cmp1"], np.float32)[0], "w_cmp2": np.asarray(inp["w_cmp2"], np.float32)[0],
            "w_out_fox": np.asarray(inp["w_out_fox"], np.float32)[0], "w_out_nsa": np.asarray(inp["w_out_nsa"], np.float32)[0],
            "w_out": np.asarray(inp["w_out"], np.float32)[0], "w_up": np.asarray(inp["w_up"], np.float32)[0],
            "w_down": np.asarray(inp["w_down"], np.float32)[0],
            "win_in": np.ascontiguousarray(np.asarray(inp["state_nsa_win"], np.float32)[0, sl].reshape(cfg.NSEQ, -1, 256)),
        }
        vp = (np.arange(S) >= pad).astype(np.float32)
        m["tvp_in"] = np.ascontiguousarray(np.tile(vp.reshape(16, -1), (8, 1)))
        m["pnp_in"] = np.ascontiguousarray(np.tile(((1 - vp) * NEG).reshape(16, -1), (8, 1)).astype(np.float32))
        m["kwnp_in"] = np.ascontiguousarray(((1 - vp) * NEG).reshape(128, -1).astype(np.float32))
        npad = 8 * (7 - c)
        nf = np.arange(cfg.NTp * 128)
        m["cnegp_in"] = np.where((nf < npad) | (nf >= S // 16 - 1), NEG, 0.0).astype(np.float32)[None, :]
        m0 = 2 * (7 - c)
        jo_ = np.arange(cfg.NOWN)[:, None, None]; pi_ = np.arange(128)[None, :, None]; mm_ = np.arange(cfg.NMp)[None, None, :]
        cur_ = (128 * (8 * jo_ + 7) + pi_) // 64
        forced_ = (mm_ == m0) | (mm_ == cur_) | (mm_ == cur_ - 1)
        m["addp_in"] = np.ascontiguousarray(np.where((mm_ > cur_) | (mm_ < m0), -1e30, np.where(forced_, 100.0 + mm_, 0.0)).astype(np.float32))
        m["ptab"] = np.ascontiguousarray(np.asarray(inp["page_table"], np.int32)[sl])
        m["pool_fox"] = pool_fox; m["pool_lf"] = pool_lf; m["pool_nsa"] = pool_nsa
        if cfg.dbg_ob:
            obp = np.asarray(inp["dbg_ob_p"], np.float32).reshape(S, 512)
            m["dbg_ob_p_in"] = np.concatenate([obp[(8 * j + c) * 128:(8 * j + c + 1) * 128] for j in range(cfg.NOWN)], 0)
            m["dbg_ob_s_in"] = np.ascontiguousarray(np.asarray(inp["dbg_ob_s"], np.float32)[sl].reshape(cfg.NSR, 512))
        m.update(consts)
        maps.append(m)
    return maps


def run(cfg, inp):
    nc = build(cfg)
    maps = make_in_maps(cfg, inp)
    res = run_bass_kernel_spmd(nc, maps, core_ids=list(range(NCORES)))
    return res.results


def assemble(cfg, inp, R):
    S, NB, NOWN = cfg.S, cfg.NB, cfg.NOWN
    NSEQT = cfg.NSEQ * NCORES

    def own_scatter(key, width):
        out = np.zeros((S, width), np.float32)
        for c in range(NCORES):
            r = R[c][key]
            for j in range(NOWN):
                b = 8 * j + c
                out[b * 128:(b + 1) * 128] = r[j * 128:(j + 1) * 128]
        return out

    def cat(key):
        return np.concatenate([R[c][key] for c in range(NCORES)], 0)
    fkv_p = own_scatter("o_fkv_p", 1024).reshape(1, 1, S, 2, 8, 64)
    lf_p = own_scatter("o_lf_p", 8).reshape(1, 1, S, 8)
    nkv_p = own_scatter("o_nkv_p", 512).reshape(1, 1, S, 4, 2, 64)
    win_p = own_scatter("o_win_p", 256)[S - min(512, S):].reshape(1, 1, min(512, S), 2, 2, 64)
    fkv_s = cat("o_fkv_s").reshape(1, NSEQT, 8, 2, 8, 64)
    lf_s = cat("o_lf_s").reshape(1, NSEQT, 8, 8)
    nkv_s = cat("o_nkv_s").reshape(1, NSEQT, 8, 4, 2, 64)
    wnew = cat("o_wnew_s").reshape(NSEQT, 8, 2, 2, 64)
    win_s = cat("o_win_s").reshape(1, NSEQT, -1, 2, 2, 64)
    y_p = own_scatter("o_y_p", D).reshape(1, S, D)
    y_s = cat("o_y_s").reshape(NSEQT, 8, D)
    dbg = dict(oa_p=own_scatter("dbg_oa_p", 512), oa_s=cat("dbg_oa_s").astype(np.float32),
               ob_p=own_scatter("dbg_ob_p", 512), ob_s=cat("dbg_ob_s").astype(np.float32))
    return dict(dbg=dbg, y_p=y_p, y_s=y_s, fkv_p=fkv_p, lf_p=lf_p, nkv_p=nkv_p, win_p=win_p, fkv_s=fkv_s, lf_s=lf_s,
                nkv_s=nkv_s, wnew=wnew, win_s=win_s)


def kernel(**inputs):
    cfg = Cfg()
    R = run(cfg, inputs)
    A = assemble(cfg, inputs, R)
    names = ["y_p", "y_s", "fkv_p", "fkv_s", "lf_p", "lf_s", "nkv_p", "nkv_s", "win_p", "win_s"]
    return tuple(np.ascontiguousarray(A[n], dtype=np.float32) for n in names)
```

```python
import contextlib
import numpy as np
import concourse.bass as bass
import concourse.mybir as mybir
from concourse.bass_utils import run_bass_kernel_spmd

F32 = mybir.dt.float32
BF16 = mybir.dt.bfloat16
I32 = mybir.dt.int32
AF = mybir.ActivationFunctionType
ALU = mybir.AluOpType
AX = mybir.AxisListType

D = 1024
IN_W = 4896
O_QA, O_KA, O_VA, O_ZF, O_QB, O_ZKV, O_ZG, O_ZM = 0, 512, 1024, 1536, 1544, 2056, 2824, 2848
EPS = 1e-6
NEG = -30000.0
NCORES = 8


class Sched:
    ENG = ("pe", "act", "dve", "pool", "sp")

    def __init__(self, nc, nlanes=6):
        self.nc = nc
        self.ops = {e: [] for e in self.ENG}
        self.cnt = {}
        self.seen = {e: {} for e in self.ENG}
        self.lw = {}
        self.rd = {}
        self.lanes = {"sp": [f"L_sp{i}" for i in range(nlanes)],
                      "pool": [f"L_pool{i}" for i in range(nlanes)],
                      "act": [f"L_act{i}" for i in range(2)]}
        self.lane_rr = {"sp": 0, "pool": 0, "act": 0}
        self.semkeys = list(self.ENG)
        for v in self.lanes.values():
            self.semkeys += v
        for k in self.semkeys:
            self.cnt[k] = 0
        self.sems = {}
        self.n_inst = 0
        self.local = None
        self.sfx = ""

    def _deps(self, reads, writes):
        toks = {}

        def add(t):
            for k, v in t.items():
                if toks.get(k, 0) < v:
                    toks[k] = v
        for r in reads:
            if r in self.lw:
                add(self.lw[r])
        for w in writes:
            if w in self.lw:
                add(self.lw[w])
            if w in self.rd:
                add(self.rd[w])
        return toks

    def _commit(self, tok, reads, writes):
        k, v = tok
        for r in reads:
            d = self.rd.setdefault(r, {})
            if d.get(k, 0) < v:
                d[k] = v
        for w in writes:
            self.lw[w] = {k: v}
            self.rd[w] = {}

    def _waits(self, eng, toks):
        for k, v in toks.items():
            if k == "pe" and eng == "pe":
                continue
            if self.seen[eng].get(k, 0) >= v:
                continue
            self.seen[eng][k] = v
            self.ops[eng].append(("w", k, v))

    def _rn(self, names):
        if self.local is None:
            return names
        return [n + self.sfx if n in self.local else n for n in names]

    def op(self, eng, fn, reads=(), writes=()):
        reads, writes = self._rn(reads), self._rn(writes)
        toks = self._deps(reads, writes)
        self._waits(eng, toks)
        self.cnt[eng] += 1
        self.ops[eng].append(("i", fn, eng, 1))
        self._commit((eng, self.cnt[eng]), reads, writes)
        self.n_inst += 1

    def dma(self, eng, fn, reads=(), writes=()):
        reads, writes = self._rn(reads), self._rn(writes)
        toks = self._deps(reads, writes)
        lanes = self.lanes[eng]
        lane = lanes[self.lane_rr[eng] % len(lanes)]
        self.lane_rr[eng] += 1
        if self.cnt[lane] > 0:
            toks[lane] = max(toks.get(lane, 0), self.cnt[lane])
        self._waits(eng, toks)
        self.cnt[lane] += 16
        self.ops[eng].append(("i", fn, lane, 16))
        self._commit((lane, self.cnt[lane]), reads, writes)
        self.n_inst += 1

    def barrier(self):
        for e in self.ENG:
            toks = {k: v for k, v in self.cnt.items() if v > 0 and k != e}
            self._waits(e, toks)

    def flush(self, stack_sems):
        nc = self.nc
        self.barrier()
        for k in self.semkeys:
            if k not in self.sems:
                self.sems[k] = stack_sems.enter_context(nc.semaphore("s_" + k))
        sems = self.sems
        ops = self.ops

        def replay(name, eng):
            for o in ops[name]:
                if o[0] == "w":
                    eng.wait_ge(sems[o[1]], o[2])
                else:
                    o[1](eng).then_inc(sems[o[2]], o[3])
        with nc.Block() as block:
            @block.tensor
            def _(e):
                replay("pe", e)

            @block.scalar
            def _(e):
                replay("act", e)

            @block.vector
            def _(e):
                replay("dve", e)

            @block.gpsimd
            def _(e):
                replay("pool", e)

            @block.sync
            def _(e):
                replay("sp", e)
        self.ops = {e: [] for e in self.ENG}


class Cfg:
    def __init__(self, S=16384, PAST=16384, NSEQ=4, NPOOL=5120):
        self.S, self.PAST, self.NSEQ, self.NPOOL = S, PAST, NSEQ, NPOOL
        self.NB = S // 128
        self.NOWN = self.NB // 8
        self.NQ = self.NOWN * 128
        self.NPG = PAST // 128
        self.NSR = NSEQ * 8
        self.dbg_ob = False
        self.nsa = True
        self.dbg_gate = None
        self.NTp = -(-(S // 16 - 1) // 128)
        self.NMp = -(-(S // 64) // 128) * 128
        self.NTs = -(-(PAST // 16 - 1) // 128)
        self.NMs = -(-(PAST // 64 + 1) // 128) * 128


def build(cfg):
    nc = bass.Bass("TRN2", target_bir_lowering=False)
    S, NB, NOWN, NQ, NSEQ, NSR = cfg.S, cfg.NB, cfg.NOWN, cfg.NQ, cfg.NSEQ, cfg.NSR
    NPR = cfg.NPOOL * 128

    def din(name, shape, dt=F32):
        return nc.dram_tensor(name, list(shape), dt, kind="ExternalInput").ap()

    def dout(name, shape, dt=F32):
        return nc.dram_tensor(name, list(shape), dt, kind="ExternalOutput").ap()

    def dscr(name, shape, dt=F32):
        return nc.dram_tensor(name, list(shape), dt, kind="Internal").ap()

    xf = din("xf", [S, D])
    xs = din("xs", [NSR, D])
    cvec = din("cvec", [1 + NSEQ, D])
    w_ada = din("w_ada", [D, 6 * D])
    b_ada = din("b_ada", [1, 6 * D])
    g_norm = din("g_norm", [2, D])
    w_in = din("w_in", [D, IN_W])
    b_forget = din("b_forget", [1, 8])
    g_qk_fox = din("g_qk_fox", [2, 64])
    g_qk_nsa = din("g_qk_nsa", [4, 64])
    sel5 = din("sel5", [1 + NSEQ, 128 + NSR])
    ident_in = din("ident", [128, 128])

    o_fkv_p = dout("o_fkv_p", [NQ, 1024])
    o_lf_p = dout("o_lf_p", [NQ, 8])
    o_nkv_p = dout("o_nkv_p", [NQ, 512])
    o_win_p = dout("o_win_p", [NQ, 256])
    o_fkv_s = dout("o_fkv_s", [NSR, 1024])
    o_lf_s = dout("o_lf_s", [NSR, 8])
    o_nkv_s = dout("o_nkv_s", [NSR, 512])
    o_wnew_s = dout("o_wnew_s", [NSR, 256])
    tri_in = din("tri_in", [128, 128])
    iota_in = din("iota_in", [128, 1])
    bt_in = din("bt_in", [128, 128])
    LS_ = cfg.PAST + 128
    tvp_in = din("tvp_in", [128, S // 16]); pnp_in = din("pnp_in", [128, S // 16]); kwnp_in = din("kwnp_in", [128, S // 128])
    tvs_in = din("tvs_in", [128, LS_ // 16]); pns_in = din("pns_in", [128, LS_ // 16]); kwns_in = din("kwns_in", [128, LS_ // 128])
    ptab = din("ptab", [NSEQ, cfg.NPG], I32)
    pool_fox = din("pool_fox", [NPR, 1024])
    pool_lf = din("pool_lf", [NPR, 8])
    pool_nsa = din("pool_nsa", [NPR, 512])
    w_out_fox = din("w_out_fox", [512, D]); w_out_nsa = din("w_out_nsa", [512, D]); w_out = din("w_out", [D, D])
    w_up = din("w_up", [D, 4 * D]); w_down = din("w_down", [4 * D, D])
    if cfg.dbg_ob:
        dbg_ob_p_in = din("dbg_ob_p_in", [NQ, 512]); dbg_ob_s_in = din("dbg_ob_s_in", [NSR, 512])
    rel_bias = din("rel_bias", [32, 8]); pe_cmp = din("pe_cmp", [2, 32, 64])
    w_cmp1 = din("w_cmp1", [2, 2048, 128]); w_cmp2 = din("w_cmp2", [2, 128, 64])
    ohc_in = din("ohc_in", [33, 4096]); oh1_in = din("oh1_in", [33, 768]); ee_in = din("ee_in", [128, 8192])
    wcp_in = din("wcp_in", [cfg.NTp * 128, cfg.NMp]); wcs_in = din("wcs_in", [cfg.NTs * 128, cfg.NMs])
    cnegp_in = din("cnegp_in", [1, cfg.NTp * 128]); cnegs_in = din("cnegs_in", [1, cfg.NTs * 128])
    addp_in = din("addp_in", [NOWN, 128, cfg.NMp]); adds_in = din("adds_in", [8, cfg.NMs])
    dbg_ob_p = dout("dbg_ob_p", [NQ, 512], BF16); dbg_ob_s = dout("dbg_ob_s", [NSR, 512], BF16)
    dbg_oa_p = dout("dbg_oa_p", [NQ, 512], BF16)
    dbg_oa_s = dout("dbg_oa_s", [NSR, 512], BF16)
    WB = min(512, cfg.PAST)
    win_in = din("win_in", [NSEQ, WB, 256])
    o_win_s = dout("o_win_s", [NSEQ, WB, 256])
    o_y_p = dout("o_y_p", [NQ, D])
    o_y_s = dout("o_y_s", [NSR, D])

    sc = Sched(nc, nlanes=12)
    st = contextlib.ExitStack()
    with st:
        def sb(name, shape, dt=F32):
            return st.enter_context(nc.sbuf_tensor(name, list(shape), dt))

        def ps(name, shape, dt=F32):
            return st.enter_context(nc.psum_tensor(name, list(shape), dt))

        ident_f = sb("ident_f", [128, 128])
        ident_b = sb("ident_b", [128, 128], BF16)
        modrows = sb("modrows", [1 + NSEQ, 6 * D])
        sel_sb = sb("sel_sb", [1 + NSEQ, 128 + NSR])
        gk_b = sb("gk_b", [128, 8, 64])
        gq_b = sb("gq_b", [128, 8, 64])
        gnq_b = sb("gnq_b", [128, 8, 64])
        gsel_b = sb("gsel_b", [128, 2, 64])
        gwin_b = sb("gwin_b", [128, 2, 64])
        bf_b = sb("bf_b", [128, 8])
        neghalf = sb("neghalf", [128, 8])
        ones_f = sb("ones_f", [128, 128])
        gn = sb("gn", [128, 2, D])
        gb_res = sb("gb_res", [128, NOWN, 24])
        gbs_res = sb("gbs_res", [8, NSEQ, 24])
        tri_b = sb("tri_b", [128, 128], BF16)
        ones_b = sb("ones_b", [128, 512], BF16)
        zeros_b = sb("zeros_b", [128, 512], BF16)
        stW = contextlib.ExitStack()
        w_in_b = stW.enter_context(nc.sbuf_tensor("w_in_b", [128, 8, 2848], BF16))

        sc.dma("sp", lambda e: e.dma_start(out=ident_f[:], in_=ident_in[:, :]), [], ["ident_f"])
        sc.op("dve", lambda e: e.tensor_copy(out=ident_b[:], in_=ident_f[:]), ["ident_f"], ["ident_b"])
        sc.op("dve", lambda e: e.memset(neghalf[:], -0.5), [], ["neghalf"])
        sc.op("dve", lambda e: e.memset(ones_f[:], 1.0), [], ["ones_f"])
        sc.dma("sp", lambda e: e.dma_start(out=sel_sb[:], in_=sel5[:, :]), [], ["sel_sb"])

        with contextlib.ExitStack() as st0:
            def sb0(name, shape, dt=F32):
                return st0.enter_context(nc.sbuf_tensor(name, list(shape), dt))
            NR = 1 + NSEQ
            c_sb = sb0("c_sb", [NR, D])
            sig = sb0("sig", [NR, D])
            cT = sb0("cT", [128, 8, NR])
            wada = [sb0(f"wada{i}", [128, 8, 512]) for i in range(2)]
            bada = sb0("bada", [NR, 6 * D])
            small = sb0("small", [128, 8 + 6 * 64])
            ps0 = st0.enter_context(nc.psum_tensor("ps0", [128, 512], F32))
            ps1 = st0.enter_context(nc.psum_tensor("ps1", [128, 512], F32))
            sc.dma("sp", lambda e: e.dma_start(out=c_sb[:], in_=cvec[:, :]), [], ["c_sb"])
            sc.dma("sp", lambda e: e.dma_start(out=bada[:], in_=b_ada.partition_broadcast(NR)), [], ["bada"])
            sc.dma("sp", lambda e: e.dma_start(out=gn[:, 0, :], in_=g_norm[0:1, :].partition_broadcast(128)), [], ["gn"])
            sc.dma("sp", lambda e: e.dma_start(out=gn[:, 1, :], in_=g_norm[1:2, :].partition_broadcast(128)), [], ["gn"])
            sc.dma("sp", lambda e: e.dma_start(out=small[:, 0:8], in_=b_forget.partition_broadcast(128)), [], ["small"])
            sc.dma("sp", lambda e: e.dma_start(
                out=small[:, 8:8 + 128], in_=g_qk_fox.rearrange("a d -> (a d)").unsqueeze(0).partition_broadcast(128)), [], ["small"])
            sc.dma("sp", lambda e: e.dma_start(
                out=small[:, 136:136 + 256], in_=g_qk_nsa.rearrange("a d -> (a d)").unsqueeze(0).partition_broadcast(128)), [], ["small"])
            sc.op("act", lambda e: e.activation(out=sig[:], in_=c_sb[:], func=AF.Sigmoid), ["c_sb"], ["sig"])
            sc.op("dve", lambda e: e.tensor_tensor(out=sig[:], in0=sig[:], in1=c_sb[:], op=ALU.mult), ["sig", "c_sb"], ["sig"])
            for k in range(8):
                sc.op("pe", lambda e, k=k: e.transpose(out=ps0[:, k * NR:(k + 1) * NR], in_=sig[:, k * 128:(k + 1) * 128], identity=ident_f[0:NR, 0:NR]),
                      ["sig", "ident_f"], ["ps0"])
            sc.op("dve", lambda e: e.tensor_copy(out=cT[:].rearrange("p k r -> p (k r)"), in_=ps0[:, 0:8 * NR]), ["ps0"], ["cT"])
            for g in range(12):
                wt = wada[g % 2]
                wn = f"wada{g % 2}"
                sc.dma("sp", lambda e, g=g, wt=wt: e.dma_start(
                    out=wt[:], in_=w_ada[:, g * 512:(g + 1) * 512].rearrange("(k p) n -> p k n", p=128)), [], [wn])
                for k in range(8):
                    sc.op("pe", lambda e, k=k, wt=wt: e.matmul(ps1[0:NR, :], lhsT=cT[:, k, :], rhs=wt[:, k, :],
                                                               start=(k == 0), stop=(k == 7)), ["cT", wn], ["ps1"])
                sc.op("dve", lambda e, g=g: e.tensor_tensor(out=modrows[:, g * 512:(g + 1) * 512], in0=ps1[0:NR, :],
                                                             in1=bada[:, g * 512:(g + 1) * 512], op=ALU.add),
                      ["ps1", "bada"], ["modrows"])
            def bcast_mod(dst, dname, which, rows, c0):
                for half in range(2):
                    sc.op("pe", lambda e, half=half: e.matmul(
                        ps0[0:rows, :], lhsT=sel_sb[:, c0:c0 + rows],
                        rhs=modrows[:, which * D + half * 512: which * D + half * 512 + 512], start=True, stop=True),
                        ["sel_sb", "modrows"], ["ps0"])
                    sc.op("dve", lambda e, half=half: e.tensor_copy(out=dst[:, half * 512:(half + 1) * 512], in_=ps0[0:rows, :]),
                          ["ps0"], [dname])

            def mk_gain(dst, dname, rows, gi):
                sc.op("dve", lambda e: e.scalar_tensor_tensor(out=dst, in0=dst, scalar=1.0, in1=gn[0:rows, gi, :],
                                                              op0=ALU.add, op1=ALU.mult), [dname, "gn"], [dname])
            sc.op("dve", lambda e: e.tensor_copy(out=bf_b[:], in_=small[:, 0:8]), ["small"], ["bf_b"])
            for h in range(8):
                sc.op("dve", lambda e, h=h: e.tensor_scalar(out=gq_b[:, h, :], in0=small[:, 8:72], scalar1=0.125, scalar2=None,
                                                            op0=ALU.mult), ["small"], ["gq_b"])
                sc.op("dve", lambda e, h=h: e.tensor_copy(out=gk_b[:, h, :], in_=small[:, 72:136]), ["small"], ["gk_b"])
                sc.op("dve", lambda e, h=h: e.tensor_scalar(out=gnq_b[:, h, :], in0=small[:, 136:200], scalar1=0.125, scalar2=None,
                                                            op0=ALU.mult), ["small"], ["gnq_b"])
            for g in range(2):
                sc.op("dve", lambda e, g=g: e.tensor_copy(out=gsel_b[:, g, :], in_=small[:, 136 + 128:136 + 192]), ["small"], ["gsel_b"])
                sc.op("dve", lambda e, g=g: e.tensor_copy(out=gwin_b[:, g, :], in_=small[:, 136 + 192:136 + 256]), ["small"], ["gwin_b"])
            for k in range(8):
                for half in range(2):
                    wt = wada[(2 * k + half) % 2]
                    wn = f"wada{(2 * k + half) % 2}"
                    c0 = half * 1424
                    sc.dma("sp", lambda e, k=k, c0=c0, wt=wt: e.dma_start(
                        out=wt[:].rearrange("p k n -> p (k n)")[:, 0:1424], in_=w_in[k * 128:(k + 1) * 128, c0:c0 + 1424]), [], [wn])
                    sc.op("pool", lambda e, k=k, c0=c0, wt=wt: e.tensor_copy(
                        out=w_in_b[:, k, c0:c0 + 1424], in_=wt[:].rearrange("p k n -> p (k n)")[:, 0:1424]), [wn], ["w_in_b"])
            sc.flush(st)

        LS = cfg.PAST + 128
        NPG = cfg.NPG

        class Ctx:
            pass

        def mk_ctx(name, L, nqc):
            c = Ctx()
            c.name, c.L, c.NBk, c.nqc = name, L, L // 128, nqc
            c.KT = dscr(f"KT_{name}", [70, 8, L], BF16)
            c.V = dscr(f"V_{name}", [8, 128, c.NBk, 65], BF16)
            c.QT = dscr(f"QT_{name}", [70, 8, nqc], BF16)
            c.LF = dscr(f"LF_{name}", [8, L], F32)
            c.CUM = dscr(f"CUM_{name}", [8, L], F32)
            c.XC = dscr(f"XC_{name}", [2, 128, L], BF16)
            c.KS = dscr(f"KS_{name}", [2, 64, L], BF16)
            c.KW = dscr(f"KW_{name}", [2, 65, L], BF16)
            c.VS = dscr(f"VS_{name}", [2, 128, c.NBk, 65], BF16)
            c.VW = dscr(f"VW_{name}", [2, 128, c.NBk, 65], BF16)
            c.QN = dscr(f"QN_{name}", [65, 8, nqc], BF16)
            return c
        ctx_p = mk_ctx("p", S, NQ)
        ctx_s = [mk_ctx(f"s{b}", LS, 8) for b in range(NSEQ)]
        OAS = dscr("OAS", [NSR, 512], BF16)
        OAP = dscr("OAP", [NQ, 512], BF16)
        OBP = dscr("OBP", [NQ, 512], BF16)
        OBS = dscr("OBS", [NSR, 512], BF16)

        sc.op("dve", lambda e: e.memset(ones_b[:], 1.0), [], ["ones_b"])
        sc.op("dve", lambda e: e.memset(zeros_b[:], 0.0), [], ["zeros_b"])

        def bcast_rows(dst, dname, which, rows, c0, pst, pname):
            for half in range(2):
                sc.op("pe", lambda e, half=half: e.matmul(
                    pst[0:rows, :], lhsT=sel_sb[:, c0:c0 + rows],
                    rhs=modrows[:, which * D + half * 512: which * D + half * 512 + 512], start=True, stop=True),
                    ["sel_sb", "modrows"], [pname])
                sc.op("dve", lambda e, half=half: e.tensor_copy(out=dst[:, half * 512:(half + 1) * 512], in_=pst[0:rows, :]),
                      [pname], [dname])

        def load_mod(Mt, rows, c0, whichs, gi, pst, pname):
            bcast_rows(Mt[0:rows, 0, :], "Mp", whichs[0], rows, c0, pst, pname)
            sc.op("dve", lambda e: e.scalar_tensor_tensor(out=Mt[0:rows, 0, :], in0=Mt[0:rows, 0, :], scalar=1.0, in1=gn[0:rows, gi, :],
                                                          op0=ALU.add, op1=ALU.mult), ["Mp", "gn"], ["Mp"])
            bcast_rows(Mt[0:rows, 1, :], "Mp", whichs[1], rows, c0, pst, pname)
            if whichs[2] is not None:
                bcast_rows(Mt[0:rows, 2, :], "Mp", whichs[2], rows, c0, pst, pname)

        with contextlib.ExitStack() as stA:
            def sbA(name, shape, dt=F32):
                return stA.enter_context(nc.sbuf_tensor(name, list(shape), dt))

            def psA(name, shape, dt=F32):
                return stA.enter_context(nc.psum_tensor(name, list(shape), dt))
            MpA = sbA("MpA", [128, 3, D])
            trif = sbA("trif", [128, 128])
            wpage = sbA("wpage", [128, 256])
            ps_tr = psA("ps_tr", [128, 1024], BF16)
            ps_a = psA("ps_a", [128, 512]); ps_b = psA("ps_b", [128, 512])
            ps_c = psA("ps_c", [128, 512]); ps_d = psA("ps_d", [128, 512])
            ps_kt = psA("ps_kt", [128, 8, 128], BF16)
            ps_nt = psA("ps_nt", [128, 6, 128], BF16)
            ps_lf = psA("ps_lf", [8, 128])

            sc.dma("sp", lambda e: e.dma_start(out=trif[:], in_=tri_in[:, :]), [], ["trif"])
            sc.op("dve", lambda e: e.tensor_copy(out=tri_b[:], in_=trif[:]), ["trif"], ["tri_b"])

            LOCAL_A = {"xt", "junk", "tmpf", "hb", "hT", "ssum", "rstd", "sq", "hs", "hr", "kvout", "nkvout", "qf", "lfo",
                       "kb", "nb", "stg_k", "stg_x", "stg_n", "stg_l", "vaug", "vsaug", "vwaug"}

            def make_setA(sfx):
                sc.local, sc.sfx = LOCAL_A, sfx
                xt = sbA(sfx + "xt", [128, D])
                junk = sbA(sfx + "junk", [128, D], BF16)
                tmpf = sbA(sfx + "tmpf", [128, D])
                hb = sbA(sfx + "hb", [128, D], BF16)
                hT = sbA(sfx + "hT", [128, 8, 128], BF16)
                ssum = sbA(sfx + "ssum", [128, 1])
                rstd = sbA(sfx + "rstd", [128, 1])
                sq = sbA(sfx + "sq", [128, 512])
                hs = sbA(sfx + "hs", [128, 8]); hr = sbA(sfx + "hr", [128, 8])
                kvout = sbA(sfx + "kvout", [128, 1024])
                nkvout = sbA(sfx + "nkvout", [128, 768])
                qf = sbA(sfx + "qf", [128, 512])
                lfo = sbA(sfx + "lfo", [128, 8])
                kb = sbA(sfx + "kb", [128, 512], BF16)
                nb = sbA(sfx + "nb", [128, 768], BF16)
                stg_k = sbA(sfx + "stg_k", [64, 8, 128], BF16)
                stg_x = sbA(sfx + "stg_x", [128, 2, 128], BF16)
                stg_n = sbA(sfx + "stg_n", [64, 4, 128], BF16)
                stg_l = sbA(sfx + "stg_l", [8, 128])
                vaug = sbA(sfx + "vaug", [128, 8, 65], BF16)
                vsaug = sbA(sfx + "vsaug", [128, 2, 65], BF16)
                vwaug = sbA(sfx + "vwaug", [128, 2, 65], BF16)
                sc.op("dve", lambda e: e.memset(vaug[:], 1.0), [], ["vaug"])
                sc.op("dve", lambda e: e.memset(vsaug[:], 1.0), [], ["vsaug"])
                sc.op("dve", lambda e: e.memset(vwaug[:], 1.0), [], ["vwaug"])
                def headnorm(src, nh, gain, dst, dstname, rows, srcname):
                    sc.op("act", lambda e: e.activation(out=sq[0:rows, 0:nh * 64], in_=src, func=AF.Square), [srcname], ["sq"])
                    sc.op("dve", lambda e: e.tensor_reduce(out=hs[0:rows, 0:nh], in_=sq[0:rows, 0:nh * 64].rearrange("p (h d) -> p h d", d=64),
                                                           axis=AX.X, op=ALU.add), ["sq"], ["hs"])
                    sc.op("dve", lambda e: e.tensor_scalar(out=hs[0:rows, 0:nh], in0=hs[0:rows, 0:nh], scalar1=1.0 / 64, scalar2=EPS,
                                                           op0=ALU.mult, op1=ALU.add), ["hs"], ["hs"])
                    sc.op("pool", lambda e: e.tensor_tensor(out=hr[0:rows, 0:nh], in0=hs[0:rows, 0:nh], in1=neghalf[0:rows, 0:nh], op=ALU.pow),
                          ["hs", "neghalf"], ["hr"])
                    sc.op("dve", lambda e: e.tensor_tensor(out=dst, in0=src.rearrange("p (h d) -> p h d", d=64),
                                                           in1=hr[0:rows, 0:nh].unsqueeze(2).to_broadcast([rows, nh, 64]), op=ALU.mult),
                          [srcname, "hr"], [dstname])
                    sc.op("dve", lambda e: e.tensor_tensor(out=dst, in0=dst, in1=gain, op=ALU.mult), [dstname], [dstname])

                def tr_heads(src_bf, srcname, rows, nh, pst, pname, stg, sname, slot0=0, width=64):
                    for h in range(nh):
                        sc.op("pe", lambda e, h=h: e.transpose(out=pst[0:width, slot0 + h, 0:rows], in_=src_bf[0:rows, h * width:(h + 1) * width],
                                                               identity=ident_b[0:rows, 0:rows]), [srcname, "ident_b"], [pname])
                    if rows < 128:
                        sc.op("dve", lambda e: e.memset(stg[0:width, slot0:slot0 + nh, :], 0.0), [], [sname])
                    sc.op("act", lambda e: e.activation(out=stg[0:width, slot0:slot0 + nh, 0:rows], in_=pst[0:width, slot0:slot0 + nh, 0:rows],
                                                        func=AF.Copy), [pname], [sname])

                def kside_store(c, blk, rows, srcn, fk=None, fv=None, lf=None, kcmp=None, vcmp=None, ksel=None, vsel=None,
                                kwin=None, vwin=None):
                    cs = slice(blk * 128, (blk + 1) * 128)
                    if fk is not None:
                        sc.op("dve", lambda e: e.tensor_copy(out=kb[0:rows, :], in_=fk), [srcn], ["kb"])
                        tr_heads(kb, "kb", rows, 8, ps_kt, "ps_kt", stg_k, "stg_k")
                        sc.dma("sp", lambda e: e.dma_start(out=c.KT[0:64, :, cs], in_=stg_k[:]),
                               ["stg_k"], ["KT_" + c.name])
                    if fv is not None:
                        if rows < 128:
                            sc.op("dve", lambda e: e.memset(vaug[:, :, 0:64], 0.0), [], ["vaug"])
                        sc.op("act", lambda e: e.activation(out=vaug[0:rows, :, 0:64], in_=fv.rearrange("p (h d) -> p h d", d=64), func=AF.Copy),
                              [srcn], ["vaug"])
                        sc.dma("sp", lambda e: e.dma_start(out=c.V[:, :, blk, :].rearrange("h p d -> p h d"), in_=vaug[:]),
                               ["vaug"], ["V_" + c.name])
                    if lf is not None:
                        sc.op("pe", lambda e: e.transpose(out=ps_lf[0:8, 0:rows], in_=lf, identity=ident_f[0:rows, 0:rows]),
                              [srcn, "ident_f"], ["ps_lf"])
                        if rows < 128:
                            sc.op("dve", lambda e: e.memset(stg_l[:], 0.0), [], ["stg_l"])
                        sc.op("dve", lambda e: e.tensor_copy(out=stg_l[:, 0:rows], in_=ps_lf[0:8, 0:rows]), ["ps_lf"], ["stg_l"])
                        sc.dma("sp", lambda e: e.dma_start(out=c.LF[:, cs], in_=stg_l[:]), ["stg_l"], ["LF_" + c.name])
                    if kcmp is not None:
                        sc.op("dve", lambda e: e.tensor_copy(out=nb[0:rows, 0:128], in_=kcmp), [srcn], ["nb"])
                        sc.op("dve", lambda e: e.tensor_copy(out=nb[0:rows, 128:256], in_=vcmp), [srcn], ["nb"])
                        sc.op("dve", lambda e: e.tensor_copy(out=nb[0:rows, 256:384], in_=ksel), [srcn], ["nb"])
                        tr_heads(nb[:, 0:256], "nb", rows, 2, ps_nt, "ps_nt", stg_x, "stg_x", 0, 128)
                        sc.dma("sp", lambda e: e.dma_start(out=c.XC[:, :, cs].rearrange("k p t -> p k t"), in_=stg_x[:]),
                               ["stg_x"], ["XC_" + c.name])
                        tr_heads(nb[:, 256:384], "nb", rows, 2, ps_nt, "ps_nt", stg_n, "stg_n", 0, 64)
                        sc.dma("sp", lambda e: e.dma_start(out=c.KS[:, :, cs].rearrange("g p t -> p g t"), in_=stg_n[:, 0:2, :]),
                               ["stg_n"], ["KS_" + c.name])
                        if rows < 128:
                            sc.op("dve", lambda e: e.memset(vsaug[:, :, 0:64], 0.0), [], ["vsaug"])
                        sc.op("act", lambda e: e.activation(out=vsaug[0:rows, :, 0:64], in_=vsel.rearrange("p (h d) -> p h d", d=64), func=AF.Copy),
                              [srcn], ["vsaug"])
                        sc.dma("sp", lambda e: e.dma_start(out=c.VS[:, :, blk, :].rearrange("h p d -> p h d"), in_=vsaug[:]),
                               ["vsaug"], ["VS_" + c.name])
                    if kwin is not None:
                        sc.op("dve", lambda e: e.tensor_copy(out=nb[0:rows, 512:640], in_=kwin), [srcn], ["nb"])
                        tr_heads(nb[:, 512:640], "nb", rows, 2, ps_nt, "ps_nt", stg_n, "stg_n", 2, 64)
                        sc.dma("sp", lambda e: e.dma_start(out=c.KW[:, 0:64, cs].rearrange("g p t -> p g t"), in_=stg_n[:, 2:4, :]),
                               ["stg_n"], ["KW_" + c.name])
                        if rows < 128:
                            sc.op("dve", lambda e: e.memset(vwaug[:, :, 0:64], 0.0), [], ["vwaug"])
                        sc.op("act", lambda e: e.activation(out=vwaug[0:rows, :, 0:64], in_=vwin.rearrange("p (h d) -> p h d", d=64), func=AF.Copy),
                              [srcn], ["vwaug"])
                        sc.dma("sp", lambda e: e.dma_start(out=c.VW[:, :, blk, :].rearrange("h p d -> p h d"), in_=vwaug[:]),
                               ["vwaug"], ["VW_" + c.name])

                def proj_tile(x_ap, rows, G1, B1, g1n, b1n, own, outs, c, blk, qcol0, gb_dst, gbn):
                    sc.dma("sp", lambda e: e.dma_start(out=xt[0:rows, :], in_=x_ap), [], ["xt"])
                    sc.op("act", lambda e: e.activation(out=junk[0:rows, :], in_=xt[0:rows, :], func=AF.Square, accum_out=ssum[0:rows, :]),
                          ["xt"], ["junk", "ssum"])
                    sc.op("dve", lambda e: e.tensor_scalar(out=ssum[0:rows, :], in0=ssum[0:rows, :], scalar1=1.0 / D, scalar2=EPS,
                                                           op0=ALU.mult, op1=ALU.add), ["ssum"], ["ssum"])
                    sc.op("pool", lambda e: e.tensor_tensor(out=rstd[0:rows, :], in0=ssum[0:rows, :], in1=neghalf[0:rows, 0:1], op=ALU.pow),
                          ["ssum", "neghalf"], ["rstd"])
                    sc.op("dve", lambda e: e.scalar_tensor_tensor(out=tmpf[0:rows, :], in0=xt[0:rows, :], scalar=rstd[0:rows, :], in1=G1,
                                                                  op0=ALU.mult, op1=ALU.mult), ["xt", "rstd", g1n], ["tmpf"])
                    sc.op("dve", lambda e: e.tensor_tensor(out=hb[0:rows, :], in0=tmpf[0:rows, :], in1=B1, op=ALU.add),
                          ["tmpf", b1n], ["hb"])
                    for k in range(8):
                        sc.op("pe", lambda e, k=k: e.transpose(out=ps_tr[:, k * 128:k * 128 + rows], in_=hb[0:rows, k * 128:(k + 1) * 128],
                                                               identity=ident_b[0:rows, 0:rows]), ["hb", "ident_b"], ["ps_tr"])
                    sc.op("act", lambda e: e.activation(out=hT[:, :, 0:rows], in_=ps_tr[:].rearrange("p (k t) -> p k t", t=128)[:, :, 0:rows],
                                                        func=AF.Copy), ["ps_tr"], ["hT"])

                    def mm(pst, pname, c0, n, o0=0):
                        for k in range(8):
                            sc.op("pe", lambda e, k=k: e.matmul(pst[0:rows, o0:o0 + n], lhsT=hT[:, k, 0:rows], rhs=w_in_b[:, k, c0:c0 + n],
                                                                start=(k == 0), stop=(k == 7)), ["hT", "w_in_b"], [pname])
                    mm(ps_a, "ps_a", O_KA, 512)
                    mm(ps_b, "ps_b", O_VA, 512)
                    mm(ps_c, "ps_c", O_ZKV, 512)
                    mm(ps_d, "ps_d", O_ZKV + 512, 256)
                    mm(ps_d, "ps_d", O_ZF, 8, 256)
                    headnorm(ps_a[0:rows, :], 8, gk_b[0:rows], kvout[0:rows, 0:512].rearrange("p (h d) -> p h d", d=64), "kvout", rows, "ps_a")
                    sc.op("act", lambda e: e.activation(out=kvout[0:rows, 512:1024], in_=ps_b[0:rows, :], func=AF.Copy), ["ps_b"], ["kvout"])
                    sc.op("act", lambda e: e.activation(out=nkvout[0:rows, 0:512], in_=ps_c[0:rows, :], func=AF.Copy), ["ps_c"], ["nkvout"])
                    sc.op("act", lambda e: e.activation(out=nkvout[0:rows, 512:768], in_=ps_d[0:rows, 0:256], func=AF.Copy), ["ps_d"], ["nkvout"])
                    headnorm(ps_c[0:rows, 256:384], 2, gsel_b[0:rows], nkvout[0:rows, 256:384].rearrange("p (h d) -> p h d", d=64), "nkvout", rows, "ps_c")
                    headnorm(ps_d[0:rows, 0:128], 2, gwin_b[0:rows], nkvout[0:rows, 512:640].rearrange("p (h d) -> p h d", d=64), "nkvout", rows, "ps_d")
                    sc.op("dve", lambda e: e.tensor_tensor(out=lfo[0:rows, :], in0=ps_d[0:rows, 256:264], in1=bf_b[0:rows, :], op=ALU.add),
                          ["ps_d", "bf_b"], ["lfo"])
                    sc.op("act", lambda e: e.activation(out=lfo[0:rows, :], in_=lfo[0:rows, :], func=AF.Exp, scale=-1.0), ["lfo"], ["lfo"])
                    sc.op("act", lambda e: e.activation(out=lfo[0:rows, :], in_=lfo[0:rows, :], func=AF.Ln, bias=1.0), ["lfo"], ["lfo"])
                    sc.op("dve", lambda e: e.tensor_scalar(out=lfo[0:rows, :], in0=lfo[0:rows, :], scalar1=-1.0, scalar2=None, op0=ALU.mult),
                          ["lfo"], ["lfo"])
                    if outs is not None:
                        o_kv, o_lf, o_nkv, o_win = outs
                        sc.dma("pool", lambda e: e.dma_start(out=o_kv, in_=kvout[0:rows, :]), ["kvout"], [])
                        sc.dma("pool", lambda e: e.dma_start(out=o_lf, in_=lfo[0:rows, :]), ["lfo"], [])
                        sc.dma("pool", lambda e: e.dma_start(out=o_nkv, in_=nkvout[0:rows, 0:512]), ["nkvout"], [])
                        for oo in o_win:
                            sc.dma("pool", lambda e, oo=oo: e.dma_start(out=oo, in_=nkvout[0:rows, 512:768]), ["nkvout"], [])
                    kside_store(c, blk, rows, "kvout", fk=kvout[0:rows, 0:512], fv=kvout[0:rows, 512:1024], lf=lfo[0:rows, :])
                    kside_store(c, blk, rows, "nkvout", kcmp=nkvout[0:rows, 0:128], vcmp=nkvout[0:rows, 128:256], ksel=nkvout[0:rows, 256:384],
                                vsel=nkvout[0:rows, 384:512], kwin=nkvout[0:rows, 512:640], vwin=nkvout[0:rows, 640:768])
                    if own:
                        mm(ps_a, "ps_a", O_QA, 512)
                        mm(ps_b, "ps_b", O_QB, 512)
                        mm(ps_c, "ps_c", O_ZG, 24)
                        headnorm(ps_a[0:rows, :], 8, gq_b[0:rows], qf[0:rows, :].rearrange("p (h d) -> p h d", d=64), "qf", rows, "ps_a")
                        sc.op("dve", lambda e: e.tensor_copy(out=kb[0:rows, :], in_=qf[0:rows, :]), ["qf"], ["kb"])
                        tr_heads(kb, "kb", rows, 8, ps_kt, "ps_kt", stg_k, "stg_k")
                        sc.dma("sp", lambda e: e.dma_start(out=c.QT[0:64, :, qcol0:qcol0 + rows], in_=stg_k[:, :, 0:rows]),
                               ["stg_k"], ["QT_" + c.name])
                        headnorm(ps_b[0:rows, :], 8, gnq_b[0:rows], qf[0:rows, :].rearrange("p (h d) -> p h d", d=64), "qf", rows, "ps_b")
                        sc.op("dve", lambda e: e.tensor_copy(out=kb[0:rows, :], in_=qf[0:rows, :]), ["qf"], ["kb"])
                        tr_heads(kb, "kb", rows, 8, ps_kt, "ps_kt", stg_k, "stg_k")
                        sc.dma("sp", lambda e: e.dma_start(out=c.QN[0:64, :, qcol0:qcol0 + rows], in_=stg_k[:, :, 0:rows]),
                               ["stg_k"], ["QN_" + c.name])
                        sc.op("act", lambda e: e.activation(out=gb_dst, in_=ps_c[0:rows, 0:24], func=AF.Sigmoid), ["ps_c"], [gbn])
                        if cfg.dbg_gate is not None:
                            sc.op("dve", lambda e: e.memset(gb_dst, 0.0), [gbn], [gbn])
                            sc.op("dve", lambda e: e.memset(gb_dst.rearrange("p (h i) -> p h i", i=3)[:, :, cfg.dbg_gate], 1.0), [gbn], [gbn])


                sc.local = None
                return proj_tile, kside_store
            setsA = [make_setA("_a0"), make_setA("_a1")]
            tcount = {"n": 0}

            def proj_tile(*a, **k):
                i = tcount["n"] % 2
                tcount["n"] += 1
                sc.local, sc.sfx = LOCAL_A, f"_a{i}"
                setsA[i][0](*a, **k)
                sc.local = None

            def kside_store(*a, **k):
                sc.local, sc.sfx = LOCAL_A, "_a0"
                setsA[0][1](*a, **k)
                sc.local = None
            load_mod(MpA, 128, 0, (1, 0, None), 0, ps_a, "ps_a")
            for f in range(NB):
                own = (f % 8 == 7)
                j = f // 8
                outs = None
                if own:
                    outs = (o_fkv_p[j * 128:(j + 1) * 128, :], o_lf_p[j * 128:(j + 1) * 128, :],
                            o_nkv_p[j * 128:(j + 1) * 128, :], [o_win_p[j * 128:(j + 1) * 128, :]])
                proj_tile(xf[f * 128:(f + 1) * 128, :], 128, MpA[:, 0, :], MpA[:, 1, :], "Mp", "Mp", own, outs, ctx_p, f, j * 128,
                          gb_res[:, j, :], "gb_res")
            for b in range(NSEQ):
                outs = (o_fkv_s[b * 8:(b + 1) * 8, :], o_lf_s[b * 8:(b + 1) * 8, :], o_nkv_s[b * 8:(b + 1) * 8, :],
                        [o_wnew_s[b * 8:(b + 1) * 8, :], o_win_s[b, WB - 8:WB, :]])
                load_mod(MpA, 8, 128 + 8 * b, (1, 0, None), 0, ps_a, "ps_a")
                proj_tile(xs[b * 8:(b + 1) * 8, :], 8, MpA[0:8, 0, :], MpA[0:8, 1, :], "Mp", "Mp", True, outs, ctx_s[b], NPG, 0,
                          gbs_res[:, b, :], "gbs_res")
                sc.dma("sp", lambda e, b=b: e.dma_start(out=o_win_s[b, 0:WB - 8, :], in_=win_in[b, 8:WB, :]), [], [])

            for b in range(NSEQ):
                c = ctx_s[b]
                for i in range(WB // 128):
                    sc.dma("sp", lambda e, b=b, i=i: e.dma_start(out=wpage[:], in_=win_in[b, i * 128:(i + 1) * 128, :]), [], ["wpage"])
                    kside_store(c, NPG - WB // 128 + i, 128, "wpage", kwin=wpage[:, 0:128], vwin=wpage[:, 128:256])
            sc.flush(st)
        stW.close()

        with contextlib.ExitStack() as stG:
            def sbG(name, shape, dt=F32):
                return stG.enter_context(nc.sbuf_tensor("G_" + name, list(shape), dt))

            def psG(name, shape, dt=F32):
                return stG.enter_context(nc.psum_tensor("G_" + name, list(shape), dt))
            ptb_i = sbG("ptb_i", [128, NSEQ * NPG], I32)
            ptb_f = sbG("ptb_f", [128, NSEQ * NPG])
            idx_i = sbG("idx_i", [128, NSEQ * NPG], I32)
            iota_p = sbG("iota_p", [128, 1])
            sc.dma("sp", lambda e: e.dma_start(out=iota_p[:], in_=iota_in[:, :]), [], ["iota_p"])
            sc.dma("sp", lambda e: e.dma_start(out=ptb_i[:], in_=ptab.rearrange("b n -> (b n)").unsqueeze(0).partition_broadcast(128)),
                   [], ["ptb_i"])
            sc.op("dve", lambda e: e.tensor_copy(out=ptb_f[:], in_=ptb_i[:]), ["ptb_i"], ["ptb_f"])
            sc.op("dve", lambda e: e.tensor_scalar(out=ptb_f[:], in0=ptb_f[:], scalar1=128.0, scalar2=iota_p[:, 0:1],
                                                   op0=ALU.mult, op1=ALU.add), ["ptb_f", "iota_p"], ["ptb_f"])
            sc.op("dve", lambda e: e.tensor_copy(out=idx_i[:], in_=ptb_f[:]), ["ptb_f"], ["idx_i"])
            NPAR = 3
            GP = min(4, NPG)
            TS = []
            for p in range(NPAR):
                T = {}
                T["fpage"] = sbG(f"fpage{p}", [128, 1024]); T["npage"] = sbG(f"npage{p}", [128, 512]); T["lpage"] = sbG(f"lpage{p}", [128, 8])
                T["kb"] = sbG(f"kb{p}", [128, 512], BF16); T["nb"] = sbG(f"nb{p}", [128, 384], BF16)
                TS.append(T)
            GS = []
            for p in range(2):
                Gt = {}
                Gt["stg_k"] = sbG(f"stg_k{p}", [128, 4, GP * 128], BF16); Gt["stg_x"] = sbG(f"stg_x{p}", [128, 3, GP * 128], BF16)
                Gt["stg_l"] = sbG(f"stg_l{p}", [8, GP * 128])
                Gt["vaug"] = sbG(f"vaug{p}", [128, 8, GP, 65], BF16); Gt["vsaug"] = sbG(f"vsaug{p}", [128, 2, GP, 65], BF16)
                sc.op("dve", lambda e, Gt=Gt: e.memset(Gt["vaug"][:], 1.0), [], [f"vaug{p}"])
                sc.op("dve", lambda e, Gt=Gt: e.memset(Gt["vsaug"][:], 1.0), [], [f"vsaug{p}"])
                GS.append(Gt)
            PSG = []
            for p in range(2):
                PSG.append({"kt": psG(f"ps_kt{p}", [128, 8, 128], BF16)[:, 0:4, :], "nt": psG(f"ps_nt{p}", [128, 8, 128], BF16)[:, 0:3, :],
                            "lf": psG(f"ps_lf{p}", [128, 512])})
            pcount = 0
            gcount = 0
            for b in range(NSEQ):
                c = ctx_s[b]
                nm = c.name
                for pg0 in range(0, NPG, GP):
                    gq = gcount % 2
                    gcount += 1
                    Gt = GS[gq]

                    def RG(x, gq=gq):
                        return f"{x}{gq}"
                    for sl in range(GP):
                        pg = pg0 + sl
                        p = pcount % NPAR
                        q = pcount % 2
                        pcount += 1
                        T = TS[p]
                        P = PSG[q]
                        col = b * NPG + pg
                        ts_ = slice(sl * 128, (sl + 1) * 128)

                        def R(x, p=p):
                            return f"{x}{p}"

                        def Q(x, q=q):
                            return f"G{x}{q}"
                        sc.dma("pool", lambda e, col=col, T=T: e.indirect_dma_start(
                            out=T["fpage"][:, :], out_offset=None, in_=pool_fox[:, :],
                            in_offset=bass.IndirectOffsetOnAxis(ap=idx_i[:, col:col + 1], axis=0)), ["idx_i"], [R("fpage")])
                        sc.dma("pool", lambda e, col=col, T=T: e.indirect_dma_start(
                            out=T["npage"][:, :], out_offset=None, in_=pool_nsa[:, :],
                            in_offset=bass.IndirectOffsetOnAxis(ap=idx_i[:, col:col + 1], axis=0)), ["idx_i"], [R("npage")])
                        sc.dma("pool", lambda e, col=col, T=T: e.indirect_dma_start(
                            out=T["lpage"][:, :], out_offset=None, in_=pool_lf[:, :],
                            in_offset=bass.IndirectOffsetOnAxis(ap=idx_i[:, col:col + 1], axis=0)), ["idx_i"], [R("lpage")])
                        sc.op("dve", lambda e, T=T: e.tensor_copy(out=T["kb"][:], in_=T["fpage"][:, 0:512]), [R("fpage")], [R("kb")])
                        for a in range(4):
                            sc.op("pe", lambda e, a=a, T=T, P=P: e.transpose(out=P["kt"][:, a, :], in_=T["kb"][:, a * 128:(a + 1) * 128], identity=ident_b[:]),
                                  [R("kb"), "ident_b"], [Q("kt")])
                        sc.op("act", lambda e, Gt=Gt, P=P, ts_=ts_: e.activation(out=Gt["stg_k"][:, :, ts_], in_=P["kt"][:], func=AF.Copy),
                              [Q("kt")], [RG("stg_k")])
                        sc.op("act", lambda e, T=T, Gt=Gt, sl=sl: e.activation(out=Gt["vaug"][:, :, sl, 0:64],
                                                                              in_=T["fpage"][:, 512:1024].rearrange("p (h d) -> p h d", d=64),
                                                                              func=AF.Copy), [R("fpage")], [RG("vaug")])
                        sc.op("pe", lambda e, T=T, P=P: e.transpose(out=P["lf"][0:8, 0:128], in_=T["lpage"][:, :], identity=ident_f[:]),
                              [R("lpage"), "ident_f"], [Q("lf")])
                        sc.op("dve", lambda e, Gt=Gt, P=P, ts_=ts_: e.tensor_copy(out=Gt["stg_l"][:, ts_], in_=P["lf"][0:8, 0:128]), [Q("lf")], [RG("stg_l")])
                        sc.op("dve", lambda e, T=T: e.tensor_copy(out=T["nb"][:], in_=T["npage"][:, 0:384]), [R("npage")], [R("nb")])
                        for a in range(3):
                            sc.op("pe", lambda e, a=a, T=T, P=P: e.transpose(out=P["nt"][:, a, :], in_=T["nb"][:, a * 128:(a + 1) * 128], identity=ident_b[:]),
                                  [R("nb"), "ident_b"], [Q("nt")])
                        sc.op("act", lambda e, Gt=Gt, P=P, ts_=ts_: e.activation(out=Gt["stg_x"][:, :, ts_], in_=P["nt"][:], func=AF.Copy),
                              [Q("nt")], [RG("stg_x")])
                        sc.op("act", lambda e, T=T, Gt=Gt, sl=sl: e.activation(out=Gt["vsaug"][:, :, sl, 0:64],
                                                                              in_=T["npage"][:, 384:512].rearrange("p (h d) -> p h d", d=64),
                                                                              func=AF.Copy), [R("npage")], [RG("vsaug")])
                    cs = slice(pg0 * 128, (pg0 + GP) * 128)
                    ktv = c.KT[0:64, :, cs].rearrange("p (a two) t -> p a two t", two=2)
                    sc.dma("sp", lambda e, Gt=Gt, ktv=ktv: e.dma_start(out=ktv[:, :, 0, :], in_=Gt["stg_k"][0:64, :, :]), [RG("stg_k")], ["KT_" + nm])
                    sc.dma("sp", lambda e, Gt=Gt, ktv=ktv: e.dma_start(out=ktv[:, :, 1, :], in_=Gt["stg_k"][64:128, :, :]), [RG("stg_k")], ["KT_" + nm])
                    sc.dma("sp", lambda e, Gt=Gt, c=c, pg0=pg0: e.dma_start(out=c.V[:, :, pg0:pg0 + GP, :].rearrange("h p b d -> p h b d"), in_=Gt["vaug"][:]),
                           [RG("vaug")], ["V_" + nm])
                    sc.dma("sp", lambda e, Gt=Gt, c=c, cs=cs: e.dma_start(out=c.LF[:, cs], in_=Gt["stg_l"][:]), [RG("stg_l")], ["LF_" + nm])
                    sc.dma("sp", lambda e, Gt=Gt, c=c, cs=cs: e.dma_start(out=c.XC[:, :, cs].rearrange("k p t -> p k t"), in_=Gt["stg_x"][:, 0:2, :]),
                           [RG("stg_x")], ["XC_" + nm])
                    sc.dma("sp", lambda e, Gt=Gt, c=c, cs=cs: e.dma_start(out=c.KS[0, :, cs], in_=Gt["stg_x"][0:64, 2, :]), [RG("stg_x")], ["KS_" + nm])
                    sc.dma("sp", lambda e, Gt=Gt, c=c, cs=cs: e.dma_start(out=c.KS[1, :, cs], in_=Gt["stg_x"][64:128, 2, :]), [RG("stg_x")], ["KS_" + nm])
                    sc.dma("sp", lambda e, Gt=Gt, c=c, pg0=pg0: e.dma_start(out=c.VS[:, :, pg0:pg0 + GP, :].rearrange("h p b d -> p h b d"), in_=Gt["vsaug"][:]),
                           [RG("vsaug")], ["VS_" + nm])
            sc.flush(st)

        with contextlib.ExitStack() as stC:
            def sbC(name, shape, dt=F32):
                return stC.enter_context(nc.sbuf_tensor(name, list(shape), dt))
            LsM = max(S, LS) // 16
            lf_f = sbC("lf_f", [128, LsM])
            tv_f = sbC("tv_f", [128, LsM])
            pn_f = sbC("pn_f", [128, LsM])
            cum_f = sbC("cum_f", [128, LsM])
            r_f = sbC("r_f", [128, LsM])
            hi_b = sbC("hi_b", [128, LsM], BF16)
            mid_b = sbC("mid_b", [128, LsM], BF16)
            lo_b = sbC("lo_b", [128, LsM], BF16)
            tot = sbC("tot", [128, 1])
            offs = sbC("offs", [128, 1])
            btm = sbC("btm", [128, 128])
            NQM = max(NQ, 8)
            cq = sbC("cq", [8, NQM])
            cqr = sbC("cqr", [8, NQM])
            cq_b = sbC("cq_b", [8, 3, NQM], BF16)
            kwn_f = sbC("kwn_f", [128, max(S, LS) // 128])
            kwn_b = sbC("kwn_b", [128, max(S, LS) // 128], BF16)
            ps_o = stC.enter_context(nc.psum_tensor("ps_o", [128, 8], F32))
            ones_c = sbC("ones_c", [128, LsM], BF16)
            ones_q = sbC("ones_q", [8, 3, NQM], BF16)
            sc.op("dve", lambda e: e.memset(ones_c[:], 1.0), [], ["ones_c"])
            sc.op("dve", lambda e: e.memset(ones_q[:], 1.0), [], ["ones_q"])
            sc.dma("sp", lambda e: e.dma_start(out=btm[:], in_=bt_in[:, :]), [], ["btm"])

            def crows(c, tv_in, pn_in, kwn_in, qsel):
                Ls = c.L // 16
                nm = c.name
                sc.dma("sp", lambda e: e.dma_start(out=lf_f[:, 0:Ls], in_=c.LF.rearrange("h (s t) -> (h s) t", s=16)), ["LF_" + nm], ["lf_f"])
                sc.dma("sp", lambda e: e.dma_start(out=tv_f[:, 0:Ls], in_=tv_in), [], ["tv_f"])
                sc.dma("sp", lambda e: e.dma_start(out=pn_f[:, 0:Ls], in_=pn_in), [], ["pn_f"])
                sc.op("dve", lambda e: e.tensor_tensor(out=lf_f[:, 0:Ls], in0=lf_f[:, 0:Ls], in1=tv_f[:, 0:Ls], op=ALU.mult),
                      ["lf_f", "tv_f"], ["lf_f"])
                sc.op("dve", lambda e: e.memset(r_f[:, 0:Ls], 1.0), [], ["r_f"])
                sc.op("dve", lambda e: e.tensor_tensor_scan(out=cum_f[:, 0:Ls], data0=r_f[:, 0:Ls], data1=lf_f[:, 0:Ls], initial=0.0,
                                                            op0=ALU.mult, op1=ALU.add), ["r_f", "lf_f"], ["cum_f"])
                sc.op("dve", lambda e: e.tensor_copy(out=tot[:], in_=cum_f[:, Ls - 1:Ls]), ["cum_f"], ["tot"])
                sc.op("pe", lambda e: e.matmul(ps_o[:, 0:1], lhsT=btm[:], rhs=tot[:], start=True, stop=True), ["btm", "tot"], ["ps_o"])
                sc.op("dve", lambda e: e.tensor_copy(out=offs[:], in_=ps_o[:, 0:1]), ["ps_o"], ["offs"])
                sc.op("dve", lambda e: e.tensor_scalar(out=cum_f[:, 0:Ls], in0=cum_f[:, 0:Ls], scalar1=offs[:, 0:1], scalar2=None, op0=ALU.add),
                      ["cum_f", "offs"], ["cum_f"])
                sc.dma("sp", lambda e: e.dma_start(out=c.CUM.rearrange("h (s t) -> (h s) t", s=16), in_=cum_f[:, 0:Ls]), ["cum_f"], ["CUM_" + nm])
                sc.op("dve", lambda e: e.scalar_tensor_tensor(out=r_f[:, 0:Ls], in0=cum_f[:, 0:Ls], scalar=-1.0, in1=pn_f[:, 0:Ls],
                                                              op0=ALU.mult, op1=ALU.add), ["cum_f", "pn_f"], ["r_f"])
                sc.op("dve", lambda e: e.tensor_copy(out=hi_b[:, 0:Ls], in_=r_f[:, 0:Ls]), ["r_f"], ["hi_b"])
                sc.op("dve", lambda e: e.tensor_tensor(out=r_f[:, 0:Ls], in0=r_f[:, 0:Ls], in1=hi_b[:, 0:Ls], op=ALU.subtract), ["r_f", "hi_b"], ["r_f"])
                sc.op("dve", lambda e: e.tensor_copy(out=mid_b[:, 0:Ls], in_=r_f[:, 0:Ls]), ["r_f"], ["mid_b"])
                sc.op("dve", lambda e: e.tensor_tensor(out=r_f[:, 0:Ls], in0=r_f[:, 0:Ls], in1=mid_b[:, 0:Ls], op=ALU.subtract), ["r_f", "mid_b"], ["r_f"])
                sc.op("dve", lambda e: e.tensor_copy(out=lo_b[:, 0:Ls], in_=r_f[:, 0:Ls]), ["r_f"], ["lo_b"])
                for i, (t, tn) in enumerate([(hi_b, "hi_b"), (mid_b, "mid_b"), (lo_b, "lo_b")]):
                    sc.dma("sp", lambda e, i=i, t=t: e.dma_start(out=c.KT[67 + i, :, :].rearrange("h (s t) -> (h s) t", s=16), in_=t[:, 0:Ls]),
                           [tn], ["KT_" + nm])
                    sc.dma("sp", lambda e, i=i: e.dma_start(out=c.KT[64 + i, :, :].rearrange("h (s t) -> (h s) t", s=16),
                                                            in_=ones_c[:, 0:Ls]), ["ones_c"], ["KT_" + nm])
                Lb = c.L // 128
                sc.dma("sp", lambda e: e.dma_start(out=kwn_f[:, 0:Lb], in_=kwn_in), [], ["kwn_f"])
                sc.op("dve", lambda e: e.tensor_copy(out=kwn_b[:, 0:Lb], in_=kwn_f[:, 0:Lb]), ["kwn_f"], ["kwn_b"])
                for g in range(2):
                    sc.dma("sp", lambda e, g=g: e.dma_start(out=c.KW[g, 64, :].rearrange("(p t) -> p t", p=128), in_=kwn_b[:, 0:Lb]),
                           ["kwn_b"], ["KW_" + nm])
                nq = c.nqc
                sc.dma("sp", lambda e: e.dma_start(out=qsel(c.CUM)[1], in_=qsel(c.CUM)[0]), ["CUM_" + nm], ["cq"])
                sc.op("dve", lambda e: e.tensor_copy(out=cq_b[:, 0, 0:nq], in_=cq[:, 0:nq]), ["cq"], ["cq_b"])
                sc.op("dve", lambda e: e.tensor_tensor(out=cqr[:, 0:nq], in0=cq[:, 0:nq], in1=cq_b[:, 0, 0:nq], op=ALU.subtract), ["cq", "cq_b"], ["cqr"])
                sc.op("dve", lambda e: e.tensor_copy(out=cq_b[:, 1, 0:nq], in_=cqr[:, 0:nq]), ["cqr"], ["cq_b"])
                sc.op("dve", lambda e: e.tensor_tensor(out=cqr[:, 0:nq], in0=cqr[:, 0:nq], in1=cq_b[:, 1, 0:nq], op=ALU.subtract), ["cqr", "cq_b"], ["cqr"])
                sc.op("dve", lambda e: e.tensor_copy(out=cq_b[:, 2, 0:nq], in_=cqr[:, 0:nq]), ["cqr"], ["cq_b"])
                sc.dma("sp", lambda e: e.dma_start(out=c.QT[64:67, :, :].rearrange("a h n -> h a n"), in_=cq_b[:, :, 0:nq]), ["cq_b"], ["QT_" + nm])
                sc.dma("sp", lambda e: e.dma_start(out=c.QT[67:70, :, :].rearrange("a h n -> h a n"), in_=ones_q[:, :, 0:nq]), ["ones_q"], ["QT_" + nm])
                sc.dma("sp", lambda e: e.dma_start(out=c.QN[64, :, :], in_=ones_q[:, 0, 0:nq]), ["ones_q"], ["QN_" + nm])

            crows(ctx_p, tvp_in[:, :], pnp_in[:, :], kwnp_in[:, :],
                  lambda CUM: (CUM.rearrange("h (j e t) -> h j e t", e=8, t=128)[:, :, 7, :],
                               cq[:, 0:NQ].rearrange("h (j t) -> h j t", t=128)))
            for b in range(NSEQ):
                crows(ctx_s[b], tvs_in[:, :], pns_in[:, :], kwns_in[:, :], lambda CUM: (CUM[:, cfg.PAST:cfg.PAST + 8], cq[:, 0:8]))
            sc.flush(st)

        with contextlib.ExitStack() as stB:
            def sbB(name, shape, dt=F32):
                return stB.enter_context(nc.sbuf_tensor(name, list(shape), dt))
            LM = max(S, LS)
            KTt = [sbB(f"KTt{i}", [70, LM], BF16) for i in range(2)]
            Vt = [sbB(f"Vt{i}", [128, LM // 128, 65], BF16) for i in range(2)]
            QTt = [sbB(f"QTt{i}", [70, max(NQ, 8)], BF16) for i in range(2)]
            pT = [sbB(f"pT{i}", [128, 512], BF16) for i in range(2)]
            rec = sbB("rec", [128, 4])
            oa_res = sbB("oa_res", [128, NOWN, 512], BF16)
            oas8 = sbB("oas8", [8, 512], BF16)
            ps_s = [stB.enter_context(nc.psum_tensor(f"ps_s{i}", [128, 512], F32)) for i in range(2)]
            po = stB.enter_context(nc.psum_tensor("po", [128, 4, 65], F32))

            def attn(KT, KTn, Ka, V, Vn, QT, QTn, q0, w, nsub, blocks, dst, kb=1):
                last = {}
                for (f, jjmin, diag) in blocks:
                    for jj in range(jjmin, nsub):
                        last[jj] = f
                steps = []
                for i in range(0, len(blocks), kb):
                    grp = []
                    coff = 0
                    for (f, jjmin, diag) in blocks[i:i + kb]:
                        grp.append((f, jjmin, diag, coff))
                        coff += (nsub - jjmin) * w
                    steps.append((grp, coff))
                sc.op("pe", lambda e: e.matmul(po[0:w, 0:nsub, :], lhsT=zeros_b[:, 0:w], rhs=zeros_b[:, 0:nsub * 65].rearrange("p (j d) -> p j d", d=65),
                                               start=True, stop=False), ["zeros_b"], ["po"])

                def qk(t):
                    grp, n = steps[t]
                    pst = ps_s[t % 2]
                    pn = f"ps_s{t % 2}"
                    for (f, jjmin, diag, coff) in grp:
                        nb_ = (nsub - jjmin) * w
                        sc.op("pe", lambda e, f=f, jjmin=jjmin, coff=coff, nb_=nb_, diag=diag: e.matmul(
                            pst[:, coff:coff + nb_], lhsT=KT[0:Ka, f * 128:(f + 1) * 128],
                            rhs=QT[0:Ka, q0 + jjmin * w:q0 + nsub * w], start=True, stop=not diag), [KTn, QTn], [pn])
                        if diag:
                            sc.op("pe", lambda e, coff=coff: e.matmul(pst[:, coff:coff + w], lhsT=ident_b[:], rhs=tri_b[:, 0:w], start=False, stop=True),
                                  ["ident_b", "tri_b"], [pn])
                qk(0)
                for t, (grp, n) in enumerate(steps):
                    pst = ps_s[t % 2]
                    sc.op("act", lambda e, pst=pst, t=t, n=n: e.activation(out=pT[t % 2][:, 0:n], in_=pst[:, 0:n], func=AF.Exp),
                          [f"ps_s{t % 2}"], [f"pT{t % 2}"])
                    if t + 1 < len(steps):
                        qk(t + 1)
                    for (f, jjmin, diag, coff) in grp:
                        for jj in range(jjmin, nsub):
                            sc.op("pe", lambda e, t=t, jj=jj, jjmin=jjmin, f=f, coff=coff: e.matmul(
                                po[0:w, jj, :], lhsT=pT[t % 2][:, coff + (jj - jjmin) * w:coff + (jj - jjmin + 1) * w], rhs=V[:, f, :],
                                start=False, stop=(f == last[jj])), [f"pT{t % 2}", Vn], ["po"])
                sc.op("dve", lambda e: e.reciprocal(out=rec[0:w, 0:nsub], in_=po[0:w, 0:nsub, 64]), ["po"], ["rec"])
                sc.op("dve", lambda e: e.tensor_tensor(out=dst, in0=po[0:w, 0:nsub, 0:64],
                                                       in1=rec[0:w, 0:nsub].unsqueeze(2).to_broadcast([w, nsub, 64]), op=ALU.mult),
                      ["po", "rec"], ["attn_dst"])

            nsub = min(4, NOWN)
            hcount = 0

            def load_head(c, src_kt, src_v, src_qt, Ka, h, i):
                sc.dma("sp", lambda e: e.dma_start(out=KTt[i][0:Ka, 0:c.L], in_=src_kt), ["KT_" + c.name, "KS_" + c.name, "KW_" + c.name], [f"KTt{i}"])
                sc.dma("sp", lambda e: e.dma_start(out=Vt[i][:, 0:c.NBk, :], in_=src_v), ["V_" + c.name, "VS_" + c.name, "VW_" + c.name], [f"Vt{i}"])
                sc.dma("sp", lambda e: e.dma_start(out=QTt[i][0:Ka, 0:c.nqc], in_=src_qt), ["QT_" + c.name, "QN_" + c.name], [f"QTt{i}"])

            for h in range(8):
                i = hcount % 2
                hcount += 1
                load_head(ctx_p, ctx_p.KT[:, h, :], ctx_p.V[h], ctx_p.QT[:, h, :], 70, h, i)
                for J in range(NOWN // nsub):
                    Fs = [8 * (J * nsub + jj) + 7 for jj in range(nsub)]
                    blocks = []
                    for f in range(Fs[-1] + 1):
                        jjmin = min(jj for jj in range(nsub) if Fs[jj] >= f)
                        blocks.append((f, jjmin, f == Fs[jjmin]))
                    attn(KTt[i], f"KTt{i}", 70, Vt[i], f"Vt{i}", QTt[i], f"QTt{i}", J * nsub * 128, 128, nsub, blocks,
                         oa_res[:, J * nsub:(J + 1) * nsub, h * 64:(h + 1) * 64])
            for b in range(NSEQ):
                c = ctx_s[b]
                for h in range(8):
                    i = hcount % 2
                    hcount += 1
                    load_head(c, c.KT[:, h, :], c.V[h], c.QT[:, h, :], 70, h, i)
                    blocks = [(f, 0, f == NPG) for f in range(NPG + 1)]
                    attn(KTt[i], f"KTt{i}", 70, Vt[i], f"Vt{i}", QTt[i], f"QTt{i}", 0, 8, 1, blocks,
                         oas8[:, h * 64:(h + 1) * 64].unsqueeze(1), kb=16)
                sc.dma("sp", lambda e, b=b: e.dma_start(out=OAS[b * 8:(b + 1) * 8, :], in_=oas8[:]), ["attn_dst"], ["OAS"])
            for j in range(NOWN):
                sc.dma("sp", lambda e, j=j: e.dma_start(out=OAP[j * 128:(j + 1) * 128, :], in_=oa_res[:, j, :]), ["attn_dst"], ["OAP"])
                sc.dma("sp", lambda e, j=j: e.dma_start(out=dbg_oa_p[j * 128:(j + 1) * 128, :], in_=oa_res[:, j, :]), ["attn_dst"], [])
            sc.dma("sp", lambda e: e.dma_start(out=dbg_oa_s[:, :], in_=OAS[:, :]), ["OAS"], [])
            sc.flush(st)
        if cfg.nsa:
            LPC, OFFC, LP1, OFF1 = 4096, 1856, 768, 128
            BVC = dscr("BVC", [8, 128, LPC], BF16)
            BV1 = dscr("BV1", [8, 128, LP1], BF16)
            with contextlib.ExitStack() as stN:
                def sbN(name, shape, dt=F32):
                    return stN.enter_context(nc.sbuf_tensor("N_" + name, list(shape), dt))

                def psN(name, shape, dt=F32):
                    return stN.enter_context(nc.psum_tensor("N_" + name, list(shape), dt))

                def Kof(F, w):
                    nlast = (128 * F + (w - 1) - 31) // 16
                    tb = nlast // 128
                    return 128 * F - 2048 * tb - 31, tb
                Kvars = []
                for jo in range(NOWN):
                    k_, _ = Kof(8 * jo + 7, 128)
                    if k_ not in Kvars:
                        Kvars.append(k_)
                ks_, _ = Kof(NPG, 8)
                if ks_ not in Kvars:
                    Kvars.append(ks_)
                NV = len(Kvars)
                TCB = sbN("TCB", [128, NV, 8, 128], BF16)
                TSW = sbN("TSW", [128, 5, 8, 128], BF16)
                EEB = dscr("EEB", [128, 8192], BF16)
                gkc = sbN("gkc", [64, 1])
                hbias2 = sbN("hbias2", [128, 2])
                w2b2 = sbN("w2b2", [128, 2, 64], BF16)
                W1B = dscr("W1B", [2, 128, 32, 128], BF16)
                with contextlib.ExitStack() as st0n:
                    def sb0n(name, shape, dt=F32):
                        return st0n.enter_context(nc.sbuf_tensor("N0_" + name, list(shape), dt))
                    rb = sb0n("rb", [33, 8]); rb31 = sb0n("rb31", [32, 8])
                    ohc_sb = sb0n("ohc_sb", [33, LPC]); oh1_sb = sb0n("oh1_sb", [33, LP1])
                    vrow = sb0n("vrow", [8, LPC]); vrow_b = sb0n("vrow_b", [8, LPC], BF16)
                    eestg = sb0n("eestg", [128, 2048])
                    eeb = sb0n("eeb", [128, 2048], BF16)
                    ps_v0 = st0n.enter_context(nc.psum_tensor("N0_ps_v0", [8, 512], F32))
                    sc.dma("sp", lambda e: e.dma_start(out=rb[0:32, :], in_=rel_bias[:, :]), [], ["rb"])
                    sc.dma("sp", lambda e: e.dma_start(out=rb31[:], in_=rel_bias[31:32, :].partition_broadcast(32)), [], ["rb31"])
                    sc.dma("sp", lambda e: e.dma_start(out=ohc_sb[:], in_=ohc_in[:, :]), [], ["ohc_sb"])
                    sc.dma("sp", lambda e: e.dma_start(out=oh1_sb[:], in_=oh1_in[:, :]), [], ["oh1_sb"])
                    sc.dma("sp", lambda e: e.dma_start(out=gkc[:], in_=g_qk_nsa[1:2, :].rearrange("a d -> d a"), allow_slow_non_contiguous=True), [], ["gkc"])
                    sc.op("dve", lambda e: e.tensor_tensor(out=rb[0:32, :], in0=rb[0:32, :], in1=rb31[:], op=ALU.subtract), ["rb", "rb31"], ["rb"])
                    sc.op("dve", lambda e: e.memset(rb[32:33, :], NEG), ["rb"], ["rb"])
                    for q in range(4):
                        sc.dma("sp", lambda e, q=q: e.dma_start(out=eestg[:], in_=ee_in[:, q * 2048:(q + 1) * 2048]), [], ["eestg"])
                        sc.op("pool", lambda e, q=q: e.tensor_copy(out=eeb[:], in_=eestg[:]), ["eestg"], ["eeb"])
                        sc.dma("sp", lambda e, q=q: e.dma_start(out=EEB[:, q * 2048:(q + 1) * 2048], in_=eeb[:]), ["eeb"], ["EEB"])

                    def mk_v(oh_sb, ohn, Lp, BV, bvn):
                        for c0 in range(0, Lp, 512):
                            n = min(512, Lp - c0)
                            sc.op("pe", lambda e, c0=c0, n=n: e.matmul(ps_v0[0:8, 0:n], lhsT=rb[0:33, 0:8], rhs=oh_sb[0:33, c0:c0 + n],
                                                                       start=True, stop=True), ["rb", ohn], ["ps_v0"])
                            sc.op("dve", lambda e, c0=c0, n=n: e.tensor_copy(out=vrow[:, c0:c0 + n], in_=ps_v0[0:8, 0:n]), ["ps_v0"], ["vrow"])
                        sc.op("dve", lambda e: e.tensor_copy(out=vrow_b[:, 0:Lp], in_=vrow[:, 0:Lp]), ["vrow"], ["vrow_b"])
                        for r0 in range(0, 128, 16):
                            sc.dma("sp", lambda e, r0=r0: e.dma_start(out=BV[:, r0:r0 + 16, :], in_=vrow_b[:, 0:Lp].unsqueeze(1).to_broadcast([8, 16, Lp])),
                                   ["vrow_b"], [bvn])
                    w1f = sb0n("w1f", [128, 32, 128]); w1bb = sb0n("w1bb", [128, 32, 128], BF16)
                    pef = sb0n("pef", [32, 64]); pebb = sb0n("pebb", [64, 32], BF16)
                    w2f = sb0n("w2f", [128, 64])
                    ps_w = st0n.enter_context(nc.psum_tensor("N0_ps_w", [128, 512], F32))
                    for kv in range(2):
                        for half in range(2):
                            sc.dma("sp", lambda e, kv=kv, half=half: e.dma_start(
                                out=w1f[half * 64:(half + 1) * 64], in_=w_cmp1[kv].rearrange("(r d) h -> d r h", d=64)), [], ["w1f"])
                        sc.op("pool", lambda e: e.tensor_copy(out=w1bb[:], in_=w1f[:]), ["w1f"], ["w1bb"])
                        sc.dma("sp", lambda e, kv=kv: e.dma_start(out=W1B[kv], in_=w1bb[:]), ["w1bb"], ["W1B"])
                        sc.dma("sp", lambda e, kv=kv: e.dma_start(out=pef[:], in_=pe_cmp[kv]), [], ["pef"])
                        sc.op("pe", lambda e: e.transpose(out=ps_w[0:64, 0:32], in_=pef[0:32, 0:64], identity=ident_f[0:32, 0:32]),
                              ["pef", "ident_f"], ["ps_w"])
                        sc.op("dve", lambda e: e.tensor_copy(out=pebb[:], in_=ps_w[0:64, 0:32]), ["ps_w"], ["pebb"])
                        for r in range(32):
                            sc.op("pe", lambda e, r=r: e.matmul(ps_w[:, 64:65], lhsT=w1bb[0:64, r, :], rhs=pebb[0:64, r:r + 1],
                                                                start=(r == 0), stop=(r == 31)), ["w1bb", "pebb"], ["ps_w"])
                        sc.op("dve", lambda e, kv=kv: e.tensor_copy(out=hbias2[:, kv:kv + 1], in_=ps_w[:, 64:65]), ["ps_w"], ["hbias2"])
                        sc.dma("sp", lambda e, kv=kv: e.dma_start(out=w2f[:], in_=w_cmp2[kv]), [], ["w2f"])
                        sc.op("dve", lambda e, kv=kv: e.tensor_copy(out=w2b2[:, kv, :], in_=w2f[:]), ["w2f"], ["w2b2"])
                    mk_v(ohc_sb, "ohc_sb", LPC, BVC, "BVC")
                    for vi, Kv in enumerate(Kvars):
                        for h in range(8):
                            sc.dma("sp", lambda e, vi=vi, Kv=Kv, h=h: e.dma_start(
                                out=TCB[:, vi, h, :], in_=bass.AP(tensor=BVC.tensor, offset=h * 128 * LPC + Kv + OFFC, ap=[[LPC - 16, 128], [1, 128]])),
                                ["BVC"], ["TCB"])
                    mk_v(oh1_sb, "oh1_sb", LP1, BV1, "BV1")
                    for di in range(5):
                        for h in range(8):
                            sc.dma("sp", lambda e, di=di, h=h: e.dma_start(
                                out=TSW[:, di, h, :], in_=bass.AP(tensor=BV1.tensor, offset=h * 128 * LP1 + di * 128 + OFF1, ap=[[LP1 - 1, 128], [1, 128]])),
                                ["BV1"], ["TSW"])
                    sc.flush(st)

                LM = max(S, LS)
                NTM = max(cfg.NTp, cfg.NTs)
                NMM = max(cfg.NMp, cfg.NMs)
                KC = sbN("KC", [65, 2, NTM * 128], BF16)
                VCW = sbN("VCW", [128, 2, NTM, 65 + NMM], BF16)
                att = {}
                KWt = sbN("KWt", [65, 5, 128], BF16)
                VWt = sbN("VWt", [128, 5, 65], BF16)
                ob_res = sbN("ob_res", [128, NOWN, 512], BF16)
                obs8 = sbN("obs8", [8, 512], BF16)
                obg = sbN("obg", [128, 4, 64])
                imp = sbN("imp", [128, NMM]); addt = sbN("addt", [128, NMM]); wk = sbN("wk", [128, NMM])
                mk = sbN("mk", [128, NMM]); mk2 = sbN("mk2", [128, NMM])
                negb = sbN("negb", [128, NMM], BF16)
                neg4 = sbN("neg4", [128, NMM // 128, 4, 128], BF16)
                m8a = sbN("m8a", [128, 8]); m8b = sbN("m8b", [128, 8])
                rz = sbN("rz", [128, 4]); coef = sbN("coef", [128, 4])
                pTn = [sbN(f"pTn{i}", [128, 512], BF16) for i in range(2)]
                ps_s = [psN(f"ps_s{i}", [128, 512]) for i in range(2)]
                po_c = psN("po_c", [128, 65 + NMM])
                po_s = psN("po_s", [128, 4, 65])
                po_w = psN("po_w", [128, 4, 65])
                ps_t = psN("ps_t", [128, NMM // 128, 128], BF16)
                step = {"t": 0}

                def compress(c, Lc, NT, NM, wc_in, cneg_in):
                    nblk = Lc // 16 - 1
                    nm = c.name
                    with contextlib.ExitStack() as stc:
                        def sbc(name, shape, dt=F32):
                            return stc.enter_context(nc.sbuf_tensor(f"C{nm}_" + name, list(shape), dt))
                        xct = sbc("xct", [128, c.L], BF16)
                        w1b = sbc("w1b", [128, 32, 128], BF16)
                        xs_ = sbc("xs", [128, 512]); x2 = sbc("x2", [128, 512]); sg = sbc("sg", [128, 512])
                        hidT = sbc("hidT", [128, 512], BF16)
                        sqb = sbc("sqb", [64, 512], BF16)
                        rs_ = sbc("rs", [64, 512]); rr = sbc("rr", [64, 512])
                        nh64 = sbc("nh64", [64, 512])
                        wstg_ = sbc("wstg", [128, NM])
                        cn_f = sbc("cn_f", [65, NT * 128])
                        ps_h = ps_s[0]
                        ps_k = ps_s[1]
                        ps_q = stc.enter_context(nc.psum_tensor(f"C{nm}_ps_q", [128, 512], F32))
                        sc.op("dve", lambda e: e.memset(nh64[:], -0.5), [], ["nh64"])
                        sc.op("dve", lambda e: e.memset(KC[:], 0.0), ["KC"], ["KC"])
                        sc.op("dve", lambda e: e.memset(VCW[:], 0.0), ["VCW"], ["VCW"])
                        sc.op("dve", lambda e: e.memset(VCW[:, :, :, 64:65], 1.0), ["VCW"], ["VCW"])
                        sc.dma("sp", lambda e: e.dma_start(out=cn_f[64:65, :], in_=cneg_in), [], ["cn_f"])
                        for g in range(2):
                            sc.op("dve", lambda e, g=g: e.tensor_copy(out=KC[64:65, g, 0:NT * 128], in_=cn_f[64:65, :]), ["cn_f", "KC"], ["KC"])
                        for t in range(NT):
                            sc.dma("sp", lambda e, t=t: e.dma_start(out=wstg_[:], in_=wc_in[t * 128:(t + 1) * 128, :]), [], ["wstg_"])
                            for g in range(2):
                                sc.op("dve", lambda e, t=t, g=g: e.tensor_copy(out=VCW[:, g, t, 65:65 + NM], in_=wstg_[:]), ["wstg_", "VCW"], ["VCW"])
                        for kv in range(2):
                            sc.dma("sp", lambda e, kv=kv: e.dma_start(out=xct[:], in_=c.XC[kv]), ["XC_" + nm], ["xct"])
                            sc.dma("sp", lambda e, kv=kv: e.dma_start(out=w1b[:], in_=W1B[kv]), ["W1B"], ["w1b"])
                            hbias = hbias2[:, kv:kv + 1]
                            w2b = w2b2[:, kv, :]
                            xv = xct[:, 0:Lc].rearrange("p (n s) -> p n s", s=16)
                            for g in range(2):
                                for n0 in range(0, nblk, 512):
                                    nn = min(512, nblk - n0)
                                    for r in range(32):
                                        sc.op("pe", lambda e, r=r, g=g, n0=n0, nn=nn: e.matmul(
                                            ps_h[:, 0:nn], lhsT=w1b[g * 64:(g + 1) * 64, r, :],
                                            rhs=xv[g * 64:(g + 1) * 64, n0 + r // 16:n0 + r // 16 + nn, r % 16],
                                            start=(r == 0), stop=(r == 31)), ["w1b", "xct"], ["ps_h"])
                                    sc.op("act", lambda e, nn=nn, hbias=hbias: e.activation(out=xs_[:, 0:nn], in_=ps_h[:, 0:nn], func=AF.Identity, bias=hbias),
                                          ["ps_h", "hbias2"], ["xs"])
                                    sc.op("dve", lambda e, nn=nn: e.tensor_tensor(out=x2[:, 0:nn], in0=xs_[:, 0:nn], in1=xs_[:, 0:nn], op=ALU.mult), ["xs"], ["x2"])
                                    sc.op("dve", lambda e, nn=nn: e.tensor_scalar(out=x2[:, 0:nn], in0=x2[:, 0:nn], scalar1=0.044715, scalar2=1.0,
                                                                                 op0=ALU.mult, op1=ALU.add), ["x2"], ["x2"])
                                    sc.op("dve", lambda e, nn=nn: e.tensor_tensor(out=x2[:, 0:nn], in0=x2[:, 0:nn], in1=xs_[:, 0:nn], op=ALU.mult), ["x2", "xs"], ["x2"])
                                    sc.op("act", lambda e, nn=nn: e.activation(out=sg[:, 0:nn], in_=x2[:, 0:nn], func=AF.Sigmoid, scale=1.5957691216057308),
                                          ["x2"], ["sg"])
                                    sc.op("dve", lambda e, nn=nn: e.tensor_tensor(out=hidT[:, 0:nn], in0=xs_[:, 0:nn], in1=sg[:, 0:nn], op=ALU.mult),
                                          ["xs", "sg"], ["hidT"])
                                    if kv == 0:
                                        sc.op("pe", lambda e, nn=nn, w2b=w2b: e.matmul(ps_k[0:64, 0:nn], lhsT=w2b, rhs=hidT[:, 0:nn], start=True, stop=True),
                                              ["w2b2", "hidT"], ["ps_k"])
                                        sc.op("act", lambda e, nn=nn: e.activation(out=sqb[:, 0:nn], in_=ps_k[0:64, 0:nn], func=AF.Square), ["ps_k"], ["sqb"])
                                        sc.op("pe", lambda e, nn=nn: e.matmul(ps_q[0:64, 0:nn], lhsT=ones_b[0:64, 0:64], rhs=sqb[:, 0:nn], start=True, stop=True),
                                              ["ones_b", "sqb"], ["ps_q"])
                                        sc.op("dve", lambda e, nn=nn: e.tensor_scalar(out=rs_[:, 0:nn], in0=ps_q[0:64, 0:nn], scalar1=1.0 / 64, scalar2=EPS,
                                                                                     op0=ALU.mult, op1=ALU.add), ["ps_q"], ["rs"])
                                        sc.op("act", lambda e, nn=nn: e.activation(out=rs_[:, 0:nn], in_=rs_[:, 0:nn], func=AF.Sqrt), ["rs"], ["rs"])
                                        sc.op("dve", lambda e, nn=nn: e.reciprocal(out=rr[:, 0:nn], in_=rs_[:, 0:nn]), ["rs"], ["rr"])
                                        sc.op("dve", lambda e, nn=nn: e.tensor_tensor(out=rr[:, 0:nn], in0=rr[:, 0:nn], in1=ps_k[0:64, 0:nn], op=ALU.mult),
                                              ["rr", "ps_k"], ["rr"])
                                        sc.op("dve", lambda e, nn=nn, g=g, n0=n0: e.tensor_scalar(out=KC[0:64, g, n0:n0 + nn], in0=rr[:, 0:nn], scalar1=gkc[:, 0:1],
                                                                                                scalar2=None, op0=ALU.mult), ["rr", "gkc", "KC"], ["KC"])
                                    else:
                                        for sub in range(0, nn, 128):
                                            ns = min(128, nn - sub)
                                            sc.op("pe", lambda e, sub=sub, ns=ns, w2b=w2b: e.matmul(ps_k[0:ns, 0:64], lhsT=hidT[:, sub:sub + ns], rhs=w2b,
                                                                                           start=True, stop=True), ["hidT", "w2b2"], ["ps_k"])
                                            sc.op("act", lambda e, sub=sub, ns=ns, g=g, n0=n0: e.activation(
                                                out=VCW[0:ns, g, (n0 + sub) // 128, 0:64], in_=ps_k[0:ns, 0:64], func=AF.Copy), ["ps_k", "VCW"], ["VCW"])
                        sc.flush(st)

                pend = []

                def branch_block(KTap, Ka, rhs_q, extra, Vap, po, w, first, last, Vn, Ktn):
                    pend.append((KTap, rhs_q, extra, Vap, po, w, last, Vn, Ktn))

                def run_pending():
                    base = step["t"]

                    def qk(i):
                        KTap, rhs_q, extra, Vap, po, w, last, Vn, Ktn = pend[i]
                        tt = base + i
                        pst, pn = ps_s[tt % 2], f"ps_s{tt % 2}"
                        sc.op("pe", lambda e: e.matmul(pst[:, 0:4 * w].rearrange("p (j q) -> p j q", j=4), lhsT=KTap, rhs=rhs_q,
                                                       start=True, stop=(len(extra) == 0)), [Ktn, "QNg"], [pn])
                        for k_, (l_, r_, rn) in enumerate(extra):
                            sc.op("pe", lambda e, l_=l_, r_=r_, k_=k_: e.matmul(pst[:, 0:4 * w].rearrange("p (j q) -> p j q", j=4), lhsT=l_, rhs=r_,
                                                                               start=False, stop=(k_ == len(extra) - 1)), rn, [pn])
                    if pend:
                        qk(0)
                    for i in range(len(pend)):
                        KTap, rhs_q, extra, Vap, po, w, last, Vn, Ktn = pend[i]
                        tt = base + i
                        pst, pn = ps_s[tt % 2], f"ps_s{tt % 2}"
                        pt, ptn = pTn[tt % 2], f"pTn{tt % 2}"
                        sc.op("act", lambda e, pt=pt, pst=pst, w=w: e.activation(out=pt[:, 0:4 * w], in_=pst[:, 0:4 * w], func=AF.Exp), [pn], [ptn])
                        if i + 1 < len(pend):
                            qk(i + 1)
                        for j in range(4):
                            sc.op("pe", lambda e, j=j, pt=pt, po=po, w=w, Vap=Vap, last=last: e.matmul(
                                po[0:w, j, :], lhsT=pt[:, j * w:(j + 1) * w], rhs=Vap, start=False, stop=last), [ptn, Vn], ["po_sw"])
                    step["t"] += len(pend)
                    pend.clear()

                def nsa_qblock(c, g, F, w, q0, gb_ap, gbn, add_ap, NM, dst):
                    nch = NM // 128
                    KSg, VSg, QNg, EE = att["KSg"], att["VSg"], att["QNg"], att["EE"]
                    K_, tb = Kof(F, w)
                    vi = Kvars.index(K_)
                    for j in range(4):
                        h = 4 * g + j
                        for t in range(tb + 1):
                            tt = step["t"]
                            step["t"] += 1
                            pst, pn = ps_s[tt % 2], f"ps_s{tt % 2}"
                            pt, ptn = pTn[tt % 2], f"pTn{tt % 2}"
                            sc.op("pe", lambda e, t=t, j=j, pst=pst: e.matmul(pst[:, 0:w], lhsT=KC[0:65, g, t * 128:(t + 1) * 128],
                                                                           rhs=QNg[0:65, j, q0:q0 + w], start=True, stop=(t != tb)), ["KC", "QNg"], [pn])
                            if t == tb:
                                sc.op("pe", lambda e, pst=pst, h=h: e.matmul(pst[:, 0:w], lhsT=ident_b[:], rhs=TCB[:, vi, h, 0:w], start=False, stop=True),
                                      ["ident_b", "TCB"], [pn])
                            sc.op("act", lambda e, pst=pst, pt=pt: e.activation(out=pt[:, 0:w], in_=pst[:, 0:w], func=AF.Exp), [pn], [ptn])
                            sc.op("pe", lambda e, t=t, pt=pt: e.matmul(po_c[0:w, 0:65 + NM], lhsT=pt[:, 0:w], rhs=VCW[:, g, t, 0:65 + NM],
                                                                     start=(t == 0), stop=(t == tb)), [ptn, "VCW"], ["po_c"])
                        sc.op("dve", lambda e, j=j: e.tensor_scalar(out=rz[0:w, j:j + 1], in0=po_c[0:w, 64:65], scalar1=1e-30, scalar2=None, op0=ALU.max),
                              ["po_c"], ["rz"])
                        sc.op("dve", lambda e, j=j: e.reciprocal(out=rz[0:w, j:j + 1], in_=rz[0:w, j:j + 1]), ["rz"], ["rz"])
                        sc.op("dve", lambda e, j=j, h=h: e.tensor_tensor(out=coef[0:w, j:j + 1], in0=rz[0:w, j:j + 1], in1=gb_ap[:, 3 * h:3 * h + 1], op=ALU.mult),
                              ["rz", gbn], ["coef"])
                        sc.op("dve", lambda e, j=j: e.tensor_scalar(out=obg[0:w, j, :], in0=po_c[0:w, 0:64], scalar1=coef[0:w, j:j + 1], scalar2=None,
                                                                    op0=ALU.mult), ["po_c", "coef"], ["obg"])
                        if j == 0:
                            sc.op("dve", lambda e, j=j: e.tensor_scalar(out=imp[0:w, 0:NM], in0=po_c[0:w, 65:65 + NM], scalar1=rz[0:w, j:j + 1], scalar2=None,
                                                                        op0=ALU.mult), ["po_c", "rz"], ["imp"])
                        else:
                            sc.op("dve", lambda e, j=j: e.scalar_tensor_tensor(out=imp[0:w, 0:NM], in0=po_c[0:w, 65:65 + NM], scalar=rz[0:w, j:j + 1],
                                                                               in1=imp[0:w, 0:NM], op0=ALU.mult, op1=ALU.add), ["po_c", "rz", "imp"], ["imp"])
                    for po in (po_s, po_w):
                        sc.op("pe", lambda e, po=po: e.matmul(po[0:w, :, :], lhsT=zeros_b[:, 0:w], rhs=zeros_b[:, 0:260].rearrange("p (j d) -> p j d", d=65),
                                                              start=True, stop=False), ["zeros_b"], ["po_sw"])
                    f0 = max(0, F - 4)
                    nwb = F - f0 + 1
                    sc.dma("sp", lambda e: e.dma_start(out=KWt[:, 0:nwb, :], in_=c.KW[g][:, f0 * 128:(F + 1) * 128].rearrange("p (b t) -> p b t", t=128)),
                           ["KW_" + c.name], ["KWt"])
                    sc.dma("sp", lambda e: e.dma_start(out=VWt[:, 0:nwb, :], in_=c.VW[g][:, f0:F + 1, :]), ["VW_" + c.name], ["VWt"])
                    rqa = QNg[0:65, :, q0:q0 + w]
                    for f in range(f0, F + 1):
                        extra = [(ident_b[:], TSW[:, F - f, 4 * g:4 * g + 4, 0:w], ["ident_b", "TSW"])]
                        branch_block(KWt[0:65, f - f0, :], 65, rqa, extra, VWt[:, f - f0, :], po_w, w, f == f0, f == F, "VWt", "KWt")
                    run_pending()
                    sc.dma("sp", lambda e: e.dma_start(out=addt[0:w, 0:NM], in_=add_ap), [], ["addt"])
                    sc.op("dve", lambda e: e.tensor_tensor(out=imp[0:w, 0:NM], in0=imp[0:w, 0:NM], in1=addt[0:w, 0:NM], op=ALU.add), ["imp", "addt"], ["imp"])
                    sc.op("dve", lambda e: e.max(out=m8a[0:w, :], in_=imp[0:w, 0:NM]), ["imp"], ["m8a"])
                    sc.op("dve", lambda e: e.match_replace(out=wk[0:w, 0:NM], in_to_replace=m8a[0:w, :], in_values=imp[0:w, 0:NM], imm_value=-1e30),
                          ["imp", "m8a"], ["wk"])
                    sc.op("dve", lambda e: e.max(out=m8b[0:w, :], in_=wk[0:w, 0:NM]), ["wk"], ["m8b"])
                    sc.op("dve", lambda e: e.tensor_scalar(out=mk[0:w, 0:NM], in0=imp[0:w, 0:NM], scalar1=m8b[0:w, 7:8], scalar2=None, op0=ALU.is_ge),
                          ["imp", "m8b"], ["mk"])
                    sc.op("dve", lambda e: e.tensor_scalar(out=mk2[0:w, 0:NM], in0=imp[0:w, 0:NM], scalar1=-1e29, scalar2=None, op0=ALU.is_gt),
                          ["imp"], ["mk2"])
                    sc.op("dve", lambda e: e.tensor_tensor(out=mk[0:w, 0:NM], in0=mk[0:w, 0:NM], in1=mk2[0:w, 0:NM], op=ALU.mult), ["mk", "mk2"], ["mk"])
                    sc.op("dve", lambda e: e.tensor_scalar(out=negb[0:w, 0:NM], in0=mk[0:w, 0:NM], scalar1=-NEG, scalar2=NEG, op0=ALU.mult, op1=ALU.add),
                          ["mk"], ["negb"])
                    for ch in range(nch):
                        sc.op("pe", lambda e, ch=ch: e.transpose(out=ps_t[:, ch, 0:w], in_=negb[0:w, ch * 128:(ch + 1) * 128], identity=ident_b[0:w, 0:w]),
                              ["negb", "ident_b"], ["ps_t"])
                    for j in range(4):
                        sc.op("act", lambda e, j=j: e.activation(out=neg4[:, 0:nch, j, 0:w], in_=ps_t[:, 0:nch, 0:w], func=AF.Copy), ["ps_t"], ["neg4"])
                    rq = QNg[0:64, :, q0:q0 + w]
                    for f in range(F + 1):
                        extra = [(EE[:, (f % 64) * 128:(f % 64 + 1) * 128], neg4[:, f // 64, :, 0:w], ["EE", "neg4"])]
                        if f == F:
                            extra.append((ident_b[:], TSW[:, 0, 4 * g:4 * g + 4, 0:w], ["ident_b", "TSW"]))
                        elif f == F - 1:
                            extra.append((ident_b[:], TSW[:, 1, 4 * g:4 * g + 4, 0:w], ["ident_b", "TSW"]))
                        branch_block(KSg[0:64, f * 128:(f + 1) * 128], 64, rq, extra, VSg[:, f, :], po_s, w, f == 0, f == F, "VSg", "KSg")
                    run_pending()
                    for (po, gi) in ((po_s, 1), (po_w, 2)):
                        sc.op("dve", lambda e, po=po: e.tensor_scalar(out=rz[0:w, 0:4], in0=po[0:w, :, 64], scalar1=1e-30, scalar2=None, op0=ALU.max),
                              ["po_sw"], ["rz"])
                        sc.op("dve", lambda e: e.reciprocal(out=rz[0:w, 0:4], in_=rz[0:w, 0:4]), ["rz"], ["rz"])
                        sc.op("dve", lambda e, gi=gi: e.tensor_tensor(out=coef[0:w, 0:4], in0=rz[0:w, 0:4],
                                                                      in1=gb_ap[:, 12 * g:12 * g + 12].rearrange("p (j i) -> p j i", i=3)[:, :, gi], op=ALU.mult),
                              ["rz", gbn], ["coef"])
                        for j in range(4):
                            sc.op("dve", lambda e, j=j, po=po: e.scalar_tensor_tensor(out=obg[0:w, j, :], in0=po[0:w, j, 0:64], scalar=coef[0:w, j:j + 1],
                                                                                     in1=obg[0:w, j, :], op0=ALU.mult, op1=ALU.add),
                                  ["po_sw", "coef", "obg"], ["obg"])
                    sc.op("dve", lambda e: e.tensor_copy(out=dst, in_=obg[0:w, :, :].rearrange("p j d -> p (j d)")), ["obg"], ["ob_dst"])

                def nsa_ctx(c, Lc, NT, NM, wc_in, cneg_in, qblocks):
                    compress(c, Lc, NT, NM, wc_in, cneg_in)
                    with contextlib.ExitStack() as st2:
                        KSg = st2.enter_context(nc.sbuf_tensor(f"A{c.name}_KSg", [64, c.L], BF16))
                        VSg = st2.enter_context(nc.sbuf_tensor(f"A{c.name}_VSg", [128, c.NBk, 65], BF16))
                        QNg = st2.enter_context(nc.sbuf_tensor(f"A{c.name}_QNg", [65, 4, c.nqc], BF16))
                        EE = st2.enter_context(nc.sbuf_tensor(f"A{c.name}_EE", [128, 8192], BF16))
                        att["KSg"], att["VSg"], att["QNg"], att["EE"] = KSg, VSg, QNg, EE
                        sc.dma("sp", lambda e: e.dma_start(out=EE[:], in_=EEB[:, :]), ["EEB"], ["EE"])
                        for g in range(2):
                            sc.dma("sp", lambda e, g=g: e.dma_start(out=KSg[:, 0:c.L], in_=c.KS[g]), ["KS_" + c.name], ["KSg"])
                            sc.dma("sp", lambda e, g=g: e.dma_start(out=VSg[:, 0:c.NBk, :], in_=c.VS[g]), ["VS_" + c.name], ["VSg"])
                            sc.dma("sp", lambda e, g=g: e.dma_start(out=QNg[:, :, 0:c.nqc], in_=c.QN[:, 4 * g:4 * g + 4, :]), ["QN_" + c.name], ["QNg"])
                            for qb in qblocks:
                                qb(c, g)
                        sc.flush(st)

                qbl = []
                for jo in range(NOWN):
                    qbl.append(lambda c, g, jo=jo: nsa_qblock(c, g, 8 * jo + 7, 128, jo * 128, gb_res[:, jo, :], "gb_res",
                                                              addp_in[jo, :, :], cfg.NMp, ob_res[:, jo, g * 256:(g + 1) * 256]))
                nsa_ctx(ctx_p, S, cfg.NTp, cfg.NMp, wcp_in, cnegp_in[:, :], qbl)
                for j in range(NOWN):
                    sc.dma("sp", lambda e, j=j: e.dma_start(out=OBP[j * 128:(j + 1) * 128, :], in_=ob_res[:, j, :]), ["ob_dst"], ["OBP"])
                    sc.dma("sp", lambda e, j=j: e.dma_start(out=dbg_ob_p[j * 128:(j + 1) * 128, :], in_=ob_res[:, j, :]), ["ob_dst"], [])
                for b in range(NSEQ):
                    qbl = [lambda c, g, b=b: nsa_qblock(c, g, NPG, 8, 0, gbs_res[:, b, :], "gbs_res", adds_in[:, :], cfg.NMs,
                                                        obs8[:, g * 256:(g + 1) * 256])]
                    nsa_ctx(ctx_s[b], cfg.PAST, cfg.NTs, cfg.NMs, wcs_in, cnegs_in[:, :], qbl)
                    sc.dma("sp", lambda e, b=b: e.dma_start(out=OBS[b * 8:(b + 1) * 8, :], in_=obs8[:]), ["ob_dst"], ["OBS"])
                sc.dma("sp", lambda e: e.dma_start(out=dbg_ob_s[:, :], in_=OBS[:, :]), ["OBS"], [])
                sc.flush(st)
        if cfg.dbg_ob:
            with contextlib.ExitStack() as stX:
                obf = stX.enter_context(nc.sbuf_tensor("obf", [128, 512], F32))
                obb = stX.enter_context(nc.sbuf_tensor("obb", [128, 512], BF16))
                for j in range(NOWN):
                    sc.dma("sp", lambda e, j=j: e.dma_start(out=obf[:], in_=dbg_ob_p_in[j * 128:(j + 1) * 128, :]), [], ["obf"])
                    sc.op("dve", lambda e, j=j: e.tensor_copy(out=obb[:], in_=obf[:]), ["obf"], ["obb"])
                    sc.dma("sp", lambda e, j=j: e.dma_start(out=OBP[j * 128:(j + 1) * 128, :], in_=obb[:]), ["obb"], ["OBP"])
                sc.dma("sp", lambda e: e.dma_start(out=obf[0:NSR, :], in_=dbg_ob_s_in[:, :]), [], ["obf"])
                sc.op("dve", lambda e: e.tensor_copy(out=obb[0:NSR, :], in_=obf[0:NSR, :]), ["obf"], ["obb"])
                sc.dma("sp", lambda e: e.dma_start(out=OBS[:, :], in_=obb[0:NSR, :]), ["obb"], ["OBS"])
                sc.flush(st)
        Y1 = dscr("Y1", [NQ + NSR, D], F32)

        def cast_weight(dst, dname, src_rows_fn, nchunk, ncol, stgs):
            cnt = 0
            for k in range(nchunk):
                for c0 in range(0, ncol, 2048):
                    n = min(2048, ncol - c0)
                    stg, sn = stgs[cnt % 2]
                    cnt += 1
                    sc.dma("sp", lambda e, k=k, c0=c0, n=n, stg=stg: e.dma_start(out=stg[:, 0:n], in_=src_rows_fn(k)[:, c0:c0 + n]), [], [sn])
                    sc.op("pool", lambda e, k=k, c0=c0, n=n, stg=stg: e.tensor_copy(out=dst[:, k, c0:c0 + n], in_=stg[:, 0:n]), [sn], [dname])

        tiles = [("p", j, 128) for j in range(NOWN)] + [("s", b, 8) for b in range(NSEQ)]

        with contextlib.ExitStack() as stD:
            def sbD(name, shape, dt=F32):
                return stD.enter_context(nc.sbuf_tensor("D_" + name, list(shape), dt))

            def psD(name, shape, dt=F32):
                return stD.enter_context(nc.psum_tensor("D_" + name, list(shape), dt))
            wstg = [(sbD(f"wstg{i}", [128, 2048]), f"wstg{i}") for i in range(2)]
            w_zm = sbD("w_zm", [128, 8, 2048], BF16)
            wof = sbD("wof", [128, 4, 1024], BF16)
            won = sbD("won", [128, 4, 1024], BF16)
            wo = sbD("wo", [128, 8, 1024], BF16)
            Mp = sbD("Mp", [128, 3, D])
            xt = sbD("xt", [128, D])
            junk = sbD("junk", [128, D], BF16)
            tmpf = sbD("tmpf", [128, D])
            hb = sbD("hb", [128, D], BF16)
            hT = sbD("hT", [128, 8, 128], BF16)
            ssum = sbD("ssum", [128, 1]); rstd = sbD("rstd", [128, 1])
            gmt = sbD("gmt", [128, 2048], BF16)
            oat = sbD("oat", [128, 512], BF16); obt = sbD("obt", [128, 512], BF16)
            oaT = sbD("oaT", [128, 4, 128], BF16); obT = sbD("obT", [128, 4, 128], BF16)
            mt = sbD("mt", [128, D]); mb = sbD("mb", [128, D], BF16)
            mT = sbD("mT", [128, 8, 128], BF16)
            y1t = sbD("y1t", [128, D])
            ps_tr = psD("ps_tr", [128, 1024], BF16)
            ps_a = psD("ps_a", [128, 512]); ps_b = psD("ps_b", [128, 512])
            ps_c = psD("ps_c", [128, 512]); ps_d = psD("ps_d", [128, 512])
            cast_weight(w_zm, "w_zm", lambda k: w_in[k * 128:(k + 1) * 128, O_ZM:O_ZM + 2048], 8, 2048, wstg)
            cast_weight(wof, "wof", lambda k: w_out_fox[k * 128:(k + 1) * 128, :], 4, 1024, wstg)
            cast_weight(won, "won", lambda k: w_out_nsa[k * 128:(k + 1) * 128, :], 4, 1024, wstg)
            cast_weight(wo, "wo", lambda k: w_out[k * 128:(k + 1) * 128, :], 8, 1024, wstg)

            def norm_T(src, srcn, rows, G, B):
                sc.op("act", lambda e: e.activation(out=junk[0:rows, :], in_=src, func=AF.Square, accum_out=ssum[0:rows, :]),
                      [srcn], ["junk", "ssum"])
                sc.op("dve", lambda e: e.tensor_scalar(out=ssum[0:rows, :], in0=ssum[0:rows, :], scalar1=1.0 / D, scalar2=EPS,
                                                       op0=ALU.mult, op1=ALU.add), ["ssum"], ["ssum"])
                sc.op("pool", lambda e: e.tensor_tensor(out=rstd[0:rows, :], in0=ssum[0:rows, :], in1=neghalf[0:rows, 0:1], op=ALU.pow),
                      ["ssum", "neghalf"], ["rstd"])
                sc.op("dve", lambda e: e.scalar_tensor_tensor(out=tmpf[0:rows, :], in0=src, scalar=rstd[0:rows, :], in1=G,
                                                              op0=ALU.mult, op1=ALU.mult), [srcn, "rstd", "Mp"], ["tmpf"])
                sc.op("dve", lambda e: e.tensor_tensor(out=hb[0:rows, :], in0=tmpf[0:rows, :], in1=B, op=ALU.add), ["tmpf", "Mp"], ["hb"])
                trans8(hb, "hb", rows, hT, "hT")

            def trans8(src, srcn, rows, dstT, dstn, nchunk=8, c_off=0):
                for k in range(nchunk):
                    sc.op("pe", lambda e, k=k: e.transpose(out=ps_tr[:, k * 128:k * 128 + rows], in_=src[0:rows, (c_off + k) * 128:(c_off + k + 1) * 128],
                                                           identity=ident_b[0:rows, 0:rows]), [srcn, "ident_b"], ["ps_tr"])
                sc.op("act", lambda e: e.activation(out=dstT[:, 0:nchunk, 0:rows],
                                                    in_=ps_tr[:].rearrange("p (k t) -> p k t", t=128)[:, 0:nchunk, 0:rows], func=AF.Copy),
                      ["ps_tr"], [dstn])

            kind_state = {"k": None}

            def do_tileD(kind, idx, rows):
                kind_loaded = kind_state["k"]
                if kind == "p":
                    x_ap = xf[(8 * idx + 7) * 128:(8 * idx + 8) * 128, :]
                    y1_ap = Y1[idx * 128:(idx + 1) * 128, :]
                    oa_ap, ob_ap = OAP[idx * 128:(idx + 1) * 128, :], OBP[idx * 128:(idx + 1) * 128, :]
                    if kind_loaded != "p":
                        load_mod(Mp, 128, 0, (1, 0, 2), 0, ps_a, "ps_a")
                        kind_state["k"] = "p"
                else:
                    x_ap = xs[idx * 8:(idx + 1) * 8, :]
                    y1_ap = Y1[NQ + idx * 8:NQ + (idx + 1) * 8, :]
                    oa_ap, ob_ap = OAS[idx * 8:(idx + 1) * 8, :], OBS[idx * 8:(idx + 1) * 8, :]
                    load_mod(Mp, 8, 128 + 8 * idx, (1, 0, 2), 0, ps_a, "ps_a")
                    kind_state["k"] = "s"
                sc.dma("sp", lambda e, x_ap=x_ap, rows=rows: e.dma_start(out=xt[0:rows, :], in_=x_ap), [], ["xt"])
                norm_T(xt[0:rows, :], "xt", rows, Mp[0:rows, 0, :], Mp[0:rows, 1, :])
                for g4 in range(4):
                    pst, pn = [(ps_a, "ps_a"), (ps_b, "ps_b")][g4 % 2]
                    for k in range(8):
                        sc.op("pe", lambda e, k=k, g4=g4, pst=pst: e.matmul(pst[0:rows, :], lhsT=hT[:, k, 0:rows], rhs=w_zm[:, k, g4 * 512:(g4 + 1) * 512],
                                                                        start=(k == 0), stop=(k == 7)), ["hT", "w_zm"], [pn])
                    sc.op("act", lambda e, g4=g4, pst=pst: e.activation(out=gmt[0:rows, g4 * 512:(g4 + 1) * 512], in_=pst[0:rows, :], func=AF.Sigmoid),
                          [pn], ["gmt"])
                sc.dma("sp", lambda e, oa_ap=oa_ap, rows=rows: e.dma_start(out=oat[0:rows, :], in_=oa_ap), ["OAS", "OAP"], ["oat"])
                sc.dma("sp", lambda e, ob_ap=ob_ap, rows=rows: e.dma_start(out=obt[0:rows, :], in_=ob_ap), ["OBS", "OBP"], ["obt"])
                oa_src, oa_n, ob_src, ob_n = oat, "oat", obt, "obt"
                trans8(oa_src, oa_n, rows, oaT, "oaT", 4)
                trans8(ob_src, ob_n, rows, obT, "obT", 4)
                for half in range(2):
                    hs_ = slice(half * 512, (half + 1) * 512)
                    for k in range(4):
                        sc.op("pe", lambda e, k=k, hs_=hs_: e.matmul(ps_c[0:rows, :], lhsT=oaT[:, k, 0:rows], rhs=wof[:, k, hs_],
                                                                   start=(k == 0), stop=(k == 3)), ["oaT", "wof"], ["ps_c"])
                    for k in range(4):
                        sc.op("pe", lambda e, k=k, hs_=hs_: e.matmul(ps_d[0:rows, :], lhsT=obT[:, k, 0:rows], rhs=won[:, k, hs_],
                                                                   start=(k == 0), stop=(k == 3)), ["obT", "won"], ["ps_d"])
                    sc.op("dve", lambda e, hs_=hs_: e.tensor_tensor(out=mt[0:rows, hs_], in0=ps_c[0:rows, :], in1=gmt[0:rows, hs_], op=ALU.mult),
                          ["ps_c", "gmt"], ["mt"])
                    sc.op("dve", lambda e, half=half: e.tensor_tensor(out=tmpf[0:rows, 0:512], in0=ps_d[0:rows, :],
                                                                      in1=gmt[0:rows, 1024 + half * 512:1024 + (half + 1) * 512], op=ALU.mult),
                          ["ps_d", "gmt"], ["tmpf"])
                    sc.op("dve", lambda e, hs_=hs_: e.tensor_tensor(out=mb[0:rows, hs_], in0=mt[0:rows, hs_], in1=tmpf[0:rows, 0:512], op=ALU.add),
                          ["mt", "tmpf"], ["mb"])
                trans8(mb, "mb", rows, mT, "mT")
                for half in range(2):
                    hs_ = slice(half * 512, (half + 1) * 512)
                    for k in range(8):
                        sc.op("pe", lambda e, k=k, hs_=hs_: e.matmul(ps_c[0:rows, :], lhsT=mT[:, k, 0:rows], rhs=wo[:, k, hs_],
                                                                   start=(k == 0), stop=(k == 7)), ["mT", "wo"], ["ps_c"])
                    sc.op("dve", lambda e, hs_=hs_: e.tensor_tensor(out=tmpf[0:rows, hs_], in0=ps_c[0:rows, :], in1=Mp[0:rows, 2, hs_], op=ALU.mult),
                          ["ps_c", "Mp"], ["tmpf"])
                    sc.op("dve", lambda e, hs_=hs_: e.tensor_tensor(out=y1t[0:rows, hs_], in0=tmpf[0:rows, hs_], in1=xt[0:rows, hs_], op=ALU.add),
                          ["tmpf", "xt"], ["y1t"])
                sc.dma("sp", lambda e, y1_ap=y1_ap, rows=rows: e.dma_start(out=y1_ap, in_=y1t[0:rows, :]), ["y1t"], ["Y1"])
            for (kind, idx, rows) in tiles:
                do_tileD(kind, idx, rows)
            sc.flush(st)

        YP = dscr("YP", [NQ + NSR, D], F32)
        with contextlib.ExitStack() as stE:
            def sbE(name, shape, dt=F32):
                return stE.enter_context(nc.sbuf_tensor("E_" + name, list(shape), dt))

            def psE(name, shape, dt=F32):
                return stE.enter_context(nc.psum_tensor("E_" + name, list(shape), dt))
            wstg = [(sbE(f"wstg{i}", [128, 2048]), f"wstg{i}") for i in range(2)]
            wup = sbE("wup", [128, 8, 2048], BF16)
            wdn = sbE("wdn", [128, 16, 1024], BF16)
            Mp = sbE("Mp", [128, 3, D])
            y1t = sbE("y1t", [128, D])
            junk = sbE("junk", [128, D], BF16)
            tmpf = sbE("tmpf", [128, D])
            hb = sbE("hb", [128, D], BF16)
            hT = sbE("hT", [128, 8, 128], BF16)
            ssum = sbE("ssum", [128, 1]); rstd = sbE("rstd", [128, 1])
            rl = sbE("rl", [128, 512])
            ub = sbE("ub", [128, 2048], BF16)
            uT = sbE("uT", [128, 16, 128], BF16)
            yt = sbE("yt", [128, D])
            ypt = sbE("ypt", [128, D])
            ps_tr = psE("ps_tr", [128, 1024], BF16)
            ps_a = psE("ps_a", [128, 512]); ps_b = psE("ps_b", [128, 512])
            kstate = {"k": None}

            def do_tileE(hf, kind, idx, rows):
                kind_loaded = kstate["k"]
                if True:
                    if kind == "p":
                        r0 = idx * 128
                        out_ap = o_y_p[idx * 128:(idx + 1) * 128, :]
                        c0 = 0
                    else:
                        r0 = NQ + idx * 8
                        out_ap = o_y_s[idx * 8:(idx + 1) * 8, :]
                        c0 = 128 + 8 * idx
                    y1_ap = Y1[r0:r0 + rows, :]
                    yp_ap = YP[r0:r0 + rows, :]
                    if kind != kind_loaded or kind == "s":
                        load_mod(Mp, rows, c0, (4, 3, 5), 1, ps_a, "ps_a")
                        kstate["k"] = kind
                    sc.dma("sp", lambda e, y1_ap=y1_ap, rows=rows: e.dma_start(out=y1t[0:rows, :], in_=y1_ap), ["Y1"], ["y1t"])
                    sc.op("act", lambda e, rows=rows: e.activation(out=junk[0:rows, :], in_=y1t[0:rows, :], func=AF.Square, accum_out=ssum[0:rows, :]),
                          ["y1t"], ["junk", "ssum"])
                    sc.op("dve", lambda e, rows=rows: e.tensor_scalar(out=ssum[0:rows, :], in0=ssum[0:rows, :], scalar1=1.0 / D, scalar2=EPS,
                                                                      op0=ALU.mult, op1=ALU.add), ["ssum"], ["ssum"])
                    sc.op("pool", lambda e, rows=rows: e.tensor_tensor(out=rstd[0:rows, :], in0=ssum[0:rows, :], in1=neghalf[0:rows, 0:1], op=ALU.pow),
                          ["ssum", "neghalf"], ["rstd"])
                    sc.op("dve", lambda e, rows=rows: e.scalar_tensor_tensor(out=tmpf[0:rows, :], in0=y1t[0:rows, :], scalar=rstd[0:rows, :],
                                                                             in1=Mp[0:rows, 0, :], op0=ALU.mult, op1=ALU.mult),
                          ["y1t", "rstd", "Mp"], ["tmpf"])
                    sc.op("dve", lambda e, rows=rows: e.tensor_tensor(out=hb[0:rows, :], in0=tmpf[0:rows, :], in1=Mp[0:rows, 1, :], op=ALU.add),
                          ["tmpf", "Mp"], ["hb"])
                    for k in range(8):
                        sc.op("pe", lambda e, k=k, rows=rows: e.transpose(out=ps_tr[:, k * 128:k * 128 + rows], in_=hb[0:rows, k * 128:(k + 1) * 128],
                                                                          identity=ident_b[0:rows, 0:rows]), ["hb", "ident_b"], ["ps_tr"])
                    sc.op("act", lambda e, rows=rows: e.activation(out=hT[:, :, 0:rows], in_=ps_tr[:].rearrange("p (k t) -> p k t", t=128)[:, :, 0:rows],
                                                                   func=AF.Copy), ["ps_tr"], ["hT"])
                    for g4 in range(4):
                        pst, pn = [(ps_a, "ps_a"), (ps_b, "ps_b")][g4 % 2]
                        for k in range(8):
                            sc.op("pe", lambda e, k=k, g4=g4, pst=pst, rows=rows: e.matmul(pst[0:rows, :], lhsT=hT[:, k, 0:rows],
                                                                                        rhs=wup[:, k, g4 * 512:(g4 + 1) * 512],
                                                                                        start=(k == 0), stop=(k == 7)), ["hT", "wup"], [pn])
                        sc.op("act", lambda e, pst=pst, rows=rows: e.activation(out=rl[0:rows, :], in_=pst[0:rows, :], func=AF.Relu), [pn], ["rl"])
                        sc.op("dve", lambda e, g4=g4, rows=rows: e.tensor_tensor(out=ub[0:rows, g4 * 512:(g4 + 1) * 512], in0=rl[0:rows, :], in1=rl[0:rows, :],
                                                                                op=ALU.mult), ["rl"], ["ub"])
                    for q2 in range(2):
                        for k in range(8):
                            sc.op("pe", lambda e, k=k, q2=q2, rows=rows: e.transpose(out=ps_tr[:, k * 128:k * 128 + rows],
                                                                                    in_=ub[0:rows, (q2 * 8 + k) * 128:(q2 * 8 + k + 1) * 128],
                                                                                    identity=ident_b[0:rows, 0:rows]), ["ub", "ident_b"], ["ps_tr"])
                        sc.op("act", lambda e, q2=q2, rows=rows: e.activation(out=uT[:, q2 * 8:(q2 + 1) * 8, 0:rows],
                                                                              in_=ps_tr[:].rearrange("p (k t) -> p k t", t=128)[:, :, 0:rows], func=AF.Copy),
                              ["ps_tr"], ["uT"])
                    if hf == 1:
                        sc.dma("sp", lambda e, yp_ap=yp_ap, rows=rows: e.dma_start(out=ypt[0:rows, :], in_=yp_ap), ["YP"], ["ypt"])
                    for half in range(2):
                        hs_ = slice(half * 512, (half + 1) * 512)
                        for k in range(16):
                            sc.op("pe", lambda e, k=k, hs_=hs_, rows=rows: e.matmul(ps_a[0:rows, :], lhsT=uT[:, k, 0:rows], rhs=wdn[:, k, hs_],
                                                                                  start=(k == 0), stop=(k == 15)), ["uT", "wdn"], ["ps_a"])
                        if hf == 0:
                            sc.op("act", lambda e, hs_=hs_, rows=rows: e.activation(out=ypt[0:rows, hs_], in_=ps_a[0:rows, :], func=AF.Copy),
                                  ["ps_a"], ["ypt"])
                        else:
                            sc.op("dve", lambda e, hs_=hs_, rows=rows: e.tensor_tensor(out=tmpf[0:rows, hs_], in0=ps_a[0:rows, :], in1=ypt[0:rows, hs_], op=ALU.add),
                                  ["ps_a", "ypt"], ["tmpf"])
                            sc.op("dve", lambda e, hs_=hs_, rows=rows: e.tensor_tensor(out=tmpf[0:rows, hs_], in0=tmpf[0:rows, hs_], in1=Mp[0:rows, 2, hs_], op=ALU.mult),
                                  ["tmpf", "Mp"], ["tmpf"])
                            sc.op("dve", lambda e, hs_=hs_, rows=rows: e.tensor_tensor(out=yt[0:rows, hs_], in0=tmpf[0:rows, hs_], in1=y1t[0:rows, hs_], op=ALU.add),
                                  ["tmpf", "y1t"], ["yt"])
                    if hf == 0:
                        sc.dma("sp", lambda e, yp_ap=yp_ap, rows=rows: e.dma_start(out=yp_ap, in_=ypt[0:rows, :]), ["ypt"], ["YP"])
                    else:
                        sc.dma("pool", lambda e, out_ap=out_ap, rows=rows: e.dma_start(out=out_ap, in_=yt[0:rows, :]), ["yt"], [])
            for hf in range(2):
                cast_weight(wup, "wup", lambda k, hf=hf: w_up[k * 128:(k + 1) * 128, hf * 2048:(hf + 1) * 2048], 8, 2048, wstg)
                cast_weight(wdn, "wdn", lambda k, hf=hf: w_down[hf * 2048 + k * 128:hf * 2048 + (k + 1) * 128, :], 16, 1024, wstg)
                kstate["k"] = None
                for (kind, idx, rows) in tiles:
                    do_tileE(hf, kind, idx, rows)
            sc.flush(st)
    return nc


def host_consts(cfg):
    NR = 1 + cfg.NSEQ
    sel = np.zeros((NR, 128 + cfg.NSR), np.float32)
    sel[0, 0:128] = 1.0
    for r in range(cfg.NSR):
        sel[1 + r // 8, 128 + r] = 1.0
    ki = np.arange(128)[:, None]; qi = np.arange(128)[None, :]
    tri = np.where(ki <= qi, 0.0, NEG).astype(np.float32)
    hs = np.arange(128)
    bt = ((hs[:, None] // 16 == hs[None, :] // 16) & (hs[:, None] % 16 < hs[None, :] % 16)).astype(np.float32)
    LS = cfg.PAST + 128
    vs = (np.arange(LS) < cfg.PAST + 8).astype(np.float32)
    out = {"sel5": sel, "ident": np.eye(128, dtype=np.float32), "tri_in": tri, "iota_in": np.arange(128, dtype=np.float32)[:, None],
           "bt_in": bt, "tvs_in": np.tile(vs.reshape(16, -1), (8, 1)), "pns_in": np.tile(((1 - vs) * NEG).reshape(16, -1), (8, 1)),
           "kwns_in": ((1 - vs) * NEG).reshape(128, -1)}
    def bucket(d):
        d = np.maximum(d, 0)
        far = 16 + (np.log(np.maximum(d, 1).astype(np.float32) / np.float32(16)) / np.float32(np.log(8.0)) * np.float32(16)).astype(np.int32)
        return np.where(d < 16, d, np.minimum(far, 31))
    dc = np.arange(4096) - 1856
    ohc = np.zeros((33, 4096), np.float32)
    ohc[bucket(dc), np.arange(4096)] = (dc >= 0)
    ohc[32] = (dc < 0)
    d1 = np.arange(768) - 128
    ok1 = (d1 >= 0) & (d1 <= 512)
    oh1 = np.zeros((33, 768), np.float32)
    oh1[bucket(d1), np.arange(768)] = ok1
    oh1[32] = ~ok1
    out["ohc_in"] = ohc; out["oh1_in"] = oh1
    out["ee_in"] = (np.arange(128)[:, None] == (np.arange(8192)[None, :] // 64)).astype(np.float32)

    def wmat(NT, NM):
        c0 = np.arange(NT * 128)[:, None] * 16
        s0 = np.arange(NM)[None, :] * 64
        sh = np.minimum(c0 + 32, s0 + 64) - np.maximum(c0, s0)
        return (np.maximum(sh, 0) / 32.0).astype(np.float32)
    out["wcp_in"] = wmat(cfg.NTp, cfg.NMp); out["wcs_in"] = wmat(cfg.NTs, cfg.NMs)
    ns = np.arange(cfg.NTs * 128)
    out["cnegs_in"] = np.where(ns >= cfg.PAST // 16 - 1, NEG, 0.0)[None, :]
    pos = cfg.PAST + np.arange(8)[:, None]; m = np.arange(cfg.NMs)[None, :]
    cur = pos // 64
    forced = (m == 0) | (m == cur) | (m == cur - 1)
    out["adds_in"] = np.where(m > cur, -1e30, np.where(forced, 100.0 + m, 0.0))
    return {k: np.ascontiguousarray(v, dtype=np.float32) for k, v in out.items()}


def make_in_maps(cfg, inp):
    S, NB = cfg.S, cfg.NB
    x = np.asarray(inp["x_prompt"], np.float32).reshape(S, D)
    consts = host_consts(cfg)
    pool_fox = np.asarray(inp["cache_fox_kv"], np.float32).reshape(-1, 1024)
    pool_lf = np.asarray(inp["cache_fox_logf"], np.float32).reshape(-1, 8)
    pool_nsa = np.asarray(inp["cache_nsa_kv"], np.float32).reshape(-1, 512)
    maps = []
    for c in range(NCORES):
        pad = (7 - c) * 128
        xfr = np.zeros((S, D), np.float32)
        xfr[pad:] = x[:S - pad]
        sl = slice(c * cfg.NSEQ, (c + 1) * cfg.NSEQ)
        m = {
            "xf": xfr,
            "xs": np.ascontiguousarray(np.asarray(inp["x_sample"], np.float32)[sl].reshape(cfg.NSR, D)),
            "cvec": np.concatenate([np.asarray(inp["c_prompt"], np.float32), np.asarray(inp["c_sample"], np.float32)[sl]], 0),
            "w_ada": np.asarray(inp["w_ada"], np.float32)[0],
            "b_ada": np.asarray(inp["b_ada"], np.float32),
            "g_norm": np.asarray(inp["g_norm"], np.float32)[0],
            "w_in": np.asarray(inp["w_in"], np.float32)[0],
            "b_forget": np.asarray(inp["b_forget"], np.float32),
            "g_qk_fox": np.asarray(inp["g_qk_fox"], np.float32)[0],
            "g_qk_nsa": np.asarray(inp["g_qk_nsa"], np.float32)[0],
            "rel_bias": np.asarray(inp["rel_bias"], np.float32), "pe_cmp": np.asarray(inp["pe_cmp"], np.float32)[0],
            "w_cmp1": np.asarray(inp["w_cmp1"], np.float32)[0], "w_cmp2": np.asarray(inp["w_cmp2"], np.float32)[0],
            "w_out_fox": np.asarray(inp["w_out_fox"], np.float32)[0], "w_out_nsa": np.asarray(inp["w_out_nsa"], np.float32)[0],
            "w_out": np.asarray(inp["w_out"], np.float32)[0], "w_up": np.asarray(inp["w_up"], np.float32)[0],
            "w_down": np.asarray(inp["w_down"], np.float32)[0],
            "win_in": np.ascontiguousarray(np.asarray(inp["state_nsa_win"], np.float32)[0, sl].reshape(cfg.NSEQ, -1, 256)),
        }
        vp = (np.arange(S) >= pad).astype(np.float32)
        m["tvp_in"] = np.ascontiguousarray(np.tile(vp.reshape(16, -1), (8, 1)))
        m["pnp_in"] = np.ascontiguousarray(np.tile(((1 - vp) * NEG).reshape(16, -1), (8, 1)).astype(np.float32))
        m["kwnp_in"] = np.ascontiguousarray(((1 - vp) * NEG).reshape(128, -1).astype(np.float32))
        npad = 8 * (7 - c)
        nf = np.arange(cfg.NTp * 128)
        m["cnegp_in"] = np.where((nf < npad) | (nf >= S // 16 - 1), NEG, 0.0).astype(np.float32)[None, :]
        m0 = 2 * (7 - c)
        jo_ = np.arange(cfg.NOWN)[:, None, None]; pi_ = np.arange(128)[None, :, None]; mm_ = np.arange(cfg.NMp)[None, None, :]
        cur_ = (128 * (8 * jo_ + 7) + pi_) // 64
        forced_ = (mm_ == m0) | (mm_ == cur_) | (mm_ == cur_ - 1)
        m["addp_in"] = np.ascontiguousarray(np.where((mm_ > cur_) | (mm_ < m0), -1e30, np.where(forced_, 100.0 + mm_, 0.0)).astype(np.float32))
        m["ptab"] = np.ascontiguousarray(np.asarray(inp["page_table"], np.int32)[sl])
        m["pool_fox"] = pool_fox; m["pool_lf"] = pool_lf; m["pool_nsa"] = pool_nsa
        if cfg.dbg_ob:
            obp = np.asarray(inp["dbg_ob_p"], np.float32).reshape(S, 512)
            m["dbg_ob_p_in"] = np.concatenate([obp[(8 * j + c) * 128:(8 * j + c + 1) * 128] for j in range(cfg.NOWN)], 0)
            m["dbg_ob_s_in"] = np.ascontiguousarray(np.asarray(inp["dbg_ob_s"], np.float32)[sl].reshape(cfg.NSR, 512))
        m.update(consts)
        maps.append(m)
    return maps


def run(cfg, inp):
    nc = build(cfg)
    maps = make_in_maps(cfg, inp)
    res = run_bass_kernel_spmd(nc, maps, core_ids=list(range(NCORES)))
    return res.results


def assemble(cfg, inp, R):
    S, NB, NOWN = cfg.S, cfg.NB, cfg.NOWN
    NSEQT = cfg.NSEQ * NCORES

    def own_scatter(key, width):
        out = np.zeros((S, width), np.float32)
        for c in range(NCORES):
            r = R[c][key]
            for j in range(NOWN):
                b = 8 * j + c
                out[b * 128:(b + 1) * 128] = r[j * 128:(j + 1) * 128]
        return out

    def cat(key):
        return np.concatenate([R[c][key] for c in range(NCORES)], 0)
    fkv_p = own_scatter("o_fkv_p", 1024).reshape(1, 1, S, 2, 8, 64)
    lf_p = own_scatter("o_lf_p", 8).reshape(1, 1, S, 8)
    nkv_p = own_scatter("o_nkv_p", 512).reshape(1, 1, S, 4, 2, 64)
    win_p = own_scatter("o_win_p", 256)[S - min(512, S):].reshape(1, 1, min(512, S), 2, 2, 64)
    fkv_s = cat("o_fkv_s").reshape(1, NSEQT, 8, 2, 8, 64)
    lf_s = cat("o_lf_s").reshape(1, NSEQT, 8, 8)
    nkv_s = cat("o_nkv_s").reshape(1, NSEQT, 8, 4, 2, 64)
    wnew = cat("o_wnew_s").reshape(NSEQT, 8, 2, 2, 64)
    win_s = cat("o_win_s").reshape(1, NSEQT, -1, 2, 2, 64)
    y_p = own_scatter("o_y_p", D).reshape(1, S, D)
    y_s = cat("o_y_s").reshape(NSEQT, 8, D)
    dbg = dict(oa_p=own_scatter("dbg_oa_p", 512), oa_s=cat("dbg_oa_s").astype(np.float32),
               ob_p=own_scatter("dbg_ob_p", 512), ob_s=cat("dbg_ob_s").astype(np.float32))
    return dict(dbg=dbg, y_p=y_p, y_s=y_s, fkv_p=fkv_p, lf_p=lf_p, nkv_p=nkv_p, win_p=win_p, fkv_s=fkv_s, lf_s=lf_s,
                nkv_s=nkv_s, wnew=wnew, win_s=win_s)


def kernel(**inputs):
    cfg = Cfg()
    R = run(cfg, inputs)
    A = assemble(cfg, inputs, R)
    names = ["y_p", "y_s", "fkv_p", "fkv_s", "lf_p", "lf_s", "nkv_p", "nkv_s", "win_p", "win_s"]
    return tuple(np.ascontiguousarray(A[n], dtype=np.float32) for n in names)
```

```python
import contextlib
import numpy as np
import concourse.bass as bass
import concourse.mybir as mybir
from concourse.bass_utils import run_bass_kernel_spmd

F32 = mybir.dt.float32
BF16 = mybir.dt.bfloat16
I32 = mybir.dt.int32
AF = mybir.ActivationFunctionType
ALU = mybir.AluOpType
AX = mybir.AxisListType

D = 1024
IN_W = 4896
O_QA, O_KA, O_VA, O_ZF, O_QB, O_ZKV, O_ZG, O_ZM = 0, 512, 1024, 1536, 1544, 2056, 2824, 2848
EPS = 1e-6
NEG = -30000.0
NCORES = 8


class Sched:
    ENG = ("pe", "act", "dve", "pool", "sp")

    def __init__(self, nc, nlanes=6):
        self.nc = nc
        self.ops = {e: [] for e in self.ENG}
        self.cnt = {}
        self.seen = {e: {} for e in self.ENG}
        self.lw = {}
        self.rd = {}
        self.lanes = {"sp": [f"L_sp{i}" for i in range(nlanes)],
                      "pool": [f"L_pool{i}" for i in range(nlanes)],
                      "act": [f"L_act{i}" for i in range(2)]}
        self.lane_rr = {"sp": 0, "pool": 0, "act": 0}
        self.semkeys = list(self.ENG)
        for v in self.lanes.values():
            self.semkeys += v
        for k in self.semkeys:
            self.cnt[k] = 0
        self.sems = {}
        self.n_inst = 0
        self.local = None
        self.sfx = ""

    def _deps(self, reads, writes):
        toks = {}

        def add(t):
            for k, v in t.items():
                if toks.get(k, 0) < v:
                    toks[k] = v
        for r in reads:
            if r in self.lw:
                add(self.lw[r])
        for w in writes:
            if w in self.lw:
                add(self.lw[w])
            if w in self.rd:
                add(self.rd[w])
        return toks

    def _commit(self, tok, reads, writes):
        k, v = tok
        for r in reads:
            d = self.rd.setdefault(r, {})
            if d.get(k, 0) < v:
                d[k] = v
        for w in writes:
            self.lw[w] = {k: v}
            self.rd[w] = {}

    def _waits(self, eng, toks):
        for k, v in toks.items():
            if k == "pe" and eng == "pe":
                continue
            if self.seen[eng].get(k, 0) >= v:
                continue
            self.seen[eng][k] = v
            self.ops[eng].append(("w", k, v))

    def _rn(self, names):
        if self.local is None:
            return names
        return [n + self.sfx if n in self.local else n for n in names]

    def op(self, eng, fn, reads=(), writes=()):
        reads, writes = self._rn(reads), self._rn(writes)
        toks = self._deps(reads, writes)
        self._waits(eng, toks)
        self.cnt[eng] += 1
        self.ops[eng].append(("i", fn, eng, 1))
        self._commit((eng, self.cnt[eng]), reads, writes)
        self.n_inst += 1

    def dma(self, eng, fn, reads=(), writes=()):
        reads, writes = self._rn(reads), self._rn(writes)
        toks = self._deps(reads, writes)
        lanes = self.lanes[eng]
        lane = lanes[self.lane_rr[eng] % len(lanes)]
        self.lane_rr[eng] += 1
        if self.cnt[lane] > 0:
            toks[lane] = max(toks.get(lane, 0), self.cnt[lane])
        self._waits(eng, toks)
        self.cnt[lane] += 16
        self.ops[eng].append(("i", fn, lane, 16))
        self._commit((lane, self.cnt[lane]), reads, writes)
        self.n_inst += 1

    def barrier(self):
        for e in self.ENG:
            toks = {k: v for k, v in self.cnt.items() if v > 0 and k != e}
            self._waits(e, toks)

    def flush(self, stack_sems):
        nc = self.nc
        self.barrier()
        for k in self.semkeys:
            if k not in self.sems:
                self.sems[k] = stack_sems.enter_context(nc.semaphore("s_" + k))
        sems = self.sems
        ops = self.ops

        def replay(name, eng):
            for o in ops[name]:
                if o[0] == "w":
                    eng.wait_ge(sems[o[1]], o[2])
                else:
                    o[1](eng).then_inc(sems[o[2]], o[3])
        with nc.Block() as block:
            @block.tensor
            def _(e):
                replay("pe", e)

            @block.scalar
            def _(e):
                replay("act", e)

            @block.vector
            def _(e):
                replay("dve", e)

            @block.gpsimd
            def _(e):
                replay("pool", e)

            @block.sync
            def _(e):
                replay("sp", e)
        self.ops = {e: [] for e in self.ENG}


class Cfg:
    def __init__(self, S=16384, PAST=16384, NSEQ=4, NPOOL=5120):
        self.S, self.PAST, self.NSEQ, self.NPOOL = S, PAST, NSEQ, NPOOL
        self.NB = S // 128
        self.NOWN = self.NB // 8
        self.NQ = self.NOWN * 128
        self.NPG = PAST // 128
        self.NSR = NSEQ * 8
        self.dbg_ob = False
        self.nsa = True
        self.dbg_gate = None
        self.NTp = -(-(S // 16 - 1) // 128)
        self.NMp = -(-(S // 64) // 128) * 128
        self.NTs = -(-(PAST // 16 - 1) // 128)
        self.NMs = -(-(PAST // 64 + 1) // 128) * 128


def build(cfg):
    nc = bass.Bass("TRN2", target_bir_lowering=False)
    S, NB, NOWN, NQ, NSEQ, NSR = cfg.S, cfg.NB, cfg.NOWN, cfg.NQ, cfg.NSEQ, cfg.NSR
    NPR = cfg.NPOOL * 128

    def din(name, shape, dt=F32):
        return nc.dram_tensor(name, list(shape), dt, kind="ExternalInput").ap()

    def dout(name, shape, dt=F32):
        return nc.dram_tensor(name, list(shape), dt, kind="ExternalOutput").ap()

    def dscr(name, shape, dt=F32):
        return nc.dram_tensor(name, list(shape), dt, kind="Internal").ap()

    xf = din("xf", [S, D])
    xs = din("xs", [NSR, D])
    cvec = din("cvec", [1 + NSEQ, D])
    w_ada = din("w_ada", [D, 6 * D])
    b_ada = din("b_ada", [1, 6 * D])
    g_norm = din("g_norm", [2, D])
    w_in = din("w_in", [D, IN_W])
    b_forget = din("b_forget", [1, 8])
    g_qk_fox = din("g_qk_fox", [2, 64])
    g_qk_nsa = din("g_qk_nsa", [4, 64])
    sel5 = din("sel5", [1 + NSEQ, 128 + NSR])
    ident_in = din("ident", [128, 128])

    o_fkv_p = dout("o_fkv_p", [NQ, 1024])
    o_lf_p = dout("o_lf_p", [NQ, 8])
    o_nkv_p = dout("o_nkv_p", [NQ, 512])
    o_win_p = dout("o_win_p", [NQ, 256])
    o_fkv_s = dout("o_fkv_s", [NSR, 1024])
    o_lf_s = dout("o_lf_s", [NSR, 8])
    o_nkv_s = dout("o_nkv_s", [NSR, 512])
    o_wnew_s = dout("o_wnew_s", [NSR, 256])
    tri_in = din("tri_in", [128, 128])
    iota_in = din("iota_in", [128, 1])
    bt_in = din("bt_in", [128, 128])
    LS_ = cfg.PAST + 128
    tvp_in = din("tvp_in", [128, S // 16]); pnp_in = din("pnp_in", [128, S // 16]); kwnp_in = din("kwnp_in", [128, S // 128])
    tvs_in = din("tvs_in", [128, LS_ // 16]); pns_in = din("pns_in", [128, LS_ // 16]); kwns_in = din("kwns_in", [128, LS_ // 128])
    ptab = din("ptab", [NSEQ, cfg.NPG], I32)
    pool_fox = din("pool_fox", [NPR, 1024])
    pool_lf = din("pool_lf", [NPR, 8])
    pool_nsa = din("pool_nsa", [NPR, 512])
    w_out_fox = din("w_out_fox", [512, D]); w_out_nsa = din("w_out_nsa", [512, D]); w_out = din("w_out", [D, D])
    w_up = din("w_up", [D, 4 * D]); w_down = din("w_down", [4 * D, D])
    if cfg.dbg_ob:
        dbg_ob_p_in = din("dbg_ob_p_in", [NQ, 512]); dbg_ob_s_in = din("dbg_ob_s_in", [NSR, 512])
    rel_bias = din("rel_bias", [32, 8]); pe_cmp = din("pe_cmp", [2, 32, 64])
    w_cmp1 = din("w_cmp1", [2, 2048, 128]); w_cmp2 = din("w_cmp2", [2, 128, 64])
    ohc_in = din("ohc_in", [33, 4096]); oh1_in = din("oh1_in", [33, 768]); ee_in = din("ee_in", [128, 8192])
    wcp_in = din("wcp_in", [cfg.NTp * 128, cfg.NMp]); wcs_in = din("wcs_in", [cfg.NTs * 128, cfg.NMs])
    cnegp_in = din("cnegp_in", [1, cfg.NTp * 128]); cnegs_in = din("cnegs_in", [1, cfg.NTs * 128])
    addp_in = din("addp_in", [NOWN, 128, cfg.NMp]); adds_in = din("adds_in", [8, cfg.NMs])
    dbg_ob_p = dout("dbg_ob_p", [NQ, 512], BF16); dbg_ob_s = dout("dbg_ob_s", [NSR, 512], BF16)
    dbg_oa_p = dout("dbg_oa_p", [NQ, 512], BF16)
    dbg_oa_s = dout("dbg_oa_s", [NSR, 512], BF16)
    WB = min(512, cfg.PAST)
    win_in = din("win_in", [NSEQ, WB, 256])
    o_win_s = dout("o_win_s", [NSEQ, WB, 256])
    o_y_p = dout("o_y_p", [NQ, D])
    o_y_s = dout("o_y_s", [NSR, D])

    sc = Sched(nc, nlanes=12)
    st = contextlib.ExitStack()
    with st:
        def sb(name, shape, dt=F32):
            return st.enter_context(nc.sbuf_tensor(name, list(shape), dt))

        def ps(name, shape, dt=F32):
            return st.enter_context(nc.psum_tensor(name, list(shape), dt))

        ident_f = sb("ident_f", [128, 128])
        ident_b = sb("ident_b", [128, 128], BF16)
        modrows = sb("modrows", [1 + NSEQ, 6 * D])
        sel_sb = sb("sel_sb", [1 + NSEQ, 128 + NSR])
        gk_b = sb("gk_b", [128, 8, 64])
        gq_b = sb("gq_b", [128, 8, 64])
        gnq_b = sb("gnq_b", [128, 8, 64])
        gsel_b = sb("gsel_b", [128, 2, 64])
        gwin_b = sb("gwin_b", [128, 2, 64])
        bf_b = sb("bf_b", [128, 8])
        neghalf = sb("neghalf", [128, 8])
        ones_f = sb("ones_f", [128, 128])
        gn = sb("gn", [128, 2, D])
        gb_res = sb("gb_res", [128, NOWN, 24])
        gbs_res = sb("gbs_res", [8, NSEQ, 24])
        tri_b = sb("tri_b", [128, 128], BF16)
        ones_b = sb("ones_b", [128, 512], BF16)
        zeros_b = sb("zeros_b", [128, 512], BF16)
        stW = contextlib.ExitStack()
        w_in_b = stW.enter_context(nc.sbuf_tensor("w_in_b", [128, 8, 2848], BF16))

        sc.dma("sp", lambda e: e.dma_start(out=ident_f[:], in_=ident_in[:, :]), [], ["ident_f"])
        sc.op("dve", lambda e: e.tensor_copy(out=ident_b[:], in_=ident_f[:]), ["ident_f"], ["ident_b"])
        sc.op("dve", lambda e: e.memset(neghalf[:], -0.5), [], ["neghalf"])
        sc.op("dve", lambda e: e.memset(ones_f[:], 1.0), [], ["ones_f"])
        sc.dma("sp", lambda e: e.dma_start(out=sel_sb[:], in_=sel5[:, :]), [], ["sel_sb"])

        with contextlib.ExitStack() as st0:
            def sb0(name, shape, dt=F32):
                return st0.enter_context(nc.sbuf_tensor(name, list(shape), dt))
            NR = 1 + NSEQ
            c_sb = sb0("c_sb", [NR, D])
            sig = sb0("sig", [NR, D])
            cT = sb0("cT", [128, 8, NR])
            wada = [sb0(f"wada{i}", [128, 8, 512]) for i in range(2)]
            bada = sb0("bada", [NR, 6 * D])
            small = sb0("small", [128, 8 + 6 * 64])
            ps0 = st0.enter_context(nc.psum_tensor("ps0", [128, 512], F32))
            ps1 = st0.enter_context(nc.psum_tensor("ps1", [128, 512], F32))
            sc.dma("sp", lambda e: e.dma_start(out=c_sb[:], in_=cvec[:, :]), [], ["c_sb"])
            sc.dma("sp", lambda e: e.dma_start(out=bada[:], in_=b_ada.partition_broadcast(NR)), [], ["bada"])
            sc.dma("sp", lambda e: e.dma_start(out=gn[:, 0, :], in_=g_norm[0:1, :].partition_broadcast(128)), [], ["gn"])
            sc.dma("sp", lambda e: e.dma_start(out=gn[:, 1, :], in_=g_norm[1:2, :].partition_broadcast(128)), [], ["gn"])
            sc.dma("sp", lambda e: e.dma_start(out=small[:, 0:8], in_=b_forget.partition_broadcast(128)), [], ["small"])
            sc.dma("sp", lambda e: e.dma_start(
                out=small[:, 8:8 + 128], in_=g_qk_fox.rearrange("a d -> (a d)").unsqueeze(0).partition_broadcast(128)), [], ["small"])
            sc.dma("sp", lambda e: e.dma_start(
                out=small[:, 136:136 + 256], in_=g_qk_nsa.rearrange("a d -> (a d)").unsqueeze(0).partition_broadcast(128)), [], ["small"])
            sc.op("act", lambda e: e.activation(out=sig[:], in_=c_sb[:], func=AF.Sigmoid), ["c_sb"], ["sig"])
            sc.op("dve", lambda e: e.tensor_tensor(out=sig[:], in0=sig[:], in1=c_sb[:], op=ALU.mult), ["sig", "c_sb"], ["sig"])
            for k in range(8):
                sc.op("pe", lambda e, k=k: e.transpose(out=ps0[:, k * NR:(k + 1) * NR], in_=sig[:, k * 128:(k + 1) * 128], identity=ident_f[0:NR, 0:NR]),
                      ["sig", "ident_f"], ["ps0"])
            sc.op("dve", lambda e: e.tensor_copy(out=cT[:].rearrange("p k r -> p (k r)"), in_=ps0[:, 0:8 * NR]), ["ps0"], ["cT"])
            for g in range(12):
                wt = wada[g % 2]
                wn = f"wada{g % 2}"
                sc.dma("sp", lambda e, g=g, wt=wt: e.dma_start(
                    out=wt[:], in_=w_ada[:, g * 512:(g + 1) * 512].rearrange("(k p) n -> p k n", p=128)), [], [wn])
                for k in range(8):
                    sc.op("pe", lambda e, k=k, wt=wt: e.matmul(ps1[0:NR, :], lhsT=cT[:, k, :], rhs=wt[:, k, :],
                                                               start=(k == 0), stop=(k == 7)), ["cT", wn], ["ps1"])
                sc.op("dve", lambda e, g=g: e.tensor_tensor(out=modrows[:, g * 512:(g + 1) * 512], in0=ps1[0:NR, :],
                                                             in1=bada[:, g * 512:(g + 1) * 512], op=ALU.add),
                      ["ps1", "bada"], ["modrows"])
            def bcast_mod(dst, dname, which, rows, c0):
                for half in range(2):
                    sc.op("pe", lambda e, half=half: e.matmul(
                        ps0[0:rows, :], lhsT=sel_sb[:, c0:c0 + rows],
                        rhs=modrows[:, which * D + half * 512: which * D + half * 512 + 512], start=True, stop=True),
                        ["sel_sb", "modrows"], ["ps0"])
                    sc.op("dve", lambda e, half=half: e.tensor_copy(out=dst[:, half * 512:(half + 1) * 512], in_=ps0[0:rows, :]),
                          ["ps0"], [dname])

            def mk_gain(dst, dname, rows, gi):
                sc.op("dve", lambda e: e.scalar_tensor_tensor(out=dst, in0=dst, scalar=1.0, in1=gn[0:rows, gi, :],
                                                              op0=ALU.add, op1=ALU.mult), [dname, "gn"], [dname])
            sc.op("dve", lambda e: e.tensor_copy(out=bf_b[:], in_=small[:, 0:8]), ["small"], ["bf_b"])
            for h in range(8):
                sc.op("dve", lambda e, h=h: e.tensor_scalar(out=gq_b[:, h, :], in0=small[:, 8:72], scalar1=0.125, scalar2=None,
                                                            op0=ALU.mult), ["small"], ["gq_b"])
                sc.op("dve", lambda e, h=h: e.tensor_copy(out=gk_b[:, h, :], in_=small[:, 72:136]), ["small"], ["gk_b"])
                sc.op("dve", lambda e, h=h: e.tensor_scalar(out=gnq_b[:, h, :], in0=small[:, 136:200], scalar1=0.125, scalar2=None,
                                                            op0=ALU.mult), ["small"], ["gnq_b"])
            for g in range(2):
                sc.op("dve", lambda e, g=g: e.tensor_copy(out=gsel_b[:, g, :], in_=small[:, 136 + 128:136 + 192]), ["small"], ["gsel_b"])
                sc.op("dve", lambda e, g=g: e.tensor_copy(out=gwin_b[:, g, :], in_=small[:, 136 + 192:136 + 256]), ["small"], ["gwin_b"])
            for k in range(8):
                for half in range(2):
                    wt = wada[(2 * k + half) % 2]
                    wn = f"wada{(2 * k + half) % 2}"
                    c0 = half * 1424
                    sc.dma("sp", lambda e, k=k, c0=c0, wt=wt: e.dma_start(
                        out=wt[:].rearrange("p k n -> p (k n)")[:, 0:1424], in_=w_in[k * 128:(k + 1) * 128, c0:c0 + 1424]), [], [wn])
                    sc.op("pool", lambda e, k=k, c0=c0, wt=wt: e.tensor_copy(
                        out=w_in_b[:, k, c0:c0 + 1424], in_=wt[:].rearrange("p k n -> p (k n)")[:, 0:1424]), [wn], ["w_in_b"])
            sc.flush(st)

        LS = cfg.PAST + 128
        NPG = cfg.NPG

        class Ctx:
            pass

        def mk_ctx(name, L, nqc):
            c = Ctx()
            c.name, c.L, c.NBk, c.nqc = name, L, L // 128, nqc
            c.KT = dscr(f"KT_{name}", [70, 8, L], BF16)
            c.V = dscr(f"V_{name}", [8, 128, c.NBk, 65], BF16)
            c.QT = dscr(f"QT_{name}", [70, 8, nqc], BF16)
            c.LF = dscr(f"LF_{name}", [8, L], F32)
            c.CUM = dscr(f"CUM_{name}", [8, L], F32)
            c.XC = dscr(f"XC_{name}", [2, 128, L], BF16)
            c.KS = dscr(f"KS_{name}", [2, 64, L], BF16)
            c.KW = dscr(f"KW_{name}", [2, 65, L], BF16)
            c.VS = dscr(f"VS_{name}", [2, 128, c.NBk, 65], BF16)
            c.VW = dscr(f"VW_{name}", [2, 128, c.NBk, 65], BF16)
            c.QN = dscr(f"QN_{name}", [65, 8, nqc], BF16)
            return c
        ctx_p = mk_ctx("p", S, NQ)
        ctx_s = [mk_ctx(f"s{b}", LS, 8) for b in range(NSEQ)]
        OAS = dscr("OAS", [NSR, 512], BF16)
        OAP = dscr("OAP", [NQ, 512], BF16)
        OBP = dscr("OBP", [NQ, 512], BF16)
        OBS = dscr("OBS", [NSR, 512], BF16)

        sc.op("dve", lambda e: e.memset(ones_b[:], 1.0), [], ["ones_b"])
        sc.op("dve", lambda e: e.memset(zeros_b[:], 0.0), [], ["zeros_b"])

        def bcast_rows(dst, dname, which, rows, c0, pst, pname):
            for half in range(2):
                sc.op("pe", lambda e, half=half: e.matmul(
                    pst[0:rows, :], lhsT=sel_sb[:, c0:c0 + rows],
                    rhs=modrows[:, which * D + half * 512: which * D + half * 512 + 512], start=True, stop=True),
                    ["sel_sb", "modrows"], [pname])
                sc.op("dve", lambda e, half=half: e.tensor_copy(out=dst[:, half * 512:(half + 1) * 512], in_=pst[0:rows, :]),
                      [pname], [dname])

        def load_mod(Mt, rows, c0, whichs, gi, pst, pname):
            bcast_rows(Mt[0:rows, 0, :], "Mp", whichs[0], rows, c0, pst, pname)
            sc.op("dve", lambda e: e.scalar_tensor_tensor(out=Mt[0:rows, 0, :], in0=Mt[0:rows, 0, :], scalar=1.0, in1=gn[0:rows, gi, :],
                                                          op0=ALU.add, op1=ALU.mult), ["Mp", "gn"], ["Mp"])
            bcast_rows(Mt[0:rows, 1, :], "Mp", whichs[1], rows, c0, pst, pname)
            if whichs[2] is not None:
                bcast_rows(Mt[0:rows, 2, :], "Mp", whichs[2], rows, c0, pst, pname)

        with contextlib.ExitStack() as stA:
            def sbA(name, shape, dt=F32):
                return stA.enter_context(nc.sbuf_tensor(name, list(shape), dt))

            def psA(name, shape, dt=F32):
                return stA.enter_context(nc.psum_tensor(name, list(shape), dt))
            MpA = sbA("MpA", [128, 3, D])
            trif = sbA("trif", [128, 128])
            wpage = sbA("wpage", [128, 256])
            ps_tr = psA("ps_tr", [128, 1024], BF16)
            ps_a = psA("ps_a", [128, 512]); ps_b = psA("ps_b", [128, 512])
            ps_c = psA("ps_c", [128, 512]); ps_d = psA("ps_d", [128, 512])
            ps_kt = psA("ps_kt", [128, 8, 128], BF16)
            ps_nt = psA("ps_nt", [128, 6, 128], BF16)
            ps_lf = psA("ps_lf", [8, 128])

            sc.dma("sp", lambda e: e.dma_start(out=trif[:], in_=tri_in[:, :]), [], ["trif"])
            sc.op("dve", lambda e: e.tensor_copy(out=tri_b[:], in_=trif[:]), ["trif"], ["tri_b"])

            LOCAL_A = {"xt", "junk", "tmpf", "hb", "hT", "ssum", "rstd", "sq", "hs", "hr", "kvout", "nkvout", "qf", "lfo",
                       "kb", "nb", "stg_k", "stg_x", "stg_n", "stg_l", "vaug", "vsaug", "vwaug"}

            def make_setA(sfx):
                sc.local, sc.sfx = LOCAL_A, sfx
                xt = sbA(sfx + "xt", [128, D])
                junk = sbA(sfx + "junk", [128, D], BF16)
                tmpf = sbA(sfx + "tmpf", [128, D])
                hb = sbA(sfx + "hb", [128, D], BF16)
                hT = sbA(sfx + "hT", [128, 8, 128], BF16)
                ssum = sbA(sfx + "ssum", [128, 1])
                rstd = sbA(sfx + "rstd", [128, 1])
                sq = sbA(sfx + "sq", [128, 512])
                hs = sbA(sfx + "hs", [128, 8]); hr = sbA(sfx + "hr", [128, 8])
                kvout = sbA(sfx + "kvout", [128, 1024])
                nkvout = sbA(sfx + "nkvout", [128, 768])
                qf = sbA(sfx + "qf", [128, 512])
                lfo = sbA(sfx + "lfo", [128, 8])
                kb = sbA(sfx + "kb", [128, 512], BF16)
                nb = sbA(sfx + "nb", [128, 768], BF16)
                stg_k = sbA(sfx + "stg_k", [64, 8, 128], BF16)
                stg_x = sbA(sfx + "stg_x", [128, 2, 128], BF16)
                stg_n = sbA(sfx + "stg_n", [64, 4, 128], BF16)
                stg_l = sbA(sfx + "stg_l", [8, 128])
                vaug = sbA(sfx + "vaug", [128, 8, 65], BF16)
                vsaug = sbA(sfx + "vsaug", [128, 2, 65], BF16)
                vwaug = sbA(sfx + "vwaug", [128, 2, 65], BF16)
                sc.op("dve", lambda e: e.memset(vaug[:], 1.0), [], ["vaug"])
                sc.op("dve", lambda e: e.memset(vsaug[:], 1.0), [], ["vsaug"])
                sc.op("dve", lambda e: e.memset(vwaug[:], 1.0), [], ["vwaug"])
                def headnorm(src, nh, gain, dst, dstname, rows, srcname):
                    sc.op("act", lambda e: e.activation(out=sq[0:rows, 0:nh * 64], in_=src, func=AF.Square), [srcname], ["sq"])
                    sc.op("dve", lambda e: e.tensor_reduce(out=hs[0:rows, 0:nh], in_=sq[0:rows, 0:nh * 64].rearrange("p (h d) -> p h d", d=64),
                                                           axis=AX.X, op=ALU.add), ["sq"], ["hs"])
                    sc.op("dve", lambda e: e.tensor_scalar(out=hs[0:rows, 0:nh], in0=hs[0:rows, 0:nh], scalar1=1.0 / 64, scalar2=EPS,
                                                           op0=ALU.mult, op1=ALU.add), ["hs"], ["hs"])
                    sc.op("pool", lambda e: e.tensor_tensor(out=hr[0:rows, 0:nh], in0=hs[0:rows, 0:nh], in1=neghalf[0:rows, 0:nh], op=ALU.pow),
                          ["hs", "neghalf"], ["hr"])
                    sc.op("dve", lambda e: e.tensor_tensor(out=dst, in0=src.rearrange("p (h d) -> p h d", d=64),
                                                           in1=hr[0:rows, 0:nh].unsqueeze(2).to_broadcast([rows, nh, 64]), op=ALU.mult),
                          [srcname, "hr"], [dstname])
                    sc.op("dve", lambda e: e.tensor_tensor(out=dst, in0=dst, in1=gain, op=ALU.mult), [dstname], [dstname])

                def tr_heads(src_bf, srcname, rows, nh, pst, pname, stg, sname, slot0=0, width=64):
                    for h in range(nh):
                        sc.op("pe", lambda e, h=h: e.transpose(out=pst[0:width, slot0 + h, 0:rows], in_=src_bf[0:rows, h * width:(h + 1) * width],
                                                               identity=ident_b[0:rows, 0:rows]), [srcname, "ident_b"], [pname])
                    if rows < 128:
                        sc.op("dve", lambda e: e.memset(stg[0:width, slot0:slot0 + nh, :], 0.0), [], [sname])
                    sc.op("act", lambda e: e.activation(out=stg[0:width, slot0:slot0 + nh, 0:rows], in_=pst[0:width, slot0:slot0 + nh, 0:rows],
                                                        func=AF.Copy), [pname], [sname])

                def kside_store(c, blk, rows, srcn, fk=None, fv=None, lf=None, kcmp=None, vcmp=None, ksel=None, vsel=None,
                                kwin=None, vwin=None):
                    cs = slice(blk * 128, (blk + 1) * 128)
                    if fk is not None:
                        sc.op("dve", lambda e: e.tensor_copy(out=kb[0:rows, :], in_=fk), [srcn], ["kb"])
                        tr_heads(kb, "kb", rows, 8, ps_kt, "ps_kt", stg_k, "stg_k")
                        sc.dma("sp", lambda e: e.dma_start(out=c.KT[0:64, :, cs], in_=stg_k[:]),
                               ["stg_k"], ["KT_" + c.name])
                    if fv is not None:
                        if rows < 128:
                            sc.op("dve", lambda e: e.memset(vaug[:, :, 0:64], 0.0), [], ["vaug"])
                        sc.op("act", lambda e: e.activation(out=vaug[0:rows, :, 0:64], in_=fv.rearrange("p (h d) -> p h d", d=64), func=AF.Copy),
                              [srcn], ["vaug"])
                        sc.dma("sp", lambda e: e.dma_start(out=c.V[:, :, blk, :].rearrange("h p d -> p h d"), in_=vaug[:]),
                               ["vaug"], ["V_" + c.name])
                    if lf is not None:
                        sc.op("pe", lambda e: e.transpose(out=ps_lf[0:8, 0:rows], in_=lf, identity=ident_f[0:rows, 0:rows]),
                              [srcn, "ident_f"], ["ps_lf"])
                        if rows < 128:
                            sc.op("dve", lambda e: e.memset(stg_l[:], 0.0), [], ["stg_l"])
                        sc.op("dve", lambda e: e.tensor_copy(out=stg_l[:, 0:rows], in_=ps_lf[0:8, 0:rows]), ["ps_lf"], ["stg_l"])
                        sc.dma("sp", lambda e: e.dma_start(out=c.LF[:, cs], in_=stg_l[:]), ["stg_l"], ["LF_" + c.name])
                    if kcmp is not None:
                        sc.op("dve", lambda e: e.tensor_copy(out=nb[0:rows, 0:128], in_=kcmp), [srcn], ["nb"])
                        sc.op("dve", lambda e: e.tensor_copy(out=nb[0:rows, 128:256], in_=vcmp), [srcn], ["nb"])
                        sc.op("dve", lambda e: e.tensor_copy(out=nb[0:rows, 256:384], in_=ksel), [srcn], ["nb"])
                        tr_heads(nb[:, 0:256], "nb", rows, 2, ps_nt, "ps_nt", stg_x, "stg_x", 0, 128)
                        sc.dma("sp", lambda e: e.dma_start(out=c.XC[:, :, cs].rearrange("k p t -> p k t"), in_=stg_x[:]),
                               ["stg_x"], ["XC_" + c.name])
                        tr_heads(nb[:, 256:384], "nb", rows, 2, ps_nt, "ps_nt", stg_n, "stg_n", 0, 64)
                        sc.dma("sp", lambda e: e.dma_start(out=c.KS[:, :, cs].rearrange("g p t -> p g t"), in_=stg_n[:, 0:2, :]),
                               ["stg_n"], ["KS_" + c.name])
                        if rows < 128:
                            sc.op("dve", lambda e: e.memset(vsaug[:, :, 0:64], 0.0), [], ["vsaug"])
                        sc.op("act", lambda e: e.activation(out=vsaug[0:rows, :, 0:64], in_=vsel.rearrange("p (h d) -> p h d", d=64), func=AF.Copy),
                              [srcn], ["vsaug"])
                        sc.dma("sp", lambda e: e.dma_start(out=c.VS[:, :, blk, :].rearrange("h p d -> p h d"), in_=vsaug[:]),
                               ["vsaug"], ["VS_" + c.name])
                    if kwin is not None:
                        sc.op("dve", lambda e: e.tensor_copy(out=nb[0:rows, 512:640], in_=kwin), [srcn], ["nb"])
                        tr_heads(nb[:, 512:640], "nb", rows, 2, ps_nt, "ps_nt", stg_n, "stg_n", 2, 64)
                        sc.dma("sp", lambda e: e.dma_start(out=c.KW[:, 0:64, cs].rearrange("g p t -> p g t"), in_=stg_n[:, 2:4, :]),
                               ["stg_n"], ["KW_" + c.name])
                        if rows < 128:
                            sc.op("dve", lambda e: e.memset(vwaug[:, :, 0:64], 0.0), [], ["vwaug"])
                        sc.op("act", lambda e: e.activation(out=vwaug[0:rows, :, 0:64], in_=vwin.rearrange("p (h d) -> p h d", d=64), func=AF.Copy),
                              [srcn], ["vwaug"])
                        sc.dma("sp", lambda e: e.dma_start(out=c.VW[:, :, blk, :].rearrange("h p d -> p h d"), in_=vwaug[:]),
                               ["vwaug"], ["VW_" + c.name])

                def proj_tile(x_ap, rows, G1, B1, g1n, b1n, own, outs, c, blk, qcol0, gb_dst, gbn):
                    sc.dma("sp", lambda e: e.dma_start(out=xt[0:rows, :], in_=x_ap), [], ["xt"])
                    sc.op("act", lambda e: e.activation(out=junk[0:rows, :], in_=xt[0:rows, :], func=AF.Square, accum_out=ssum[0:rows, :]),
                          ["xt"], ["junk", "ssum"])
                    sc.op("dve", lambda e: e.tensor_scalar(out=ssum[0:rows, :], in0=ssum[0:rows, :], scalar1=1.0 / D, scalar2=EPS,
                                                           op0=ALU.mult, op1=ALU.add), ["ssum"], ["ssum"])
                    sc.op("pool", lambda e: e.tensor_tensor(out=rstd[0:rows, :], in0=ssum[0:rows, :], in1=neghalf[0:rows, 0:1], op=ALU.pow),
                          ["ssum", "neghalf"], ["rstd"])
                    sc.op("dve", lambda e: e.scalar_tensor_tensor(out=tmpf[0:rows, :], in0=xt[0:rows, :], scalar=rstd[0:rows, :], in1=G1,
                                                                  op0=ALU.mult, op1=ALU.mult), ["xt", "rstd", g1n], ["tmpf"])
                    sc.op("dve", lambda e: e.tensor_tensor(out=hb[0:rows, :], in0=tmpf[0:rows, :], in1=B1, op=ALU.add),
                          ["tmpf", b1n], ["hb"])
                    for k in range(8):
                        sc.op("pe", lambda e, k=k: e.transpose(out=ps_tr[:, k * 128:k * 128 + rows], in_=hb[0:rows, k * 128:(k + 1) * 128],
                                                               identity=ident_b[0:rows, 0:rows]), ["hb", "ident_b"], ["ps_tr"])
                    sc.op("act", lambda e: e.activation(out=hT[:, :, 0:rows], in_=ps_tr[:].rearrange("p (k t) -> p k t", t=128)[:, :, 0:rows],
                                                        func=AF.Copy), ["ps_tr"], ["hT"])
                    yield

                    def mm(pst, pname, c0, n, o0=0):
                        for k in range(8):
                            sc.op("pe", lambda e, k=k: e.matmul(pst[0:rows, o0:o0 + n], lhsT=hT[:, k, 0:rows], rhs=w_in_b[:, k, c0:c0 + n],
                                                                start=(k == 0), stop=(k == 7)), ["hT", "w_in_b"], [pname])
                    mm(ps_a, "ps_a", O_KA, 512)
                    mm(ps_b, "ps_b", O_VA, 512)
                    mm(ps_c, "ps_c", O_ZKV, 512)
                    mm(ps_d, "ps_d", O_ZKV + 512, 256)
                    mm(ps_d, "ps_d", O_ZF, 8, 256)
                    headnorm(ps_a[0:rows, :], 8, gk_b[0:rows], kvout[0:rows, 0:512].rearrange("p (h d) -> p h d", d=64), "kvout", rows, "ps_a")
                    sc.op("act", lambda e: e.activation(out=kvout[0:rows, 512:1024], in_=ps_b[0:rows, :], func=AF.Copy), ["ps_b"], ["kvout"])
                    sc.op("act", lambda e: e.activation(out=nkvout[0:rows, 0:512], in_=ps_c[0:rows, :], func=AF.Copy), ["ps_c"], ["nkvout"])
                    sc.op("act", lambda e: e.activation(out=nkvout[0:rows, 512:768], in_=ps_d[0:rows, 0:256], func=AF.Copy), ["ps_d"], ["nkvout"])
                    headnorm(ps_c[0:rows, 256:384], 2, gsel_b[0:rows], nkvout[0:rows, 256:384].rearrange("p (h d) -> p h d", d=64), "nkvout", rows, "ps_c")
                    headnorm(ps_d[0:rows, 0:128], 2, gwin_b[0:rows], nkvout[0:rows, 512:640].rearrange("p (h d) -> p h d", d=64), "nkvout", rows, "ps_d")
                    sc.op("dve", lambda e: e.tensor_tensor(out=lfo[0:rows, :], in0=ps_d[0:rows, 256:264], in1=bf_b[0:rows, :], op=ALU.add),
                          ["ps_d", "bf_b"], ["lfo"])
                    sc.op("act", lambda e: e.activation(out=lfo[0:rows, :], in_=lfo[0:rows, :], func=AF.Exp, scale=-1.0), ["lfo"], ["lfo"])
                    sc.op("act", lambda e: e.activation(out=lfo[0:rows, :], in_=lfo[0:rows, :], func=AF.Ln, bias=1.0), ["lfo"], ["lfo"])
                    sc.op("dve", lambda e: e.tensor_scalar(out=lfo[0:rows, :], in0=lfo[0:rows, :], scalar1=-1.0, scalar2=None, op0=ALU.mult),
                          ["lfo"], ["lfo"])
                    if outs is not None:
                        o_kv, o_lf, o_nkv, o_win = outs
                        sc.dma("pool", lambda e: e.dma_start(out=o_kv, in_=kvout[0:rows, :]), ["kvout"], [])
                        sc.dma("pool", lambda e: e.dma_start(out=o_lf, in_=lfo[0:rows, :]), ["lfo"], [])
                        sc.dma("pool", lambda e: e.dma_start(out=o_nkv, in_=nkvout[0:rows, 0:512]), ["nkvout"], [])
                        for oo in o_win:
                            sc.dma("pool", lambda e, oo=oo: e.dma_start(out=oo, in_=nkvout[0:rows, 512:768]), ["nkvout"], [])
                    kside_store(c, blk, rows, "kvout", fk=kvout[0:rows, 0:512], fv=kvout[0:rows, 512:1024], lf=lfo[0:rows, :])
                    kside_store(c, blk, rows, "nkvout", kcmp=nkvout[0:rows, 0:128], vcmp=nkvout[0:rows, 128:256], ksel=nkvout[0:rows, 256:384],
                                vsel=nkvout[0:rows, 384:512], kwin=nkvout[0:rows, 512:640], vwin=nkvout[0:rows, 640:768])
                    if own:
                        mm(ps_a, "ps_a", O_QA, 512)
                        mm(ps_b, "ps_b", O_QB, 512)
                        mm(ps_c, "ps_c", O_ZG, 24)
                        headnorm(ps_a[0:rows, :], 8, gq_b[0:rows], qf[0:rows, :].rearrange("p (h d) -> p h d", d=64), "qf", rows, "ps_a")
                        sc.op("dve", lambda e: e.tensor_copy(out=kb[0:rows, :], in_=qf[0:rows, :]), ["qf"], ["kb"])
                        tr_heads(kb, "kb", rows, 8, ps_kt, "ps_kt", stg_k, "stg_k")
                        sc.dma("sp", lambda e: e.dma_start(out=c.QT[0:64, :, qcol0:qcol0 + rows], in_=stg_k[:, :, 0:rows]),
                               ["stg_k"], ["QT_" + c.name])
                        headnorm(ps_b[0:rows, :], 8, gnq_b[0:rows], qf[0:rows, :].rearrange("p (h d) -> p h d", d=64), "qf", rows, "ps_b")
                        sc.op("dve", lambda e: e.tensor_copy(out=kb[0:rows, :], in_=qf[0:rows, :]), ["qf"], ["kb"])
                        tr_heads(kb, "kb", rows, 8, ps_kt, "ps_kt", stg_k, "stg_k")
                        sc.dma("sp", lambda e: e.dma_start(out=c.QN[0:64, :, qcol0:qcol0 + rows], in_=stg_k[:, :, 0:rows]),
                               ["stg_k"], ["QN_" + c.name])
                        sc.op("act", lambda e: e.activation(out=gb_dst, in_=ps_c[0:rows, 0:24], func=AF.Sigmoid), ["ps_c"], [gbn])
                        if cfg.dbg_gate is not None:
                            sc.op("dve", lambda e: e.memset(gb_dst, 0.0), [gbn], [gbn])
                            sc.op("dve", lambda e: e.memset(gb_dst.rearrange("p (h i) -> p h i", i=3)[:, :, cfg.dbg_gate], 1.0), [gbn], [gbn])


                sc.local = None
                return proj_tile, kside_store
            setsA = [make_setA("_a0"), make_setA("_a1")]
            tcount = {"n": 0}

            pendA = {"g": None, "sfx": None}

            def flush_pendingA():
                if pendA["g"] is not None:
                    sc.local, sc.sfx = LOCAL_A, pendA["sfx"]
                    for _ in pendA["g"]:
                        pass
                    sc.local = None
                    pendA["g"] = None

            def proj_tile(*a, **k):
                i = tcount["n"] % 2
                tcount["n"] += 1
                sc.local, sc.sfx = LOCAL_A, f"_a{i}"
                g_ = setsA[i][0](*a, **k)
                next(g_)
                sc.local = None
                flush_pendingA()
                pendA["g"], pendA["sfx"] = g_, f"_a{i}"

            def kside_store(*a, **k):
                sc.local, sc.sfx = LOCAL_A, "_a0"
                setsA[0][1](*a, **k)
                sc.local = None
            load_mod(MpA, 128, 0, (1, 0, None), 0, ps_a, "ps_a")
            for f in range(NB):
                own = (f % 8 == 7)
                j = f // 8
                outs = None
                if own:
                    outs = (o_fkv_p[j * 128:(j + 1) * 128, :], o_lf_p[j * 128:(j + 1) * 128, :],
                            o_nkv_p[j * 128:(j + 1) * 128, :], [o_win_p[j * 128:(j + 1) * 128, :]])
                proj_tile(xf[f * 128:(f + 1) * 128, :], 128, MpA[:, 0, :], MpA[:, 1, :], "Mp", "Mp", own, outs, ctx_p, f, j * 128,
                          gb_res[:, j, :], "gb_res")
            for b in range(NSEQ):
                outs = (o_fkv_s[b * 8:(b + 1) * 8, :], o_lf_s[b * 8:(b + 1) * 8, :], o_nkv_s[b * 8:(b + 1) * 8, :],
                        [o_wnew_s[b * 8:(b + 1) * 8, :], o_win_s[b, WB - 8:WB, :]])
                load_mod(MpA, 8, 128 + 8 * b, (1, 0, None), 0, ps_a, "ps_a")
                proj_tile(xs[b * 8:(b + 1) * 8, :], 8, MpA[0:8, 0, :], MpA[0:8, 1, :], "Mp", "Mp", True, outs, ctx_s[b], NPG, 0,
                          gbs_res[:, b, :], "gbs_res")
                sc.dma("sp", lambda e, b=b: e.dma_start(out=o_win_s[b, 0:WB - 8, :], in_=win_in[b, 8:WB, :]), [], [])

            flush_pendingA()
            for b in range(NSEQ):
                c = ctx_s[b]
                for i in range(WB // 128):
                    sc.dma("sp", lambda e, b=b, i=i: e.dma_start(out=wpage[:], in_=win_in[b, i * 128:(i + 1) * 128, :]), [], ["wpage"])
                    kside_store(c, NPG - WB // 128 + i, 128, "wpage", kwin=wpage[:, 0:128], vwin=wpage[:, 128:256])
            sc.flush(st)
        stW.close()

        with contextlib.ExitStack() as stG:
            def sbG(name, shape, dt=F32):
                return stG.enter_context(nc.sbuf_tensor("G_" + name, list(shape), dt))

            def psG(name, shape, dt=F32):
                return stG.enter_context(nc.psum_tensor("G_" + name, list(shape), dt))
            ptb_i = sbG("ptb_i", [128, NSEQ * NPG], I32)
            ptb_f = sbG("ptb_f", [128, NSEQ * NPG])
            idx_i = sbG("idx_i", [128, NSEQ * NPG], I32)
            iota_p = sbG("iota_p", [128, 1])
            sc.dma("sp", lambda e: e.dma_start(out=iota_p[:], in_=iota_in[:, :]), [], ["iota_p"])
            sc.dma("sp", lambda e: e.dma_start(out=ptb_i[:], in_=ptab.rearrange("b n -> (b n)").unsqueeze(0).partition_broadcast(128)),
                   [], ["ptb_i"])
            sc.op("dve", lambda e: e.tensor_copy(out=ptb_f[:], in_=ptb_i[:]), ["ptb_i"], ["ptb_f"])
            sc.op("dve", lambda e: e.tensor_scalar(out=ptb_f[:], in0=ptb_f[:], scalar1=128.0, scalar2=iota_p[:, 0:1],
                                                   op0=ALU.mult, op1=ALU.add), ["ptb_f", "iota_p"], ["ptb_f"])
            sc.op("dve", lambda e: e.tensor_copy(out=idx_i[:], in_=ptb_f[:]), ["ptb_f"], ["idx_i"])
            NPAR = 3
            GP = min(4, NPG)
            TS = []
            for p in range(NPAR):
                T = {}
                T["fpage"] = sbG(f"fpage{p}", [128, 1024]); T["npage"] = sbG(f"npage{p}", [128, 512]); T["lpage"] = sbG(f"lpage{p}", [128, 8])
                T["kb"] = sbG(f"kb{p}", [128, 512], BF16); T["nb"] = sbG(f"nb{p}", [128, 384], BF16)
                TS.append(T)
            GS = []
            for p in range(2):
                Gt = {}
                Gt["stg_k"] = sbG(f"stg_k{p}", [128, 4, GP * 128], BF16); Gt["stg_x"] = sbG(f"stg_x{p}", [128, 3, GP * 128], BF16)
                Gt["stg_l"] = sbG(f"stg_l{p}", [8, GP * 128])
                Gt["vaug"] = sbG(f"vaug{p}", [128, 8, GP, 65], BF16); Gt["vsaug"] = sbG(f"vsaug{p}", [128, 2, GP, 65], BF16)
                sc.op("dve", lambda e, Gt=Gt: e.memset(Gt["vaug"][:], 1.0), [], [f"vaug{p}"])
                sc.op("dve", lambda e, Gt=Gt: e.memset(Gt["vsaug"][:], 1.0), [], [f"vsaug{p}"])
                GS.append(Gt)
            PSG = []
            for p in range(2):
                PSG.append({"kt": psG(f"ps_kt{p}", [128, 8, 128], BF16)[:, 0:4, :], "nt": psG(f"ps_nt{p}", [128, 8, 128], BF16)[:, 0:3, :],
                            "lf": psG(f"ps_lf{p}", [128, 512])})
            pcount = 0
            gcount = 0
            for b in range(NSEQ):
                c = ctx_s[b]
                nm = c.name
                for pg0 in range(0, NPG, GP):
                    gq = gcount % 2
                    gcount += 1
                    Gt = GS[gq]

                    def RG(x, gq=gq):
                        return f"{x}{gq}"
                    for sl in range(GP):
                        pg = pg0 + sl
                        p = pcount % NPAR
                        q = pcount % 2
                        pcount += 1
                        T = TS[p]
                        P = PSG[q]
                        col = b * NPG + pg
                        ts_ = slice(sl * 128, (sl + 1) * 128)

                        def R(x, p=p):
                            return f"{x}{p}"

                        def Q(x, q=q):
                            return f"G{x}{q}"
                        sc.dma("pool", lambda e, col=col, T=T: e.indirect_dma_start(
                            out=T["fpage"][:, :], out_offset=None, in_=pool_fox[:, :],
                            in_offset=bass.IndirectOffsetOnAxis(ap=idx_i[:, col:col + 1], axis=0)), ["idx_i"], [R("fpage")])
                        sc.dma("pool", lambda e, col=col, T=T: e.indirect_dma_start(
                            out=T["npage"][:, :], out_offset=None, in_=pool_nsa[:, :],
                            in_offset=bass.IndirectOffsetOnAxis(ap=idx_i[:, col:col + 1], axis=0)), ["idx_i"], [R("npage")])
                        sc.dma("pool", lambda e, col=col, T=T: e.indirect_dma_start(
                            out=T["lpage"][:, :], out_offset=None, in_=pool_lf[:, :],
                            in_offset=bass.IndirectOffsetOnAxis(ap=idx_i[:, col:col + 1], axis=0)), ["idx_i"], [R("lpage")])
                        sc.op("dve", lambda e, T=T: e.tensor_copy(out=T["kb"][:], in_=T["fpage"][:, 0:512]), [R("fpage")], [R("kb")])
                        for a in range(4):
                            sc.op("pe", lambda e, a=a, T=T, P=P: e.transpose(out=P["kt"][:, a, :], in_=T["kb"][:, a * 128:(a + 1) * 128], identity=ident_b[:]),
                                  [R("kb"), "ident_b"], [Q("kt")])
                        sc.op("act", lambda e, Gt=Gt, P=P, ts_=ts_: e.activation(out=Gt["stg_k"][:, :, ts_], in_=P["kt"][:], func=AF.Copy),
                              [Q("kt")], [RG("stg_k")])
                        sc.op("act", lambda e, T=T, Gt=Gt, sl=sl: e.activation(out=Gt["vaug"][:, :, sl, 0:64],
                                                                              in_=T["fpage"][:, 512:1024].rearrange("p (h d) -> p h d", d=64),
                                                                              func=AF.Copy), [R("fpage")], [RG("vaug")])
                        sc.op("pe", lambda e, T=T, P=P: e.transpose(out=P["lf"][0:8, 0:128], in_=T["lpage"][:, :], identity=ident_f[:]),
                              [R("lpage"), "ident_f"], [Q("lf")])
                        sc.op("dve", lambda e, Gt=Gt, P=P, ts_=ts_: e.tensor_copy(out=Gt["stg_l"][:, ts_], in_=P["lf"][0:8, 0:128]), [Q("lf")], [RG("stg_l")])
                        sc.op("dve", lambda e, T=T: e.tensor_copy(out=T["nb"][:], in_=T["npage"][:, 0:384]), [R("npage")], [R("nb")])
                        for a in range(3):
                            sc.op("pe", lambda e, a=a, T=T, P=P: e.transpose(out=P["nt"][:, a, :], in_=T["nb"][:, a * 128:(a + 1) * 128], identity=ident_b[:]),
                                  [R("nb"), "ident_b"], [Q("nt")])
                        sc.op("act", lambda e, Gt=Gt, P=P, ts_=ts_: e.activation(out=Gt["stg_x"][:, :, ts_], in_=P["nt"][:], func=AF.Copy),
                              [Q("nt")], [RG("stg_x")])
                        sc.op("act", lambda e, T=T, Gt=Gt, sl=sl: e.activation(out=Gt["vsaug"][:, :, sl, 0:64],
                                                                              in_=T["npage"][:, 384:512].rearrange("p (h d) -> p h d", d=64),
                                                                              func=AF.Copy), [R("npage")], [RG("vsaug")])
                    cs = slice(pg0 * 128, (pg0 + GP) * 128)
                    ktv = c.KT[0:64, :, cs].rearrange("p (a two) t -> p a two t", two=2)
                    sc.dma("sp", lambda e, Gt=Gt, ktv=ktv: e.dma_start(out=ktv[:, :, 0, :], in_=Gt["stg_k"][0:64, :, :]), [RG("stg_k")], ["KT_" + nm])
                    sc.dma("sp", lambda e, Gt=Gt, ktv=ktv: e.dma_start(out=ktv[:, :, 1, :], in_=Gt["stg_k"][64:128, :, :]), [RG("stg_k")], ["KT_" + nm])
                    sc.dma("sp", lambda e, Gt=Gt, c=c, pg0=pg0: e.dma_start(out=c.V[:, :, pg0:pg0 + GP, :].rearrange("h p b d -> p h b d"), in_=Gt["vaug"][:]),
                           [RG("vaug")], ["V_" + nm])
                    sc.dma("sp", lambda e, Gt=Gt, c=c, cs=cs: e.dma_start(out=c.LF[:, cs], in_=Gt["stg_l"][:]), [RG("stg_l")], ["LF_" + nm])
                    sc.dma("sp", lambda e, Gt=Gt, c=c, cs=cs: e.dma_start(out=c.XC[:, :, cs].rearrange("k p t -> p k t"), in_=Gt["stg_x"][:, 0:2, :]),
                           [RG("stg_x")], ["XC_" + nm])
                    sc.dma("sp", lambda e, Gt=Gt, c=c, cs=cs: e.dma_start(out=c.KS[0, :, cs], in_=Gt["stg_x"][0:64, 2, :]), [RG("stg_x")], ["KS_" + nm])
                    sc.dma("sp", lambda e, Gt=Gt, c=c, cs=cs: e.dma_start(out=c.KS[1, :, cs], in_=Gt["stg_x"][64:128, 2, :]), [RG("stg_x")], ["KS_" + nm])
                    sc.dma("sp", lambda e, Gt=Gt, c=c, pg0=pg0: e.dma_start(out=c.VS[:, :, pg0:pg0 + GP, :].rearrange("h p b d -> p h b d"), in_=Gt["vsaug"][:]),
                           [RG("vsaug")], ["VS_" + nm])
            sc.flush(st)

        with contextlib.ExitStack() as stC:
            def sbC(name, shape, dt=F32):
                return stC.enter_context(nc.sbuf_tensor(name, list(shape), dt))
            LsM = max(S, LS) // 16
            lf_f = sbC("lf_f", [128, LsM])
            tv_f = sbC("tv_f", [128, LsM])
            pn_f = sbC("pn_f", [128, LsM])
            cum_f = sbC("cum_f", [128, LsM])
            r_f = sbC("r_f", [128, LsM])
            hi_b = sbC("hi_b", [128, LsM], BF16)
            mid_b = sbC("mid_b", [128, LsM], BF16)
            lo_b = sbC("lo_b", [128, LsM], BF16)
            tot = sbC("tot", [128, 1])
            offs = sbC("offs", [128, 1])
            btm = sbC("btm", [128, 128])
            NQM = max(NQ, 8)
            cq = sbC("cq", [8, NQM])
            cqr = sbC("cqr", [8, NQM])
            cq_b = sbC("cq_b", [8, 3, NQM], BF16)
            kwn_f = sbC("kwn_f", [128, max(S, LS) // 128])
            kwn_b = sbC("kwn_b", [128, max(S, LS) // 128], BF16)
            ps_o = stC.enter_context(nc.psum_tensor("ps_o", [128, 8], F32))
            ones_c = sbC("ones_c", [128, LsM], BF16)
            ones_q = sbC("ones_q", [8, 3, NQM], BF16)
            sc.op("dve", lambda e: e.memset(ones_c[:], 1.0), [], ["ones_c"])
            sc.op("dve", lambda e: e.memset(ones_q[:], 1.0), [], ["ones_q"])
            sc.dma("sp", lambda e: e.dma_start(out=btm[:], in_=bt_in[:, :]), [], ["btm"])

            def crows(c, tv_in, pn_in, kwn_in, qsel):
                Ls = c.L // 16
                nm = c.name
                sc.dma("sp", lambda e: e.dma_start(out=lf_f[:, 0:Ls], in_=c.LF.rearrange("h (s t) -> (h s) t", s=16)), ["LF_" + nm], ["lf_f"])
                sc.dma("sp", lambda e: e.dma_start(out=tv_f[:, 0:Ls], in_=tv_in), [], ["tv_f"])
                sc.dma("sp", lambda e: e.dma_start(out=pn_f[:, 0:Ls], in_=pn_in), [], ["pn_f"])
                sc.op("dve", lambda e: e.tensor_tensor(out=lf_f[:, 0:Ls], in0=lf_f[:, 0:Ls], in1=tv_f[:, 0:Ls], op=ALU.mult),
                      ["lf_f", "tv_f"], ["lf_f"])
                sc.op("dve", lambda e: e.memset(r_f[:, 0:Ls], 1.0), [], ["r_f"])
                sc.op("dve", lambda e: e.tensor_tensor_scan(out=cum_f[:, 0:Ls], data0=r_f[:, 0:Ls], data1=lf_f[:, 0:Ls], initial=0.0,
                                                            op0=ALU.mult, op1=ALU.add), ["r_f", "lf_f"], ["cum_f"])
                sc.op("dve", lambda e: e.tensor_copy(out=tot[:], in_=cum_f[:, Ls - 1:Ls]), ["cum_f"], ["tot"])
                sc.op("pe", lambda e: e.matmul(ps_o[:, 0:1], lhsT=btm[:], rhs=tot[:], start=True, stop=True), ["btm", "tot"], ["ps_o"])
                sc.op("dve", lambda e: e.tensor_copy(out=offs[:], in_=ps_o[:, 0:1]), ["ps_o"], ["offs"])
                sc.op("dve", lambda e: e.tensor_scalar(out=cum_f[:, 0:Ls], in0=cum_f[:, 0:Ls], scalar1=offs[:, 0:1], scalar2=None, op0=ALU.add),
                      ["cum_f", "offs"], ["cum_f"])
                sc.dma("sp", lambda e: e.dma_start(out=c.CUM.rearrange("h (s t) -> (h s) t", s=16), in_=cum_f[:, 0:Ls]), ["cum_f"], ["CUM_" + nm])
                sc.op("dve", lambda e: e.scalar_tensor_tensor(out=r_f[:, 0:Ls], in0=cum_f[:, 0:Ls], scalar=-1.0, in1=pn_f[:, 0:Ls],
                                                              op0=ALU.mult, op1=ALU.add), ["cum_f", "pn_f"], ["r_f"])
                sc.op("dve", lambda e: e.tensor_copy(out=hi_b[:, 0:Ls], in_=r_f[:, 0:Ls]), ["r_f"], ["hi_b"])
                sc.op("dve", lambda e: e.tensor_tensor(out=r_f[:, 0:Ls], in0=r_f[:, 0:Ls], in1=hi_b[:, 0:Ls], op=ALU.subtract), ["r_f", "hi_b"], ["r_f"])
                sc.op("dve", lambda e: e.tensor_copy(out=mid_b[:, 0:Ls], in_=r_f[:, 0:Ls]), ["r_f"], ["mid_b"])
                sc.op("dve", lambda e: e.tensor_tensor(out=r_f[:, 0:Ls], in0=r_f[:, 0:Ls], in1=mid_b[:, 0:Ls], op=ALU.subtract), ["r_f", "mid_b"], ["r_f"])
                sc.op("dve", lambda e: e.tensor_copy(out=lo_b[:, 0:Ls], in_=r_f[:, 0:Ls]), ["r_f"], ["lo_b"])
                for i, (t, tn) in enumerate([(hi_b, "hi_b"), (mid_b, "mid_b"), (lo_b, "lo_b")]):
                    sc.dma("sp", lambda e, i=i, t=t: e.dma_start(out=c.KT[67 + i, :, :].rearrange("h (s t) -> (h s) t", s=16), in_=t[:, 0:Ls]),
                           [tn], ["KT_" + nm])
                    sc.dma("sp", lambda e, i=i: e.dma_start(out=c.KT[64 + i, :, :].rearrange("h (s t) -> (h s) t", s=16),
                                                            in_=ones_c[:, 0:Ls]), ["ones_c"], ["KT_" + nm])
                Lb = c.L // 128
                sc.dma("sp", lambda e: e.dma_start(out=kwn_f[:, 0:Lb], in_=kwn_in), [], ["kwn_f"])
                sc.op("dve", lambda e: e.tensor_copy(out=kwn_b[:, 0:Lb], in_=kwn_f[:, 0:Lb]), ["kwn_f"], ["kwn_b"])
                for g in range(2):
                    sc.dma("sp", lambda e, g=g: e.dma_start(out=c.KW[g, 64, :].rearrange("(p t) -> p t", p=128), in_=kwn_b[:, 0:Lb]),
                           ["kwn_b"], ["KW_" + nm])
                nq = c.nqc
                sc.dma("sp", lambda e: e.dma_start(out=qsel(c.CUM)[1], in_=qsel(c.CUM)[0]), ["CUM_" + nm], ["cq"])
                sc.op("dve", lambda e: e.tensor_copy(out=cq_b[:, 0, 0:nq], in_=cq[:, 0:nq]), ["cq"], ["cq_b"])
                sc.op("dve", lambda e: e.tensor_tensor(out=cqr[:, 0:nq], in0=cq[:, 0:nq], in1=cq_b[:, 0, 0:nq], op=ALU.subtract), ["cq", "cq_b"], ["cqr"])
                sc.op("dve", lambda e: e.tensor_copy(out=cq_b[:, 1, 0:nq], in_=cqr[:, 0:nq]), ["cqr"], ["cq_b"])
                sc.op("dve", lambda e: e.tensor_tensor(out=cqr[:, 0:nq], in0=cqr[:, 0:nq], in1=cq_b[:, 1, 0:nq], op=ALU.subtract), ["cqr", "cq_b"], ["cqr"])
                sc.op("dve", lambda e: e.tensor_copy(out=cq_b[:, 2, 0:nq], in_=cqr[:, 0:nq]), ["cqr"], ["cq_b"])
                sc.dma("sp", lambda e: e.dma_start(out=c.QT[64:67, :, :].rearrange("a h n -> h a n"), in_=cq_b[:, :, 0:nq]), ["cq_b"], ["QT_" + nm])
                sc.dma("sp", lambda e: e.dma_start(out=c.QT[67:70, :, :].rearrange("a h n -> h a n"), in_=ones_q[:, :, 0:nq]), ["ones_q"], ["QT_" + nm])
                sc.dma("sp", lambda e: e.dma_start(out=c.QN[64, :, :], in_=ones_q[:, 0, 0:nq]), ["ones_q"], ["QN_" + nm])

            crows(ctx_p, tvp_in[:, :], pnp_in[:, :], kwnp_in[:, :],
                  lambda CUM: (CUM.rearrange("h (j e t) -> h j e t", e=8, t=128)[:, :, 7, :],
                               cq[:, 0:NQ].rearrange("h (j t) -> h j t", t=128)))
            for b in range(NSEQ):
                crows(ctx_s[b], tvs_in[:, :], pns_in[:, :], kwns_in[:, :], lambda CUM: (CUM[:, cfg.PAST:cfg.PAST + 8], cq[:, 0:8]))
            sc.flush(st)

        with contextlib.ExitStack() as stB:
            def sbB(name, shape, dt=F32):
                return stB.enter_context(nc.sbuf_tensor(name, list(shape), dt))
            LM = max(S, LS)
            KTt = [sbB(f"KTt{i}", [70, LM], BF16) for i in range(2)]
            Vt = [sbB(f"Vt{i}", [128, LM // 128, 65], BF16) for i in range(2)]
            QTt = [sbB(f"QTt{i}", [70, max(NQ, 8)], BF16) for i in range(2)]
            pT = [sbB(f"pT{i}", [128, 512], BF16) for i in range(2)]
            rec = sbB("rec", [128, 4])
            oa_res = sbB("oa_res", [128, NOWN, 512], BF16)
            oas8 = sbB("oas8", [8, 512], BF16)
            ps_s = [stB.enter_context(nc.psum_tensor(f"ps_s{i}", [128, 512], F32)) for i in range(2)]
            po = stB.enter_context(nc.psum_tensor("po", [128, 4, 65], F32))

            def attn(KT, KTn, Ka, V, Vn, QT, QTn, q0, w, nsub, blocks, dst, kb=1):
                last = {}
                for (f, jjmin, diag) in blocks:
                    for jj in range(jjmin, nsub):
                        last[jj] = f
                steps = []
                for i in range(0, len(blocks), kb):
                    grp = []
                    coff = 0
                    for (f, jjmin, diag) in blocks[i:i + kb]:
                        grp.append((f, jjmin, diag, coff))
                        coff += (nsub - jjmin) * w
                    steps.append((grp, coff))
                sc.op("pe", lambda e: e.matmul(po[0:w, 0:nsub, :], lhsT=zeros_b[:, 0:w], rhs=zeros_b[:, 0:nsub * 65].rearrange("p (j d) -> p j d", d=65),
                                               start=True, stop=False), ["zeros_b"], ["po"])

                def qk(t):
                    grp, n = steps[t]
                    pst = ps_s[t % 2]
                    pn = f"ps_s{t % 2}"
                    for (f, jjmin, diag, coff) in grp:
                        nb_ = (nsub - jjmin) * w
                        sc.op("pe", lambda e, f=f, jjmin=jjmin, coff=coff, nb_=nb_, diag=diag: e.matmul(
                            pst[:, coff:coff + nb_], lhsT=KT[0:Ka, f * 128:(f + 1) * 128],
                            rhs=QT[0:Ka, q0 + jjmin * w:q0 + nsub * w], start=True, stop=not diag), [KTn, QTn], [pn])
                        if diag:
                            sc.op("pe", lambda e, coff=coff: e.matmul(pst[:, coff:coff + w], lhsT=ident_b[:], rhs=tri_b[:, 0:w], start=False, stop=True),
                                  ["ident_b", "tri_b"], [pn])
                qk(0)
                for t, (grp, n) in enumerate(steps):
                    pst = ps_s[t % 2]
                    sc.op("act", lambda e, pst=pst, t=t, n=n: e.activation(out=pT[t % 2][:, 0:n], in_=pst[:, 0:n], func=AF.Exp),
                          [f"ps_s{t % 2}"], [f"pT{t % 2}"])
                    if t + 1 < len(steps):
                        qk(t + 1)
                    for (f, jjmin, diag, coff) in grp:
                        for jj in range(jjmin, nsub):
                            sc.op("pe", lambda e, t=t, jj=jj, jjmin=jjmin, f=f, coff=coff: e.matmul(
                                po[0:w, jj, :], lhsT=pT[t % 2][:, coff + (jj - jjmin) * w:coff + (jj - jjmin + 1) * w], rhs=V[:, f, :],
                                start=False, stop=(f == last[jj])), [f"pT{t % 2}", Vn], ["po"])
                sc.op("dve", lambda e: e.reciprocal(out=rec[0:w, 0:nsub], in_=po[0:w, 0:nsub, 64]), ["po"], ["rec"])
                sc.op("dve", lambda e: e.tensor_tensor(out=dst, in0=po[0:w, 0:nsub, 0:64],
                                                       in1=rec[0:w, 0:nsub].unsqueeze(2).to_broadcast([w, nsub, 64]), op=ALU.mult),
                      ["po", "rec"], ["attn_dst"])

            nsub = min(4, NOWN)
            hcount = 0

            def load_head(c, src_kt, src_v, src_qt, Ka, h, i):
                sc.dma("sp", lambda e: e.dma_start(out=KTt[i][0:Ka, 0:c.L], in_=src_kt), ["KT_" + c.name, "KS_" + c.name, "KW_" + c.name], [f"KTt{i}"])
                sc.dma("sp", lambda e: e.dma_start(out=Vt[i][:, 0:c.NBk, :], in_=src_v), ["V_" + c.name, "VS_" + c.name, "VW_" + c.name], [f"Vt{i}"])
                sc.dma("sp", lambda e: e.dma_start(out=QTt[i][0:Ka, 0:c.nqc], in_=src_qt), ["QT_" + c.name, "QN_" + c.name], [f"QTt{i}"])

            for h in range(8):
                i = hcount % 2
                hcount += 1
                load_head(ctx_p, ctx_p.KT[:, h, :], ctx_p.V[h], ctx_p.QT[:, h, :], 70, h, i)
                for J in range(NOWN // nsub):
                    Fs = [8 * (J * nsub + jj) + 7 for jj in range(nsub)]
                    blocks = []
                    for f in range(Fs[-1] + 1):
                        jjmin = min(jj for jj in range(nsub) if Fs[jj] >= f)
                        blocks.append((f, jjmin, f == Fs[jjmin]))
                    attn(KTt[i], f"KTt{i}", 70, Vt[i], f"Vt{i}", QTt[i], f"QTt{i}", J * nsub * 128, 128, nsub, blocks,
                         oa_res[:, J * nsub:(J + 1) * nsub, h * 64:(h + 1) * 64])
            for b in range(NSEQ):
                c = ctx_s[b]
                for h in range(8):
                    i = hcount % 2
                    hcount += 1
                    load_head(c, c.KT[:, h, :], c.V[h], c.QT[:, h, :], 70, h, i)
                    blocks = [(f, 0, f == NPG) for f in range(NPG + 1)]
                    attn(KTt[i], f"KTt{i}", 70, Vt[i], f"Vt{i}", QTt[i], f"QTt{i}", 0, 8, 1, blocks,
                         oas8[:, h * 64:(h + 1) * 64].unsqueeze(1), kb=16)
                sc.dma("sp", lambda e, b=b: e.dma_start(out=OAS[b * 8:(b + 1) * 8, :], in_=oas8[:]), ["attn_dst"], ["OAS"])
            for j in range(NOWN):
                sc.dma("sp", lambda e, j=j: e.dma_start(out=OAP[j * 128:(j + 1) * 128, :], in_=oa_res[:, j, :]), ["attn_dst"], ["OAP"])
                sc.dma("sp", lambda e, j=j: e.dma_start(out=dbg_oa_p[j * 128:(j + 1) * 128, :], in_=oa_res[:, j, :]), ["attn_dst"], [])
            sc.dma("sp", lambda e: e.dma_start(out=dbg_oa_s[:, :], in_=OAS[:, :]), ["OAS"], [])
            sc.flush(st)
        if cfg.nsa:
            LPC, OFFC, LP1, OFF1 = 4096, 1856, 768, 128
            BVC = dscr("BVC", [8, 128, LPC], BF16)
            BV1 = dscr("BV1", [8, 128, LP1], BF16)
            with contextlib.ExitStack() as stN:
                def sbN(name, shape, dt=F32):
                    return stN.enter_context(nc.sbuf_tensor("N_" + name, list(shape), dt))

                def psN(name, shape, dt=F32):
                    return stN.enter_context(nc.psum_tensor("N_" + name, list(shape), dt))

                def Kof(F, w):
                    nlast = (128 * F + (w - 1) - 31) // 16
                    tb = nlast // 128
                    return 128 * F - 2048 * tb - 31, tb
                Kvars = []
                for jo in range(NOWN):
                    k_, _ = Kof(8 * jo + 7, 128)
                    if k_ not in Kvars:
                        Kvars.append(k_)
                ks_, _ = Kof(NPG, 8)
                if ks_ not in Kvars:
                    Kvars.append(ks_)
                NV = len(Kvars)
                TCB = sbN("TCB", [128, NV, 8, 128], BF16)
                TSW = sbN("TSW", [128, 5, 8, 128], BF16)
                EEB = dscr("EEB", [128, 8192], BF16)
                gkc = sbN("gkc", [64, 1])
                hbias2 = sbN("hbias2", [128, 2])
                w2b2 = sbN("w2b2", [128, 2, 64], BF16)
                W1B = dscr("W1B", [2, 128, 32, 128], BF16)
                with contextlib.ExitStack() as st0n:
                    def sb0n(name, shape, dt=F32):
                        return st0n.enter_context(nc.sbuf_tensor("N0_" + name, list(shape), dt))
                    rb = sb0n("rb", [33, 8]); rb31 = sb0n("rb31", [32, 8])
                    ohc_sb = sb0n("ohc_sb", [33, LPC]); oh1_sb = sb0n("oh1_sb", [33, LP1])
                    vrow = sb0n("vrow", [8, LPC]); vrow_b = sb0n("vrow_b", [8, LPC], BF16)
                    eestg = sb0n("eestg", [128, 2048])
                    eeb = sb0n("eeb", [128, 2048], BF16)
                    ps_v0 = st0n.enter_context(nc.psum_tensor("N0_ps_v0", [8, 512], F32))
                    sc.dma("sp", lambda e: e.dma_start(out=rb[0:32, :], in_=rel_bias[:, :]), [], ["rb"])
                    sc.dma("sp", lambda e: e.dma_start(out=rb31[:], in_=rel_bias[31:32, :].partition_broadcast(32)), [], ["rb31"])
                    sc.dma("sp", lambda e: e.dma_start(out=ohc_sb[:], in_=ohc_in[:, :]), [], ["ohc_sb"])
                    sc.dma("sp", lambda e: e.dma_start(out=oh1_sb[:], in_=oh1_in[:, :]), [], ["oh1_sb"])
                    sc.dma("sp", lambda e: e.dma_start(out=gkc[:], in_=g_qk_nsa[1:2, :].rearrange("a d -> d a"), allow_slow_non_contiguous=True), [], ["gkc"])
                    sc.op("dve", lambda e: e.tensor_tensor(out=rb[0:32, :], in0=rb[0:32, :], in1=rb31[:], op=ALU.subtract), ["rb", "rb31"], ["rb"])
                    sc.op("dve", lambda e: e.memset(rb[32:33, :], NEG), ["rb"], ["rb"])
                    for q in range(4):
                        sc.dma("sp", lambda e, q=q: e.dma_start(out=eestg[:], in_=ee_in[:, q * 2048:(q + 1) * 2048]), [], ["eestg"])
                        sc.op("pool", lambda e, q=q: e.tensor_copy(out=eeb[:], in_=eestg[:]), ["eestg"], ["eeb"])
                        sc.dma("sp", lambda e, q=q: e.dma_start(out=EEB[:, q * 2048:(q + 1) * 2048], in_=eeb[:]), ["eeb"], ["EEB"])

                    def mk_v(oh_sb, ohn, Lp, BV, bvn):
                        for c0 in range(0, Lp, 512):
                            n = min(512, Lp - c0)
                            sc.op("pe", lambda e, c0=c0, n=n: e.matmul(ps_v0[0:8, 0:n], lhsT=rb[0:33, 0:8], rhs=oh_sb[0:33, c0:c0 + n],
                                                                       start=True, stop=True), ["rb", ohn], ["ps_v0"])
                            sc.op("dve", lambda e, c0=c0, n=n: e.tensor_copy(out=vrow[:, c0:c0 + n], in_=ps_v0[0:8, 0:n]), ["ps_v0"], ["vrow"])
                        sc.op("dve", lambda e: e.tensor_copy(out=vrow_b[:, 0:Lp], in_=vrow[:, 0:Lp]), ["vrow"], ["vrow_b"])
                        for r0 in range(0, 128, 16):
                            sc.dma("sp", lambda e, r0=r0: e.dma_start(out=BV[:, r0:r0 + 16, :], in_=vrow_b[:, 0:Lp].unsqueeze(1).to_broadcast([8, 16, Lp])),
                                   ["vrow_b"], [bvn])
                    w1f = sb0n("w1f", [128, 32, 128]); w1bb = sb0n("w1bb", [128, 32, 128], BF16)
                    pef = sb0n("pef", [32, 64]); pebb = sb0n("pebb", [64, 32], BF16)
                    w2f = sb0n("w2f", [128, 64])
                    ps_w = st0n.enter_context(nc.psum_tensor("N0_ps_w", [128, 512], F32))
                    for kv in range(2):
                        for half in range(2):
                            sc.dma("sp", lambda e, kv=kv, half=half: e.dma_start(
                                out=w1f[half * 64:(half + 1) * 64], in_=w_cmp1[kv].rearrange("(r d) h -> d r h", d=64)), [], ["w1f"])
                        sc.op("pool", lambda e: e.tensor_copy(out=w1bb[:], in_=w1f[:]), ["w1f"], ["w1bb"])
                        sc.dma("sp", lambda e, kv=kv: e.dma_start(out=W1B[kv], in_=w1bb[:]), ["w1bb"], ["W1B"])
                        sc.dma("sp", lambda e, kv=kv: e.dma_start(out=pef[:], in_=pe_cmp[kv]), [], ["pef"])
                        sc.op("pe", lambda e: e.transpose(out=ps_w[0:64, 0:32], in_=pef[0:32, 0:64], identity=ident_f[0:32, 0:32]),
                              ["pef", "ident_f"], ["ps_w"])
                        sc.op("dve", lambda e: e.tensor_copy(out=pebb[:], in_=ps_w[0:64, 0:32]), ["ps_w"], ["pebb"])
                        for r in range(32):
                            sc.op("pe", lambda e, r=r: e.matmul(ps_w[:, 64:65], lhsT=w1bb[0:64, r, :], rhs=pebb[0:64, r:r + 1],
                                                                start=(r == 0), stop=(r == 31)), ["w1bb", "pebb"], ["ps_w"])
                        sc.op("dve", lambda e, kv=kv: e.tensor_copy(out=hbias2[:, kv:kv + 1], in_=ps_w[:, 64:65]), ["ps_w"], ["hbias2"])
                        sc.dma("sp", lambda e, kv=kv: e.dma_start(out=w2f[:], in_=w_cmp2[kv]), [], ["w2f"])
                        sc.op("dve", lambda e, kv=kv: e.tensor_copy(out=w2b2[:, kv, :], in_=w2f[:]), ["w2f"], ["w2b2"])
                    mk_v(ohc_sb, "ohc_sb", LPC, BVC, "BVC")
                    for vi, Kv in enumerate(Kvars):
                        for h in range(8):
                            sc.dma("sp", lambda e, vi=vi, Kv=Kv, h=h: e.dma_start(
                                out=TCB[:, vi, h, :], in_=bass.AP(tensor=BVC.tensor, offset=h * 128 * LPC + Kv + OFFC, ap=[[LPC - 16, 128], [1, 128]])),
                                ["BVC"], ["TCB"])
                    mk_v(oh1_sb, "oh1_sb", LP1, BV1, "BV1")
                    for di in range(5):
                        for h in range(8):
                            sc.dma("sp", lambda e, di=di, h=h: e.dma_start(
                                out=TSW[:, di, h, :], in_=bass.AP(tensor=BV1.tensor, offset=h * 128 * LP1 + di * 128 + OFF1, ap=[[LP1 - 1, 128], [1, 128]])),
                                ["BV1"], ["TSW"])
                    sc.flush(st)

                LM = max(S, LS)
                NTM = max(cfg.NTp, cfg.NTs)
                NMM = max(cfg.NMp, cfg.NMs)
                KC = sbN("KC", [65, 2, NTM * 128], BF16)
                VCW = sbN("VCW", [128, 2, NTM, 65 + NMM], BF16)
                att = {}
                KWt = sbN("KWt", [65, 5, 128], BF16)
                VWt = sbN("VWt", [128, 5, 65], BF16)
                ob_res = sbN("ob_res", [128, NOWN, 512], BF16)
                obs8 = sbN("obs8", [8, 512], BF16)
                obg = sbN("obg", [128, 4, 64])
                imp = sbN("imp", [128, NMM]); addt = sbN("addt", [128, NMM]); wk = sbN("wk", [128, NMM])
                mk = sbN("mk", [128, NMM]); mk2 = sbN("mk2", [128, NMM])
                negb = sbN("negb", [128, NMM], BF16)
                neg4 = sbN("neg4", [128, NMM // 128, 4, 128], BF16)
                m8a = sbN("m8a", [128, 8]); m8b = sbN("m8b", [128, 8])
                rz = sbN("rz", [128, 4]); coef = sbN("coef", [128, 4])
                pTn = [sbN(f"pTn{i}", [128, 512], BF16) for i in range(2)]
                ps_s = [psN(f"ps_s{i}", [128, 512]) for i in range(2)]
                po_c = psN("po_c", [128, 65 + NMM])
                po_s = psN("po_s", [128, 4, 65])
                po_w = psN("po_w", [128, 4, 65])
                ps_t = psN("ps_t", [128, NMM // 128, 128], BF16)
                step = {"t": 0}

                def compress(c, Lc, NT, NM, wc_in, cneg_in):
                    nblk = Lc // 16 - 1
                    nm = c.name
                    with contextlib.ExitStack() as stc:
                        def sbc(name, shape, dt=F32):
                            return stc.enter_context(nc.sbuf_tensor(f"C{nm}_" + name, list(shape), dt))
                        xct = sbc("xct", [128, c.L], BF16)
                        w1b = sbc("w1b", [128, 32, 128], BF16)
                        xs_ = sbc("xs", [128, 512]); x2 = sbc("x2", [128, 512]); sg = sbc("sg", [128, 512])
                        hidT = sbc("hidT", [128, 512], BF16)
                        sqb = sbc("sqb", [64, 512], BF16)
                        rs_ = sbc("rs", [64, 512]); rr = sbc("rr", [64, 512])
                        nh64 = sbc("nh64", [64, 512])
                        wstg_ = sbc("wstg", [128, NM])
                        cn_f = sbc("cn_f", [65, NT * 128])
                        ps_h = ps_s[0]
                        ps_k = ps_s[1]
                        ps_q = stc.enter_context(nc.psum_tensor(f"C{nm}_ps_q", [128, 512], F32))
                        sc.op("dve", lambda e: e.memset(nh64[:], -0.5), [], ["nh64"])
                        sc.op("dve", lambda e: e.memset(KC[:], 0.0), ["KC"], ["KC"])
                        sc.op("dve", lambda e: e.memset(VCW[:], 0.0), ["VCW"], ["VCW"])
                        sc.op("dve", lambda e: e.memset(VCW[:, :, :, 64:65], 1.0), ["VCW"], ["VCW"])
                        sc.dma("sp", lambda e: e.dma_start(out=cn_f[64:65, :], in_=cneg_in), [], ["cn_f"])
                        for g in range(2):
                            sc.op("dve", lambda e, g=g: e.tensor_copy(out=KC[64:65, g, 0:NT * 128], in_=cn_f[64:65, :]), ["cn_f", "KC"], ["KC"])
                        for t in range(NT):
                            sc.dma("sp", lambda e, t=t: e.dma_start(out=wstg_[:], in_=wc_in[t * 128:(t + 1) * 128, :]), [], ["wstg_"])
                            for g in range(2):
                                sc.op("dve", lambda e, t=t, g=g: e.tensor_copy(out=VCW[:, g, t, 65:65 + NM], in_=wstg_[:]), ["wstg_", "VCW"], ["VCW"])
                        for kv in range(2):
                            sc.dma("sp", lambda e, kv=kv: e.dma_start(out=xct[:], in_=c.XC[kv]), ["XC_" + nm], ["xct"])
                            sc.dma("sp", lambda e, kv=kv: e.dma_start(out=w1b[:], in_=W1B[kv]), ["W1B"], ["w1b"])
                            hbias = hbias2[:, kv:kv + 1]
                            w2b = w2b2[:, kv, :]
                            xv = xct[:, 0:Lc].rearrange("p (n s) -> p n s", s=16)
                            for g in range(2):
                                for n0 in range(0, nblk, 512):
                                    nn = min(512, nblk - n0)
                                    for r in range(32):
                                        sc.op("pe", lambda e, r=r, g=g, n0=n0, nn=nn: e.matmul(
                                            ps_h[:, 0:nn], lhsT=w1b[g * 64:(g + 1) * 64, r, :],
                                            rhs=xv[g * 64:(g + 1) * 64, n0 + r // 16:n0 + r // 16 + nn, r % 16],
                                            start=(r == 0), stop=(r == 31)), ["w1b", "xct"], ["ps_h"])
                                    sc.op("act", lambda e, nn=nn, hbias=hbias: e.activation(out=xs_[:, 0:nn], in_=ps_h[:, 0:nn], func=AF.Identity, bias=hbias),
                                          ["ps_h", "hbias2"], ["xs"])
                                    sc.op("dve", lambda e, nn=nn: e.tensor_tensor(out=x2[:, 0:nn], in0=xs_[:, 0:nn], in1=xs_[:, 0:nn], op=ALU.mult), ["xs"], ["x2"])
                                    sc.op("dve", lambda e, nn=nn: e.tensor_scalar(out=x2[:, 0:nn], in0=x2[:, 0:nn], scalar1=0.044715, scalar2=1.0,
                                                                                 op0=ALU.mult, op1=ALU.add), ["x2"], ["x2"])
                                    sc.op("dve", lambda e, nn=nn: e.tensor_tensor(out=x2[:, 0:nn], in0=x2[:, 0:nn], in1=xs_[:, 0:nn], op=ALU.mult), ["x2", "xs"], ["x2"])
                                    sc.op("act", lambda e, nn=nn: e.activation(out=sg[:, 0:nn], in_=x2[:, 0:nn], func=AF.Sigmoid, scale=1.5957691216057308),
                                          ["x2"], ["sg"])
                                    sc.op("dve", lambda e, nn=nn: e.tensor_tensor(out=hidT[:, 0:nn], in0=xs_[:, 0:nn], in1=sg[:, 0:nn], op=ALU.mult),
                                          ["xs", "sg"], ["hidT"])
                                    if kv == 0:
                                        sc.op("pe", lambda e, nn=nn, w2b=w2b: e.matmul(ps_k[0:64, 0:nn], lhsT=w2b, rhs=hidT[:, 0:nn], start=True, stop=True),
                                              ["w2b2", "hidT"], ["ps_k"])
                                        sc.op("act", lambda e, nn=nn: e.activation(out=sqb[:, 0:nn], in_=ps_k[0:64, 0:nn], func=AF.Square), ["ps_k"], ["sqb"])
                                        sc.op("pe", lambda e, nn=nn: e.matmul(ps_q[0:64, 0:nn], lhsT=ones_b[0:64, 0:64], rhs=sqb[:, 0:nn], start=True, stop=True),
                                              ["ones_b", "sqb"], ["ps_q"])
                                        sc.op("dve", lambda e, nn=nn: e.tensor_scalar(out=rs_[:, 0:nn], in0=ps_q[0:64, 0:nn], scalar1=1.0 / 64, scalar2=EPS,
                                                                                     op0=ALU.mult, op1=ALU.add), ["ps_q"], ["rs"])
                                        sc.op("act", lambda e, nn=nn: e.activation(out=rs_[:, 0:nn], in_=rs_[:, 0:nn], func=AF.Sqrt), ["rs"], ["rs"])
                                        sc.op("dve", lambda e, nn=nn: e.reciprocal(out=rr[:, 0:nn], in_=rs_[:, 0:nn]), ["rs"], ["rr"])
                                        sc.op("dve", lambda e, nn=nn: e.tensor_tensor(out=rr[:, 0:nn], in0=rr[:, 0:nn], in1=ps_k[0:64, 0:nn], op=ALU.mult),
                                              ["rr", "ps_k"], ["rr"])
                                        sc.op("dve", lambda e, nn=nn, g=g, n0=n0: e.tensor_scalar(out=KC[0:64, g, n0:n0 + nn], in0=rr[:, 0:nn], scalar1=gkc[:, 0:1],
                                                                                                scalar2=None, op0=ALU.mult), ["rr", "gkc", "KC"], ["KC"])
                                    else:
                                        for sub in range(0, nn, 128):
                                            ns = min(128, nn - sub)
                                            sc.op("pe", lambda e, sub=sub, ns=ns, w2b=w2b: e.matmul(ps_k[0:ns, 0:64], lhsT=hidT[:, sub:sub + ns], rhs=w2b,
                                                                                           start=True, stop=True), ["hidT", "w2b2"], ["ps_k"])
                                            sc.op("act", lambda e, sub=sub, ns=ns, g=g, n0=n0: e.activation(
                                                out=VCW[0:ns, g, (n0 + sub) // 128, 0:64], in_=ps_k[0:ns, 0:64], func=AF.Copy), ["ps_k", "VCW"], ["VCW"])
                        sc.flush(st)

                pend = []

                def branch_block(KTap, Ka, rhs_q, extra, Vap, po, w, first, last, Vn, Ktn):
                    pend.append((KTap, rhs_q, extra, Vap, po, w, last, Vn, Ktn))

                def run_pending():
                    base = step["t"]

                    def qk(i):
                        KTap, rhs_q, extra, Vap, po, w, last, Vn, Ktn = pend[i]
                        tt = base + i
                        pst, pn = ps_s[tt % 2], f"ps_s{tt % 2}"
                        sc.op("pe", lambda e: e.matmul(pst[:, 0:4 * w].rearrange("p (j q) -> p j q", j=4), lhsT=KTap, rhs=rhs_q,
                                                       start=True, stop=(len(extra) == 0)), [Ktn, "QNg"], [pn])
                        for k_, (l_, r_, rn) in enumerate(extra):
                            sc.op("pe", lambda e, l_=l_, r_=r_, k_=k_: e.matmul(pst[:, 0:4 * w].rearrange("p (j q) -> p j q", j=4), lhsT=l_, rhs=r_,
                                                                               start=False, stop=(k_ == len(extra) - 1)), rn, [pn])
                    if pend:
                        qk(0)
                    for i in range(len(pend)):
                        KTap, rhs_q, extra, Vap, po, w, last, Vn, Ktn = pend[i]
                        tt = base + i
                        pst, pn = ps_s[tt % 2], f"ps_s{tt % 2}"
                        pt, ptn = pTn[tt % 2], f"pTn{tt % 2}"
                        sc.op("act", lambda e, pt=pt, pst=pst, w=w: e.activation(out=pt[:, 0:4 * w], in_=pst[:, 0:4 * w], func=AF.Exp), [pn], [ptn])
                        if i + 1 < len(pend):
                            qk(i + 1)
                        for j in range(4):
                            sc.op("pe", lambda e, j=j, pt=pt, po=po, w=w, Vap=Vap, last=last: e.matmul(
                                po[0:w, j, :], lhsT=pt[:, j * w:(j + 1) * w], rhs=Vap, start=False, stop=last), [ptn, Vn], ["po_sw"])
                    step["t"] += len(pend)
                    pend.clear()

                def nsa_qblock(c, g, F, w, q0, gb_ap, gbn, add_ap, NM, dst):
                    nch = NM // 128
                    KSg, VSg, QNg, EE = att["KSg"], att["VSg"], att["QNg"], att["EE"]
                    K_, tb = Kof(F, w)
                    vi = Kvars.index(K_)
                    for j in range(4):
                        h = 4 * g + j
                        for t in range(tb + 1):
                            tt = step["t"]
                            step["t"] += 1
                            pst, pn = ps_s[tt % 2], f"ps_s{tt % 2}"
                            pt, ptn = pTn[tt % 2], f"pTn{tt % 2}"
                            sc.op("pe", lambda e, t=t, j=j, pst=pst: e.matmul(pst[:, 0:w], lhsT=KC[0:65, g, t * 128:(t + 1) * 128],
                                                                           rhs=QNg[0:65, j, q0:q0 + w], start=True, stop=(t != tb)), ["KC", "QNg"], [pn])
                            if t == tb:
                                sc.op("pe", lambda e, pst=pst, h=h: e.matmul(pst[:, 0:w], lhsT=ident_b[:], rhs=TCB[:, vi, h, 0:w], start=False, stop=True),
                                      ["ident_b", "TCB"], [pn])
                            sc.op("act", lambda e, pst=pst, pt=pt: e.activation(out=pt[:, 0:w], in_=pst[:, 0:w], func=AF.Exp), [pn], [ptn])
                            sc.op("pe", lambda e, t=t, pt=pt: e.matmul(po_c[0:w, 0:65 + NM], lhsT=pt[:, 0:w], rhs=VCW[:, g, t, 0:65 + NM],
                                                                     start=(t == 0), stop=(t == tb)), [ptn, "VCW"], ["po_c"])
                        sc.op("dve", lambda e, j=j: e.tensor_scalar(out=rz[0:w, j:j + 1], in0=po_c[0:w, 64:65], scalar1=1e-30, scalar2=None, op0=ALU.max),
                              ["po_c"], ["rz"])
                        sc.op("dve", lambda e, j=j: e.reciprocal(out=rz[0:w, j:j + 1], in_=rz[0:w, j:j + 1]), ["rz"], ["rz"])
                        sc.op("dve", lambda e, j=j, h=h: e.tensor_tensor(out=coef[0:w, j:j + 1], in0=rz[0:w, j:j + 1], in1=gb_ap[:, 3 * h:3 * h + 1], op=ALU.mult),
                              ["rz", gbn], ["coef"])
                        sc.op("dve", lambda e, j=j: e.tensor_scalar(out=obg[0:w, j, :], in0=po_c[0:w, 0:64], scalar1=coef[0:w, j:j + 1], scalar2=None,
                                                                    op0=ALU.mult), ["po_c", "coef"], ["obg"])
                        if j == 0:
                            sc.op("dve", lambda e, j=j: e.tensor_scalar(out=imp[0:w, 0:NM], in0=po_c[0:w, 65:65 + NM], scalar1=rz[0:w, j:j + 1], scalar2=None,
                                                                        op0=ALU.mult), ["po_c", "rz"], ["imp"])
                        else:
                            sc.op("dve", lambda e, j=j: e.scalar_tensor_tensor(out=imp[0:w, 0:NM], in0=po_c[0:w, 65:65 + NM], scalar=rz[0:w, j:j + 1],
                                                                               in1=imp[0:w, 0:NM], op0=ALU.mult, op1=ALU.add), ["po_c", "rz", "imp"], ["imp"])
                    for po in (po_s, po_w):
                        sc.op("pe", lambda e, po=po: e.matmul(po[0:w, :, :], lhsT=zeros_b[:, 0:w], rhs=zeros_b[:, 0:260].rearrange("p (j d) -> p j d", d=65),
                                                              start=True, stop=False), ["zeros_b"], ["po_sw"])
                    f0 = max(0, F - 4)
                    nwb = F - f0 + 1
                    sc.dma("sp", lambda e: e.dma_start(out=KWt[:, 0:nwb, :], in_=c.KW[g][:, f0 * 128:(F + 1) * 128].rearrange("p (b t) -> p b t", t=128)),
                           ["KW_" + c.name], ["KWt"])
                    sc.dma("sp", lambda e: e.dma_start(out=VWt[:, 0:nwb, :], in_=c.VW[g][:, f0:F + 1, :]), ["VW_" + c.name], ["VWt"])
                    rqa = QNg[0:65, :, q0:q0 + w]
                    for f in range(f0, F + 1):
                        extra = [(ident_b[:], TSW[:, F - f, 4 * g:4 * g + 4, 0:w], ["ident_b", "TSW"])]
                        branch_block(KWt[0:65, f - f0, :], 65, rqa, extra, VWt[:, f - f0, :], po_w, w, f == f0, f == F, "VWt", "KWt")
                    run_pending()
                    sc.dma("sp", lambda e: e.dma_start(out=addt[0:w, 0:NM], in_=add_ap), [], ["addt"])
                    sc.op("dve", lambda e: e.tensor_tensor(out=imp[0:w, 0:NM], in0=imp[0:w, 0:NM], in1=addt[0:w, 0:NM], op=ALU.add), ["imp", "addt"], ["imp"])
                    sc.op("dve", lambda e: e.max(out=m8a[0:w, :], in_=imp[0:w, 0:NM]), ["imp"], ["m8a"])
                    sc.op("dve", lambda e: e.match_replace(out=wk[0:w, 0:NM], in_to_replace=m8a[0:w, :], in_values=imp[0:w, 0:NM], imm_value=-1e30),
                          ["imp", "m8a"], ["wk"])
                    sc.op("dve", lambda e: e.max(out=m8b[0:w, :], in_=wk[0:w, 0:NM]), ["wk"], ["m8b"])
                    sc.op("dve", lambda e: e.tensor_scalar(out=mk[0:w, 0:NM], in0=imp[0:w, 0:NM], scalar1=m8b[0:w, 7:8], scalar2=None, op0=ALU.is_ge),
                          ["imp", "m8b"], ["mk"])
                    sc.op("dve", lambda e: e.tensor_scalar(out=mk2[0:w, 0:NM], in0=imp[0:w, 0:NM], scalar1=-1e29, scalar2=None, op0=ALU.is_gt),
                          ["imp"], ["mk2"])
                    sc.op("dve", lambda e: e.tensor_tensor(out=mk[0:w, 0:NM], in0=mk[0:w, 0:NM], in1=mk2[0:w, 0:NM], op=ALU.mult), ["mk", "mk2"], ["mk"])
                    sc.op("dve", lambda e: e.tensor_scalar(out=negb[0:w, 0:NM], in0=mk[0:w, 0:NM], scalar1=-NEG, scalar2=NEG, op0=ALU.mult, op1=ALU.add),
                          ["mk"], ["negb"])
                    for ch in range(nch):
                        sc.op("pe", lambda e, ch=ch: e.transpose(out=ps_t[:, ch, 0:w], in_=negb[0:w, ch * 128:(ch + 1) * 128], identity=ident_b[0:w, 0:w]),
                              ["negb", "ident_b"], ["ps_t"])
                    for j in range(4):
                        sc.op("act", lambda e, j=j: e.activation(out=neg4[:, 0:nch, j, 0:w], in_=ps_t[:, 0:nch, 0:w], func=AF.Copy), ["ps_t"], ["neg4"])
                    rq = QNg[0:64, :, q0:q0 + w]
                    for f in range(F + 1):
                        extra = [(EE[:, (f % 64) * 128:(f % 64 + 1) * 128], neg4[:, f // 64, :, 0:w], ["EE", "neg4"])]
                        if f == F:
                            extra.append((ident_b[:], TSW[:, 0, 4 * g:4 * g + 4, 0:w], ["ident_b", "TSW"]))
                        elif f == F - 1:
                            extra.append((ident_b[:], TSW[:, 1, 4 * g:4 * g + 4, 0:w], ["ident_b", "TSW"]))
                        branch_block(KSg[0:64, f * 128:(f + 1) * 128], 64, rq, extra, VSg[:, f, :], po_s, w, f == 0, f == F, "VSg", "KSg")
                    run_pending()
                    for (po, gi) in ((po_s, 1), (po_w, 2)):
                        sc.op("dve", lambda e, po=po: e.tensor_scalar(out=rz[0:w, 0:4], in0=po[0:w, :, 64], scalar1=1e-30, scalar2=None, op0=ALU.max),
                              ["po_sw"], ["rz"])
                        sc.op("dve", lambda e: e.reciprocal(out=rz[0:w, 0:4], in_=rz[0:w, 0:4]), ["rz"], ["rz"])
                        sc.op("dve", lambda e, gi=gi: e.tensor_tensor(out=coef[0:w, 0:4], in0=rz[0:w, 0:4],
                                                                      in1=gb_ap[:, 12 * g:12 * g + 12].rearrange("p (j i) -> p j i", i=3)[:, :, gi], op=ALU.mult),
                              ["rz", gbn], ["coef"])
                        for j in range(4):
                            sc.op("dve", lambda e, j=j, po=po: e.scalar_tensor_tensor(out=obg[0:w, j, :], in0=po[0:w, j, 0:64], scalar=coef[0:w, j:j + 1],
                                                                                     in1=obg[0:w, j, :], op0=ALU.mult, op1=ALU.add),
                                  ["po_sw", "coef", "obg"], ["obg"])
                    sc.op("dve", lambda e: e.tensor_copy(out=dst, in_=obg[0:w, :, :].rearrange("p j d -> p (j d)")), ["obg"], ["ob_dst"])

                def nsa_ctx(c, Lc, NT, NM, wc_in, cneg_in, qblocks):
                    compress(c, Lc, NT, NM, wc_in, cneg_in)
                    with contextlib.ExitStack() as st2:
                        KSg = st2.enter_context(nc.sbuf_tensor(f"A{c.name}_KSg", [64, c.L], BF16))
                        VSg = st2.enter_context(nc.sbuf_tensor(f"A{c.name}_VSg", [128, c.NBk, 65], BF16))
                        QNg = st2.enter_context(nc.sbuf_tensor(f"A{c.name}_QNg", [65, 4, c.nqc], BF16))
                        EE = st2.enter_context(nc.sbuf_tensor(f"A{c.name}_EE", [128, 8192], BF16))
                        att["KSg"], att["VSg"], att["QNg"], att["EE"] = KSg, VSg, QNg, EE
                        sc.dma("sp", lambda e: e.dma_start(out=EE[:], in_=EEB[:, :]), ["EEB"], ["EE"])
                        for g in range(2):
                            sc.dma("sp", lambda e, g=g: e.dma_start(out=KSg[:, 0:c.L], in_=c.KS[g]), ["KS_" + c.name], ["KSg"])
                            sc.dma("sp", lambda e, g=g: e.dma_start(out=VSg[:, 0:c.NBk, :], in_=c.VS[g]), ["VS_" + c.name], ["VSg"])
                            sc.dma("sp", lambda e, g=g: e.dma_start(out=QNg[:, :, 0:c.nqc], in_=c.QN[:, 4 * g:4 * g + 4, :]), ["QN_" + c.name], ["QNg"])
                            for qb in qblocks:
                                qb(c, g)
                        sc.flush(st)

                qbl = []
                for jo in range(NOWN):
                    qbl.append(lambda c, g, jo=jo: nsa_qblock(c, g, 8 * jo + 7, 128, jo * 128, gb_res[:, jo, :], "gb_res",
                                                              addp_in[jo, :, :], cfg.NMp, ob_res[:, jo, g * 256:(g + 1) * 256]))
                nsa_ctx(ctx_p, S, cfg.NTp, cfg.NMp, wcp_in, cnegp_in[:, :], qbl)
                for j in range(NOWN):
                    sc.dma("sp", lambda e, j=j: e.dma_start(out=OBP[j * 128:(j + 1) * 128, :], in_=ob_res[:, j, :]), ["ob_dst"], ["OBP"])
                    sc.dma("sp", lambda e, j=j: e.dma_start(out=dbg_ob_p[j * 128:(j + 1) * 128, :], in_=ob_res[:, j, :]), ["ob_dst"], [])
                for b in range(NSEQ):
                    qbl = [lambda c, g, b=b: nsa_qblock(c, g, NPG, 8, 0, gbs_res[:, b, :], "gbs_res", adds_in[:, :], cfg.NMs,
                                                        obs8[:, g * 256:(g + 1) * 256])]
                    nsa_ctx(ctx_s[b], cfg.PAST, cfg.NTs, cfg.NMs, wcs_in, cnegs_in[:, :], qbl)
                    sc.dma("sp", lambda e, b=b: e.dma_start(out=OBS[b * 8:(b + 1) * 8, :], in_=obs8[:]), ["ob_dst"], ["OBS"])
                sc.dma("sp", lambda e: e.dma_start(out=dbg_ob_s[:, :], in_=OBS[:, :]), ["OBS"], [])
                sc.flush(st)
        if cfg.dbg_ob:
            with contextlib.ExitStack() as stX:
                obf = stX.enter_context(nc.sbuf_tensor("obf", [128, 512], F32))
                obb = stX.enter_context(nc.sbuf_tensor("obb", [128, 512], BF16))
                for j in range(NOWN):
                    sc.dma("sp", lambda e, j=j: e.dma_start(out=obf[:], in_=dbg_ob_p_in[j * 128:(j + 1) * 128, :]), [], ["obf"])
                    sc.op("dve", lambda e, j=j: e.tensor_copy(out=obb[:], in_=obf[:]), ["obf"], ["obb"])
                    sc.dma("sp", lambda e, j=j: e.dma_start(out=OBP[j * 128:(j + 1) * 128, :], in_=obb[:]), ["obb"], ["OBP"])
                sc.dma("sp", lambda e: e.dma_start(out=obf[0:NSR, :], in_=dbg_ob_s_in[:, :]), [], ["obf"])
                sc.op("dve", lambda e: e.tensor_copy(out=obb[0:NSR, :], in_=obf[0:NSR, :]), ["obf"], ["obb"])
                sc.dma("sp", lambda e: e.dma_start(out=OBS[:, :], in_=obb[0:NSR, :]), ["obb"], ["OBS"])
                sc.flush(st)
        Y1 = dscr("Y1", [NQ + NSR, D], F32)

        def cast_weight(dst, dname, src_rows_fn, nchunk, ncol, stgs):
            cnt = 0
            for k in range(nchunk):
                for c0 in range(0, ncol, 2048):
                    n = min(2048, ncol - c0)
                    stg, sn = stgs[cnt % 2]
                    cnt += 1
                    sc.dma("sp", lambda e, k=k, c0=c0, n=n, stg=stg: e.dma_start(out=stg[:, 0:n], in_=src_rows_fn(k)[:, c0:c0 + n]), [], [sn])
                    sc.op("pool", lambda e, k=k, c0=c0, n=n, stg=stg: e.tensor_copy(out=dst[:, k, c0:c0 + n], in_=stg[:, 0:n]), [sn], [dname])

        tiles = [("p", j, 128) for j in range(NOWN)] + [("s", b, 8) for b in range(NSEQ)]

        with contextlib.ExitStack() as stD:
            def sbD(name, shape, dt=F32):
                return stD.enter_context(nc.sbuf_tensor("D_" + name, list(shape), dt))

            def psD(name, shape, dt=F32):
                return stD.enter_context(nc.psum_tensor("D_" + name, list(shape), dt))
            wstg = [(sbD(f"wstg{i}", [128, 2048]), f"wstg{i}") for i in range(2)]
            w_zm = sbD("w_zm", [128, 8, 2048], BF16)
            wof = sbD("wof", [128, 4, 1024], BF16)
            won = sbD("won", [128, 4, 1024], BF16)
            wo = sbD("wo", [128, 8, 1024], BF16)
            Mp = sbD("Mp", [128, 3, D])
            xt = sbD("xt", [128, D])
            junk = sbD("junk", [128, D], BF16)
            tmpf = sbD("tmpf", [128, D])
            hb = sbD("hb", [128, D], BF16)
            hT = sbD("hT", [128, 8, 128], BF16)
            ssum = sbD("ssum", [128, 1]); rstd = sbD("rstd", [128, 1])
            gmt = sbD("gmt", [128, 2048], BF16)
            oat = sbD("oat", [128, 512], BF16); obt = sbD("obt", [128, 512], BF16)
            oaT = sbD("oaT", [128, 4, 128], BF16); obT = sbD("obT", [128, 4, 128], BF16)
            mt = sbD("mt", [128, D]); mb = sbD("mb", [128, D], BF16)
            mT = sbD("mT", [128, 8, 128], BF16)
            y1t = sbD("y1t", [128, D])
            ps_tr = psD("ps_tr", [128, 1024], BF16)
            ps_a = psD("ps_a", [128, 512]); ps_b = psD("ps_b", [128, 512])
            ps_c = psD("ps_c", [128, 512]); ps_d = psD("ps_d", [128, 512])
            cast_weight(w_zm, "w_zm", lambda k: w_in[k * 128:(k + 1) * 128, O_ZM:O_ZM + 2048], 8, 2048, wstg)
            cast_weight(wof, "wof", lambda k: w_out_fox[k * 128:(k + 1) * 128, :], 4, 1024, wstg)
            cast_weight(won, "won", lambda k: w_out_nsa[k * 128:(k + 1) * 128, :], 4, 1024, wstg)
            cast_weight(wo, "wo", lambda k: w_out[k * 128:(k + 1) * 128, :], 8, 1024, wstg)

            def norm_T(src, srcn, rows, G, B):
                sc.op("act", lambda e: e.activation(out=junk[0:rows, :], in_=src, func=AF.Square, accum_out=ssum[0:rows, :]),
                      [srcn], ["junk", "ssum"])
                sc.op("dve", lambda e: e.tensor_scalar(out=ssum[0:rows, :], in0=ssum[0:rows, :], scalar1=1.0 / D, scalar2=EPS,
                                                       op0=ALU.mult, op1=ALU.add), ["ssum"], ["ssum"])
                sc.op("pool", lambda e: e.tensor_tensor(out=rstd[0:rows, :], in0=ssum[0:rows, :], in1=neghalf[0:rows, 0:1], op=ALU.pow),
                      ["ssum", "neghalf"], ["rstd"])
                sc.op("dve", lambda e: e.scalar_tensor_tensor(out=tmpf[0:rows, :], in0=src, scalar=rstd[0:rows, :], in1=G,
                                                              op0=ALU.mult, op1=ALU.mult), [srcn, "rstd", "Mp"], ["tmpf"])
                sc.op("dve", lambda e: e.tensor_tensor(out=hb[0:rows, :], in0=tmpf[0:rows, :], in1=B, op=ALU.add), ["tmpf", "Mp"], ["hb"])
                trans8(hb, "hb", rows, hT, "hT")

            def trans8(src, srcn, rows, dstT, dstn, nchunk=8, c_off=0):
                for k in range(nchunk):
                    sc.op("pe", lambda e, k=k: e.transpose(out=ps_tr[:, k * 128:k * 128 + rows], in_=src[0:rows, (c_off + k) * 128:(c_off + k + 1) * 128],
                                                           identity=ident_b[0:rows, 0:rows]), [srcn, "ident_b"], ["ps_tr"])
                sc.op("act", lambda e: e.activation(out=dstT[:, 0:nchunk, 0:rows],
                                                    in_=ps_tr[:].rearrange("p (k t) -> p k t", t=128)[:, 0:nchunk, 0:rows], func=AF.Copy),
                      ["ps_tr"], [dstn])

            kind_state = {"k": None}

            def do_tileD(kind, idx, rows):
                kind_loaded = kind_state["k"]
                if kind == "p":
                    x_ap = xf[(8 * idx + 7) * 128:(8 * idx + 8) * 128, :]
                    y1_ap = Y1[idx * 128:(idx + 1) * 128, :]
                    oa_ap, ob_ap = OAP[idx * 128:(idx + 1) * 128, :], OBP[idx * 128:(idx + 1) * 128, :]
                    if kind_loaded != "p":
                        load_mod(Mp, 128, 0, (1, 0, 2), 0, ps_a, "ps_a")
                        kind_state["k"] = "p"
                else:
                    x_ap = xs[idx * 8:(idx + 1) * 8, :]
                    y1_ap = Y1[NQ + idx * 8:NQ + (idx + 1) * 8, :]
                    oa_ap, ob_ap = OAS[idx * 8:(idx + 1) * 8, :], OBS[idx * 8:(idx + 1) * 8, :]
                    load_mod(Mp, 8, 128 + 8 * idx, (1, 0, 2), 0, ps_a, "ps_a")
                    kind_state["k"] = "s"
                sc.dma("sp", lambda e, x_ap=x_ap, rows=rows: e.dma_start(out=xt[0:rows, :], in_=x_ap), [], ["xt"])
                norm_T(xt[0:rows, :], "xt", rows, Mp[0:rows, 0, :], Mp[0:rows, 1, :])
                for g4 in range(4):
                    pst, pn = [(ps_a, "ps_a"), (ps_b, "ps_b")][g4 % 2]
                    for k in range(8):
                        sc.op("pe", lambda e, k=k, g4=g4, pst=pst: e.matmul(pst[0:rows, :], lhsT=hT[:, k, 0:rows], rhs=w_zm[:, k, g4 * 512:(g4 + 1) * 512],
                                                                        start=(k == 0), stop=(k == 7)), ["hT", "w_zm"], [pn])
                    sc.op("act", lambda e, g4=g4, pst=pst: e.activation(out=gmt[0:rows, g4 * 512:(g4 + 1) * 512], in_=pst[0:rows, :], func=AF.Sigmoid),
                          [pn], ["gmt"])
                sc.dma("sp", lambda e, oa_ap=oa_ap, rows=rows: e.dma_start(out=oat[0:rows, :], in_=oa_ap), ["OAS", "OAP"], ["oat"])
                sc.dma("sp", lambda e, ob_ap=ob_ap, rows=rows: e.dma_start(out=obt[0:rows, :], in_=ob_ap), ["OBS", "OBP"], ["obt"])
                oa_src, oa_n, ob_src, ob_n = oat, "oat", obt, "obt"
                trans8(oa_src, oa_n, rows, oaT, "oaT", 4)
                trans8(ob_src, ob_n, rows, obT, "obT", 4)
                for half in range(2):
                    hs_ = slice(half * 512, (half + 1) * 512)
                    for k in range(4):
                        sc.op("pe", lambda e, k=k, hs_=hs_: e.matmul(ps_c[0:rows, :], lhsT=oaT[:, k, 0:rows], rhs=wof[:, k, hs_],
                                                                   start=(k == 0), stop=(k == 3)), ["oaT", "wof"], ["ps_c"])
                    for k in range(4):
                        sc.op("pe", lambda e, k=k, hs_=hs_: e.matmul(ps_d[0:rows, :], lhsT=obT[:, k, 0:rows], rhs=won[:, k, hs_],
                                                                   start=(k == 0), stop=(k == 3)), ["obT", "won"], ["ps_d"])
                    sc.op("dve", lambda e, hs_=hs_: e.tensor_tensor(out=mt[0:rows, hs_], in0=ps_c[0:rows, :], in1=gmt[0:rows, hs_], op=ALU.mult),
                          ["ps_c", "gmt"], ["mt"])
                    sc.op("dve", lambda e, half=half: e.tensor_tensor(out=tmpf[0:rows, 0:512], in0=ps_d[0:rows, :],
                                                                      in1=gmt[0:rows, 1024 + half * 512:1024 + (half + 1) * 512], op=ALU.mult),
                          ["ps_d", "gmt"], ["tmpf"])
                    sc.op("dve", lambda e, hs_=hs_: e.tensor_tensor(out=mb[0:rows, hs_], in0=mt[0:rows, hs_], in1=tmpf[0:rows, 0:512], op=ALU.add),
                          ["mt", "tmpf"], ["mb"])
                trans8(mb, "mb", rows, mT, "mT")
                for half in range(2):
                    hs_ = slice(half * 512, (half + 1) * 512)
                    for k in range(8):
                        sc.op("pe", lambda e, k=k, hs_=hs_: e.matmul(ps_c[0:rows, :], lhsT=mT[:, k, 0:rows], rhs=wo[:, k, hs_],
                                                                   start=(k == 0), stop=(k == 7)), ["mT", "wo"], ["ps_c"])
                    sc.op("dve", lambda e, hs_=hs_: e.tensor_tensor(out=tmpf[0:rows, hs_], in0=ps_c[0:rows, :], in1=Mp[0:rows, 2, hs_], op=ALU.mult),
                          ["ps_c", "Mp"], ["tmpf"])
                    sc.op("dve", lambda e, hs_=hs_: e.tensor_tensor(out=y1t[0:rows, hs_], in0=tmpf[0:rows, hs_], in1=xt[0:rows, hs_], op=ALU.add),
                          ["tmpf", "xt"], ["y1t"])
                sc.dma("sp", lambda e, y1_ap=y1_ap, rows=rows: e.dma_start(out=y1_ap, in_=y1t[0:rows, :]), ["y1t"], ["Y1"])
            for (kind, idx, rows) in tiles:
                do_tileD(kind, idx, rows)
            sc.flush(st)

        YP = dscr("YP", [NQ + NSR, D], F32)
        with contextlib.ExitStack() as stE:
            def sbE(name, shape, dt=F32):
                return stE.enter_context(nc.sbuf_tensor("E_" + name, list(shape), dt))

            def psE(name, shape, dt=F32):
                return stE.enter_context(nc.psum_tensor("E_" + name, list(shape), dt))
            wstg = [(sbE(f"wstg{i}", [128, 2048]), f"wstg{i}") for i in range(2)]
            wup = sbE("wup", [128, 8, 2048], BF16)
            wdn = sbE("wdn", [128, 16, 1024], BF16)
            Mp = sbE("Mp", [128, 3, D])
            y1t = sbE("y1t", [128, D])
            junk = sbE("junk", [128, D], BF16)
            tmpf = sbE("tmpf", [128, D])
            hb = sbE("hb", [128, D], BF16)
            hT = sbE("hT", [128, 8, 128], BF16)
            ssum = sbE("ssum", [128, 1]); rstd = sbE("rstd", [128, 1])
            rl = sbE("rl", [128, 512])
            ub = sbE("ub", [128, 2048], BF16)
            uT = sbE("uT", [128, 16, 128], BF16)
            yt = sbE("yt", [128, D])
            ypt = sbE("ypt", [128, D])
            ps_tr = psE("ps_tr", [128, 1024], BF16)
            ps_a = psE("ps_a", [128, 512]); ps_b = psE("ps_b", [128, 512])
            kstate = {"k": None}

            def do_tileE(hf, kind, idx, rows):
                kind_loaded = kstate["k"]
                if True:
                    if kind == "p":
                        r0 = idx * 128
                        out_ap = o_y_p[idx * 128:(idx + 1) * 128, :]
                        c0 = 0
                    else:
                        r0 = NQ + idx * 8
                        out_ap = o_y_s[idx * 8:(idx + 1) * 8, :]
                        c0 = 128 + 8 * idx
                    y1_ap = Y1[r0:r0 + rows, :]
                    yp_ap = YP[r0:r0 + rows, :]
                    if kind != kind_loaded or kind == "s":
                        load_mod(Mp, rows, c0, (4, 3, 5), 1, ps_a, "ps_a")
                        kstate["k"] = kind
                    sc.dma("sp", lambda e, y1_ap=y1_ap, rows=rows: e.dma_start(out=y1t[0:rows, :], in_=y1_ap), ["Y1"], ["y1t"])
                    sc.op("act", lambda e, rows=rows: e.activation(out=junk[0:rows, :], in_=y1t[0:rows, :], func=AF.Square, accum_out=ssum[0:rows, :]),
                          ["y1t"], ["junk", "ssum"])
                    sc.op("dve", lambda e, rows=rows: e.tensor_scalar(out=ssum[0:rows, :], in0=ssum[0:rows, :], scalar1=1.0 / D, scalar2=EPS,
                                                                      op0=ALU.mult, op1=ALU.add), ["ssum"], ["ssum"])
                    sc.op("pool", lambda e, rows=rows: e.tensor_tensor(out=rstd[0:rows, :], in0=ssum[0:rows, :], in1=neghalf[0:rows, 0:1], op=ALU.pow),
                          ["ssum", "neghalf"], ["rstd"])
                    sc.op("dve", lambda e, rows=rows: e.scalar_tensor_tensor(out=tmpf[0:rows, :], in0=y1t[0:rows, :], scalar=rstd[0:rows, :],
                                                                             in1=Mp[0:rows, 0, :], op0=ALU.mult, op1=ALU.mult),
                          ["y1t", "rstd", "Mp"], ["tmpf"])
                    sc.op("dve", lambda e, rows=rows: e.tensor_tensor(out=hb[0:rows, :], in0=tmpf[0:rows, :], in1=Mp[0:rows, 1, :], op=ALU.add),
                          ["tmpf", "Mp"], ["hb"])
                    for k in range(8):
                        sc.op("pe", lambda e, k=k, rows=rows: e.transpose(out=ps_tr[:, k * 128:k * 128 + rows], in_=hb[0:rows, k * 128:(k + 1) * 128],
                                                                          identity=ident_b[0:rows, 0:rows]), ["hb", "ident_b"], ["ps_tr"])
                    sc.op("act", lambda e, rows=rows: e.activation(out=hT[:, :, 0:rows], in_=ps_tr[:].rearrange("p (k t) -> p k t", t=128)[:, :, 0:rows],
                                                                   func=AF.Copy), ["ps_tr"], ["hT"])
                    for g4 in range(4):
                        pst, pn = [(ps_a, "ps_a"), (ps_b, "ps_b")][g4 % 2]
                        for k in range(8):
                            sc.op("pe", lambda e, k=k, g4=g4, pst=pst, rows=rows: e.matmul(pst[0:rows, :], lhsT=hT[:, k, 0:rows],
                                                                                        rhs=wup[:, k, g4 * 512:(g4 + 1) * 512],
                                                                                        start=(k == 0), stop=(k == 7)), ["hT", "wup"], [pn])
                        sc.op("act", lambda e, pst=pst, rows=rows: e.activation(out=rl[0:rows, :], in_=pst[0:rows, :], func=AF.Relu), [pn], ["rl"])
                        sc.op("dve", lambda e, g4=g4, rows=rows: e.tensor_tensor(out=ub[0:rows, g4 * 512:(g4 + 1) * 512], in0=rl[0:rows, :], in1=rl[0:rows, :],
                                                                                op=ALU.mult), ["rl"], ["ub"])
                    for q2 in range(2):
                        for k in range(8):
                            sc.op("pe", lambda e, k=k, q2=q2, rows=rows: e.transpose(out=ps_tr[:, k * 128:k * 128 + rows],
                                                                                    in_=ub[0:rows, (q2 * 8 + k) * 128:(q2 * 8 + k + 1) * 128],
                                                                                    identity=ident_b[0:rows, 0:rows]), ["ub", "ident_b"], ["ps_tr"])
                        sc.op("act", lambda e, q2=q2, rows=rows: e.activation(out=uT[:, q2 * 8:(q2 + 1) * 8, 0:rows],
                                                                              in_=ps_tr[:].rearrange("p (k t) -> p k t", t=128)[:, :, 0:rows], func=AF.Copy),
                              ["ps_tr"], ["uT"])
                    if hf == 1:
                        sc.dma("sp", lambda e, yp_ap=yp_ap, rows=rows: e.dma_start(out=ypt[0:rows, :], in_=yp_ap), ["YP"], ["ypt"])
                    for half in range(2):
                        hs_ = slice(half * 512, (half + 1) * 512)
                        for k in range(16):
                            sc.op("pe", lambda e, k=k, hs_=hs_, rows=rows: e.matmul(ps_a[0:rows, :], lhsT=uT[:, k, 0:rows], rhs=wdn[:, k, hs_],
                                                                                  start=(k == 0), stop=(k == 15)), ["uT", "wdn"], ["ps_a"])
                        if hf == 0:
                            sc.op("act", lambda e, hs_=hs_, rows=rows: e.activation(out=ypt[0:rows, hs_], in_=ps_a[0:rows, :], func=AF.Copy),
                                  ["ps_a"], ["ypt"])
                        else:
                            sc.op("dve", lambda e, hs_=hs_, rows=rows: e.tensor_tensor(out=tmpf[0:rows, hs_], in0=ps_a[0:rows, :], in1=ypt[0:rows, hs_], op=ALU.add),
                                  ["ps_a", "ypt"], ["tmpf"])
                            sc.op("dve", lambda e, hs_=hs_, rows=rows: e.tensor_tensor(out=tmpf[0:rows, hs_], in0=tmpf[0:rows, hs_], in1=Mp[0:rows, 2, hs_], op=ALU.mult),
                                  ["tmpf", "Mp"], ["tmpf"])
                            sc.op("dve", lambda e, hs_=hs_, rows=rows: e.tensor_tensor(out=yt[0:rows, hs_], in0=tmpf[0:rows, hs_], in1=y1t[0:rows, hs_], op=ALU.add),
                                  ["tmpf", "y1t"], ["yt"])
                    if hf == 0:
                        sc.dma("sp", lambda e, yp_ap=yp_ap, rows=rows: e.dma_start(out=yp_ap, in_=ypt[0:rows, :]), ["ypt"], ["YP"])
                    else:
                        sc.dma("pool", lambda e, out_ap=out_ap, rows=rows: e.dma_start(out=out_ap, in_=yt[0:rows, :]), ["yt"], [])
            for hf in range(2):
                cast_weight(wup, "wup", lambda k, hf=hf: w_up[k * 128:(k + 1) * 128, hf * 2048:(hf + 1) * 2048], 8, 2048, wstg)
                cast_weight(wdn, "wdn", lambda k, hf=hf: w_down[hf * 2048 + k * 128:hf * 2048 + (k + 1) * 128, :], 16, 1024, wstg)
                kstate["k"] = None
                for (kind, idx, rows) in tiles:
                    do_tileE(hf, kind, idx, rows)
            sc.flush(st)
    return nc


def host_consts(cfg):
    NR = 1 + cfg.NSEQ
    sel = np.zeros((NR, 128 + cfg.NSR), np.float32)
    sel[0, 0:128] = 1.0
    for r in range(cfg.NSR):
        sel[1 + r // 8, 128 + r] = 1.0
    ki = np.arange(128)[:, None]; qi = np.arange(128)[None, :]
    tri = np.where(ki <= qi, 0.0, NEG).astype(np.float32)
    hs = np.arange(128)
    bt = ((hs[:, None] // 16 == hs[None, :] // 16) & (hs[:, None] % 16 < hs[None, :] % 16)).astype(np.float32)
    LS = cfg.PAST + 128
    vs = (np.arange(LS) < cfg.PAST + 8).astype(np.float32)
    out = {"sel5": sel, "ident": np.eye(128, dtype=np.float32), "tri_in": tri, "iota_in": np.arange(128, dtype=np.float32)[:, None],
           "bt_in": bt, "tvs_in": np.tile(vs.reshape(16, -1), (8, 1)), "pns_in": np.tile(((1 - vs) * NEG).reshape(16, -1), (8, 1)),
           "kwns_in": ((1 - vs) * NEG).reshape(128, -1)}
    def bucket(d):
        d = np.maximum(d, 0)
        far = 16 + (np.log(np.maximum(d, 1).astype(np.float32) / np.float32(16)) / np.float32(np.log(8.0)) * np.float32(16)).astype(np.int32)
        return np.where(d < 16, d, np.minimum(far, 31))
    dc = np.arange(4096) - 1856
    ohc = np.zeros((33, 4096), np.float32)
    ohc[bucket(dc), np.arange(4096)] = (dc >= 0)
    ohc[32] = (dc < 0)
    d1 = np.arange(768) - 128
    ok1 = (d1 >= 0) & (d1 <= 512)
    oh1 = np.zeros((33, 768), np.float32)
    oh1[bucket(d1), np.arange(768)] = ok1
    oh1[32] = ~ok1
    out["ohc_in"] = ohc; out["oh1_in"] = oh1
    out["ee_in"] = (np.arange(128)[:, None] == (np.arange(8192)[None, :] // 64)).astype(np.float32)

    def wmat(NT, NM):
        c0 = np.arange(NT * 128)[:, None] * 16
        s0 = np.arange(NM)[None, :] * 64
        sh = np.minimum(c0 + 32, s0 + 64) - np.maximum(c0, s0)
        return (np.maximum(sh, 0) / 32.0).astype(np.float32)
    out["wcp_in"] = wmat(cfg.NTp, cfg.NMp); out["wcs_in"] = wmat(cfg.NTs, cfg.NMs)
    ns = np.arange(cfg.NTs * 128)
    out["cnegs_in"] = np.where(ns >= cfg.PAST // 16 - 1, NEG, 0.0)[None, :]
    pos = cfg.PAST + np.arange(8)[:, None]; m = np.arange(cfg.NMs)[None, :]
    cur = pos // 64
    forced = (m == 0) | (m == cur) | (m == cur - 1)
    out["adds_in"] = np.where(m > cur, -1e30, np.where(forced, 100.0 + m, 0.0))
    return {k: np.ascontiguousarray(v, dtype=np.float32) for k, v in out.items()}


def make_in_maps(cfg, inp):
    S, NB = cfg.S, cfg.NB
    x = np.asarray(inp["x_prompt"], np.float32).reshape(S, D)
    consts = host_consts(cfg)
    pool_fox = np.asarray(inp["cache_fox_kv"], np.float32).reshape(-1, 1024)
    pool_lf = np.asarray(inp["cache_fox_logf"], np.float32).reshape(-1, 8)
    pool_nsa = np.asarray(inp["cache_nsa_kv"], np.float32).reshape(-1, 512)
    maps = []
    for c in range(NCORES):
        pad = (7 - c) * 128
        xfr = np.zeros((S, D), np.float32)
        xfr[pad:] = x[:S - pad]
        sl = slice(c * cfg.NSEQ, (c + 1) * cfg.NSEQ)
        m = {
            "xf": xfr,
            "xs": np.ascontiguousarray(np.asarray(inp["x_sample"], np.float32)[sl].reshape(cfg.NSR, D)),
            "cvec": np.concatenate([np.asarray(inp["c_prompt"], np.float32), np.asarray(inp["c_sample"], np.float32)[sl]], 0),
            "w_ada": np.asarray(inp["w_ada"], np.float32)[0],
            "b_ada": np.asarray(inp["b_ada"], np.float32),
            "g_norm": np.asarray(inp["g_norm"], np.float32)[0],
            "w_in": np.asarray(inp["w_in"], np.float32)[0],
            "b_forget": np.asarray(inp["b_forget"], np.float32),
            "g_qk_fox": np.asarray(inp["g_qk_fox"], np.float32)[0],
            "g_qk_nsa": np.asarray(inp["g_qk_nsa"], np.float32)[0],
            "rel_bias": np.asarray(inp["rel_bias"], np.float32), "pe_cmp": np.asarray(inp["pe_cmp"], np.float32)[0],
            "w_cmp1": np.asarray(inp["w_cmp1"], np.float32)[0], "w_cmp2": np.asarray(inp["w_cmp2"], np.float32)[0],
            "w_out_fox": np.asarray(inp["w_out_fox"], np.float32)[0], "w_out_nsa": np.asarray(inp["w_out_nsa"], np.float32)[0],
            "w_out": np.asarray(inp["w_out"], np.float32)[0], "w_up": np.asarray(inp["w_up"], np.float32)[0],
            "w_down": np.asarray(inp["w_down"], np.float32)[0],
            "win_in": np.ascontiguousarray(np.asarray(inp["state_nsa_win"], np.float32)[0, sl].reshape(cfg.NSEQ, -1, 256)),
        }
        vp = (np.arange(S) >= pad).astype(np.float32)
        m["tvp_in"] = np.ascontiguousarray(np.tile(vp.reshape(16, -1), (8, 1)))
        m["pnp_in"] = np.ascontiguousarray(np.tile(((1 - vp) * NEG).reshape(16, -1), (8, 1)).astype(np.float32))
        m["kwnp_in"] = np.ascontiguousarray(((1 - vp) * NEG).reshape(128, -1).astype(np.float32))
        npad = 8 * (7 - c)
        nf = np.arange(cfg.NTp * 128)
        m["cnegp_in"] = np.where((nf < npad) | (nf >= S // 16 - 1), NEG, 0.0).astype(np.float32)[None, :]
        m0 = 2 * (7 - c)
        jo_ = np.arange(cfg.NOWN)[:, None, None]; pi_ = np.arange(128)[None, :, None]; mm_ = np.arange(cfg.NMp)[None, None, :]
        cur_ = (128 * (8 * jo_ + 7) + pi_) // 64
        forced_ = (mm_ == m0) | (mm_ == cur_) | (mm_ == cur_ - 1)
        m["addp_in"] = np.ascontiguousarray(np.where((mm_ > cur_) | (mm_ < m0), -1e30, np.where(forced_, 100.0 + mm_, 0.0)).astype(np.float32))
        m["ptab"] = np.ascontiguousarray(np.asarray(inp["page_table"], np.int32)[sl])
        m["pool_fox"] = pool_fox; m["pool_lf"] = pool_lf; m["pool_nsa"] = pool_nsa
        if cfg.dbg_ob:
            obp = np.asarray(inp["dbg_ob_p"], np.float32).reshape(S, 512)
            m["dbg_ob_p_in"] = np.concatenate([obp[(8 * j + c) * 128:(8 * j + c + 1) * 128] for j in range(cfg.NOWN)], 0)
            m["dbg_ob_s_in"] = np.ascontiguousarray(np.asarray(inp["dbg_ob_s"], np.float32)[sl].reshape(cfg.NSR, 512))
        m.update(consts)
        maps.append(m)
    return maps


def run(cfg, inp):
    nc = build(cfg)
    maps = make_in_maps(cfg, inp)
    res = run_bass_kernel_spmd(nc, maps, core_ids=list(range(NCORES)))
    return res.results


def assemble(cfg, inp, R):
    S, NB, NOWN = cfg.S, cfg.NB, cfg.NOWN
    NSEQT = cfg.NSEQ * NCORES

    def own_scatter(key, width):
        out = np.zeros((S, width), np.float32)
        for c in range(NCORES):
            r = R[c][key]
            for j in range(NOWN):
                b = 8 * j + c
                out[b * 128:(b + 1) * 128] = r[j * 128:(j + 1) * 128]
        return out

    def cat(key):
        return np.concatenate([R[c][key] for c in range(NCORES)], 0)
    fkv_p = own_scatter("o_fkv_p", 1024).reshape(1, 1, S, 2, 8, 64)
    lf_p = own_scatter("o_lf_p", 8).reshape(1, 1, S, 8)
    nkv_p = own_scatter("o_nkv_p", 512).reshape(1, 1, S, 4, 2, 64)
    win_p = own_scatter("o_win_p", 256)[S - min(512, S):].reshape(1, 1, min(512, S), 2, 2, 64)
    fkv_s = cat("o_fkv_s").reshape(1, NSEQT, 8, 2, 8, 64)
    lf_s = cat("o_lf_s").reshape(1, NSEQT, 8, 8)
    nkv_s = cat("o_nkv_s").reshape(1, NSEQT, 8, 4, 2, 64)
    wnew = cat("o_wnew_s").reshape(NSEQT, 8, 2, 2, 64)
    win_s = cat("o_win_s").reshape(1, NSEQT, -1, 2, 2, 64)
    y_p = own_scatter("o_y_p", D).reshape(1, S, D)
    y_s = cat("o_y_s").reshape(NSEQT, 8, D)
    dbg = dict(oa_p=own_scatter("dbg_oa_p", 512), oa_s=cat("dbg_oa_s").astype(np.float32),
               ob_p=own_scatter("dbg_ob_p", 512), ob_s=cat("dbg_ob_s").astype(np.float32))
    return dict(dbg=dbg, y_p=y_p, y_s=y_s, fkv_p=fkv_p, lf_p=lf_p, nkv_p=nkv_p, win_p=win_p, fkv_s=fkv_s, lf_s=lf_s,
                nkv_s=nkv_s, wnew=wnew, win_s=win_s)


def kernel(**inputs):
    cfg = Cfg()
    R = run(cfg, inputs)
    A = assemble(cfg, inputs, R)
    names = ["y_p", "y_s", "fkv_p", "fkv_s", "lf_p", "lf_s", "nkv_p", "nkv_s", "win_p", "win_s"]
    return tuple(np.ascontiguousarray(A[n], dtype=np.float32) for n in names)
```
